# Optimizing a Trainium2 kernel written in Bass

```python
import math
import jax, jax.numpy as jnp
from jax import lax
import numpy as np


D_MODEL = 1024
BATCH = 8
SEQ = 4096
DEPTH = 1
DEC_BATCH = 2
DEC_SEQ = 8192
PAST_LEN = 128

HEAD_DIM = 64
D_MIX = D_MODEL
D_A = D_MIX // 2
D_B = D_MIX - D_A
HA = D_A // HEAD_DIM
HKV_A = 2
G_A = HA // HKV_A
HB = D_B // (2 * HEAD_DIM)
WINDOW = 128
BLK = 128
N_BUCKETS = 32
HALF_BUCKETS = N_BUCKETS // 2
MAX_EXACT = HALF_BUCKETS // 2
MAX_DIST = 128
ALPHA = (2.0 * DEPTH) ** 0.25
BETA = (8.0 * DEPTH) ** -0.25
LN_EPS = 1e-5
SUBLN_EPS = 1e-5
NEG_INF = -1e30
SPLIT_SIZES = (D_A, HKV_A * HEAD_DIM, HKV_A * HEAD_DIM, D_A, D_B, D_B, D_B, D_B)
SPLIT_OFFSETS = tuple(int(v) for v in np.cumsum(SPLIT_SIZES)[:-1])
D_IN = int(sum(SPLIT_SIZES))

kernel_name = 'hybrid_swa_diffattn_deepnorm_encoder'


def t5_bucket(rel):
    sign = jnp.where(rel > 0, HALF_BUCKETS, 0)
    n = jnp.abs(rel)
    nf = jnp.maximum(n, 1).astype(jnp.float32)
    large = MAX_EXACT + (jnp.log(nf / MAX_EXACT) / math.log(MAX_DIST / MAX_EXACT)
                         * (HALF_BUCKETS - MAX_EXACT)).astype(jnp.int32)
    large = jnp.minimum(large, HALF_BUCKETS - 1)
    return (sign + jnp.where(n < MAX_EXACT, n, large)).astype(jnp.int32)


def layer_norm(x, g, b):
    xf = x.astype(jnp.float32)
    mu = jnp.mean(xf, -1, keepdims=True)
    var = jnp.mean(jnp.square(xf - mu), -1, keepdims=True)
    y = (xf - mu) * lax.rsqrt(var + LN_EPS) * g.astype(jnp.float32) + b.astype(jnp.float32)
    return y.astype(x.dtype)


def rms_norm(x, w):
    xf = x.astype(jnp.float32)
    y = xf * lax.rsqrt(jnp.mean(jnp.square(xf), -1, keepdims=True) + SUBLN_EPS) * w.astype(jnp.float32)
    return y.astype(x.dtype)


def window_gqa(q, k, v, sink, rel_bias):
    B, S, _ = q.shape
    NB = S // BLK
    qb = q.reshape(B, NB, BLK, HKV_A, G_A, HEAD_DIM)
    pad = ((0, 0), (WINDOW, WINDOW), (0, 0), (0, 0))
    kp = jnp.pad(k.reshape(B, S, HKV_A, HEAD_DIM), pad).reshape(B, NB + 2, BLK, HKV_A, HEAD_DIM)
    vp = jnp.pad(v.reshape(B, S, HKV_A, HEAD_DIM), pad).reshape(B, NB + 2, BLK, HKV_A, HEAD_DIM)
    kw = jnp.concatenate([kp[:, :-2], kp[:, 1:-1], kp[:, 2:]], axis=2)
    vw = jnp.concatenate([vp[:, :-2], vp[:, 1:-1], vp[:, 2:]], axis=2)
    a = jnp.arange(BLK, dtype=jnp.int32)[:, None]
    kk = jnp.arange(3 * BLK, dtype=jnp.int32)[None, :]
    rel = kk - BLK - a
    bias = rel_bias[:, :HA][t5_bucket(rel)].astype(jnp.float32)
    bias = bias.transpose(2, 0, 1).reshape(HKV_A, G_A, BLK, 3 * BLK)
    kpos = jnp.arange(NB, dtype=jnp.int32)[:, None, None] * BLK + kk[None] - BLK
    valid = (jnp.abs(rel) <= WINDOW)[None] & (kpos >= 0) & (kpos < S)
    logits = jnp.einsum('bnqhgd,bnkhd->bnhgqk', qb, kw).astype(jnp.float32) * (HEAD_DIM ** -0.5)
    logits = jnp.where(valid[None, :, None, None], logits + bias[None, None], NEG_INF)
    s = sink.reshape(HKV_A, G_A)[None, None, :, :, None, None].astype(jnp.float32)
    m = jnp.maximum(jnp.max(logits, -1, keepdims=True), s)
    e = jnp.exp(logits - m)
    p = e / (jnp.sum(e, -1, keepdims=True) + jnp.exp(s - m))
    o = jnp.einsum('bnhgqk,bnkhd->bnqhgd', p.astype(v.dtype), vw)
    return o.reshape(B, S, HA * HEAD_DIM)


def diff_attention(q, k, v, lq1, lk1, lq2, lk2, subln_w, rel_bias, lambda_init):
    B, S, _ = q.shape
    NB = S // BLK
    E = 2 * HEAD_DIM
    qh = q.reshape(B, NB, BLK, HB, 2, HEAD_DIM).transpose(1, 0, 2, 3, 4, 5)
    kh = k.reshape(B, S, HB, 2, HEAD_DIM)
    vh = v.reshape(B, S, HB, E)
    lam = (jnp.exp(jnp.sum(lq1.astype(jnp.float32) * lk1.astype(jnp.float32)))
           - jnp.exp(jnp.sum(lq2.astype(jnp.float32) * lk2.astype(jnp.float32))) + lambda_init)
    table = rel_bias[:, HA:].astype(jnp.float32)
    kpos = jnp.arange(S, dtype=jnp.int32)

    def block(args):
        qb, i = args
        qpos = i * BLK + jnp.arange(BLK, dtype=jnp.int32)
        bias = table[t5_bucket(kpos[None, :] - qpos[:, None])].transpose(2, 0, 1)
        logits = jnp.einsum('bqhmd,bkhmd->bhmqk', qb, kh).astype(jnp.float32) * (HEAD_DIM ** -0.5)
        p = jax.nn.softmax(logits + bias[None, :, None], axis=-1)
        w = p[:, :, 0] - lam * p[:, :, 1]
        return jnp.einsum('bhqk,bkhe->bqhe', w.astype(vh.dtype), vh)

    o = lax.map(block, (qh, jnp.arange(NB, dtype=jnp.int32)))
    o = o.transpose(1, 0, 2, 3, 4).reshape(B, S, HB, E)
    o = rms_norm(o, subln_w) * (1.0 - lambda_init)
    return o.reshape(B, S, HB * E)


def hybrid_layer(x, c, w_in, w_out, w_ada, b_ada, ln_g, ln_b, sink,
                 lq1, lk1, lq2, lk2, subln_w, rel_bias, lambda_init):
    mod = jax.nn.silu(c) @ w_ada + b_ada
    shift, scale, gate = jnp.split(mod, 3, axis=-1)
    u = x * (1.0 + scale[:, None]) + shift[:, None]
    proj = u @ w_in
    qa, ka, va, ga, qb, kb, vb, gb = jnp.split(proj, SPLIT_OFFSETS, axis=-1)
    oa = window_gqa(qa, ka, va, sink, rel_bias) * jax.nn.silu(ga)
    ob = diff_attention(qb, kb, vb, lq1, lk1, lq2, lk2, subln_w, rel_bias, lambda_init) * jax.nn.silu(gb)
    h = jnp.concatenate([oa, ob], axis=-1) @ w_out
    return layer_norm(ALPHA * x + gate[:, None] * h, ln_g, ln_b)


def setup_inputs(seed: int = 0) -> dict:
    key = jax.random.key(seed)
    ks = jax.random.split(key, 20)
    f32 = jnp.float32
    col_scale = np.ones((D_IN,), np.float32)
    off = np.concatenate([[0], np.cumsum(SPLIT_SIZES)])
    for idx in (2, 6):
        col_scale[off[idx]:off[idx + 1]] = BETA
    w_in = jax.random.normal(ks[4], (DEPTH, D_MODEL, D_IN), f32) * (D_MODEL ** -0.5) * jnp.asarray(col_scale)
    return {
        'x_prompt': jax.random.normal(ks[0], (BATCH, SEQ, D_MODEL), f32),
        'x_sample': jax.random.normal(ks[1], (DEC_BATCH, DEC_SEQ, D_MODEL), f32),
        'c_prompt': jax.random.normal(ks[2], (BATCH, D_MODEL), f32),
        'c_sample': jax.random.normal(ks[3], (DEC_BATCH, D_MODEL), f32),
        'w_in': w_in,
        'w_out': jax.random.normal(ks[5], (DEPTH, D_MIX, D_MODEL), f32) * (D_MIX ** -0.5) * BETA,
        'w_ada': jax.random.normal(ks[6], (DEPTH, D_MODEL, 3 * D_MODEL), f32) * (D_MODEL ** -0.5),
        'b_ada': jax.random.normal(ks[7], (DEPTH, 3 * D_MODEL), f32) * 0.02,
        'ln_g': 1.0 + 0.02 * jax.random.normal(ks[8], (DEPTH, D_MODEL), f32),
        'ln_b': 0.02 * jax.random.normal(ks[9], (DEPTH, D_MODEL), f32),
        'attn_sink': jax.random.normal(ks[10], (DEPTH, HA), f32),
        'lambda_q1': 0.1 * jax.random.normal(ks[11], (DEPTH, HEAD_DIM), f32),
        'lambda_k1': 0.1 * jax.random.normal(ks[12], (DEPTH, HEAD_DIM), f32),
        'lambda_q2': 0.1 * jax.random.normal(ks[13], (DEPTH, HEAD_DIM), f32),
        'lambda_k2': 0.1 * jax.random.normal(ks[14], (DEPTH, HEAD_DIM), f32),
        'subln_w': 1.0 + 0.02 * jax.random.normal(ks[15], (DEPTH, 2 * HEAD_DIM), f32),
        'rel_bias': 0.5 * jax.random.normal(ks[16], (N_BUCKETS, HA + HB), f32),
    }


def reference(x_prompt, x_sample, c_prompt, c_sample, w_in, w_out, w_ada, b_ada,
              ln_g, ln_b, attn_sink, lambda_q1, lambda_k1, lambda_q2, lambda_k2,
              subln_w, rel_bias):
    xp = x_prompt
    xs = x_sample
    for l in range(DEPTH):
        lambda_init = 0.8 - 0.6 * math.exp(-0.3 * l)
        xp = hybrid_layer(xp, c_prompt, w_in[l], w_out[l], w_ada[l], b_ada[l], ln_g[l], ln_b[l],
                          attn_sink[l], lambda_q1[l], lambda_k1[l], lambda_q2[l], lambda_k2[l],
                          subln_w[l], rel_bias, lambda_init)
        xs = hybrid_layer(xs, c_sample, w_in[l], w_out[l], w_ada[l], b_ada[l], ln_g[l], ln_b[l],
                          attn_sink[l], lambda_q1[l], lambda_k1[l], lambda_q2[l], lambda_k2[l],
                          subln_w[l], rel_bias, lambda_init)
    y_prompt = xp
    y_sample = xs
    return (y_prompt, y_sample)
```

```python
import numpy as np
import concourse.bass as bass
import concourse.mybir as mybir
from concourse.bass_utils import run_bass_kernel_spmd

F32 = mybir.dt.float32
BF16 = mybir.dt.bfloat16
AF = mybir.ActivationFunctionType
ALU = mybir.AluOpType

NCORES = 8
D = 1024
DC = 8
SP_ = 4096
SS_ = 8192
SQS = 2048
ALPHA = 2.0 ** 0.25
LAMBDA_INIT = 0.2
BIG = 30000.0
LN_EPS = 1e-5
SUBLN_EPS = 1e-5
PERM = [0, 4, 1, 5, 2, 6, 3, 7]
MD = 1152


class Prog:
    ENG = ('sp', 'act', 'pe', 'dve', 'pool')

    def __init__(self, sems, dsems):
        self.sem = sems
        self.dsem = dsems
        self.dcnt = {k: 0 for k in dsems}
        self.base = {e: 0 for e in self.ENG}
        self.reset_block()

    def reset_block(self):
        self.ops = {e: [] for e in self.ENG}
        self.res = {}

    def op(self, eng, fn, reads=(), writes=(), dma=None):
        ops = self.ops[eng]
        idx = len(ops)
        if dma is not None:
            self.dcnt[dma] += 1
            ev = ('dma', dma, self.dcnt[dma])
        else:
            ev = (eng, idx)
        deps = {}

        def add(d, raw):
            if d[0] == 'dma':
                d = ('dma', d[1], self.dcnt[d[1]])
            if raw or d not in deps:
                deps[d] = raw or deps.get(d, False)

        def is_ps(k):
            return isinstance(k, str) and k.startswith('ps')

        ps_keys = {}
        for k in reads:
            if is_ps(k):
                ps_keys[k] = False
        for k in writes:
            if is_ps(k):
                ps_keys[k] = True
        for k, isw in ps_keys.items():
            r = self.res.get(k)
            if r is not None and r[0] is not None:
                add(r[0], (not isw) and r[2])
        for k in reads:
            if is_ps(k):
                continue
            r = self.res.get(k)
            if r is not None and r[0] is not None:
                add(r[0], True)
        for k in writes:
            if is_ps(k):
                continue
            r = self.res.get(k)
            if r is not None:
                if r[0] is not None:
                    add(r[0], False)
                for rv in r[1].values():
                    add(rv, False)
        for k in reads:
            if is_ps(k):
                continue
            r = self.res.setdefault(k, [None, {}])
            r[1][ev if dma is not None else eng] = ev
        for k in writes:
            if is_ps(k):
                continue
            self.res[k] = [ev, {}]
        for k, isw in ps_keys.items():
            self.res[k] = [ev, {}, isw]
        ops.append((fn, deps, ev, dma))
        return ev

    def emit(self, block):
        need = {e: set() for e in self.ENG}
        for e in self.ENG:
            for (fn, deps, ev, dma) in self.ops[e]:
                for d in deps:
                    if d[0] != 'dma':
                        need[d[0]].add(d[1])
        ordn = {}
        for e in self.ENG:
            s = sorted(need[e])
            ordn[e] = {idx: self.base[e] + i + 1 for i, idx in enumerate(s)}

        def body(en):
            def f(eng):
                seen = {}
                for idx, (fn, deps, ev, dma) in enumerate(self.ops[en]):
                    for d, raw in deps.items():
                        if d[0] == 'dma':
                            if dma is not None and d[1] == dma:
                                continue
                            key = ('dma', d[1])
                            val = 16 * d[2]
                            sem = self.dsem[d[1]]
                        else:
                            if d[0] == en:
                                if (not raw) or en == 'pe' or en == 'sp':
                                    continue
                            key = d[0]
                            val = ordn[d[0]][d[1]]
                            sem = self.sem[d[0]]
                        if seen.get(key, 0) >= val:
                            continue
                        eng.wait_ge(sem, val)
                        seen[key] = val
                    ins = fn(eng)
                    if dma is not None:
                        ins.then_inc(self.dsem[dma], 16)
                    elif idx in ordn[en]:
                        ins.then_inc(self.sem[en], 1)
            return f

        block.sync(body('sp'))
        block.scalar(body('act'))
        block.tensor(body('pe'))
        block.vector(body('dve'))
        block.gpsimd(body('pool'))
        for e in self.ENG:
            self.base[e] += len(ordn[e])
        self.reset_block()


class _CopyShim:
    def __init__(self, e):
        self.e = e

    def tensor_copy(self, out, in_):
        return self.e.copy(out=out, in_=in_)


def dap(t, off, dims):
    return bass.AP(t, off, [list(d) for d in dims])


class _Stop(Exception):
    pass


def build_program(dbg=False, kstop=0):
    nc = bass.Bass("TRN2", target_bir_lowering=False)
    if dbg:
        dbg_ot = nc.dram_tensor("dbg_ot", [128, 3072], BF16, kind="ExternalOutput")
        dbg_car = nc.dram_tensor("dbg_car", [128, 1024], BF16, kind="ExternalOutput")
        dbg_c = nc.dram_tensor("dbg_c", [128, 64], F32, kind="ExternalOutput")
        dbg_sg = nc.dram_tensor("dbg_sg", [128, 2048 + 1024], BF16, kind="ExternalOutput")

    def din(name, shape):
        return nc.dram_tensor(name, list(shape), F32, kind="ExternalInput")

    xp_d = din("xp", [SP_, D])
    xs_d = din("xs", [SS_, D])
    cT_d = din("cT", [128, 16])
    win_d = din("w_in", [D, 3328])
    wout_d = din("w_out", [D, D])
    wada_d = din("w_ada", [D, 3072])
    bcol_d = din("bcol", [128, 24])
    bgate_d = din("bgate", [1, 1024])
    lng_d = din("ln_g", [1, 1024])
    lnb_d = din("ln_b", [1, 1024])
    sink_d = din("sinkp", [1, 8])
    lamv_d = din("lamv", [1, 256])
    subw_d = din("subw", [1, 128])
    relfar_d = din("relfar", [1, 8])
    bd_d = din("bd_strip", [128, 4 * MD])
    bw_d = din("bw_strip", [128, 8 * 384])
    maskw_d = din("maskw", [128, 384])
    flags_d = din("flags", [1, 66])
    ident_d = din("identf", [128, 128])
    yp_d = nc.dram_tensor("yp", [SP_, D], F32, kind="ExternalOutput")
    ys_d = nc.dram_tensor("ys", [SQS, D], F32, kind="ExternalOutput")

    from contextlib import ExitStack
    es = ExitStack()

    def sb(name, shape, dt=F32):
        return es.enter_context(nc.sbuf_tensor("sb_" + name, list(shape), dt))

    def ps(name, shape, dt=F32):
        return es.enter_context(nc.psum_tensor("ps_" + name, list(shape), dt))

    with es:
        sems = {e: es.enter_context(nc.semaphore("s_" + e)) for e in ('act', 'pe', 'dve', 'pool')}
        dkeys = ['c0', 'wa', 'x0', 'x1', 'xb0', 'xb1', 'xb2', 'xb3', 'xb4', 'y0', 'y1', 'dbg']
        dsems = {k: es.enter_context(nc.semaphore("d_" + k)) for k in dkeys}
        P = Prog(sems, dsems)

        identf = sb("identf", [128, 128])
        identb = sb("identb", [128, 128], BF16)
        s1p = sb("s1p", [128, 16])
        shf = sb("shf", [128, 16])
        gate_bc = sb("gate_bc", [128, 2048])
        lng_bc = sb("lng_bc", [128, 1024])
        lnb_bc = sb("lnb_bc", [128, 1024])
        subw_bc = sb("subw_bc", [128, 128])
        neglam = sb("neglam", [128, 1])
        expsink = sb("expsink", [128, 8])
        cfar = sb("cfar", [128, 8])
        cdiff = sb("cdiff", [128, 4])
        farS = sb("farS", [128, 4 * 64])
        flags = sb("flags", [128, 66])
        negLR = sb("negLR", [128, 2])
        omf = sb("omf", [128, 2])
        BW = sb("BW", [128, 8 * 384], BF16)
        BXW = sb("BXW", [128, 2048], BF16)
        zero_c = sb("zero_c", [128, 1])
        eps_ln = sb("eps_ln", [128, 1])
        eps_sub = sb("eps_sub", [128, 1])

        psA = ps("psA", [128, 2048])
        psB = ps("psB", [128, 1536])
        psC = ps("psC", [128, 512])
        psA_b = psA.bitcast(BF16)
        psB_b = psB.bitcast(BF16)
        psC_b = psC.bitcast(BF16)

        def bankA(i):
            return "psA%d" % i

        def bankB(i):
            return "psB%d" % i

        with ExitStack() as es1:
            def sb1(name, shape, dt=F32):
                return es1.enter_context(nc.sbuf_tensor("sb_" + name, list(shape), dt))

            WA = sb1("WA", [128, 8 * 3072])
            cT = sb1("cT", [128, 16])
            sc = sb1("sc", [128, 16])
            sc_rep = sb1("sc_rep", [128, 2 * 8 * 128])
            bcol = sb1("bcol", [128, 24])
            bgate_bc = sb1("bgate_bc", [128, 1024])
            lamv = sb1("lamv", [128, 256])
            lamp = sb1("lamp", [128, 128])
            lams = sb1("lams", [128, 4])
            sinkb = sb1("sinkb", [128, 8])
            subw_raw = sb1("subw_raw", [128, 128])
            maskw = sb1("maskw", [128, 384])
            BWf = sb1("BWf", [128, 8 * 384])

            def ld(dst_t, src):
                P.op('sp', lambda e: e.dma_start(out=dst_t[:], in_=src), writes=[dst_t.name], dma='c0')

            ld(identf, ident_d.ap())
            ld(cT, cT_d.ap())
            ld(bcol, bcol_d.ap())
            ld(bgate_bc, dap(bgate_d, 0, [[0, 128], [1, 1024]]))
            ld(lng_bc, dap(lng_d, 0, [[0, 128], [1, 1024]]))
            ld(lnb_bc, dap(lnb_d, 0, [[0, 128], [1, 1024]]))
            ld(sinkb, dap(sink_d, 0, [[0, 128], [1, 8]]))
            ld(lamv, dap(lamv_d, 0, [[0, 128], [1, 256]]))
            ld(subw_raw, dap(subw_d, 0, [[0, 128], [1, 128]]))
            ld(cfar, dap(relfar_d, 0, [[0, 128], [1, 8]]))
            ld(flags, dap(flags_d, 0, [[0, 128], [1, 66]]))
            ld(BWf, bw_d.ap())
            ld(maskw, maskw_d.ap())
            for dc in range(8):
                P.op('sp', lambda e, dc=dc: e.dma_start(
                    out=WA[:, dc * 3072:(dc + 1) * 3072], in_=wada_d.ap()[dc * 128:(dc + 1) * 128, :]),
                    writes=[("WA", dc)], dma='wa')

            P.op('pool', lambda e: e.memset(zero_c[:], 0.0), writes=[zero_c.name])
            P.op('pool', lambda e: e.memset(eps_ln[:], LN_EPS), writes=[eps_ln.name])
            P.op('pool', lambda e: e.memset(eps_sub[:], SUBLN_EPS), writes=[eps_sub.name])
            P.op('dve', lambda e: e.tensor_copy(out=identb[:], in_=identf[:]),
                 reads=[identf.name], writes=[identb.name])
            P.op('act', lambda e: e.activation(out=sc[:], in_=cT[:], func=AF.Silu),
                 reads=[cT.name], writes=[sc.name])
            P.op('act', lambda e: e.activation(out=expsink[:], in_=sinkb[:], func=AF.Exp),
                 reads=[sinkb.name], writes=[expsink.name])
            P.op('dve', lambda e: e.tensor_tensor(out=lamp[:, 0:64], in0=lamv[:, 0:64], in1=lamv[:, 64:128], op=ALU.mult),
                 reads=[lamv.name], writes=["lamp0"])
            P.op('dve', lambda e: e.tensor_tensor(out=lamp[:, 64:128], in0=lamv[:, 128:192], in1=lamv[:, 192:256], op=ALU.mult),
                 reads=[lamv.name], writes=["lamp1"])
            P.op('dve', lambda e: e.reduce_sum(out=lams[:, 0:1], in_=lamp[:, 0:64], axis=mybir.AxisListType.X),
                 reads=["lamp0"], writes=["lams0"])
            P.op('dve', lambda e: e.reduce_sum(out=lams[:, 1:2], in_=lamp[:, 64:128], axis=mybir.AxisListType.X),
                 reads=["lamp1"], writes=["lams1"])
            P.op('act', lambda e: e.activation(out=lams[:, 2:4], in_=lams[:, 0:2], func=AF.Exp),
                 reads=["lams0", "lams1"], writes=["lams23"])
            P.op('dve', lambda e: e.tensor_tensor(out=neglam[:], in0=lams[:, 3:4], in1=lams[:, 2:3], op=ALU.subtract),
                 reads=["lams23"], writes=["neglam_a"])
            P.op('dve', lambda e: e.tensor_scalar(out=neglam[:], in0=neglam[:], scalar1=-LAMBDA_INIT, scalar2=None, op0=ALU.add),
                 reads=["neglam_a"], writes=[neglam.name])
            P.op('dve', lambda e: e.tensor_scalar(out=subw_bc[:], in0=subw_raw[:], scalar1=1.0 - LAMBDA_INIT, scalar2=None, op0=ALU.mult),
                 reads=[subw_raw.name], writes=[subw_bc.name])
            P.op('dve', lambda e: e.tensor_tensor(out=cdiff[:], in0=cfar[:, 0:4], in1=cfar[:, 4:8], op=ALU.subtract),
                 reads=[cfar.name], writes=[cdiff.name])
            for h in range(4):
                P.op('dve', lambda e, h=h: e.tensor_scalar(
                    out=farS[:, h * 64:(h + 1) * 64], in0=flags[:, 2:66],
                    scalar1=cdiff[:, h:h + 1], scalar2=cfar[:, 4 + h:5 + h], op0=ALU.mult, op1=ALU.add),
                    reads=[flags.name, cdiff.name, cfar.name], writes=[("farS", h)])
            P.op('dve', lambda e: e.tensor_scalar(out=negLR[:], in0=flags[:, 0:2], scalar1=-1.0, scalar2=BIG, op0=ALU.add, op1=ALU.mult),
                 reads=[flags.name], writes=[negLR.name])
            for h in range(8):
                P.op('dve', lambda e, h=h: e.tensor_tensor(
                    out=BWf[:, h * 384:(h + 1) * 384], in0=BWf[:, h * 384:(h + 1) * 384], in1=maskw[:], op=ALU.add),
                    reads=[BWf.name, maskw.name], writes=[("BWf", h)])
                P.op('pool', lambda e, h=h: e.tensor_copy(out=BW[:, h * 384:(h + 1) * 384], in_=BWf[:, h * 384:(h + 1) * 384]),
                     reads=[("BWf", h)], writes=[("BW", h)])
            P.op('dve', lambda e: e.tensor_scalar(
                out=dap(BXW, 0, [[2048, 128], [128, 8], [1, 128]]),
                in0=dap(BWf, 256, [[8 * 384, 128], [384, 8], [1, 128]]),
                scalar1=negLR[:, 0:1], scalar2=None, op0=ALU.add),
                reads=[("BWf", hh) for hh in range(8)] + [negLR.name], writes=["BXWL"])
            P.op('dve', lambda e: e.tensor_scalar(
                out=dap(BXW, 1024, [[2048, 128], [128, 8], [1, 128]]),
                in0=dap(BWf, 0, [[8 * 384, 128], [384, 8], [1, 128]]),
                scalar1=negLR[:, 1:2], scalar2=None, op0=ALU.add),
                reads=[("BWf", hh) for hh in range(8)] + [negLR.name], writes=["BXWR"])
            for b in range(2):
                for dc in range(8):
                    o = (b * 8 + dc) * 128
                    P.op('pool', lambda e, o=o, b=b, dc=dc: e.tensor_copy(
                        out=sc_rep[:, o:o + 128], in_=sc[:, dc * 2 + b:dc * 2 + b + 1].broadcast_to([128, 128])),
                        reads=[sc.name], writes=[("sc_rep", b, dc)])

            def mod_cols(e):
                ins = None
                for dc in range(8):
                    for fc in range(16):
                        ins = e.matmul(out=psC[:, fc * 2:fc * 2 + 2],
                                       lhsT=WA[:, dc * 3072 + fc * 128: dc * 3072 + fc * 128 + 128],
                                       rhs=sc[:, dc * 2:dc * 2 + 2],
                                       start=(dc == 0 and fc == 0), stop=(dc == 7), skip_group_check=True)
                return ins
            P.op('pe', mod_cols, reads=[("WA", dc) for dc in range(8)] + [sc.name], writes=["psC"])
            for b in range(2):
                P.op('dve', lambda e, b=b: e.tensor_tensor(
                    out=shf[:, b * 8:(b + 1) * 8], in0=psC[:, b:16:2],
                    in1=bcol[:, 0:8], op=ALU.add),
                    reads=["psC", bcol.name], writes=[("shf", b)])
                P.op('dve', lambda e, b=b: e.scalar_tensor_tensor(
                    out=s1p[:, b * 8:(b + 1) * 8], in0=psC[:, 16 + b:32:2], scalar=1.0, in1=bcol[:, 8:16],
                    op0=ALU.add, op1=ALU.add),
                    reads=["psC", bcol.name], writes=[("s1p", b)])
            for b in range(2):
                for n in range(2):
                    def gmm(e, b=b, n=n):
                        ins = None
                        for dc in range(8):
                            o = (b * 8 + dc) * 128
                            ins = e.matmul(out=psA[:, (b * 2 + n) * 512:(b * 2 + n + 1) * 512],
                                           lhsT=sc_rep[:, o:o + 128],
                                           rhs=WA[:, dc * 3072 + 2048 + n * 512: dc * 3072 + 2048 + (n + 1) * 512],
                                           start=(dc == 0), stop=(dc == 7))
                        return ins
                    P.op('pe', gmm, reads=[("WA", dc) for dc in range(8)] + [("sc_rep", b, dc) for dc in range(8)],
                         writes=[bankA(b * 2 + n)])
                    P.op('dve', lambda e, b=b, n=n: e.tensor_tensor(
                        out=gate_bc[:, b * 1024 + n * 512: b * 1024 + (n + 1) * 512],
                        in0=psA[:, (b * 2 + n) * 512:(b * 2 + n + 1) * 512],
                        in1=bgate_bc[:, n * 512:(n + 1) * 512], op=ALU.add),
                        reads=[bankA(b * 2 + n), bgate_bc.name], writes=[("gate_bc", b, n)])

            with nc.Block() as blk1:
                P.emit(blk1)

        WIs = sb("WIs", [128, 8 * 1280], BF16)
        WIg = sb("WIg", [128, 8 * 1024], BF16)
        WO = sb("WO", [128, 8 * 1024], BF16)
        BD = sb("BD", [128, 2 * MD], BF16)
        KT = sb("KT", [128, 8192], BF16)
        VA = sb("VA", [128, 64 * 130], BF16)
        CAR = sb("CAR", [128, 8192], BF16)
        FB = [sb("FB%d" % i, [128, 1024]) for i in range(2)]
        XBF = [sb("xbf%d" % i, [128, 1024], BF16) for i in range(4)]
        uT = sb("uT", [128, 8 * 768], BF16)
        QT = [sb("QT%d" % i, [128, 2 * 512], BF16) for i in range(2)]
        QAT = sb("QAT", [128, 4 * 512], BF16)
        SGb = [sb("SGb%d" % i, [128, 4 * 256], BF16) for i in range(2)]
        SGa = sb("SGa", [128, 4 * 512], BF16)
        KAT = sb("KAT", [128, 768], BF16)
        VAW = sb("VAW", [128, 6 * 2 * 66], BF16)
        OT = sb("OT", [128, 6 * 512], BF16)
        PT = [sb("PT%d" % i, [128, 1024], BF16) for i in range(2)]
        OBf = sb("OBf", [128, 4 * 2 * 128])
        OBn = sb("OBn", [128, 4 * 2 * 128], BF16)
        OAg = sb("OAg", [128, 4 * 512], BF16)
        BX = sb("BX", [128, 1024], BF16)
        BXf = sb("BXf", [128, 1024])
        rec = sb("rec", [128, 16])
        ssq = sb("ssq", [128, 8])
        rstd = sb("rstd", [128, 8])
        sqj = sb("sqj", [128, 128])
        den = sb("den", [128, 16])
        bnst = sb("bnst", [128, 12])
        mv = sb("mv", [128, 8])
        lrs = sb("lrs", [128, 4])

        xkeys = ['x0', 'x1']
        xbkeys = ['xb0', 'xb1', 'xb2', 'xb3', 'xb4']
        NPF = 3
        state = {'xslot': 0, 'evac': 0, 'yslot': 0}

        wl_state = {'i': 0}

        def stage_cast(src_ap, ncols, cast_fn, writes, reads_extra=(), dst_dims=None, engs=('pool', 'dve')):
            s_ = state['xslot'] % 2
            state['xslot'] += 1
            P.op('sp', lambda e, s_=s_: e.dma_start(
                out=(FB[s_][:, 0:ncols] if dst_dims is None else dap(FB[s_], 0, dst_dims)), in_=src_ap),
                 writes=[("FB", s_)], dma=xkeys[s_])
            eng = engs[wl_state['i'] % 2]
            wl_state['i'] += 1
            P.op(eng, lambda e, s_=s_: cast_fn(_CopyShim(e) if eng == 'act' else e, FB[s_]), reads=[("FB", s_)] + list(reads_extra), writes=writes)

        def load_static_weights():
            for dc in range(8):
                r0 = dc * 128
                def castA(e, fb, dc=dc):
                    e.tensor_copy(out=WIs[:, dc * 1280: dc * 1280 + 512], in_=fb[:, 0:512])
                    return e.tensor_copy(out=WIs[:, dc * 1280 + 1024: dc * 1280 + 1280], in_=fb[:, 512:768])
                stage_cast(win_d.ap()[r0:r0 + 128, 0:768], 768, castA, writes=["WIs"])
                def castB(e, fb, dc=dc):
                    return e.tensor_copy(out=WIs[:, dc * 1280 + 512: dc * 1280 + 1024], in_=fb[:, 0:512])
                stage_cast(win_d.ap()[r0:r0 + 128, 768:1280], 512, castB, writes=["WIs"])
                def castO(e, fb, dc=dc):
                    return e.tensor_copy(out=WO[:, dc * 1024:(dc + 1) * 1024], in_=fb[:, 0:1024])
                stage_cast(wout_d.ap()[r0:r0 + 128, :], 1024, castO, writes=["WO"])

        def load_kv_weights(heads):
            for hl, h in enumerate(heads):
                for dc in range(8):
                    src = dap(win_d, (dc * 128) * 3328 + 1792 + h * 128, [[3328, 128], [512, 2], [1, 128]])
                    def castG(e, fb, dc=dc, hl=hl):
                        return e.tensor_copy(
                            out=dap(WIg, dc * 1024 + 256 + hl * 128, [[8192, 128], [256, 2], [1, 128]]),
                            in_=dap(fb, 0, [[1024, 128], [128, 2], [1, 128]]))
                    stage_cast(src, 256, castG, writes=["WIgkv"], dst_dims=[[1024, 128], [128, 2], [1, 128]])

        def q_weight_thunks(heads):
            th = []
            for hl, h in enumerate(heads):
                for dc in range(8):
                    def t1(dc=dc, hl=hl, h=h):
                        src = dap(win_d, (dc * 128) * 3328 + 1280 + h * 128, [[3328, 128], [1536, 2], [1, 128]])
                        def castG(e, fb):
                            return e.tensor_copy(
                                out=dap(WIg, dc * 1024 + hl * 128, [[8192, 128], [768, 2], [1, 128]]),
                                in_=dap(fb, 0, [[1024, 128], [128, 2], [1, 128]]))
                        stage_cast(src, 256, castG, writes=["WIgq"], dst_dims=[[1024, 128], [128, 2], [1, 128]],
                                   engs=('act', 'dve'))
                    th.append(t1)
                def t2(hl=hl, h=h):
                    def castD1(e, fb):
                        return e.tensor_copy(out=BD[:, hl * MD: hl * MD + 1024], in_=fb[:, 0:1024])
                    stage_cast(bd_d.ap()[:, h * MD: h * MD + 1024], 1024, castD1, writes=[("BD", hl)], engs=('act', 'dve'))
                def t3(hl=hl, h=h):
                    def castD2(e, fb):
                        return e.tensor_copy(out=BD[:, hl * MD + 1024: hl * MD + MD], in_=fb[:, 0:MD - 1024])
                    stage_cast(bd_d.ap()[:, h * MD + 1024: (h + 1) * MD], MD - 1024, castD2, writes=[("BD", hl)],
                               engs=('act', 'dve'))
                th.append(t2)
                th.append(t3)
            return th

        P.op('pool', lambda e: e.memset(VA[:], 1.0), writes=["VA_all"])
        P.op('pool', lambda e: e.memset(VAW[:], 1.0), writes=["VAW_all"])
        load_static_weights()

        xb_state = {'k': 0, 'm': 0}

        def prefetch_x(src_d, row0):
            sl = xb_state['k'] % 4
            xb_state['k'] += 1
            P.op('pool', lambda e, sl=sl: e.dma_start(out=XBF[sl][:], in_=src_d.ap()[row0:row0 + 128, :]),
                 writes=[("xbf", sl)], dma=xbkeys[sl])
            return sl

        def make_uT_from(sl, bslot, b):
            par = xb_state['m'] % 2
            xb_state['m'] += 1
            if par == 0:
                evk, odk, evb, odb, evo, odo = bankA(0), "psC", psA_b, psC_b, 0, 0
            else:
                evk, odk, evb, odb, evo, odo = bankB(0), bankB(1), psB_b, psB_b, 0, 1024

            dve_dcs = [0, 2, 4, 6, 7]
            act_dcs = [1, 3, 5]

            def tsrc(dc):
                if dc in dve_dcs:
                    k = dve_dcs.index(dc)
                    return evb[:, evo + k * 128: evo + (k + 1) * 128]
                k = act_dcs.index(dc)
                return odb[:, odo + k * 128: odo + (k + 1) * 128]

            def tr(e):
                ins = None
                for dc in range(8):
                    ins = e.transpose(out=tsrc(dc), in_=XBF[sl][:, dc * 128:(dc + 1) * 128], identity=identb[:])
                return ins
            P.op('pe', tr, reads=[("xbf", sl), identb.name], writes=[evk, odk])
            for dc in range(8):
                o = dc * 768 + bslot * 128
                if dc in dve_dcs:
                    P.op('dve', lambda e, dc=dc, o=o: e.tensor_scalar(
                        out=uT[:, o:o + 128], in0=tsrc(dc),
                        scalar1=s1p[:, b * 8 + dc:b * 8 + dc + 1], scalar2=shf[:, b * 8 + dc:b * 8 + dc + 1],
                        op0=ALU.mult, op1=ALU.add),
                        reads=[evk], writes=[("uT", bslot, dc)])
                else:
                    P.op('act', lambda e, dc=dc, o=o: e.activation(
                        out=uT[:, o:o + 128], in_=tsrc(dc), func=AF.Identity,
                        bias=shf[:, b * 8 + dc:b * 8 + dc + 1], scale=s1p[:, b * 8 + dc:b * 8 + dc + 1]),
                        reads=[odk], writes=[("uT", bslot, dc)])

        def make_uT_list(src_d, items, b):
            slots = {}
            for i in range(min(NPF, len(items))):
                slots[i] = prefetch_x(src_d, items[i][1] * 128)
            for i, (bs, lb) in enumerate(items):
                if i + NPF < len(items):
                    slots[i + NPF] = prefetch_x(src_d, items[i + NPF][1] * 128)
                make_uT_from(slots[i], bs, b)

        def uT_keys(bslots):
            return [("uT", bs, dc) for bs in bslots for dc in range(8)]

        def proj_fm(W, pitch, col, bslot0, nb, pbank, reads_extra=()):
            def f(e):
                ins = None
                for dc in range(8):
                    ins = e.matmul(out=psA[:, pbank * 512: pbank * 512 + nb * 128],
                                   lhsT=W[:, dc * pitch + col: dc * pitch + col + 128],
                                   rhs=uT[:, dc * 768 + bslot0 * 128: dc * 768 + (bslot0 + nb) * 128],
                                   start=(dc == 0), stop=(dc == 7))
                return ins
            P.op('pe', f, reads=uT_keys(range(bslot0, bslot0 + nb)) + list(reads_extra), writes=[bankA(pbank)])

        def proj_tm(W, pitch, col, ncol, bslot, pbank, reads_extra=()):
            def f(e):
                ins = None
                for dc in range(8):
                    ins = e.matmul(out=psA[:, pbank * 512: pbank * 512 + ncol],
                                   lhsT=uT[:, dc * 768 + bslot * 128: dc * 768 + (bslot + 1) * 128],
                                   rhs=W[:, dc * pitch + col: dc * pitch + col + ncol],
                                   start=(dc == 0), stop=(dc == 7))
                return ins
            P.op('pe', f, reads=uT_keys([bslot]) + list(reads_extra), writes=[bankA(pbank)])

        pb_state = {'i': 0}

        def next_pbank():
            b = 1 + pb_state['i'] % 3
            pb_state['i'] += 1
            return b

        def evac_copy(out_ap, in_ap, reads, writes, scale=None, which=None):
            if which is None:
                which = state['evac'] % 2
                state['evac'] += 1
            if which == 0:
                if scale is None:
                    P.op('dve', lambda e: e.tensor_copy(out=out_ap, in_=in_ap), reads=reads, writes=writes)
                else:
                    P.op('dve', lambda e: e.tensor_scalar(out=out_ap, in0=in_ap, scalar1=scale, scalar2=None, op0=ALU.mult),
                         reads=reads, writes=writes)
            else:
                if scale is None:
                    P.op('act', lambda e: e.copy(out=out_ap, in_=in_ap), reads=reads, writes=writes)
                else:
                    P.op('act', lambda e: e.mul(out=out_ap, in_=in_ap, mul=scale), reads=reads, writes=writes)

        def run_job(x_d, y_d, b, S_kv, n_qt, is_sample, groups, next_first_heads):
            NB = S_kv // 128
            n_groups = len(groups)
            for gi, heads in enumerate(groups):
                G = len(heads)
                final = (gi == n_groups - 1)
                if not state.get('kv_preloaded'):
                    load_kv_weights(heads)
                state['kv_preloaded'] = False
                qw = q_weight_thunks(heads)
                nxt_heads = groups[gi + 1] if gi + 1 < n_groups else next_first_heads

                NT = NB // 2

                kv_slots = {}
                for blk in range(NPF):
                    kv_slots[blk] = prefetch_x(x_d, blk * 128)

                def front(n):
                    base = (n % 3) * 2
                    for k in range(2):
                        blk = 2 * n + k
                        if blk + NPF < NB:
                            kv_slots[blk + NPF] = prefetch_x(x_d, (blk + NPF) * 128)
                        make_uT_from(kv_slots[blk], base + k, b)

                def back(n):
                    base = (n % 3) * 2
                    for hl in range(G):
                        pbk = next_pbank()
                        proj_fm(WIg, 1024, 256 + hl * 128, base, 2, pbk, reads_extra=["WIgkv"])
                        evac_copy(KT[:, hl * S_kv + n * 256: hl * S_kv + (n + 1) * 256],
                                  psA[:, pbk * 512: pbk * 512 + 256],
                                  reads=[bankA(pbk)], writes=[("KT", hl, 2 * n), ("KT", hl, 2 * n + 1)])
                    for k in range(2):
                        tb = 2 * n + k
                        pbk = next_pbank()
                        proj_tm(WIg, 1024, 512, G * 128, base + k, pbk, reads_extra=["WIgkv"])
                        state['evac'] += 1
                        for hl in range(G):
                            o = (tb * G + hl) * 130
                            evac_copy(VA[:, o:o + 128], psA[:, pbk * 512 + hl * 128: pbk * 512 + (hl + 1) * 128],
                                      reads=[bankA(pbk), "VA_all"], writes=[("VA", hl, tb)], which=state['evac'] % 2)

                for n in range(NT + 1):
                    if n < NT:
                        front(n)
                    if kstop == 22:
                        raise _Stop()
                    if n >= 1:
                        back(n - 1)
                    if n >= 1 and qw:
                        qw.pop(0)()
                while qw:
                    qw.pop(0)()
                    if kstop == 23 and n == 1:
                        raise _Stop()
                    if kstop == 2 and n == 2:
                        raise _Stop()
                if kstop == 3:
                    raise _Stop()

                for t in range(n_qt):
                    if t == n_qt - 1 and nxt_heads is not None:
                        load_kv_weights(nxt_heads)
                        state['kv_preloaded'] = True
                    q_phase(x_d, y_d, b, S_kv, NB, t, n_qt, is_sample, heads, final, gi, groups)
                    if kstop == 4:
                        raise _Stop()
                    if kstop == 5 and final:
                        raise _Stop()
                if kstop == 6 and final:
                    raise _Stop()

        def tile_blocks(t, n_qt, NB, is_sample, final):
            blocks = []
            if final:
                if t > 0:
                    blocks.append((0, 4 * t - 1))
                elif is_sample:
                    blocks.append((0, NB - 1))
                for i in range(4):
                    blocks.append((1 + i, 4 * t + i))
                if t < n_qt - 1 or is_sample:
                    blocks.append((5, 4 * t + 4))
            else:
                for i in range(4):
                    blocks.append((1 + i, 4 * t + i))
            return blocks

        def prepA_steps(x_d, b, t, n_qt, NB, is_sample, final, G):
            blocks = tile_blocks(t, n_qt, NB, is_sample, final)
            par = t % 2
            n = len(blocks)
            slots = {}
            steps = []

            def pf(i):
                slots[i] = prefetch_x(x_d, blocks[i][1] * 128)

            def s_pf0():
                for i in range(min(NPF, n)):
                    pf(i)
            steps.append((s_pf0, 0))
            for i, (bs, lb) in enumerate(blocks):
                def s_tr(i=i):
                    if i + NPF < n:
                        pf(i + NPF)
                    sl = slots[i]

                    def tr(e):
                        ins = None
                        for dc in range(8):
                            ins = e.transpose(out=psC_b[:, dc * 128:(dc + 1) * 128], in_=XBF[sl][:, dc * 128:(dc + 1) * 128],
                                              identity=identb[:])
                        return ins
                    P.op('pe', tr, reads=[("xbf", sl), identb.name], writes=["psC"])

                def s_ev(bs=bs):
                    for dc in range(8):
                        o = dc * 768 + bs * 128
                        P.op('dve', lambda e, dc=dc, o=o: e.tensor_scalar(
                            out=uT[:, o:o + 128], in0=psC_b[:, dc * 128:(dc + 1) * 128],
                            scalar1=s1p[:, b * 8 + dc:b * 8 + dc + 1], scalar2=shf[:, b * 8 + dc:b * 8 + dc + 1],
                            op0=ALU.mult, op1=ALU.add),
                            reads=["psC"], writes=[("uT", bs, dc)])
                steps.append((s_tr, 0))
                steps.append((s_ev, 3))

            def fm_part(col, dcs):
                def f(e):
                    ins = None
                    for dc in dcs:
                        ins = e.matmul(out=psC[:, 0:512],
                                       lhsT=WIg[:, dc * 1024 + col: dc * 1024 + col + 128],
                                       rhs=uT[:, dc * 768 + 128: dc * 768 + 640],
                                       start=(dc == 0), stop=(dc == 7))
                    return ins
                P.op('pe', f, reads=uT_keys(range(1, 5)) + ["WIgq"], writes=["psC"])

            for hl in range(G):
                steps.append((lambda hl=hl: fm_part(hl * 128, range(0, 4)), 0))
                steps.append((lambda hl=hl: fm_part(hl * 128, range(4, 8)), 0))

                def s_qe(hl=hl):
                    P.op('dve', lambda e: e.tensor_scalar(out=QT[par][:, hl * 512:(hl + 1) * 512], in0=psC[:, 0:512],
                                                          scalar1=0.125, scalar2=None, op0=ALU.mult),
                         reads=["psC"], writes=[("QT", par, hl)])
                steps.append((s_qe, 2))
            for qb in range(4):
                def s_g(qb=qb):
                    def f(e):
                        ins = None
                        for dc in range(8):
                            ins = e.matmul(out=psC[:, 0:G * 128],
                                           lhsT=uT[:, dc * 768 + (1 + qb) * 128: dc * 768 + (2 + qb) * 128],
                                           rhs=WIg[:, dc * 1024 + 768: dc * 1024 + 768 + G * 128],
                                           start=(dc == 0), stop=(dc == 7))
                        return ins
                    P.op('pe', f, reads=uT_keys([1 + qb]) + ["WIgq"], writes=["psC"])

                def s_ge(qb=qb):
                    P.op('dve', lambda e: e.tensor_copy(out=SGb[par][:, qb * 256: qb * 256 + G * 128], in_=psC[:, 0:G * 128]),
                         reads=["psC"], writes=[("SGb", par, qb)])
                steps.append((s_g, 0))
                steps.append((s_ge, 1))
            return steps

        def bg_drain(bg):
            while bg:
                fn, _ = bg.pop(0)
                fn()

        def q_phase(x_d, y_d, b, S_kv, NB, t, n_qt, is_sample, heads, final, gi, groups):
            G = len(heads)
            par = t % 2
            blocks = tile_blocks(t, n_qt, NB, is_sample, final)
            has_left = any(bs == 0 for bs, _ in blocks)
            has_right = any(bs == 5 for bs, _ in blocks)
            if t == 0:
                make_uT_list(x_d, blocks, b)
                for hl in range(G):
                    pbk = next_pbank()
                    proj_fm(WIg, 1024, hl * 128, 1, 4, pbk, reads_extra=["WIgq"])
                    evac_copy(QT[par][:, hl * 512:(hl + 1) * 512], psA[:, pbk * 512:(pbk + 1) * 512],
                              reads=[bankA(pbk)], writes=[("QT", par, hl)], scale=0.125)
                for qb in range(4):
                    pbk = next_pbank()
                    proj_tm(WIg, 1024, 768, G * 128, 1 + qb, pbk, reads_extra=["WIgq"])
                    evac_copy(SGb[par][:, qb * 256: qb * 256 + G * 128], psA[:, pbk * 512: pbk * 512 + G * 128],
                              reads=[bankA(pbk)], writes=[("SGb", par, qb)])

            if final:
                for c in range(4):
                    pbk = next_pbank()
                    proj_fm(WIs, 1280, c * 128, 1, 4, pbk, reads_extra=["WIs"])
                    evac_copy(QAT[:, c * 512:(c + 1) * 512], psA[:, pbk * 512:(pbk + 1) * 512],
                              reads=[bankA(pbk)], writes=[("QAT", c)], scale=0.125)
                for qb in range(4):
                    pbk = next_pbank()
                    proj_tm(WIs, 1280, 512, 512, 1 + qb, pbk, reads_extra=["WIs"])
                    P.op('act', lambda e, qb=qb, pbk=pbk: e.activation(
                        out=SGa[:, qb * 512:(qb + 1) * 512], in_=psA[:, pbk * 512:(pbk + 1) * 512], func=AF.Silu),
                        reads=[bankA(pbk)], writes=[("SGa", qb)])
                bs0 = blocks[0][0]
                nbk = len(blocks)
                pieces = []
                s0 = bs0
                rem = nbk
                while rem > 0:
                    n = min(4, rem)
                    pieces.append((s0, n))
                    s0 += n
                    rem -= n
                for (s0, n) in pieces:
                    pbk = next_pbank()
                    proj_fm(WIs, 1280, 1024, s0, n, pbk, reads_extra=["WIs"])
                    evac_copy(KAT[:, s0 * 128:(s0 + n) * 128], psA[:, pbk * 512: pbk * 512 + n * 128],
                              reads=[bankA(pbk)], writes=["KAT"])
                for bs, lb in blocks:
                    pbk = next_pbank()
                    proj_tm(WIs, 1280, 1152, 128, bs, pbk, reads_extra=["WIs"])
                    P.op('dve', lambda e, bs=bs, pbk=pbk: e.tensor_copy(
                        out=dap(VAW, bs * 132, [[VAW_pitch, 128], [66, 2], [1, 64]]),
                        in_=dap(psA, pbk * 512, [[2048, 128], [64, 2], [1, 64]])),
                        reads=[bankA(pbk), "VAW_all"], writes=[("VAW", bs)])
                window_attn(t, n_qt, is_sample, blocks, has_left, has_right)

            bg = []
            if state.get('pending_post'):
                bg += state['pending_post']
                state['pending_post'] = None
            if t + 1 < n_qt:
                bg += prepA_steps(x_d, b, t + 1, n_qt, NB, is_sample, final, G)
            for hl, h in enumerate(heads):
                diff_attn(S_kv, NB, t, n_qt, is_sample, hl, h, G, bg)
            bg_drain(bg)
            post = post_diff_steps(t, heads, G, final, gi, groups, n_qt)
            if final or t == n_qt - 1:
                bg_drain(post)
            else:
                state['pending_post'] = post
            if dbg and final and (not is_sample) and t == 0:
                P.op('sp', lambda e: e.dma_start(out=dbg_ot.ap(), in_=OT[:]),
                     reads=[("OT", fc) for fc in range(6)], writes=["dbg1"], dma='dbg')
                P.op('sp', lambda e: e.dma_start(out=dbg_car.ap()[:, 0:512], in_=CAR[:, 0:512]),
                     reads=[("CAR", 0, 0)], writes=["dbg2"], dma='dbg')
                P.op('sp', lambda e: e.dma_start(out=dbg_car.ap()[:, 512:1024], in_=CAR[:, 4096:4608]),
                     reads=[("CAR", 1, 0)], writes=["dbg3"], dma='dbg')
                P.op('sp', lambda e: e.dma_start(out=dbg_c.ap()[:, 0:16], in_=s1p[:]), writes=["dbg4"], dma='dbg')
                P.op('sp', lambda e: e.dma_start(out=dbg_c.ap()[:, 16:32], in_=shf[:]), writes=["dbg5"], dma='dbg')
                P.op('sp', lambda e: e.dma_start(out=dbg_c.ap()[:, 32:33], in_=neglam[:], allow_slow_non_contiguous=True), writes=["dbg6"], dma='dbg')
                P.op('sp', lambda e: e.dma_start(out=dbg_c.ap()[:, 33:41], in_=expsink[:]), writes=["dbg7"], dma='dbg')
                P.op('sp', lambda e: e.dma_start(out=dbg_c.ap()[:, 41:49], in_=cfar[:]), writes=["dbg8"], dma='dbg')
                P.op('sp', lambda e: e.dma_start(out=dbg_sg.ap()[:, 0:2048], in_=SGa[:]),
                     reads=[("SGa", qb) for qb in range(4)], writes=["dbg9"], dma='dbg')
                P.op('sp', lambda e: e.dma_start(out=dbg_sg.ap()[:, 2048:3072], in_=SGb[0][:]),
                     reads=[("SGb", 0, qb) for qb in range(4)], writes=["dbg10"], dma='dbg')
            if final:
                finalize(x_d, y_d, b, t, n_qt, heads, gi, groups)

        VAW_pitch = 6 * 2 * 66

        def window_attn(t, n_qt, is_sample, blocks, has_left, has_right):
            present = {bs for bs, _ in blocks}
            wrapL = is_sample and t == 0
            wrapR = is_sample and t == n_qt - 1
            def win_chunk(c):
                par = c % 2
                if par == 0:
                    accU = [(psB, 0, bankB(0)), (psB, 512, bankB(1))]
                else:
                    accU = [(psB, 1024, bankB(2)), (psC, 0, "psC")]
                jbs = [jb for jb in range(6) if jb in present]
                first = [True, True]

                def geom(jb):
                    qlo = max(0, jb - 2)
                    qhi = min(3, jb)
                    nq = qhi - qlo + 1
                    return qlo, nq, nq * 128, (qlo - (jb - 2)) * 128

                def qk(jb, slot, c=c):
                    qlo, nq, N, so = geom(jb)
                    use_wrap = (wrapL and jb == 0) or (wrapR and jb == 5)

                    def f(e):
                        ins = None
                        for u in range(2):
                            e.matmul(out=psA[:, (slot * 2 + u) * 512: (slot * 2 + u) * 512 + N],
                                     lhsT=KAT[u * 64:(u + 1) * 64, jb * 128:(jb + 1) * 128],
                                     rhs=QAT[u * 64:(u + 1) * 64, c * 512 + qlo * 128: c * 512 + qlo * 128 + N],
                                     start=True, stop=False)
                        for u in range(2):
                            hp = 2 * c + u
                            if use_wrap:
                                wo = 0 if jb == 0 else 1024
                                rhs = BXW[:, wo + hp * 128: wo + hp * 128 + 128]
                            else:
                                rhs = BW[:, hp * 384 + so: hp * 384 + so + N]
                            ins = e.matmul(out=psA[:, (slot * 2 + u) * 512: (slot * 2 + u) * 512 + N],
                                           lhsT=identb[:], rhs=rhs, start=False, stop=True)
                        return ins
                    rd = ["KAT", ("QAT", c), identb.name]
                    rd += [] if use_wrap else [("BW", 2 * c), ("BW", 2 * c + 1)]
                    P.op('pe', f, reads=rd, writes=[bankA(slot * 2), bankA(slot * 2 + 1)])

                def ex(jb, slot):
                    qlo, nq, N, so = geom(jb)
                    P.op('act', lambda e: e.activation(
                        out=dap(PT[slot], 0, [[1024, 128], [384, 2], [1, N]]),
                        in_=dap(psA, slot * 1024, [[2048, 128], [512, 2], [1, N]]), func=AF.Exp),
                        reads=[bankA(slot * 2), bankA(slot * 2 + 1)], writes=[("PT", slot)])

                def pv(jb, slot):
                    qlo, nq, N, so = geom(jb)
                    sts = []
                    for u in range(2):
                        for qi in range(nq):
                            sts.append(first[u])
                            first[u] = False
                    sts = tuple(sts)

                    def f(e):
                        ins = None
                        k = 0
                        for u in range(2):
                            pt_, base, _ = accU[u]
                            for qi in range(nq):
                                qb = qlo + qi
                                off = base + qb * 66
                                ins = e.matmul(out=pt_[:, off: off + 65],
                                               lhsT=PT[slot][:, u * 384 + qi * 128: u * 384 + (qi + 1) * 128],
                                               rhs=VAW[:, (jb * 2 + u) * 66: (jb * 2 + u) * 66 + 65],
                                               start=sts[k], stop=True, skip_group_check=True)
                                k += 1
                        return ins
                    P.op('pe', f, reads=[("PT", slot), ("VAW", jb)], writes=[accU[0][2], accU[1][2]])

                qk(jbs[0], 0)
                for idx, jb in enumerate(jbs):
                    if idx + 1 < len(jbs):
                        qk(jbs[idx + 1], (idx + 1) % 2)
                    ex(jb, idx % 2)
                    pv(jb, idx % 2)
                for u in range(2):
                    pt_, base, bk = accU[u]
                    pitch = 1536 if pt_ is psB else 512
                    P.op('dve', lambda e, u=u, pt_=pt_, base=base, pitch=pitch, c=c, par=par: e.tensor_scalar(
                        out=den[:, par * 8 + u * 4: par * 8 + u * 4 + 4], in0=dap(pt_, base + 64, [[pitch, 128], [66, 4]]),
                        scalar1=expsink[:, 2 * c + u:2 * c + u + 1], scalar2=None, op0=ALU.add),
                        reads=[bk], writes=[("den", par, u)])
                P.op('dve', lambda e, par=par: e.reciprocal(out=rec[:, par * 8: par * 8 + 8], in_=den[:, par * 8: par * 8 + 8]),
                     reads=[("den", par, 0), ("den", par, 1)], writes=[("recw", par)])
                for u in range(2):
                    pt_, base, bk = accU[u]
                    for qb in range(4):
                        a = u * 4 + qb
                        off = base + qb * 66
                        hp = 2 * c + u
                        P.op('dve', lambda e, a=a, off=off, hp=hp, qb=qb, pt_=pt_, par=par: e.scalar_tensor_tensor(
                            out=OAg[:, qb * 512 + hp * 64: qb * 512 + hp * 64 + 64],
                            in0=pt_[:, off: off + 64], scalar=rec[:, par * 8 + a: par * 8 + a + 1],
                            in1=SGa[:, qb * 512 + hp * 64: qb * 512 + hp * 64 + 64],
                            op0=ALU.mult, op1=ALU.mult),
                            reads=[bk, ("recw", par), ("SGa", qb)], writes=[("OAg", qb, hp)])

            for c in range(4):
                win_chunk(c)
            for fc in range(4):
                def trw(e, fc=fc):
                    ins = None
                    for qb in range(4):
                        ins = e.transpose(out=psC_b[:, qb * 128:(qb + 1) * 128],
                                          in_=OAg[:, qb * 512 + fc * 128: qb * 512 + (fc + 1) * 128],
                                          identity=identb[:])
                    return ins
                P.op('pe', trw, reads=[("OAg", qb, hp) for qb in range(4) for hp in (2 * fc, 2 * fc + 1)] + [identb.name],
                     writes=["psC"])
                P.op('dve', lambda e, fc=fc: e.tensor_copy(out=OT[:, fc * 512:(fc + 1) * 512], in_=psC_b[:, 0:512]),
                     reads=["psC"], writes=[("OT", fc)])

        def diff_attn(S_kv, NB, t, n_qt, is_sample, hl, h, G, bg):
            par = t % 2
            kinds = []
            for i in range(NB):
                own = (not is_sample) or i < 16
                if own:
                    d = 4 * t - i
                    if -4 <= d <= 1:
                        kinds.append(('near', d))
                    elif d >= 2:
                        kinds.append(('far', cfar[:, h:h + 1], cfar.name))
                    else:
                        kinds.append(('far', cfar[:, 4 + h:5 + h], cfar.name))
                else:
                    if i == 16 and t == 3:
                        kinds.append(('wrapR',))
                    elif i == NB - 1 and t == 0:
                        kinds.append(('wrapL',))
                    else:
                        kinds.append(('far', farS[:, h * 64 + i: h * 64 + i + 1], ("farS", h)))
            if is_sample and t == 0:
                P.op('dve', lambda e: e.tensor_scalar(
                    out=BXf[:, 0:512], in0=BD[:, hl * MD + 640: hl * MD + 1152],
                    scalar1=cfar[:, 4 + h:5 + h], scalar2=flags[:, 0:1], op0=ALU.subtract, op1=ALU.mult),
                    reads=[("BD", hl), cfar.name, flags.name], writes=["BXf"])
                P.op('dve', lambda e: e.tensor_scalar(
                    out=BX[:, 0:512], in0=BXf[:, 0:512], scalar1=cfar[:, 4 + h:5 + h], scalar2=None, op0=ALU.add),
                    reads=["BXf"], writes=["BX"])
            if is_sample and t == 3:
                P.op('dve', lambda e: e.tensor_scalar(
                    out=BXf[:, 512:1024], in0=BD[:, hl * MD: hl * MD + 512],
                    scalar1=cfar[:, h:h + 1], scalar2=flags[:, 1:2], op0=ALU.subtract, op1=ALU.mult),
                    reads=[("BD", hl), cfar.name, flags.name], writes=["BXf"])
                P.op('dve', lambda e: e.tensor_scalar(
                    out=BX[:, 512:1024], in0=BXf[:, 512:1024], scalar1=cfar[:, h:h + 1], scalar2=None, op0=ALU.add),
                    reads=["BXf"], writes=["BX"])

            def qk(i):
                kind = kinds[i]
                slot = i % 2

                def f(e):
                    ins = None
                    near = kind[0] != 'far'
                    for m in range(2):
                        ins = e.matmul(out=psA[:, (slot * 2 + m) * 512:(slot * 2 + m + 1) * 512],
                                       lhsT=KT[m * 64:(m + 1) * 64, hl * S_kv + i * 128: hl * S_kv + (i + 1) * 128],
                                       rhs=QT[par][m * 64:(m + 1) * 64, hl * 512:(hl + 1) * 512],
                                       start=True, stop=not near)
                    if near:
                        if kind[0] == 'near':
                            o = hl * MD + 128 * (kind[1] + 4)
                            rhs = BD[:, o:o + 512]
                        elif kind[0] == 'wrapL':
                            rhs = BX[:, 0:512]
                        else:
                            rhs = BX[:, 512:1024]
                        for m in range(2):
                            ins = e.matmul(out=psA[:, (slot * 2 + m) * 512:(slot * 2 + m + 1) * 512],
                                           lhsT=identb[:], rhs=rhs, start=False, stop=True)
                    return ins
                rd = [("KT", hl, i), ("QT", par, hl)]
                if kind[0] == 'near':
                    rd += [("BD", hl), identb.name]
                elif kind[0] != 'far':
                    rd += ["BX", identb.name]
                P.op('pe', f, reads=rd, writes=[bankA(slot * 2), bankA(slot * 2 + 1)])

            def ex(i):
                kind = kinds[i]
                slot = i % 2
                if kind[0] == 'far':
                    bias_ap, bkey = kind[1], kind[2]
                else:
                    bias_ap, bkey = zero_c[:], zero_c.name
                P.op('act', lambda e: e.activation(out=PT[slot][:], in_=psA[:, slot * 1024:(slot + 1) * 1024],
                                                   func=AF.Exp, bias=bias_ap, scale=1.0),
                     reads=[bankA(slot * 2), bankA(slot * 2 + 1), bkey], writes=[("PT", slot)])

            def pv(i):
                slot = i % 2

                def f(e):
                    ins = None
                    for m in range(2):
                        for qb in range(4):
                            a = m * 4 + qb
                            bank = a // 3
                            off = bank * 512 + (a % 3) * 130
                            vo = (i * G + hl) * 130
                            ins = e.matmul(out=psB[:, off:off + 129],
                                           lhsT=PT[slot][:, m * 512 + qb * 128: m * 512 + (qb + 1) * 128],
                                           rhs=VA[:, vo:vo + 129],
                                           start=(i == 0 and a % 3 == 0), stop=(i == NB - 1), skip_group_check=True)
                    return ins
                P.op('pe', f, reads=[("PT", slot), ("VA", hl, i)], writes=[bankB(0), bankB(1), bankB(2)])

            qk(0)
            qk(1)
            cool = 2
            for i in range(NB):
                ex(i)
                if i + 2 < NB:
                    qk(i + 2)
                pv(i)
                if bg:
                    if cool <= 0:
                        fn, cool = bg.pop(0)
                        fn()
                    else:
                        cool -= 1
            for bank in range(3):
                na = 3 if bank < 2 else 2
                P.op('dve', lambda e, bank=bank, na=na: e.reciprocal(
                    out=rec[:, bank * 3: bank * 3 + na], in_=dap(psB, bank * 512 + 128, [[1536, 128], [130, na]])),
                    reads=[bankB(bank)], writes=[("recd", bank)])
            P.op('dve', lambda e: e.tensor_scalar(out=rec[:, 8:12], in0=rec[:, 4:8], scalar1=neglam[:, 0:1], scalar2=None, op0=ALU.mult),
                 reads=[("recd", 1), ("recd", 2), neglam.name], writes=["recn"])
            for qb in range(4):
                a0 = qb
                o0 = (a0 // 3) * 512 + (a0 % 3) * 130
                oo = (qb * 2 + hl) * 128
                P.op('dve', lambda e, a0=a0, o0=o0, oo=oo: e.tensor_scalar(
                    out=OBf[:, oo:oo + 128], in0=psB[:, o0:o0 + 128], scalar1=rec[:, a0:a0 + 1], scalar2=None, op0=ALU.mult),
                    reads=[bankB(a0 // 3), ("recd", a0 // 3)], writes=[("OBf", qb, hl)])
            for qb in range(4):
                a1 = 4 + qb
                o1 = (a1 // 3) * 512 + (a1 % 3) * 130
                P.op('dve', lambda e, qb=qb, o1=o1: e.tensor_scalar(
                    out=BXf[:, qb * 128:(qb + 1) * 128], in0=psB[:, o1:o1 + 128], scalar1=rec[:, 8 + qb:9 + qb], scalar2=None, op0=ALU.mult),
                    reads=[bankB(a1 // 3), "recn"], writes=["BXf"])
            for qb in range(4):
                oo = (qb * 2 + hl) * 128
                P.op('dve', lambda e, qb=qb, oo=oo: e.tensor_tensor(
                    out=OBf[:, oo:oo + 128], in0=OBf[:, oo:oo + 128], in1=BXf[:, qb * 128:(qb + 1) * 128], op=ALU.add),
                    reads=[("OBf", qb, hl), "BXf"], writes=[("OBf", qb, hl)])
                P.op('dve', lambda e, qb=qb, oo=oo: e.tensor_tensor(
                    out=sqj[:], in0=OBf[:, oo:oo + 128], in1=OBf[:, oo:oo + 128], op=ALU.mult),
                    reads=[("OBf", qb, hl)], writes=["sqj"])
                P.op('dve', lambda e, qb=qb: e.reduce_sum(out=ssq[:, qb * 2 + hl: qb * 2 + hl + 1], in_=sqj[:], axis=mybir.AxisListType.X),
                     reads=["sqj"], writes=[("ssq", qb, hl)])

        def post_diff_steps(t, heads, G, final, gi, groups, n_qt):
            S_q = n_qt * 512
            par = t % 2
            steps = []
            sgk = [("SGb", par, qb) for qb in range(4)]

            def s1():
                P.op('act', lambda e: e.activation(out=BXf[:], in_=SGb[par][:], func=AF.Exp, scale=-1.0),
                     reads=sgk, writes=["BXf"])
            steps.append((s1, 1))

            def s2():
                P.op('dve', lambda e: e.tensor_scalar(out=BXf[:], in0=BXf[:], scalar1=1.0, scalar2=None, op0=ALU.add),
                     reads=["BXf"], writes=["BXf"])
                P.op('dve', lambda e: e.reciprocal(out=BXf[:], in_=BXf[:]), reads=["BXf"], writes=["BXf"])
                P.op('dve', lambda e: e.tensor_tensor(out=SGb[par][:], in0=SGb[par][:], in1=BXf[:], op=ALU.mult),
                     reads=sgk + ["BXf"], writes=sgk)
            steps.append((s2, 3))

            def s3():
                P.op('act', lambda e: e.activation(out=rstd[:, 0:8], in_=ssq[:, 0:8], func=AF.Sqrt, bias=eps_sub[:], scale=1.0 / 128.0),
                     reads=[("ssq", qb, hl) for qb in range(4) for hl in range(G)] + [eps_sub.name], writes=["rstd_a"])
                P.op('dve', lambda e: e.reciprocal(out=rstd[:, 0:8], in_=rstd[:, 0:8]), reads=["rstd_a"], writes=["rstd"])
            steps.append((s3, 2))
            for qb in range(4):
                def s4(qb=qb):
                    for hl in range(G):
                        oo = (qb * 2 + hl) * 128
                        P.op('dve', lambda e, hl=hl, oo=oo: e.scalar_tensor_tensor(
                            out=OBf[:, oo:oo + 128], in0=OBf[:, oo:oo + 128], scalar=rstd[:, qb * 2 + hl: qb * 2 + hl + 1],
                            in1=subw_bc[:], op0=ALU.mult, op1=ALU.mult),
                            reads=[("OBf", qb, hl), "rstd", subw_bc.name], writes=[("OBf", qb, hl)])
                        P.op('pool', lambda e, hl=hl, oo=oo: e.tensor_tensor(
                            out=OBn[:, oo:oo + 128], in0=OBf[:, oo:oo + 128],
                            in1=SGb[par][:, qb * 256 + hl * 128: qb * 256 + (hl + 1) * 128], op=ALU.mult),
                            reads=[("OBf", qb, hl), ("SGb", par, qb)], writes=[("OBn", qb, hl)])
                steps.append((s4, 1))
            for hl, h in enumerate(heads):
                def s5(hl=hl):
                    def trd(e):
                        ins = None
                        for qb in range(4):
                            oo = (qb * 2 + hl) * 128
                            ins = e.transpose(out=psC_b[:, qb * 128:(qb + 1) * 128], in_=OBn[:, oo:oo + 128], identity=identb[:])
                        return ins
                    P.op('pe', trd, reads=[("OBn", qb, hl) for qb in range(4)] + [identb.name], writes=["psC"])

                def s6(hl=hl):
                    if final:
                        P.op('dve', lambda e: e.tensor_copy(out=OT[:, (4 + hl) * 512:(5 + hl) * 512], in_=psC_b[:, 0:512]),
                             reads=["psC"], writes=[("OT", 4 + hl)])
                    else:
                        ci = sum(len(g) for g in groups[:gi]) + hl
                        o = ci * S_q + t * 512
                        P.op('dve', lambda e: e.tensor_copy(out=CAR[:, o:o + 512], in_=psC_b[:, 0:512]),
                             reads=["psC"], writes=[("CAR", ci, t)])
                steps.append((s5, 1))
                steps.append((s6, 1))
            return steps

        def finalize(x_d, y_d, b, t, n_qt, heads, gi, groups):
            S_q = n_qt * 512
            n_car = sum(len(g) for g in groups[:gi])
            G = len(heads)
            fslots = [0, 1]

            def xload(qb):
                xs_ = fslots[qb % 2]
                row0 = (4 * t + qb) * 128
                P.op('sp', lambda e: e.dma_start(out=FB[xs_][:], in_=x_d.ap()[row0:row0 + 128, :]),
                     writes=[("FB", xs_)], dma=xkeys[xs_])

            xload(0)
            xload(1)
            for qb in range(4):
                fin_block(x_d, y_d, b, t, qb, S_q, n_car, G, fslots[qb % 2])
                if qb + 2 < 4:
                    xload(qb + 2)

        def fin_block(x_d, y_d, b, t, qb, S_q, n_car, G, xs_):
            row0 = (4 * t + qb) * 128
            ys_ = state['yslot'] % 2
            state['yslot'] += 1
            for n in range(2):
                def om(e, n=n):
                    ins = None
                    for fc in range(8):
                        if fc < 4:
                            lhsT = OT[:, fc * 512 + qb * 128: fc * 512 + (qb + 1) * 128]
                        elif fc - 4 < n_car:
                            o = (fc - 4) * S_q + t * 512 + qb * 128
                            lhsT = CAR[:, o:o + 128]
                        else:
                            o = (4 + fc - 4 - n_car) * 512 + qb * 128
                            lhsT = OT[:, o:o + 128]
                        ins = e.matmul(out=psA[:, (2 + n) * 512:(3 + n) * 512], lhsT=lhsT,
                                       rhs=WO[:, fc * 1024 + n * 512: fc * 1024 + (n + 1) * 512],
                                       start=(fc == 0), stop=(fc == 7))
                    return ins
                rd = [("OT", fc) for fc in range(4 + G)] + [("CAR", ci, t) for ci in range(n_car)] + ["WO"]
                P.op('pe', om, reads=rd, writes=[bankA(2 + n)])
            Z = FB[xs_]
            zk = ("FB", xs_)
            mo = qb * 2
            P.op('dve', lambda e: e.tensor_tensor(out=BXf[:], in0=psA[:, 1024:2048], in1=gate_bc[:, b * 1024:(b + 1) * 1024], op=ALU.mult),
                 reads=[bankA(2), bankA(3), ("gate_bc", b, 0), ("gate_bc", b, 1)], writes=["BXf"])
            P.op('dve', lambda e: e.scalar_tensor_tensor(out=Z[:], in0=Z[:], scalar=ALPHA, in1=BXf[:], op0=ALU.mult, op1=ALU.add),
                 reads=[zk, "BXf"], writes=[zk])
            P.op('dve', lambda e: e.bn_stats(out=bnst[:, 0:6], in_=Z[:, 0:512]), reads=[zk], writes=["bnst0"])
            P.op('dve', lambda e: e.bn_stats(out=bnst[:, 6:12], in_=Z[:, 512:1024]), reads=[zk], writes=["bnst1"])
            P.op('dve', lambda e: e.bn_aggr(out=mv[:, mo:mo + 2], in_=bnst[:]), reads=["bnst0", "bnst1"], writes=[("mv", qb)])
            P.op('act', lambda e: e.activation(out=lrs[:, qb:qb + 1], in_=mv[:, mo + 1:mo + 2], func=AF.Sqrt, bias=eps_ln[:], scale=1.0),
                 reads=[("mv", qb), eps_ln.name], writes=[("lrs_a", qb)])
            P.op('dve', lambda e: e.reciprocal(out=lrs[:, qb:qb + 1], in_=lrs[:, qb:qb + 1]), reads=[("lrs_a", qb)], writes=[("lrs", qb)])
            P.op('dve', lambda e: e.scalar_tensor_tensor(out=Z[:], in0=Z[:], scalar=mv[:, mo:mo + 1], in1=lng_bc[:],
                                                         op0=ALU.subtract, op1=ALU.mult),
                 reads=[zk, ("mv", qb)], writes=[zk])
            P.op('dve', lambda e: e.scalar_tensor_tensor(out=Z[:], in0=Z[:], scalar=lrs[:, qb:qb + 1], in1=lnb_bc[:],
                                                         op0=ALU.mult, op1=ALU.add),
                 reads=[zk, ("lrs", qb)], writes=[zk])
            P.op('sp', lambda e: e.dma_start(out=y_d.ap()[row0:row0 + 128, :], in_=Z[:]),
                 reads=[zk], writes=[("yout", ys_)], dma=['y0', 'y1'][ys_])

        try:
            if kstop == 1:
                raise _Stop()
            run_job(xp_d, yp_d, 0, SP_, 8, False, [[0, 1], [2, 3]], [0])
            run_job(xs_d, ys_d, 1, SS_, 4, True, [[0], [1], [2], [3]], None)
        except _Stop:
            pass
        P.op('sp', lambda e: None, reads=[("yout", 0), ("yout", 1)] + (["dbg%d" % i for i in range(1, 11)] if dbg else []))

        with nc.Block() as blk2:
            P.emit(blk2)
    return nc


_NC_CACHE = {}


def _bucket_table():
    import math
    import jax
    import jax.numpy as jnp
    try:
        dev = jax.devices('cpu')[0]
    except Exception:
        dev = None
    def run():
        rel = jnp.arange(-1400, 1401, dtype=jnp.int32)
        sign = jnp.where(rel > 0, 16, 0)
        n = jnp.abs(rel)
        nf = jnp.maximum(n, 1).astype(jnp.float32)
        large = 8 + (jnp.log(nf / 8) / math.log(128 / 8) * 8).astype(jnp.int32)
        large = jnp.minimum(large, 15)
        return np.asarray((sign + jnp.where(n < 8, n, large)).astype(jnp.int32))
    if dev is not None:
        with jax.default_device(dev):
            return run()
    return run()


def kernel(**inp):
    f = lambda k: np.ascontiguousarray(np.asarray(inp[k], dtype=np.float32))
    x_prompt = f('x_prompt'); x_sample = f('x_sample')
    c_prompt = f('c_prompt'); c_sample = f('c_sample')
    w_in = f('w_in')[0]; w_out = f('w_out')[0]; w_ada = f('w_ada')[0]; b_ada = f('b_ada')[0]
    ln_g = f('ln_g'); ln_b = f('ln_b'); sink = f('attn_sink')[0]
    rel_bias = f('rel_bias')
    subw = f('subln_w')
    lamv = np.concatenate([f('lambda_q1')[0], f('lambda_k1')[0], f('lambda_q2')[0], f('lambda_k2')[0]])[None, :]

    colperm = np.concatenate([np.arange(64) + 64 * h for h in PERM])
    w_in_l = w_in.copy()
    w_in_l[:, 0:512] = w_in[:, 0:512][:, colperm]
    w_in_l[:, 768:1280] = w_in[:, 768:1280][:, colperm]
    w_out_l = w_out.copy()
    w_out_l[0:512, :] = w_out[0:512, :][colperm, :]
    sinkp = sink[PERM][None, :]
    bt = _bucket_table()

    def bk(r):
        return bt[r + 1400]
    kk = np.arange(128)[:, None]
    mm = np.arange(MD)[None, :]
    bd_idx = bk(kk - mm + 512)
    bd_strip = np.stack([rel_bias[bd_idx, 8 + h] for h in range(4)], axis=1).reshape(128, 4 * MD)
    qq = np.arange(384)[None, :]
    relw = kk - qq + 128
    bw_idx = bk(relw)
    bw_strip = np.stack([rel_bias[bw_idx, PERM[hp]] for hp in range(8)], axis=1).reshape(128, 8 * 384)
    maskw = np.where(np.abs(relw) <= 128, 0.0, -BIG).astype(np.float32)
    relfar = np.concatenate([rel_bias[15, 8:12], rel_bias[31, 8:12]])[None, :]
    identf = np.eye(128, dtype=np.float32)

    import os
    dbg = bool(os.environ.get('KDBG'))
    kstop = int(os.environ.get('KSTOP', '0'))
    key = 'nc%d_%d' % (dbg, kstop)
    if key not in _NC_CACHE:
        _NC_CACHE[key] = build_program(dbg, kstop)
    nc = _NC_CACHE[key]

    in_maps = []
    for c in range(NCORES):
        s, j = c // 4, c % 4
        xs = np.ascontiguousarray(np.roll(x_sample[s], -SQS * j, axis=0))
        cc = np.stack([c_prompt[c], c_sample[s]], axis=0)
        cT = np.ascontiguousarray(cc.reshape(2, 8, 128).transpose(2, 1, 0).reshape(128, 16))
        fl = np.zeros((1, 66), np.float32)
        fl[0, 0] = 1.0 if j > 0 else 0.0
        fl[0, 1] = 1.0 if j < 3 else 0.0
        for i in range(64):
            fl[0, 2 + i] = 1.0 if i >= 64 - 16 * j else 0.0
        in_maps.append({
            "xp": x_prompt[c], "xs": xs, "cT": cT,
            "w_in": w_in_l, "w_out": w_out_l, "w_ada": w_ada,
            "bcol": np.ascontiguousarray(b_ada.reshape(24, 128).T),
            "bgate": np.ascontiguousarray(b_ada[None, 2048:3072]),
            "ln_g": ln_g, "ln_b": ln_b, "sinkp": np.ascontiguousarray(sinkp),
            "lamv": np.ascontiguousarray(lamv), "subw": subw,
            "relfar": np.ascontiguousarray(relfar),
            "bd_strip": np.ascontiguousarray(bd_strip.astype(np.float32)),
            "bw_strip": np.ascontiguousarray(bw_strip.astype(np.float32)),
            "maskw": maskw, "flags": fl, "identf": identf,
        })
    res = run_bass_kernel_spmd(nc, in_maps, core_ids=list(range(NCORES)))
    y_prompt = np.empty((8, SP_, D), np.float32)
    y_sample = np.empty((2, SS_, D), np.float32)
    for c in range(NCORES):
        r = res.results[c]
        s, j = c // 4, c % 4
        y_prompt[c] = np.asarray(r["yp"], np.float32)
        y_sample[s, SQS * j: SQS * (j + 1)] = np.asarray(r["ys"], np.float32)
    return (y_prompt, y_sample)
```

```python
import numpy as np
import concourse.bass as bass
import concourse.mybir as mybir
from concourse.bass_utils import run_bass_kernel_spmd

F32 = mybir.dt.float32
BF16 = mybir.dt.bfloat16
AF = mybir.ActivationFunctionType
ALU = mybir.AluOpType

NCORES = 8
D = 1024
DC = 8
SP_ = 4096
SS_ = 8192
SQS = 2048
ALPHA = 2.0 ** 0.25
LAMBDA_INIT = 0.2
BIG = 30000.0
LN_EPS = 1e-5
SUBLN_EPS = 1e-5
PERM = [0, 4, 1, 5, 2, 6, 3, 7]
MD = 1152


class Prog:
    ENG = ('sp', 'act', 'pe', 'dve', 'pool')

    def __init__(self, sems, dsems):
        self.sem = sems
        self.dsem = dsems
        self.dcnt = {k: 0 for k in dsems}
        self.base = {e: 0 for e in self.ENG}
        self.reset_block()

    def reset_block(self):
        self.ops = {e: [] for e in self.ENG}
        self.res = {}

    def op(self, eng, fn, reads=(), writes=(), dma=None):
        ops = self.ops[eng]
        idx = len(ops)
        if dma is not None:
            self.dcnt[dma] += 1
            ev = ('dma', dma, self.dcnt[dma])
        else:
            ev = (eng, idx)
        deps = {}

        def add(d, raw):
            if d[0] == 'dma':
                d = ('dma', d[1], self.dcnt[d[1]])
            if raw or d not in deps:
                deps[d] = raw or deps.get(d, False)

        def is_ps(k):
            return isinstance(k, str) and k.startswith('ps')

        ps_keys = {}
        for k in reads:
            if is_ps(k):
                ps_keys[k] = False
        for k in writes:
            if is_ps(k):
                ps_keys[k] = True
        for k, isw in ps_keys.items():
            r = self.res.get(k)
            if r is not None and r[0] is not None:
                add(r[0], (not isw) and r[2])
        for k in reads:
            if is_ps(k):
                continue
            r = self.res.get(k)
            if r is not None and r[0] is not None:
                add(r[0], True)
        for k in writes:
            if is_ps(k):
                continue
            r = self.res.get(k)
            if r is not None:
                if r[0] is not None:
                    add(r[0], False)
                for rv in r[1].values():
                    add(rv, False)
        for k in reads:
            if is_ps(k):
                continue
            r = self.res.setdefault(k, [None, {}])
            r[1][ev if dma is not None else eng] = ev
        for k in writes:
            if is_ps(k):
                continue
            self.res[k] = [ev, {}]
        for k, isw in ps_keys.items():
            self.res[k] = [ev, {}, isw]
        ops.append((fn, deps, ev, dma))
        return ev

    def emit(self, block):
        need = {e: set() for e in self.ENG}
        for e in self.ENG:
            for (fn, deps, ev, dma) in self.ops[e]:
                for d in deps:
                    if d[0] != 'dma':
                        need[d[0]].add(d[1])
        ordn = {}
        for e in self.ENG:
            s = sorted(need[e])
            ordn[e] = {idx: self.base[e] + i + 1 for i, idx in enumerate(s)}

        def body(en):
            def f(eng):
                seen = {}
                for idx, (fn, deps, ev, dma) in enumerate(self.ops[en]):
                    for d, raw in deps.items():
                        if d[0] == 'dma':
                            if dma is not None and d[1] == dma:
                                continue
                            key = ('dma', d[1])
                            val = 16 * d[2]
                            sem = self.dsem[d[1]]
                        else:
                            if d[0] == en:
                                if (not raw) or en == 'pe' or en == 'sp':
                                    continue
                            key = d[0]
                            val = ordn[d[0]][d[1]]
                            sem = self.sem[d[0]]
                        if seen.get(key, 0) >= val:
                            continue
                        eng.wait_ge(sem, val)
                        seen[key] = val
                    ins = fn(eng)
                    if dma is not None:
                        ins.then_inc(self.dsem[dma], 16)
                    elif idx in ordn[en]:
                        ins.then_inc(self.sem[en], 1)
            return f

        block.sync(body('sp'))
        block.scalar(body('act'))
        block.tensor(body('pe'))
        block.vector(body('dve'))
        block.gpsimd(body('pool'))
        for e in self.ENG:
            self.base[e] += len(ordn[e])
        self.reset_block()


class _CopyShim:
    def __init__(self, e):
        self.e = e

    def tensor_copy(self, out, in_):
        return self.e.copy(out=out, in_=in_)


def dap(t, off, dims):
    return bass.AP(t, off, [list(d) for d in dims])


class _Stop(Exception):
    pass


def build_program(dbg=False, kstop=0):
    nc = bass.Bass("TRN2", target_bir_lowering=False)
    if dbg:
        dbg_ot = nc.dram_tensor("dbg_ot", [128, 3072], BF16, kind="ExternalOutput")
        dbg_car = nc.dram_tensor("dbg_car", [128, 1024], BF16, kind="ExternalOutput")
        dbg_c = nc.dram_tensor("dbg_c", [128, 64], F32, kind="ExternalOutput")
        dbg_sg = nc.dram_tensor("dbg_sg", [128, 2048 + 1024], BF16, kind="ExternalOutput")

    def din(name, shape):
        return nc.dram_tensor(name, list(shape), F32, kind="ExternalInput")

    xp_d = din("xp", [SP_, D])
    xs_d = din("xs", [SS_, D])
    cT_d = din("cT", [128, 16])
    win_d = din("w_in", [D, 3328])
    wout_d = din("w_out", [D, D])
    wada_d = din("w_ada", [D, 3072])
    bcol_d = din("bcol", [128, 24])
    bgate_d = din("bgate", [1, 1024])
    lng_d = din("ln_g", [1, 1024])
    lnb_d = din("ln_b", [1, 1024])
    sink_d = din("sinkp", [1, 8])
    lamv_d = din("lamv", [1, 256])
    subw_d = din("subw", [1, 128])
    relfar_d = din("relfar", [1, 8])
    bd_d = din("bd_strip", [128, 4 * MD])
    bw_d = din("bw_strip", [128, 8 * 384])
    maskw_d = din("maskw", [128, 384])
    flags_d = din("flags", [1, 66])
    ident_d = din("identf", [128, 128])
    yp_d = nc.dram_tensor("yp", [SP_, D], F32, kind="ExternalOutput")
    ys_d = nc.dram_tensor("ys", [SQS, D], F32, kind="ExternalOutput")

    from contextlib import ExitStack
    es = ExitStack()

    def sb(name, shape, dt=F32):
        return es.enter_context(nc.sbuf_tensor("sb_" + name, list(shape), dt))

    def ps(name, shape, dt=F32):
        return es.enter_context(nc.psum_tensor("ps_" + name, list(shape), dt))

    with es:
        sems = {e: es.enter_context(nc.semaphore("s_" + e)) for e in ('act', 'pe', 'dve', 'pool')}
        dkeys = ['c0', 'wa', 'x0', 'x1', 'xb0', 'xb1', 'xb2', 'xb3', 'xb4', 'y0', 'y1', 'dbg']
        dsems = {k: es.enter_context(nc.semaphore("d_" + k)) for k in dkeys}
        P = Prog(sems, dsems)

        identf = sb("identf", [128, 128])
        identb = sb("identb", [128, 128], BF16)
        s1p = sb("s1p", [128, 16])
        shf = sb("shf", [128, 16])
        gate_bc = sb("gate_bc", [128, 2048])
        lng_bc = sb("lng_bc", [128, 1024])
        lnb_bc = sb("lnb_bc", [128, 1024])
        subw_bc = sb("subw_bc", [128, 128])
        neglam = sb("neglam", [128, 1])
        expsink = sb("expsink", [128, 8])
        cfar = sb("cfar", [128, 8])
        cdiff = sb("cdiff", [128, 4])
        farS = sb("farS", [128, 4 * 64])
        flags = sb("flags", [128, 66])
        negLR = sb("negLR", [128, 2])
        omf = sb("omf", [128, 2])
        BW = sb("BW", [128, 8 * 384], BF16)
        BXW = sb("BXW", [128, 2048], BF16)
        zero_c = sb("zero_c", [128, 1])
        eps_ln = sb("eps_ln", [128, 1])
        eps_sub = sb("eps_sub", [128, 1])

        psA = ps("psA", [128, 2048])
        psB = ps("psB", [128, 1536])
        psC = ps("psC", [128, 512])
        psA_b = psA.bitcast(BF16)
        psB_b = psB.bitcast(BF16)
        psC_b = psC.bitcast(BF16)

        def bankA(i):
            return "psA%d" % i

        def bankB(i):
            return "psB%d" % i

        with ExitStack() as es1:
            def sb1(name, shape, dt=F32):
                return es1.enter_context(nc.sbuf_tensor("sb_" + name, list(shape), dt))

            WA = sb1("WA", [128, 8 * 3072])
            cT = sb1("cT", [128, 16])
            sc = sb1("sc", [128, 16])
            sc_rep = sb1("sc_rep", [128, 2 * 8 * 128])
            bcol = sb1("bcol", [128, 24])
            bgate_bc = sb1("bgate_bc", [128, 1024])
            lamv = sb1("lamv", [128, 256])
            lamp = sb1("lamp", [128, 128])
            lams = sb1("lams", [128, 4])
            sinkb = sb1("sinkb", [128, 8])
            subw_raw = sb1("subw_raw", [128, 128])
            maskw = sb1("maskw", [128, 384])
            BWf = sb1("BWf", [128, 8 * 384])

            def ld(dst_t, src):
                P.op('sp', lambda e: e.dma_start(out=dst_t[:], in_=src), writes=[dst_t.name], dma='c0')

            ld(identf, ident_d.ap())
            ld(cT, cT_d.ap())
            ld(bcol, bcol_d.ap())
            ld(bgate_bc, dap(bgate_d, 0, [[0, 128], [1, 1024]]))
            ld(lng_bc, dap(lng_d, 0, [[0, 128], [1, 1024]]))
            ld(lnb_bc, dap(lnb_d, 0, [[0, 128], [1, 1024]]))
            ld(sinkb, dap(sink_d, 0, [[0, 128], [1, 8]]))
            ld(lamv, dap(lamv_d, 0, [[0, 128], [1, 256]]))
            ld(subw_raw, dap(subw_d, 0, [[0, 128], [1, 128]]))
            ld(cfar, dap(relfar_d, 0, [[0, 128], [1, 8]]))
            ld(flags, dap(flags_d, 0, [[0, 128], [1, 66]]))
            ld(BWf, bw_d.ap())
            ld(maskw, maskw_d.ap())
            for dc in range(8):
                P.op('sp', lambda e, dc=dc: e.dma_start(
                    out=WA[:, dc * 3072:(dc + 1) * 3072], in_=wada_d.ap()[dc * 128:(dc + 1) * 128, :]),
                    writes=[("WA", dc)], dma='wa')

            P.op('pool', lambda e: e.memset(zero_c[:], 0.0), writes=[zero_c.name])
            P.op('pool', lambda e: e.memset(eps_ln[:], LN_EPS), writes=[eps_ln.name])
            P.op('pool', lambda e: e.memset(eps_sub[:], SUBLN_EPS), writes=[eps_sub.name])
            P.op('dve', lambda e: e.tensor_copy(out=identb[:], in_=identf[:]),
                 reads=[identf.name], writes=[identb.name])
            P.op('act', lambda e: e.activation(out=sc[:], in_=cT[:], func=AF.Silu),
                 reads=[cT.name], writes=[sc.name])
            P.op('act', lambda e: e.activation(out=expsink[:], in_=sinkb[:], func=AF.Exp),
                 reads=[sinkb.name], writes=[expsink.name])
            P.op('dve', lambda e: e.tensor_tensor(out=lamp[:, 0:64], in0=lamv[:, 0:64], in1=lamv[:, 64:128], op=ALU.mult),
                 reads=[lamv.name], writes=["lamp0"])
            P.op('dve', lambda e: e.tensor_tensor(out=lamp[:, 64:128], in0=lamv[:, 128:192], in1=lamv[:, 192:256], op=ALU.mult),
                 reads=[lamv.name], writes=["lamp1"])
            P.op('dve', lambda e: e.reduce_sum(out=lams[:, 0:1], in_=lamp[:, 0:64], axis=mybir.AxisListType.X),
                 reads=["lamp0"], writes=["lams0"])
            P.op('dve', lambda e: e.reduce_sum(out=lams[:, 1:2], in_=lamp[:, 64:128], axis=mybir.AxisListType.X),
                 reads=["lamp1"], writes=["lams1"])
            P.op('act', lambda e: e.activation(out=lams[:, 2:4], in_=lams[:, 0:2], func=AF.Exp),
                 reads=["lams0", "lams1"], writes=["lams23"])
            P.op('dve', lambda e: e.tensor_tensor(out=neglam[:], in0=lams[:, 3:4], in1=lams[:, 2:3], op=ALU.subtract),
                 reads=["lams23"], writes=["neglam_a"])
            P.op('dve', lambda e: e.tensor_scalar(out=neglam[:], in0=neglam[:], scalar1=-LAMBDA_INIT, scalar2=None, op0=ALU.add),
                 reads=["neglam_a"], writes=[neglam.name])
            P.op('dve', lambda e: e.tensor_scalar(out=subw_bc[:], in0=subw_raw[:], scalar1=1.0 - LAMBDA_INIT, scalar2=None, op0=ALU.mult),
                 reads=[subw_raw.name], writes=[subw_bc.name])
            P.op('dve', lambda e: e.tensor_tensor(out=cdiff[:], in0=cfar[:, 0:4], in1=cfar[:, 4:8], op=ALU.subtract),
                 reads=[cfar.name], writes=[cdiff.name])
            for h in range(4):
                P.op('dve', lambda e, h=h: e.tensor_scalar(
                    out=farS[:, h * 64:(h + 1) * 64], in0=flags[:, 2:66],
                    scalar1=cdiff[:, h:h + 1], scalar2=cfar[:, 4 + h:5 + h], op0=ALU.mult, op1=ALU.add),
                    reads=[flags.name, cdiff.name, cfar.name], writes=[("farS", h)])
            P.op('dve', lambda e: e.tensor_scalar(out=negLR[:], in0=flags[:, 0:2], scalar1=-1.0, scalar2=BIG, op0=ALU.add, op1=ALU.mult),
                 reads=[flags.name], writes=[negLR.name])
            for h in range(8):
                P.op('dve', lambda e, h=h: e.tensor_tensor(
                    out=BWf[:, h * 384:(h + 1) * 384], in0=BWf[:, h * 384:(h + 1) * 384], in1=maskw[:], op=ALU.add),
                    reads=[BWf.name, maskw.name], writes=[("BWf", h)])
                P.op('pool', lambda e, h=h: e.tensor_copy(out=BW[:, h * 384:(h + 1) * 384], in_=BWf[:, h * 384:(h + 1) * 384]),
                     reads=[("BWf", h)], writes=[("BW", h)])
            P.op('dve', lambda e: e.tensor_scalar(
                out=dap(BXW, 0, [[2048, 128], [128, 8], [1, 128]]),
                in0=dap(BWf, 256, [[8 * 384, 128], [384, 8], [1, 128]]),
                scalar1=negLR[:, 0:1], scalar2=None, op0=ALU.add),
                reads=[("BWf", hh) for hh in range(8)] + [negLR.name], writes=["BXWL"])
            P.op('dve', lambda e: e.tensor_scalar(
                out=dap(BXW, 1024, [[2048, 128], [128, 8], [1, 128]]),
                in0=dap(BWf, 0, [[8 * 384, 128], [384, 8], [1, 128]]),
                scalar1=negLR[:, 1:2], scalar2=None, op0=ALU.add),
                reads=[("BWf", hh) for hh in range(8)] + [negLR.name], writes=["BXWR"])
            for b in range(2):
                for dc in range(8):
                    o = (b * 8 + dc) * 128
                    P.op('pool', lambda e, o=o, b=b, dc=dc: e.tensor_copy(
                        out=sc_rep[:, o:o + 128], in_=sc[:, dc * 2 + b:dc * 2 + b + 1].broadcast_to([128, 128])),
                        reads=[sc.name], writes=[("sc_rep", b, dc)])

            def mod_cols(e):
                ins = None
                for dc in range(8):
                    for fc in range(16):
                        ins = e.matmul(out=psC[:, fc * 2:fc * 2 + 2],
                                       lhsT=WA[:, dc * 3072 + fc * 128: dc * 3072 + fc * 128 + 128],
                                       rhs=sc[:, dc * 2:dc * 2 + 2],
                                       start=(dc == 0 and fc == 0), stop=(dc == 7), skip_group_check=True)
                return ins
            P.op('pe', mod_cols, reads=[("WA", dc) for dc in range(8)] + [sc.name], writes=["psC"])
            for b in range(2):
                P.op('dve', lambda e, b=b: e.tensor_tensor(
                    out=shf[:, b * 8:(b + 1) * 8], in0=psC[:, b:16:2],
                    in1=bcol[:, 0:8], op=ALU.add),
                    reads=["psC", bcol.name], writes=[("shf", b)])
                P.op('dve', lambda e, b=b: e.scalar_tensor_tensor(
                    out=s1p[:, b * 8:(b + 1) * 8], in0=psC[:, 16 + b:32:2], scalar=1.0, in1=bcol[:, 8:16],
                    op0=ALU.add, op1=ALU.add),
                    reads=["psC", bcol.name], writes=[("s1p", b)])
            for b in range(2):
                for n in range(2):
                    def gmm(e, b=b, n=n):
                        ins = None
                        for dc in range(8):
                            o = (b * 8 + dc) * 128
                            ins = e.matmul(out=psA[:, (b * 2 + n) * 512:(b * 2 + n + 1) * 512],
                                           lhsT=sc_rep[:, o:o + 128],
                                           rhs=WA[:, dc * 3072 + 2048 + n * 512: dc * 3072 + 2048 + (n + 1) * 512],
                                           start=(dc == 0), stop=(dc == 7))
                        return ins
                    P.op('pe', gmm, reads=[("WA", dc) for dc in range(8)] + [("sc_rep", b, dc) for dc in range(8)],
                         writes=[bankA(b * 2 + n)])
                    P.op('dve', lambda e, b=b, n=n: e.tensor_tensor(
                        out=gate_bc[:, b * 1024 + n * 512: b * 1024 + (n + 1) * 512],
                        in0=psA[:, (b * 2 + n) * 512:(b * 2 + n + 1) * 512],
                        in1=bgate_bc[:, n * 512:(n + 1) * 512], op=ALU.add),
                        reads=[bankA(b * 2 + n), bgate_bc.name], writes=[("gate_bc", b, n)])

            with nc.Block() as blk1:
                P.emit(blk1)

        WIs = sb("WIs", [128, 8 * 1280], BF16)
        WIg = sb("WIg", [128, 8 * 1024], BF16)
        WO = sb("WO", [128, 8 * 1024], BF16)
        BD = sb("BD", [128, 2 * MD], BF16)
        KT = sb("KT", [128, 8192], BF16)
        VA = sb("VA", [128, 64 * 130], BF16)
        CAR = sb("CAR", [128, 8192], BF16)
        FB = [sb("FB%d" % i, [128, 1024]) for i in range(2)]
        XBF = [sb("xbf%d" % i, [128, 1024], BF16) for i in range(4)]
        uT = sb("uT", [128, 8 * 768], BF16)
        QT = [sb("QT%d" % i, [128, 2 * 512], BF16) for i in range(2)]
        QAT = sb("QAT", [128, 4 * 512], BF16)
        SGb = [sb("SGb%d" % i, [128, 4 * 256], BF16) for i in range(2)]
        SGa = sb("SGa", [128, 4 * 512], BF16)
        KAT = sb("KAT", [128, 768], BF16)
        VAW = sb("VAW", [128, 6 * 2 * 66], BF16)
        OT = sb("OT", [128, 6 * 512], BF16)
        PT = [sb("PT%d" % i, [128, 1024], BF16) for i in range(2)]
        OBf = sb("OBf", [128, 4 * 2 * 128])
        OBn = sb("OBn", [128, 4 * 2 * 128], BF16)
        OAg = sb("OAg", [128, 4 * 512], BF16)
        BX = sb("BX", [128, 1024], BF16)
        BXf = sb("BXf", [128, 1024])
        rec = sb("rec", [128, 16])
        ssq = sb("ssq", [128, 8])
        rstd = sb("rstd", [128, 8])
        sqj = sb("sqj", [128, 128])
        den = sb("den", [128, 16])
        bnst = sb("bnst", [128, 12])
        mv = sb("mv", [128, 8])
        lrs = sb("lrs", [128, 4])

        xkeys = ['x0', 'x1']
        xbkeys = ['xb0', 'xb1', 'xb2', 'xb3', 'xb4']
        NPF = 3
        state = {'xslot': 0, 'evac': 0, 'yslot': 0}

        wl_state = {'i': 0}

        def stage_cast(src_ap, ncols, cast_fn, writes, reads_extra=(), dst_dims=None, engs=('pool', 'dve')):
            s_ = state['xslot'] % 2
            state['xslot'] += 1
            P.op('sp', lambda e, s_=s_: e.dma_start(
                out=(FB[s_][:, 0:ncols] if dst_dims is None else dap(FB[s_], 0, dst_dims)), in_=src_ap),
                 writes=[("FB", s_)], dma=xkeys[s_])
            eng = engs[wl_state['i'] % 2]
            wl_state['i'] += 1
            P.op(eng, lambda e, s_=s_: cast_fn(_CopyShim(e) if eng == 'act' else e, FB[s_]), reads=[("FB", s_)] + list(reads_extra), writes=writes)

        def load_static_weights():
            for dc in range(8):
                r0 = dc * 128
                def castA(e, fb, dc=dc):
                    e.tensor_copy(out=WIs[:, dc * 1280: dc * 1280 + 512], in_=fb[:, 0:512])
                    return e.tensor_copy(out=WIs[:, dc * 1280 + 1024: dc * 1280 + 1280], in_=fb[:, 512:768])
                stage_cast(win_d.ap()[r0:r0 + 128, 0:768], 768, castA, writes=["WIs"])
                def castB(e, fb, dc=dc):
                    return e.tensor_copy(out=WIs[:, dc * 1280 + 512: dc * 1280 + 1024], in_=fb[:, 0:512])
                stage_cast(win_d.ap()[r0:r0 + 128, 768:1280], 512, castB, writes=["WIs"])
                def castO(e, fb, dc=dc):
                    return e.tensor_copy(out=WO[:, dc * 1024:(dc + 1) * 1024], in_=fb[:, 0:1024])
                stage_cast(wout_d.ap()[r0:r0 + 128, :], 1024, castO, writes=["WO"])

        def load_kv_weights(heads):
            for hl, h in enumerate(heads):
                for dc in range(8):
                    src = dap(win_d, (dc * 128) * 3328 + 1792 + h * 128, [[3328, 128], [512, 2], [1, 128]])
                    def castG(e, fb, dc=dc, hl=hl):
                        return e.tensor_copy(
                            out=dap(WIg, dc * 1024 + 256 + hl * 128, [[8192, 128], [256, 2], [1, 128]]),
                            in_=dap(fb, 0, [[1024, 128], [128, 2], [1, 128]]))
                    stage_cast(src, 256, castG, writes=["WIgkv"], dst_dims=[[1024, 128], [128, 2], [1, 128]])

        def q_weight_thunks(heads):
            th = []
            for hl, h in enumerate(heads):
                for dc in range(8):
                    def t1(dc=dc, hl=hl, h=h):
                        src = dap(win_d, (dc * 128) * 3328 + 1280 + h * 128, [[3328, 128], [1536, 2], [1, 128]])
                        def castG(e, fb):
                            return e.tensor_copy(
                                out=dap(WIg, dc * 1024 + hl * 128, [[8192, 128], [768, 2], [1, 128]]),
                                in_=dap(fb, 0, [[1024, 128], [128, 2], [1, 128]]))
                        stage_cast(src, 256, castG, writes=["WIgq"], dst_dims=[[1024, 128], [128, 2], [1, 128]],
                                   engs=('act', 'dve'))
                    th.append(t1)
                def t2(hl=hl, h=h):
                    def castD1(e, fb):
                        return e.tensor_copy(out=BD[:, hl * MD: hl * MD + 1024], in_=fb[:, 0:1024])
                    stage_cast(bd_d.ap()[:, h * MD: h * MD + 1024], 1024, castD1, writes=[("BD", hl)], engs=('act', 'dve'))
                def t3(hl=hl, h=h):
                    def castD2(e, fb):
                        return e.tensor_copy(out=BD[:, hl * MD + 1024: hl * MD + MD], in_=fb[:, 0:MD - 1024])
                    stage_cast(bd_d.ap()[:, h * MD + 1024: (h + 1) * MD], MD - 1024, castD2, writes=[("BD", hl)],
                               engs=('act', 'dve'))
                th.append(t2)
                th.append(t3)
            return th

        P.op('pool', lambda e: e.memset(VA[:], 1.0), writes=["VA_all"])
        P.op('pool', lambda e: e.memset(VAW[:], 1.0), writes=["VAW_all"])
        load_static_weights()

        xb_state = {'k': 0, 'm': 0}

        def prefetch_x(src_d, row0):
            sl = xb_state['k'] % 4
            xb_state['k'] += 1
            P.op('pool', lambda e, sl=sl: e.dma_start(out=XBF[sl][:], in_=src_d.ap()[row0:row0 + 128, :]),
                 writes=[("xbf", sl)], dma=xbkeys[sl])
            return sl

        def make_uT_from(sl, bslot, b):
            par = xb_state['m'] % 2
            xb_state['m'] += 1
            if par == 0:
                evk, odk, evb, odb, evo, odo = bankA(0), "psC", psA_b, psC_b, 0, 0
            else:
                evk, odk, evb, odb, evo, odo = bankB(0), bankB(1), psB_b, psB_b, 0, 1024

            dve_dcs = [0, 2, 4, 6, 7]
            act_dcs = [1, 3, 5]

            def tsrc(dc):
                if dc in dve_dcs:
                    k = dve_dcs.index(dc)
                    return evb[:, evo + k * 128: evo + (k + 1) * 128]
                k = act_dcs.index(dc)
                return odb[:, odo + k * 128: odo + (k + 1) * 128]

            def tr(e):
                ins = None
                for dc in range(8):
                    ins = e.transpose(out=tsrc(dc), in_=XBF[sl][:, dc * 128:(dc + 1) * 128], identity=identb[:])
                return ins
            P.op('pe', tr, reads=[("xbf", sl), identb.name], writes=[evk, odk])
            for dc in range(8):
                o = dc * 768 + bslot * 128
                if dc in dve_dcs:
                    P.op('dve', lambda e, dc=dc, o=o: e.tensor_scalar(
                        out=uT[:, o:o + 128], in0=tsrc(dc),
                        scalar1=s1p[:, b * 8 + dc:b * 8 + dc + 1], scalar2=shf[:, b * 8 + dc:b * 8 + dc + 1],
                        op0=ALU.mult, op1=ALU.add),
                        reads=[evk], writes=[("uT", bslot, dc)])
                else:
                    P.op('act', lambda e, dc=dc, o=o: e.activation(
                        out=uT[:, o:o + 128], in_=tsrc(dc), func=AF.Identity,
                        bias=shf[:, b * 8 + dc:b * 8 + dc + 1], scale=s1p[:, b * 8 + dc:b * 8 + dc + 1]),
                        reads=[odk], writes=[("uT", bslot, dc)])

        def make_uT_list(src_d, items, b):
            slots = {}
            for i in range(min(NPF, len(items))):
                slots[i] = prefetch_x(src_d, items[i][1] * 128)
            for i, (bs, lb) in enumerate(items):
                if i + NPF < len(items):
                    slots[i + NPF] = prefetch_x(src_d, items[i + NPF][1] * 128)
                make_uT_from(slots[i], bs, b)

        def uT_keys(bslots):
            return [("uT", bs, dc) for bs in bslots for dc in range(8)]

        def proj_fm(W, pitch, col, bslot0, nb, pbank, reads_extra=()):
            def f(e):
                ins = None
                for dc in range(8):
                    ins = e.matmul(out=psA[:, pbank * 512: pbank * 512 + nb * 128],
                                   lhsT=W[:, dc * pitch + col: dc * pitch + col + 128],
                                   rhs=uT[:, dc * 768 + bslot0 * 128: dc * 768 + (bslot0 + nb) * 128],
                                   start=(dc == 0), stop=(dc == 7))
                return ins
            P.op('pe', f, reads=uT_keys(range(bslot0, bslot0 + nb)) + list(reads_extra), writes=[bankA(pbank)])

        def proj_tm(W, pitch, col, ncol, bslot, pbank, reads_extra=()):
            def f(e):
                ins = None
                for dc in range(8):
                    ins = e.matmul(out=psA[:, pbank * 512: pbank * 512 + ncol],
                                   lhsT=uT[:, dc * 768 + bslot * 128: dc * 768 + (bslot + 1) * 128],
                                   rhs=W[:, dc * pitch + col: dc * pitch + col + ncol],
                                   start=(dc == 0), stop=(dc == 7))
                return ins
            P.op('pe', f, reads=uT_keys([bslot]) + list(reads_extra), writes=[bankA(pbank)])

        pb_state = {'i': 0}

        def next_pbank():
            b = 1 + pb_state['i'] % 3
            pb_state['i'] += 1
            return b

        def evac_copy(out_ap, in_ap, reads, writes, scale=None, which=None):
            if which is None:
                which = state['evac'] % 2
                state['evac'] += 1
            if which == 0:
                if scale is None:
                    P.op('dve', lambda e: e.tensor_copy(out=out_ap, in_=in_ap), reads=reads, writes=writes)
                else:
                    P.op('dve', lambda e: e.tensor_scalar(out=out_ap, in0=in_ap, scalar1=scale, scalar2=None, op0=ALU.mult),
                         reads=reads, writes=writes)
            else:
                if scale is None:
                    P.op('act', lambda e: e.copy(out=out_ap, in_=in_ap), reads=reads, writes=writes)
                else:
                    P.op('act', lambda e: e.mul(out=out_ap, in_=in_ap, mul=scale), reads=reads, writes=writes)

        def run_job(x_d, y_d, b, S_kv, n_qt, is_sample, groups, next_first_heads):
            NB = S_kv // 128
            n_groups = len(groups)
            for gi, heads in enumerate(groups):
                G = len(heads)
                final = (gi == n_groups - 1)
                if not state.get('kv_preloaded'):
                    load_kv_weights(heads)
                state['kv_preloaded'] = False
                qw = q_weight_thunks(heads)
                nxt_heads = groups[gi + 1] if gi + 1 < n_groups else next_first_heads

                NT = NB // 2

                kv_slots = {}
                for blk in range(NPF):
                    kv_slots[blk] = prefetch_x(x_d, blk * 128)

                def front(n):
                    base = (n % 3) * 2
                    for k in range(2):
                        blk = 2 * n + k
                        if blk + NPF < NB:
                            kv_slots[blk + NPF] = prefetch_x(x_d, (blk + NPF) * 128)
                        make_uT_from(kv_slots[blk], base + k, b)

                def back(n):
                    base = (n % 3) * 2
                    for hl in range(G):
                        pbk = next_pbank()
                        proj_fm(WIg, 1024, 256 + hl * 128, base, 2, pbk, reads_extra=["WIgkv"])
                        evac_copy(KT[:, hl * S_kv + n * 256: hl * S_kv + (n + 1) * 256],
                                  psA[:, pbk * 512: pbk * 512 + 256],
                                  reads=[bankA(pbk)], writes=[("KT", hl, 2 * n), ("KT", hl, 2 * n + 1)])
                    for k in range(2):
                        tb = 2 * n + k
                        pbk = next_pbank()
                        proj_tm(WIg, 1024, 512, G * 128, base + k, pbk, reads_extra=["WIgkv"])
                        state['evac'] += 1
                        for hl in range(G):
                            o = (tb * G + hl) * 130
                            evac_copy(VA[:, o:o + 128], psA[:, pbk * 512 + hl * 128: pbk * 512 + (hl + 1) * 128],
                                      reads=[bankA(pbk), "VA_all"], writes=[("VA", hl, tb)], which=state['evac'] % 2)

                for n in range(NT + 1):
                    if n < NT:
                        front(n)
                    if kstop == 22:
                        raise _Stop()
                    if n >= 1:
                        back(n - 1)
                    if n >= 1 and qw:
                        qw.pop(0)()
                while qw:
                    qw.pop(0)()
                    if kstop == 23 and n == 1:
                        raise _Stop()
                    if kstop == 2 and n == 2:
                        raise _Stop()
                if kstop == 3:
                    raise _Stop()

                for t in range(n_qt):
                    if t == n_qt - 1 and nxt_heads is not None:
                        load_kv_weights(nxt_heads)
                        state['kv_preloaded'] = True
                    q_phase(x_d, y_d, b, S_kv, NB, t, n_qt, is_sample, heads, final, gi, groups)
                    if kstop == 4:
                        raise _Stop()
                    if kstop == 5 and final:
                        raise _Stop()
                if kstop == 6 and final:
                    raise _Stop()

        def tile_blocks(t, n_qt, NB, is_sample, final):
            blocks = []
            if final:
                if t > 0:
                    blocks.append((0, 4 * t - 1))
                elif is_sample:
                    blocks.append((0, NB - 1))
                for i in range(4):
                    blocks.append((1 + i, 4 * t + i))
                if t < n_qt - 1 or is_sample:
                    blocks.append((5, 4 * t + 4))
            else:
                for i in range(4):
                    blocks.append((1 + i, 4 * t + i))
            return blocks

        def prepA_steps(x_d, b, t, n_qt, NB, is_sample, final, G):
            blocks = tile_blocks(t, n_qt, NB, is_sample, final)
            par = t % 2
            n = len(blocks)
            slots = {}
            steps = []

            def pf(i):
                slots[i] = prefetch_x(x_d, blocks[i][1] * 128)

            def s_pf0():
                for i in range(min(NPF, n)):
                    pf(i)
            steps.append((s_pf0, 0))
            for i, (bs, lb) in enumerate(blocks):
                def s_tr(i=i):
                    if i + NPF < n:
                        pf(i + NPF)
                    sl = slots[i]

                    def tr(e):
                        ins = None
                        for dc in range(8):
                            ins = e.transpose(out=psC_b[:, dc * 128:(dc + 1) * 128], in_=XBF[sl][:, dc * 128:(dc + 1) * 128],
                                              identity=identb[:])
                        return ins
                    P.op('pe', tr, reads=[("xbf", sl), identb.name], writes=["psC"])

                def s_ev(bs=bs):
                    for dc in range(8):
                        o = dc * 768 + bs * 128
                        P.op('dve', lambda e, dc=dc, o=o: e.tensor_scalar(
                            out=uT[:, o:o + 128], in0=psC_b[:, dc * 128:(dc + 1) * 128],
                            scalar1=s1p[:, b * 8 + dc:b * 8 + dc + 1], scalar2=shf[:, b * 8 + dc:b * 8 + dc + 1],
                            op0=ALU.mult, op1=ALU.add),
                            reads=["psC"], writes=[("uT", bs, dc)])
                steps.append((s_tr, 0))
                steps.append((s_ev, 2))

            def fm_part(col, dcs):
                def f(e):
                    ins = None
                    for dc in dcs:
                        ins = e.matmul(out=psC[:, 0:512],
                                       lhsT=WIg[:, dc * 1024 + col: dc * 1024 + col + 128],
                                       rhs=uT[:, dc * 768 + 128: dc * 768 + 640],
                                       start=(dc == 0), stop=(dc == 7))
                    return ins
                P.op('pe', f, reads=uT_keys(range(1, 5)) + ["WIgq"], writes=["psC"])

            for hl in range(G):
                steps.append((lambda hl=hl: fm_part(hl * 128, range(0, 4)), 0))
                steps.append((lambda hl=hl: fm_part(hl * 128, range(4, 8)), 0))

                def s_qe(hl=hl):
                    P.op('dve', lambda e: e.tensor_scalar(out=QT[par][:, hl * 512:(hl + 1) * 512], in0=psC[:, 0:512],
                                                          scalar1=0.125, scalar2=None, op0=ALU.mult),
                         reads=["psC"], writes=[("QT", par, hl)])
                steps.append((s_qe, 2))
            for qb in range(4):
                def s_g(qb=qb):
                    def f(e):
                        ins = None
                        for dc in range(8):
                            ins = e.matmul(out=psC[:, 0:G * 128],
                                           lhsT=uT[:, dc * 768 + (1 + qb) * 128: dc * 768 + (2 + qb) * 128],
                                           rhs=WIg[:, dc * 1024 + 768: dc * 1024 + 768 + G * 128],
                                           start=(dc == 0), stop=(dc == 7))
                        return ins
                    P.op('pe', f, reads=uT_keys([1 + qb]) + ["WIgq"], writes=["psC"])

                def s_ge(qb=qb):
                    P.op('dve', lambda e: e.tensor_copy(out=SGb[par][:, qb * 256: qb * 256 + G * 128], in_=psC[:, 0:G * 128]),
                         reads=["psC"], writes=[("SGb", par, qb)])
                steps.append((s_g, 0))
                steps.append((s_ge, 1))
            return steps

        def bg_drain(bg):
            while bg:
                fn, _ = bg.pop(0)
                fn()

        def q_phase(x_d, y_d, b, S_kv, NB, t, n_qt, is_sample, heads, final, gi, groups):
            G = len(heads)
            par = t % 2
            blocks = tile_blocks(t, n_qt, NB, is_sample, final)
            has_left = any(bs == 0 for bs, _ in blocks)
            has_right = any(bs == 5 for bs, _ in blocks)
            if t == 0:
                make_uT_list(x_d, blocks, b)
                for hl in range(G):
                    pbk = next_pbank()
                    proj_fm(WIg, 1024, hl * 128, 1, 4, pbk, reads_extra=["WIgq"])
                    evac_copy(QT[par][:, hl * 512:(hl + 1) * 512], psA[:, pbk * 512:(pbk + 1) * 512],
                              reads=[bankA(pbk)], writes=[("QT", par, hl)], scale=0.125)
                for qb in range(4):
                    pbk = next_pbank()
                    proj_tm(WIg, 1024, 768, G * 128, 1 + qb, pbk, reads_extra=["WIgq"])
                    evac_copy(SGb[par][:, qb * 256: qb * 256 + G * 128], psA[:, pbk * 512: pbk * 512 + G * 128],
                              reads=[bankA(pbk)], writes=[("SGb", par, qb)])

            if final:
                for c in range(4):
                    pbk = next_pbank()
                    proj_fm(WIs, 1280, c * 128, 1, 4, pbk, reads_extra=["WIs"])
                    evac_copy(QAT[:, c * 512:(c + 1) * 512], psA[:, pbk * 512:(pbk + 1) * 512],
                              reads=[bankA(pbk)], writes=[("QAT", c)], scale=0.125)
                for qb in range(4):
                    pbk = next_pbank()
                    proj_tm(WIs, 1280, 512, 512, 1 + qb, pbk, reads_extra=["WIs"])
                    P.op('act', lambda e, qb=qb, pbk=pbk: e.activation(
                        out=SGa[:, qb * 512:(qb + 1) * 512], in_=psA[:, pbk * 512:(pbk + 1) * 512], func=AF.Silu),
                        reads=[bankA(pbk)], writes=[("SGa", qb)])
                bs0 = blocks[0][0]
                nbk = len(blocks)
                pieces = []
                s0 = bs0
                rem = nbk
                while rem > 0:
                    n = min(4, rem)
                    pieces.append((s0, n))
                    s0 += n
                    rem -= n
                for (s0, n) in pieces:
                    pbk = next_pbank()
                    proj_fm(WIs, 1280, 1024, s0, n, pbk, reads_extra=["WIs"])
                    evac_copy(KAT[:, s0 * 128:(s0 + n) * 128], psA[:, pbk * 512: pbk * 512 + n * 128],
                              reads=[bankA(pbk)], writes=["KAT"])
                for bs, lb in blocks:
                    pbk = next_pbank()
                    proj_tm(WIs, 1280, 1152, 128, bs, pbk, reads_extra=["WIs"])
                    P.op('dve', lambda e, bs=bs, pbk=pbk: e.tensor_copy(
                        out=dap(VAW, bs * 132, [[VAW_pitch, 128], [66, 2], [1, 64]]),
                        in_=dap(psA, pbk * 512, [[2048, 128], [64, 2], [1, 64]])),
                        reads=[bankA(pbk), "VAW_all"], writes=[("VAW", bs)])
                window_attn(t, n_qt, is_sample, blocks, has_left, has_right)

            bg = []
            if state.get('pending_post'):
                bg += state['pending_post']
                state['pending_post'] = None
            if t + 1 < n_qt:
                bg += prepA_steps(x_d, b, t + 1, n_qt, NB, is_sample, final, G)
            for hl, h in enumerate(heads):
                diff_attn(S_kv, NB, t, n_qt, is_sample, hl, h, G, bg)
            bg_drain(bg)
            post = post_diff_steps(t, heads, G, final, gi, groups, n_qt, fg=(final or t == n_qt - 1))
            if final or t == n_qt - 1:
                bg_drain(post)
            else:
                state['pending_post'] = post
            if dbg and final and (not is_sample) and t == 0:
                P.op('sp', lambda e: e.dma_start(out=dbg_ot.ap(), in_=OT[:]),
                     reads=[("OT", fc) for fc in range(6)], writes=["dbg1"], dma='dbg')
                P.op('sp', lambda e: e.dma_start(out=dbg_car.ap()[:, 0:512], in_=CAR[:, 0:512]),
                     reads=[("CAR", 0, 0)], writes=["dbg2"], dma='dbg')
                P.op('sp', lambda e: e.dma_start(out=dbg_car.ap()[:, 512:1024], in_=CAR[:, 4096:4608]),
                     reads=[("CAR", 1, 0)], writes=["dbg3"], dma='dbg')
                P.op('sp', lambda e: e.dma_start(out=dbg_c.ap()[:, 0:16], in_=s1p[:]), writes=["dbg4"], dma='dbg')
                P.op('sp', lambda e: e.dma_start(out=dbg_c.ap()[:, 16:32], in_=shf[:]), writes=["dbg5"], dma='dbg')
                P.op('sp', lambda e: e.dma_start(out=dbg_c.ap()[:, 32:33], in_=neglam[:], allow_slow_non_contiguous=True), writes=["dbg6"], dma='dbg')
                P.op('sp', lambda e: e.dma_start(out=dbg_c.ap()[:, 33:41], in_=expsink[:]), writes=["dbg7"], dma='dbg')
                P.op('sp', lambda e: e.dma_start(out=dbg_c.ap()[:, 41:49], in_=cfar[:]), writes=["dbg8"], dma='dbg')
                P.op('sp', lambda e: e.dma_start(out=dbg_sg.ap()[:, 0:2048], in_=SGa[:]),
                     reads=[("SGa", qb) for qb in range(4)], writes=["dbg9"], dma='dbg')
                P.op('sp', lambda e: e.dma_start(out=dbg_sg.ap()[:, 2048:3072], in_=SGb[0][:]),
                     reads=[("SGb", 0, qb) for qb in range(4)], writes=["dbg10"], dma='dbg')
            if final:
                finalize(x_d, y_d, b, t, n_qt, heads, gi, groups)

        VAW_pitch = 6 * 2 * 66

        def window_attn(t, n_qt, is_sample, blocks, has_left, has_right):
            present = {bs for bs, _ in blocks}
            wrapL = is_sample and t == 0
            wrapR = is_sample and t == n_qt - 1
            def win_chunk(c):
                par = c % 2
                if par == 0:
                    accU = [(psB, 0, bankB(0)), (psB, 512, bankB(1))]
                else:
                    accU = [(psB, 1024, bankB(2)), (psC, 0, "psC")]
                jbs = [jb for jb in range(6) if jb in present]
                first = [True, True]

                def geom(jb):
                    qlo = max(0, jb - 2)
                    qhi = min(3, jb)
                    nq = qhi - qlo + 1
                    return qlo, nq, nq * 128, (qlo - (jb - 2)) * 128

                def qk(jb, slot, c=c):
                    qlo, nq, N, so = geom(jb)
                    use_wrap = (wrapL and jb == 0) or (wrapR and jb == 5)

                    def f(e):
                        ins = None
                        for u in range(2):
                            e.matmul(out=psA[:, (slot * 2 + u) * 512: (slot * 2 + u) * 512 + N],
                                     lhsT=KAT[u * 64:(u + 1) * 64, jb * 128:(jb + 1) * 128],
                                     rhs=QAT[u * 64:(u + 1) * 64, c * 512 + qlo * 128: c * 512 + qlo * 128 + N],
                                     start=True, stop=False)
                        for u in range(2):
                            hp = 2 * c + u
                            if use_wrap:
                                wo = 0 if jb == 0 else 1024
                                rhs = BXW[:, wo + hp * 128: wo + hp * 128 + 128]
                            else:
                                rhs = BW[:, hp * 384 + so: hp * 384 + so + N]
                            ins = e.matmul(out=psA[:, (slot * 2 + u) * 512: (slot * 2 + u) * 512 + N],
                                           lhsT=identb[:], rhs=rhs, start=False, stop=True)
                        return ins
                    rd = ["KAT", ("QAT", c), identb.name]
                    rd += [] if use_wrap else [("BW", 2 * c), ("BW", 2 * c + 1)]
                    P.op('pe', f, reads=rd, writes=[bankA(slot * 2), bankA(slot * 2 + 1)])

                def ex(jb, slot):
                    qlo, nq, N, so = geom(jb)
                    P.op('act', lambda e: e.activation(
                        out=dap(PT[slot], 0, [[1024, 128], [384, 2], [1, N]]),
                        in_=dap(psA, slot * 1024, [[2048, 128], [512, 2], [1, N]]), func=AF.Exp),
                        reads=[bankA(slot * 2), bankA(slot * 2 + 1)], writes=[("PT", slot)])

                def pv(jb, slot):
                    qlo, nq, N, so = geom(jb)
                    sts = []
                    for u in range(2):
                        for qi in range(nq):
                            sts.append(first[u])
                            first[u] = False
                    sts = tuple(sts)

                    def f(e):
                        ins = None
                        k = 0
                        for u in range(2):
                            pt_, base, _ = accU[u]
                            for qi in range(nq):
                                qb = qlo + qi
                                off = base + qb * 66
                                ins = e.matmul(out=pt_[:, off: off + 65],
                                               lhsT=PT[slot][:, u * 384 + qi * 128: u * 384 + (qi + 1) * 128],
                                               rhs=VAW[:, (jb * 2 + u) * 66: (jb * 2 + u) * 66 + 65],
                                               start=sts[k], stop=True, skip_group_check=True)
                                k += 1
                        return ins
                    P.op('pe', f, reads=[("PT", slot), ("VAW", jb)], writes=[accU[0][2], accU[1][2]])

                qk(jbs[0], 0)
                for idx, jb in enumerate(jbs):
                    if idx + 1 < len(jbs):
                        qk(jbs[idx + 1], (idx + 1) % 2)
                    ex(jb, idx % 2)
                    pv(jb, idx % 2)
                for u in range(2):
                    pt_, base, bk = accU[u]
                    pitch = 1536 if pt_ is psB else 512
                    P.op('dve', lambda e, u=u, pt_=pt_, base=base, pitch=pitch, c=c, par=par: e.tensor_scalar(
                        out=den[:, par * 8 + u * 4: par * 8 + u * 4 + 4], in0=dap(pt_, base + 64, [[pitch, 128], [66, 4]]),
                        scalar1=expsink[:, 2 * c + u:2 * c + u + 1], scalar2=None, op0=ALU.add),
                        reads=[bk], writes=[("den", par, u)])
                P.op('dve', lambda e, par=par: e.reciprocal(out=rec[:, par * 8: par * 8 + 8], in_=den[:, par * 8: par * 8 + 8]),
                     reads=[("den", par, 0), ("den", par, 1)], writes=[("recw", par)])
                for u in range(2):
                    pt_, base, bk = accU[u]
                    for qb in range(4):
                        a = u * 4 + qb
                        off = base + qb * 66
                        hp = 2 * c + u
                        P.op('dve', lambda e, a=a, off=off, hp=hp, qb=qb, pt_=pt_, par=par: e.scalar_tensor_tensor(
                            out=OAg[:, qb * 512 + hp * 64: qb * 512 + hp * 64 + 64],
                            in0=pt_[:, off: off + 64], scalar=rec[:, par * 8 + a: par * 8 + a + 1],
                            in1=SGa[:, qb * 512 + hp * 64: qb * 512 + hp * 64 + 64],
                            op0=ALU.mult, op1=ALU.mult),
                            reads=[bk, ("recw", par), ("SGa", qb)], writes=[("OAg", qb, hp)])

            for c in range(4):
                win_chunk(c)
            for fc in range(4):
                def trw(e, fc=fc):
                    ins = None
                    for qb in range(4):
                        ins = e.transpose(out=psC_b[:, qb * 128:(qb + 1) * 128],
                                          in_=OAg[:, qb * 512 + fc * 128: qb * 512 + (fc + 1) * 128],
                                          identity=identb[:])
                    return ins
                P.op('pe', trw, reads=[("OAg", qb, hp) for qb in range(4) for hp in (2 * fc, 2 * fc + 1)] + [identb.name],
                     writes=["psC"])
                P.op('dve', lambda e, fc=fc: e.tensor_copy(out=OT[:, fc * 512:(fc + 1) * 512], in_=psC_b[:, 0:512]),
                     reads=["psC"], writes=[("OT", fc)])

        def diff_attn(S_kv, NB, t, n_qt, is_sample, hl, h, G, bg):
            par = t % 2
            kinds = []
            for i in range(NB):
                own = (not is_sample) or i < 16
                if own:
                    d = 4 * t - i
                    if -4 <= d <= 1:
                        kinds.append(('near', d))
                    elif d >= 2:
                        kinds.append(('far', cfar[:, h:h + 1], cfar.name))
                    else:
                        kinds.append(('far', cfar[:, 4 + h:5 + h], cfar.name))
                else:
                    if i == 16 and t == 3:
                        kinds.append(('wrapR',))
                    elif i == NB - 1 and t == 0:
                        kinds.append(('wrapL',))
                    else:
                        kinds.append(('far', farS[:, h * 64 + i: h * 64 + i + 1], ("farS", h)))
            if is_sample and t == 0:
                P.op('dve', lambda e: e.tensor_scalar(
                    out=BXf[:, 0:512], in0=BD[:, hl * MD + 640: hl * MD + 1152],
                    scalar1=cfar[:, 4 + h:5 + h], scalar2=flags[:, 0:1], op0=ALU.subtract, op1=ALU.mult),
                    reads=[("BD", hl), cfar.name, flags.name], writes=["BXf"])
                P.op('dve', lambda e: e.tensor_scalar(
                    out=BX[:, 0:512], in0=BXf[:, 0:512], scalar1=cfar[:, 4 + h:5 + h], scalar2=None, op0=ALU.add),
                    reads=["BXf"], writes=["BX"])
            if is_sample and t == 3:
                P.op('dve', lambda e: e.tensor_scalar(
                    out=BXf[:, 512:1024], in0=BD[:, hl * MD: hl * MD + 512],
                    scalar1=cfar[:, h:h + 1], scalar2=flags[:, 1:2], op0=ALU.subtract, op1=ALU.mult),
                    reads=[("BD", hl), cfar.name, flags.name], writes=["BXf"])
                P.op('dve', lambda e: e.tensor_scalar(
                    out=BX[:, 512:1024], in0=BXf[:, 512:1024], scalar1=cfar[:, h:h + 1], scalar2=None, op0=ALU.add),
                    reads=["BXf"], writes=["BX"])

            def qk(i):
                kind = kinds[i]
                slot = i % 2

                def f(e):
                    ins = None
                    near = kind[0] != 'far'
                    for m in range(2):
                        ins = e.matmul(out=psA[:, (slot * 2 + m) * 512:(slot * 2 + m + 1) * 512],
                                       lhsT=KT[m * 64:(m + 1) * 64, hl * S_kv + i * 128: hl * S_kv + (i + 1) * 128],
                                       rhs=QT[par][m * 64:(m + 1) * 64, hl * 512:(hl + 1) * 512],
                                       start=True, stop=not near)
                    if near:
                        if kind[0] == 'near':
                            o = hl * MD + 128 * (kind[1] + 4)
                            rhs = BD[:, o:o + 512]
                        elif kind[0] == 'wrapL':
                            rhs = BX[:, 0:512]
                        else:
                            rhs = BX[:, 512:1024]
                        for m in range(2):
                            ins = e.matmul(out=psA[:, (slot * 2 + m) * 512:(slot * 2 + m + 1) * 512],
                                           lhsT=identb[:], rhs=rhs, start=False, stop=True)
                    return ins
                rd = [("KT", hl, i), ("QT", par, hl)]
                if kind[0] == 'near':
                    rd += [("BD", hl), identb.name]
                elif kind[0] != 'far':
                    rd += ["BX", identb.name]
                P.op('pe', f, reads=rd, writes=[bankA(slot * 2), bankA(slot * 2 + 1)])

            def ex(i):
                kind = kinds[i]
                slot = i % 2
                if kind[0] == 'far':
                    bias_ap, bkey = kind[1], kind[2]
                else:
                    bias_ap, bkey = zero_c[:], zero_c.name
                P.op('act', lambda e: e.activation(out=PT[slot][:], in_=psA[:, slot * 1024:(slot + 1) * 1024],
                                                   func=AF.Exp, bias=bias_ap, scale=1.0),
                     reads=[bankA(slot * 2), bankA(slot * 2 + 1), bkey], writes=[("PT", slot)])

            def pv(i):
                slot = i % 2

                def f(e):
                    ins = None
                    for m in range(2):
                        for qb in range(4):
                            a = m * 4 + qb
                            bank = a // 3
                            off = bank * 512 + (a % 3) * 130
                            vo = (i * G + hl) * 130
                            ins = e.matmul(out=psB[:, off:off + 129],
                                           lhsT=PT[slot][:, m * 512 + qb * 128: m * 512 + (qb + 1) * 128],
                                           rhs=VA[:, vo:vo + 129],
                                           start=(i == 0 and a % 3 == 0), stop=(i == NB - 1), skip_group_check=True)
                    return ins
                P.op('pe', f, reads=[("PT", slot), ("VA", hl, i)], writes=[bankB(0), bankB(1), bankB(2)])

            qk(0)
            qk(1)
            cool = 2
            for i in range(NB):
                ex(i)
                if i + 2 < NB:
                    qk(i + 2)
                pv(i)
                if bg:
                    if cool <= 0:
                        fn, cool = bg.pop(0)
                        fn()
                    else:
                        cool -= 1
            for bank in range(3):
                na = 3 if bank < 2 else 2
                P.op('dve', lambda e, bank=bank, na=na: e.reciprocal(
                    out=rec[:, bank * 3: bank * 3 + na], in_=dap(psB, bank * 512 + 128, [[1536, 128], [130, na]])),
                    reads=[bankB(bank)], writes=[("recd", bank)])
            P.op('dve', lambda e: e.tensor_scalar(out=rec[:, 8:12], in0=rec[:, 4:8], scalar1=neglam[:, 0:1], scalar2=None, op0=ALU.mult),
                 reads=[("recd", 1), ("recd", 2), neglam.name], writes=["recn"])
            for qb in range(4):
                a0 = qb
                o0 = (a0 // 3) * 512 + (a0 % 3) * 130
                oo = (qb * 2 + hl) * 128
                P.op('dve', lambda e, a0=a0, o0=o0, oo=oo: e.tensor_scalar(
                    out=OBf[:, oo:oo + 128], in0=psB[:, o0:o0 + 128], scalar1=rec[:, a0:a0 + 1], scalar2=None, op0=ALU.mult),
                    reads=[bankB(a0 // 3), ("recd", a0 // 3)], writes=[("OBf", qb, hl)])
            for qb in range(4):
                a1 = 4 + qb
                o1 = (a1 // 3) * 512 + (a1 % 3) * 130
                P.op('dve', lambda e, qb=qb, o1=o1: e.tensor_scalar(
                    out=BXf[:, qb * 128:(qb + 1) * 128], in0=psB[:, o1:o1 + 128], scalar1=rec[:, 8 + qb:9 + qb], scalar2=None, op0=ALU.mult),
                    reads=[bankB(a1 // 3), "recn"], writes=["BXf"])
            for qb in range(4):
                oo = (qb * 2 + hl) * 128
                P.op('dve', lambda e, qb=qb, oo=oo: e.tensor_tensor(
                    out=OBf[:, oo:oo + 128], in0=OBf[:, oo:oo + 128], in1=BXf[:, qb * 128:(qb + 1) * 128], op=ALU.add),
                    reads=[("OBf", qb, hl), "BXf"], writes=[("OBf", qb, hl)])
                P.op('dve', lambda e, qb=qb, oo=oo: e.tensor_tensor(
                    out=sqj[:], in0=OBf[:, oo:oo + 128], in1=OBf[:, oo:oo + 128], op=ALU.mult),
                    reads=[("OBf", qb, hl)], writes=["sqj"])
                P.op('dve', lambda e, qb=qb: e.reduce_sum(out=ssq[:, qb * 2 + hl: qb * 2 + hl + 1], in_=sqj[:], axis=mybir.AxisListType.X),
                     reads=["sqj"], writes=[("ssq", qb, hl)])

        def post_diff_steps(t, heads, G, final, gi, groups, n_qt, fg=False):
            S_q = n_qt * 512
            par = t % 2
            steps = []
            sgk = [("SGb", par, qb) for qb in range(4)]

            def s1():
                P.op('act', lambda e: e.activation(out=BXf[:], in_=SGb[par][:], func=AF.Exp, scale=-1.0),
                     reads=sgk, writes=["BXf"])
            def s1f():
                P.op('act', lambda e: e.activation(out=SGb[par][:], in_=SGb[par][:], func=AF.Silu), reads=sgk, writes=sgk)
            if fg:
                steps.append((s1f, 0))
            else:
                steps.append((s1, 1))

            def s2():
                P.op('dve', lambda e: e.tensor_scalar(out=BXf[:], in0=BXf[:], scalar1=1.0, scalar2=None, op0=ALU.add),
                     reads=["BXf"], writes=["BXf"])
                P.op('dve', lambda e: e.reciprocal(out=BXf[:], in_=BXf[:]), reads=["BXf"], writes=["BXf"])
                P.op('dve', lambda e: e.tensor_tensor(out=SGb[par][:], in0=SGb[par][:], in1=BXf[:], op=ALU.mult),
                     reads=sgk + ["BXf"], writes=sgk)
            if not fg:
                steps.append((s2, 2))

            def s3():
                P.op('act', lambda e: e.activation(out=rstd[:, 0:8], in_=ssq[:, 0:8], func=AF.Sqrt, bias=eps_sub[:], scale=1.0 / 128.0),
                     reads=[("ssq", qb, hl) for qb in range(4) for hl in range(G)] + [eps_sub.name], writes=["rstd_a"])
                P.op('dve', lambda e: e.reciprocal(out=rstd[:, 0:8], in_=rstd[:, 0:8]), reads=["rstd_a"], writes=["rstd"])
            steps.append((s3, 2))
            for qb in range(4):
                def s4(qb=qb):
                    for hl in range(G):
                        oo = (qb * 2 + hl) * 128
                        P.op('dve', lambda e, hl=hl, oo=oo: e.scalar_tensor_tensor(
                            out=OBf[:, oo:oo + 128], in0=OBf[:, oo:oo + 128], scalar=rstd[:, qb * 2 + hl: qb * 2 + hl + 1],
                            in1=subw_bc[:], op0=ALU.mult, op1=ALU.mult),
                            reads=[("OBf", qb, hl), "rstd", subw_bc.name], writes=[("OBf", qb, hl)])
                        P.op('pool', lambda e, hl=hl, oo=oo: e.tensor_tensor(
                            out=OBn[:, oo:oo + 128], in0=OBf[:, oo:oo + 128],
                            in1=SGb[par][:, qb * 256 + hl * 128: qb * 256 + (hl + 1) * 128], op=ALU.mult),
                            reads=[("OBf", qb, hl), ("SGb", par, qb)], writes=[("OBn", qb, hl)])
                steps.append((s4, 1))
            for hl, h in enumerate(heads):
                def s5(hl=hl):
                    def trd(e):
                        ins = None
                        for qb in range(4):
                            oo = (qb * 2 + hl) * 128
                            ins = e.transpose(out=psC_b[:, qb * 128:(qb + 1) * 128], in_=OBn[:, oo:oo + 128], identity=identb[:])
                        return ins
                    P.op('pe', trd, reads=[("OBn", qb, hl) for qb in range(4)] + [identb.name], writes=["psC"])

                def s6(hl=hl):
                    if final:
                        P.op('dve', lambda e: e.tensor_copy(out=OT[:, (4 + hl) * 512:(5 + hl) * 512], in_=psC_b[:, 0:512]),
                             reads=["psC"], writes=[("OT", 4 + hl)])
                    else:
                        ci = sum(len(g) for g in groups[:gi]) + hl
                        o = ci * S_q + t * 512
                        P.op('dve', lambda e: e.tensor_copy(out=CAR[:, o:o + 512], in_=psC_b[:, 0:512]),
                             reads=["psC"], writes=[("CAR", ci, t)])
                steps.append((s5, 1))
                steps.append((s6, 1))
            return steps

        def finalize(x_d, y_d, b, t, n_qt, heads, gi, groups):
            S_q = n_qt * 512
            n_car = sum(len(g) for g in groups[:gi])
            G = len(heads)
            fslots = [0, 1]

            def xload(qb):
                xs_ = fslots[qb % 2]
                row0 = (4 * t + qb) * 128
                P.op('sp', lambda e: e.dma_start(out=FB[xs_][:], in_=x_d.ap()[row0:row0 + 128, :]),
                     writes=[("FB", xs_)], dma=xkeys[xs_])

            xload(0)
            xload(1)
            for qb in range(4):
                fin_block(x_d, y_d, b, t, qb, S_q, n_car, G, fslots[qb % 2])
                if qb + 2 < 4:
                    xload(qb + 2)

        def fin_block(x_d, y_d, b, t, qb, S_q, n_car, G, xs_):
            row0 = (4 * t + qb) * 128
            ys_ = state['yslot'] % 2
            state['yslot'] += 1
            for n in range(2):
                def om(e, n=n):
                    ins = None
                    for fc in range(8):
                        if fc < 4:
                            lhsT = OT[:, fc * 512 + qb * 128: fc * 512 + (qb + 1) * 128]
                        elif fc - 4 < n_car:
                            o = (fc - 4) * S_q + t * 512 + qb * 128
                            lhsT = CAR[:, o:o + 128]
                        else:
                            o = (4 + fc - 4 - n_car) * 512 + qb * 128
                            lhsT = OT[:, o:o + 128]
                        ins = e.matmul(out=psA[:, (2 + n) * 512:(3 + n) * 512], lhsT=lhsT,
                                       rhs=WO[:, fc * 1024 + n * 512: fc * 1024 + (n + 1) * 512],
                                       start=(fc == 0), stop=(fc == 7))
                    return ins
                rd = [("OT", fc) for fc in range(4 + G)] + [("CAR", ci, t) for ci in range(n_car)] + ["WO"]
                P.op('pe', om, reads=rd, writes=[bankA(2 + n)])
            Z = FB[xs_]
            zk = ("FB", xs_)
            mo = qb * 2
            P.op('dve', lambda e: e.tensor_tensor(out=BXf[:], in0=psA[:, 1024:2048], in1=gate_bc[:, b * 1024:(b + 1) * 1024], op=ALU.mult),
                 reads=[bankA(2), bankA(3), ("gate_bc", b, 0), ("gate_bc", b, 1)], writes=["BXf"])
            P.op('dve', lambda e: e.scalar_tensor_tensor(out=Z[:], in0=Z[:], scalar=ALPHA, in1=BXf[:], op0=ALU.mult, op1=ALU.add),
                 reads=[zk, "BXf"], writes=[zk])
            P.op('dve', lambda e: e.bn_stats(out=bnst[:, 0:6], in_=Z[:, 0:512]), reads=[zk], writes=["bnst0"])
            P.op('dve', lambda e: e.bn_stats(out=bnst[:, 6:12], in_=Z[:, 512:1024]), reads=[zk], writes=["bnst1"])
            P.op('dve', lambda e: e.bn_aggr(out=mv[:, mo:mo + 2], in_=bnst[:]), reads=["bnst0", "bnst1"], writes=[("mv", qb)])
            P.op('act', lambda e: e.activation(out=lrs[:, qb:qb + 1], in_=mv[:, mo + 1:mo + 2], func=AF.Sqrt, bias=eps_ln[:], scale=1.0),
                 reads=[("mv", qb), eps_ln.name], writes=[("lrs_a", qb)])
            P.op('dve', lambda e: e.reciprocal(out=lrs[:, qb:qb + 1], in_=lrs[:, qb:qb + 1]), reads=[("lrs_a", qb)], writes=[("lrs", qb)])
            P.op('dve', lambda e: e.scalar_tensor_tensor(out=Z[:], in0=Z[:], scalar=mv[:, mo:mo + 1], in1=lng_bc[:],
                                                         op0=ALU.subtract, op1=ALU.mult),
                 reads=[zk, ("mv", qb)], writes=[zk])
            P.op('dve', lambda e: e.scalar_tensor_tensor(out=Z[:], in0=Z[:], scalar=lrs[:, qb:qb + 1], in1=lnb_bc[:],
                                                         op0=ALU.mult, op1=ALU.add),
                 reads=[zk, ("lrs", qb)], writes=[zk])
            P.op('sp', lambda e: e.dma_start(out=y_d.ap()[row0:row0 + 128, :], in_=Z[:]),
                 reads=[zk], writes=[("yout", ys_)], dma=['y0', 'y1'][ys_])

        try:
            if kstop == 1:
                raise _Stop()
            run_job(xp_d, yp_d, 0, SP_, 8, False, [[0, 1], [2, 3]], [0])
            run_job(xs_d, ys_d, 1, SS_, 4, True, [[0], [1], [2], [3]], None)
        except _Stop:
            pass
        P.op('sp', lambda e: None, reads=[("yout", 0), ("yout", 1)] + (["dbg%d" % i for i in range(1, 11)] if dbg else []))

        with nc.Block() as blk2:
            P.emit(blk2)
    return nc


_NC_CACHE = {}


def _bucket_table():
    import math
    import jax
    import jax.numpy as jnp
    try:
        dev = jax.devices('cpu')[0]
    except Exception:
        dev = None
    def run():
        rel = jnp.arange(-1400, 1401, dtype=jnp.int32)
        sign = jnp.where(rel > 0, 16, 0)
        n = jnp.abs(rel)
        nf = jnp.maximum(n, 1).astype(jnp.float32)
        large = 8 + (jnp.log(nf / 8) / math.log(128 / 8) * 8).astype(jnp.int32)
        large = jnp.minimum(large, 15)
        return np.asarray((sign + jnp.where(n < 8, n, large)).astype(jnp.int32))
    if dev is not None:
        with jax.default_device(dev):
            return run()
    return run()


def kernel(**inp):
    f = lambda k: np.ascontiguousarray(np.asarray(inp[k], dtype=np.float32))
    x_prompt = f('x_prompt'); x_sample = f('x_sample')
    c_prompt = f('c_prompt'); c_sample = f('c_sample')
    w_in = f('w_in')[0]; w_out = f('w_out')[0]; w_ada = f('w_ada')[0]; b_ada = f('b_ada')[0]
    ln_g = f('ln_g'); ln_b = f('ln_b'); sink = f('attn_sink')[0]
    rel_bias = f('rel_bias')
    subw = f('subln_w')
    lamv = np.concatenate([f('lambda_q1')[0], f('lambda_k1')[0], f('lambda_q2')[0], f('lambda_k2')[0]])[None, :]

    colperm = np.concatenate([np.arange(64) + 64 * h for h in PERM])
    w_in_l = w_in.copy()
    w_in_l[:, 0:512] = w_in[:, 0:512][:, colperm]
    w_in_l[:, 768:1280] = w_in[:, 768:1280][:, colperm]
    w_out_l = w_out.copy()
    w_out_l[0:512, :] = w_out[0:512, :][colperm, :]
    sinkp = sink[PERM][None, :]
    bt = _bucket_table()

    def bk(r):
        return bt[r + 1400]
    kk = np.arange(128)[:, None]
    mm = np.arange(MD)[None, :]
    bd_idx = bk(kk - mm + 512)
    bd_strip = np.stack([rel_bias[bd_idx, 8 + h] for h in range(4)], axis=1).reshape(128, 4 * MD)
    qq = np.arange(384)[None, :]
    relw = kk - qq + 128
    bw_idx = bk(relw)
    bw_strip = np.stack([rel_bias[bw_idx, PERM[hp]] for hp in range(8)], axis=1).reshape(128, 8 * 384)
    maskw = np.where(np.abs(relw) <= 128, 0.0, -BIG).astype(np.float32)
    relfar = np.concatenate([rel_bias[15, 8:12], rel_bias[31, 8:12]])[None, :]
    identf = np.eye(128, dtype=np.float32)

    import os
    dbg = bool(os.environ.get('KDBG'))
    kstop = int(os.environ.get('KSTOP', '0'))
    key = 'nc%d_%d' % (dbg, kstop)
    if key not in _NC_CACHE:
        _NC_CACHE[key] = build_program(dbg, kstop)
    nc = _NC_CACHE[key]

    in_maps = []
    for c in range(NCORES):
        s, j = c // 4, c % 4
        xs = np.ascontiguousarray(np.roll(x_sample[s], -SQS * j, axis=0))
        cc = np.stack([c_prompt[c], c_sample[s]], axis=0)
        cT = np.ascontiguousarray(cc.reshape(2, 8, 128).transpose(2, 1, 0).reshape(128, 16))
        fl = np.zeros((1, 66), np.float32)
        fl[0, 0] = 1.0 if j > 0 else 0.0
        fl[0, 1] = 1.0 if j < 3 else 0.0
        for i in range(64):
            fl[0, 2 + i] = 1.0 if i >= 64 - 16 * j else 0.0
        in_maps.append({
            "xp": x_prompt[c], "xs": xs, "cT": cT,
            "w_in": w_in_l, "w_out": w_out_l, "w_ada": w_ada,
            "bcol": np.ascontiguousarray(b_ada.reshape(24, 128).T),
            "bgate": np.ascontiguousarray(b_ada[None, 2048:3072]),
            "ln_g": ln_g, "ln_b": ln_b, "sinkp": np.ascontiguousarray(sinkp),
            "lamv": np.ascontiguousarray(lamv), "subw": subw,
            "relfar": np.ascontiguousarray(relfar),
            "bd_strip": np.ascontiguousarray(bd_strip.astype(np.float32)),
            "bw_strip": np.ascontiguousarray(bw_strip.astype(np.float32)),
            "maskw": maskw, "flags": fl, "identf": identf,
        })
    res = run_bass_kernel_spmd(nc, in_maps, core_ids=list(range(NCORES)))
    y_prompt = np.empty((8, SP_, D), np.float32)
    y_sample = np.empty((2, SS_, D), np.float32)
    for c in range(NCORES):
        r = res.results[c]
        s, j = c // 4, c % 4
        y_prompt[c] = np.asarray(r["yp"], np.float32)
        y_sample[s, SQS * j: SQS * (j + 1)] = np.asarray(r["ys"], np.float32)
    return (y_prompt, y_sample)
```

```python
import numpy as np
import concourse.bass as bass
import concourse.mybir as mybir
from concourse.bass_utils import run_bass_kernel_spmd

F32 = mybir.dt.float32
BF16 = mybir.dt.bfloat16
AF = mybir.ActivationFunctionType
ALU = mybir.AluOpType

NCORES = 8
D = 1024
DC = 8
SP_ = 4096
SS_ = 8192
SQS = 2048
ALPHA = 2.0 ** 0.25
LAMBDA_INIT = 0.2
BIG = 30000.0
LN_EPS = 1e-5
SUBLN_EPS = 1e-5
PERM = [0, 4, 1, 5, 2, 6, 3, 7]
MD = 1152


class Prog:
    ENG = ('sp', 'act', 'pe', 'dve', 'pool')

    def __init__(self, sems, dsems):
        self.sem = sems
        self.dsem = dsems
        self.dcnt = {k: 0 for k in dsems}
        self.base = {e: 0 for e in self.ENG}
        self.reset_block()

    def reset_block(self):
        self.ops = {e: [] for e in self.ENG}
        self.res = {}

    def op(self, eng, fn, reads=(), writes=(), dma=None):
        ops = self.ops[eng]
        idx = len(ops)
        if dma is not None:
            self.dcnt[dma] += 1
            ev = ('dma', dma, self.dcnt[dma])
        else:
            ev = (eng, idx)
        deps = {}

        def add(d, raw):
            if d[0] == 'dma':
                d = ('dma', d[1], self.dcnt[d[1]])
            if raw or d not in deps:
                deps[d] = raw or deps.get(d, False)

        def is_ps(k):
            return isinstance(k, str) and k.startswith('ps')

        ps_keys = {}
        for k in reads:
            if is_ps(k):
                ps_keys[k] = False
        for k in writes:
            if is_ps(k):
                ps_keys[k] = True
        for k, isw in ps_keys.items():
            r = self.res.get(k)
            if r is not None and r[0] is not None:
                add(r[0], (not isw) and r[2])
        for k in reads:
            if is_ps(k):
                continue
            r = self.res.get(k)
            if r is not None and r[0] is not None:
                add(r[0], True)
        for k in writes:
            if is_ps(k):
                continue
            r = self.res.get(k)
            if r is not None:
                if r[0] is not None:
                    add(r[0], False)
                for rv in r[1].values():
                    add(rv, False)
        for k in reads:
            if is_ps(k):
                continue
            r = self.res.setdefault(k, [None, {}])
            r[1][ev if dma is not None else eng] = ev
        for k in writes:
            if is_ps(k):
                continue
            self.res[k] = [ev, {}]
        for k, isw in ps_keys.items():
            self.res[k] = [ev, {}, isw]
        ops.append((fn, deps, ev, dma))
        return ev

    def emit(self, block):
        need = {e: set() for e in self.ENG}
        for e in self.ENG:
            for (fn, deps, ev, dma) in self.ops[e]:
                for d in deps:
                    if d[0] != 'dma':
                        need[d[0]].add(d[1])
        ordn = {}
        for e in self.ENG:
            s = sorted(need[e])
            ordn[e] = {idx: self.base[e] + i + 1 for i, idx in enumerate(s)}

        def body(en):
            def f(eng):
                seen = {}
                for idx, (fn, deps, ev, dma) in enumerate(self.ops[en]):
                    for d, raw in deps.items():
                        if d[0] == 'dma':
                            if dma is not None and d[1] == dma:
                                continue
                            key = ('dma', d[1])
                            val = 16 * d[2]
                            sem = self.dsem[d[1]]
                        else:
                            if d[0] == en:
                                if (not raw) or en == 'pe' or en == 'sp':
                                    continue
                            key = d[0]
                            val = ordn[d[0]][d[1]]
                            sem = self.sem[d[0]]
                        if seen.get(key, 0) >= val:
                            continue
                        eng.wait_ge(sem, val)
                        seen[key] = val
                    ins = fn(eng)
                    if dma is not None:
                        ins.then_inc(self.dsem[dma], 16)
                    elif idx in ordn[en]:
                        ins.then_inc(self.sem[en], 1)
            return f

        block.sync(body('sp'))
        block.scalar(body('act'))
        block.tensor(body('pe'))
        block.vector(body('dve'))
        block.gpsimd(body('pool'))
        for e in self.ENG:
            self.base[e] += len(ordn[e])
        self.reset_block()


class _CopyShim:
    def __init__(self, e):
        self.e = e

    def tensor_copy(self, out, in_):
        return self.e.copy(out=out, in_=in_)


def dap(t, off, dims):
    return bass.AP(t, off, [list(d) for d in dims])


class _Stop(Exception):
    pass


def build_program(dbg=False, kstop=0):
    nc = bass.Bass("TRN2", target_bir_lowering=False)
    if dbg:
        dbg_ot = nc.dram_tensor("dbg_ot", [128, 3072], BF16, kind="ExternalOutput")
        dbg_car = nc.dram_tensor("dbg_car", [128, 1024], BF16, kind="ExternalOutput")
        dbg_c = nc.dram_tensor("dbg_c", [128, 64], F32, kind="ExternalOutput")
        dbg_sg = nc.dram_tensor("dbg_sg", [128, 2048 + 1024], BF16, kind="ExternalOutput")

    def din(name, shape):
        return nc.dram_tensor(name, list(shape), F32, kind="ExternalInput")

    xp_d = din("xp", [SP_, D])
    xs_d = din("xs", [SS_, D])
    cT_d = din("cT", [128, 16])
    win_d = din("w_in", [D, 3328])
    wout_d = din("w_out", [D, D])
    wada_d = din("w_ada", [D, 3072])
    bcol_d = din("bcol", [128, 24])
    bgate_d = din("bgate", [1, 1024])
    lng_d = din("ln_g", [1, 1024])
    lnb_d = din("ln_b", [1, 1024])
    sink_d = din("sinkp", [1, 8])
    lamv_d = din("lamv", [1, 256])
    subw_d = din("subw", [1, 128])
    relfar_d = din("relfar", [1, 8])
    bd_d = din("bd_strip", [128, 4 * MD])
    bw_d = din("bw_strip", [128, 8 * 384])
    maskw_d = din("maskw", [128, 384])
    flags_d = din("flags", [1, 66])
    ident_d = din("identf", [128, 128])
    yp_d = nc.dram_tensor("yp", [SP_, D], F32, kind="ExternalOutput")
    ys_d = nc.dram_tensor("ys", [SQS, D], F32, kind="ExternalOutput")

    from contextlib import ExitStack
    es = ExitStack()

    def sb(name, shape, dt=F32):
        return es.enter_context(nc.sbuf_tensor("sb_" + name, list(shape), dt))

    def ps(name, shape, dt=F32):
        return es.enter_context(nc.psum_tensor("ps_" + name, list(shape), dt))

    with es:
        sems = {e: es.enter_context(nc.semaphore("s_" + e)) for e in ('act', 'pe', 'dve', 'pool')}
        dkeys = ['c0', 'wa', 'x0', 'x1', 'xb0', 'xb1', 'xb2', 'xb3', 'xb4', 'y0', 'y1', 'dbg']
        dsems = {k: es.enter_context(nc.semaphore("d_" + k)) for k in dkeys}
        P = Prog(sems, dsems)

        identf = sb("identf", [128, 128])
        identb = sb("identb", [128, 128], BF16)
        s1p = sb("s1p", [128, 16])
        shf = sb("shf", [128, 16])
        gate_bc = sb("gate_bc", [128, 2048])
        lng_bc = sb("lng_bc", [128, 1024])
        lnb_bc = sb("lnb_bc", [128, 1024])
        subw_bc = sb("subw_bc", [128, 128])
        neglam = sb("neglam", [128, 1])
        expsink = sb("expsink", [128, 8])
        cfar = sb("cfar", [128, 8])
        cdiff = sb("cdiff", [128, 4])
        farS = sb("farS", [128, 4 * 64])
        flags = sb("flags", [128, 66])
        negLR = sb("negLR", [128, 2])
        omf = sb("omf", [128, 2])
        BW = sb("BW", [128, 8 * 384], BF16)
        BXW = sb("BXW", [128, 2048], BF16)
        zero_c = sb("zero_c", [128, 1])
        eps_ln = sb("eps_ln", [128, 1])
        eps_sub = sb("eps_sub", [128, 1])

        psA = ps("psA", [128, 2048])
        psB = ps("psB", [128, 1536])
        psC = ps("psC", [128, 512])
        psA_b = psA.bitcast(BF16)
        psB_b = psB.bitcast(BF16)
        psC_b = psC.bitcast(BF16)

        def bankA(i):
            return "psA%d" % i

        def bankB(i):
            return "psB%d" % i

        WIs = sb("WIs", [128, 8 * 1280], BF16)
        WO = sb("WO", [128, 8 * 1024], BF16)
        FB = [sb("FB%d" % i, [128, 1024]) for i in range(2)]
        xkeys = ['x0', 'x1']
        xbkeys = ['xb0', 'xb1', 'xb2', 'xb3', 'xb4']
        NPF = 3
        state = {'xslot': 0, 'evac': 0, 'yslot': 0}

        wl_state = {'i': 0}

        def stage_cast(src_ap, ncols, cast_fn, writes, reads_extra=(), dst_dims=None, engs=('pool', 'dve'), q='sp'):
            s_ = state['xslot'] % 2
            state['xslot'] += 1
            P.op(q, lambda e, s_=s_: e.dma_start(
                out=(FB[s_][:, 0:ncols] if dst_dims is None else dap(FB[s_], 0, dst_dims)), in_=src_ap),
                 writes=[("FB", s_)], dma=xkeys[s_])
            eng = engs[wl_state['i'] % 2]
            wl_state['i'] += 1
            P.op(eng, lambda e, s_=s_: cast_fn(_CopyShim(e) if eng == 'act' else e, FB[s_]), reads=[("FB", s_)] + list(reads_extra), writes=writes)

        def load_static_weights():
            for dc in range(8):
                r0 = dc * 128
                def castA(e, fb, dc=dc):
                    e.tensor_copy(out=WIs[:, dc * 1280: dc * 1280 + 512], in_=fb[:, 0:512])
                    return e.tensor_copy(out=WIs[:, dc * 1280 + 1024: dc * 1280 + 1280], in_=fb[:, 512:768])
                stage_cast(win_d.ap()[r0:r0 + 128, 0:768], 768, castA, writes=["WIs"], q='act')
                def castB(e, fb, dc=dc):
                    return e.tensor_copy(out=WIs[:, dc * 1280 + 512: dc * 1280 + 1024], in_=fb[:, 0:512])
                stage_cast(win_d.ap()[r0:r0 + 128, 768:1280], 512, castB, writes=["WIs"], q='act')
                def castO(e, fb, dc=dc):
                    return e.tensor_copy(out=WO[:, dc * 1024:(dc + 1) * 1024], in_=fb[:, 0:1024])
                stage_cast(wout_d.ap()[r0:r0 + 128, :], 1024, castO, writes=["WO"], q='act')

        with ExitStack() as es1:
            def sb1(name, shape, dt=F32):
                return es1.enter_context(nc.sbuf_tensor("sb_" + name, list(shape), dt))

            WA = sb1("WA", [128, 4 * 3072])
            cT = sb1("cT", [128, 16])
            sc = sb1("sc", [128, 16])
            sc_rep = sb1("sc_rep", [128, 2 * 8 * 128])
            bcol = sb1("bcol", [128, 24])
            bgate_bc = sb1("bgate_bc", [128, 1024])
            lamv = sb1("lamv", [128, 256])
            lamp = sb1("lamp", [128, 128])
            lams = sb1("lams", [128, 4])
            sinkb = sb1("sinkb", [128, 8])
            subw_raw = sb1("subw_raw", [128, 128])
            maskw = sb1("maskw", [128, 384])
            BWf = sb1("BWf", [128, 8 * 384])

            def ld(dst_t, src):
                P.op('sp', lambda e: e.dma_start(out=dst_t[:], in_=src), writes=[dst_t.name], dma='c0')

            ld(identf, ident_d.ap())
            ld(cT, cT_d.ap())
            ld(bcol, bcol_d.ap())
            ld(bgate_bc, dap(bgate_d, 0, [[0, 128], [1, 1024]]))
            ld(lng_bc, dap(lng_d, 0, [[0, 128], [1, 1024]]))
            ld(lnb_bc, dap(lnb_d, 0, [[0, 128], [1, 1024]]))
            ld(sinkb, dap(sink_d, 0, [[0, 128], [1, 8]]))
            ld(lamv, dap(lamv_d, 0, [[0, 128], [1, 256]]))
            ld(subw_raw, dap(subw_d, 0, [[0, 128], [1, 128]]))
            ld(cfar, dap(relfar_d, 0, [[0, 128], [1, 8]]))
            ld(flags, dap(flags_d, 0, [[0, 128], [1, 66]]))
            ld(BWf, bw_d.ap())
            ld(maskw, maskw_d.ap())
            def load_wa(dcs):
                for dc in dcs:
                    P.op('sp', lambda e, dc=dc: e.dma_start(
                        out=WA[:, (dc % 4) * 3072:(dc % 4 + 1) * 3072], in_=wada_d.ap()[dc * 128:(dc + 1) * 128, :]),
                        writes=[("WA", dc % 4)], dma='wa')
            load_wa(range(0, 4))

            P.op('pool', lambda e: e.memset(zero_c[:], 0.0), writes=[zero_c.name])
            P.op('pool', lambda e: e.memset(eps_ln[:], LN_EPS), writes=[eps_ln.name])
            P.op('pool', lambda e: e.memset(eps_sub[:], SUBLN_EPS), writes=[eps_sub.name])
            P.op('dve', lambda e: e.tensor_copy(out=identb[:], in_=identf[:]),
                 reads=[identf.name], writes=[identb.name])
            P.op('act', lambda e: e.activation(out=sc[:], in_=cT[:], func=AF.Silu),
                 reads=[cT.name], writes=[sc.name])
            P.op('act', lambda e: e.activation(out=expsink[:], in_=sinkb[:], func=AF.Exp),
                 reads=[sinkb.name], writes=[expsink.name])
            P.op('dve', lambda e: e.tensor_tensor(out=lamp[:, 0:64], in0=lamv[:, 0:64], in1=lamv[:, 64:128], op=ALU.mult),
                 reads=[lamv.name], writes=["lamp0"])
            P.op('dve', lambda e: e.tensor_tensor(out=lamp[:, 64:128], in0=lamv[:, 128:192], in1=lamv[:, 192:256], op=ALU.mult),
                 reads=[lamv.name], writes=["lamp1"])
            P.op('dve', lambda e: e.reduce_sum(out=lams[:, 0:1], in_=lamp[:, 0:64], axis=mybir.AxisListType.X),
                 reads=["lamp0"], writes=["lams0"])
            P.op('dve', lambda e: e.reduce_sum(out=lams[:, 1:2], in_=lamp[:, 64:128], axis=mybir.AxisListType.X),
                 reads=["lamp1"], writes=["lams1"])
            P.op('act', lambda e: e.activation(out=lams[:, 2:4], in_=lams[:, 0:2], func=AF.Exp),
                 reads=["lams0", "lams1"], writes=["lams23"])
            P.op('dve', lambda e: e.tensor_tensor(out=neglam[:], in0=lams[:, 3:4], in1=lams[:, 2:3], op=ALU.subtract),
                 reads=["lams23"], writes=["neglam_a"])
            P.op('dve', lambda e: e.tensor_scalar(out=neglam[:], in0=neglam[:], scalar1=-LAMBDA_INIT, scalar2=None, op0=ALU.add),
                 reads=["neglam_a"], writes=[neglam.name])
            P.op('dve', lambda e: e.tensor_scalar(out=subw_bc[:], in0=subw_raw[:], scalar1=1.0 - LAMBDA_INIT, scalar2=None, op0=ALU.mult),
                 reads=[subw_raw.name], writes=[subw_bc.name])
            P.op('dve', lambda e: e.tensor_tensor(out=cdiff[:], in0=cfar[:, 0:4], in1=cfar[:, 4:8], op=ALU.subtract),
                 reads=[cfar.name], writes=[cdiff.name])
            for h in range(4):
                P.op('dve', lambda e, h=h: e.tensor_scalar(
                    out=farS[:, h * 64:(h + 1) * 64], in0=flags[:, 2:66],
                    scalar1=cdiff[:, h:h + 1], scalar2=cfar[:, 4 + h:5 + h], op0=ALU.mult, op1=ALU.add),
                    reads=[flags.name, cdiff.name, cfar.name], writes=[("farS", h)])
            P.op('dve', lambda e: e.tensor_scalar(out=negLR[:], in0=flags[:, 0:2], scalar1=-1.0, scalar2=BIG, op0=ALU.add, op1=ALU.mult),
                 reads=[flags.name], writes=[negLR.name])
            for h in range(8):
                P.op('dve', lambda e, h=h: e.tensor_tensor(
                    out=BWf[:, h * 384:(h + 1) * 384], in0=BWf[:, h * 384:(h + 1) * 384], in1=maskw[:], op=ALU.add),
                    reads=[BWf.name, maskw.name], writes=[("BWf", h)])
                P.op('pool', lambda e, h=h: e.tensor_copy(out=BW[:, h * 384:(h + 1) * 384], in_=BWf[:, h * 384:(h + 1) * 384]),
                     reads=[("BWf", h)], writes=[("BW", h)])
            P.op('dve', lambda e: e.tensor_scalar(
                out=dap(BXW, 0, [[2048, 128], [128, 8], [1, 128]]),
                in0=dap(BWf, 256, [[8 * 384, 128], [384, 8], [1, 128]]),
                scalar1=negLR[:, 0:1], scalar2=None, op0=ALU.add),
                reads=[("BWf", hh) for hh in range(8)] + [negLR.name], writes=["BXWL"])
            P.op('dve', lambda e: e.tensor_scalar(
                out=dap(BXW, 1024, [[2048, 128], [128, 8], [1, 128]]),
                in0=dap(BWf, 0, [[8 * 384, 128], [384, 8], [1, 128]]),
                scalar1=negLR[:, 1:2], scalar2=None, op0=ALU.add),
                reads=[("BWf", hh) for hh in range(8)] + [negLR.name], writes=["BXWR"])
            for b in range(2):
                for dc in range(8):
                    o = (b * 8 + dc) * 128
                    P.op('pool', lambda e, o=o, b=b, dc=dc: e.tensor_copy(
                        out=sc_rep[:, o:o + 128], in_=sc[:, dc * 2 + b:dc * 2 + b + 1].broadcast_to([128, 128])),
                        reads=[sc.name], writes=[("sc_rep", b, dc)])

            def mod_cols_half(half):
                def f(e):
                    ins = None
                    for dc in range(half * 4, half * 4 + 4):
                        for fc in range(16):
                            ins = e.matmul(out=psC[:, fc * 2:fc * 2 + 2],
                                           lhsT=WA[:, (dc % 4) * 3072 + fc * 128: (dc % 4) * 3072 + fc * 128 + 128],
                                           rhs=sc[:, dc * 2:dc * 2 + 2],
                                           start=(dc == 0 and fc == 0), stop=(dc == 7), skip_group_check=True)
                    return ins
                P.op('pe', f, reads=[("WA", k) for k in range(4)] + [sc.name], writes=["psC"])

            def gate_half(half):
                for b in range(2):
                    for n in range(2):
                        def gmm(e, b=b, n=n):
                            ins = None
                            for dc in range(half * 4, half * 4 + 4):
                                o = (b * 8 + dc) * 128
                                ins = e.matmul(out=psA[:, (b * 2 + n) * 512:(b * 2 + n + 1) * 512],
                                               lhsT=sc_rep[:, o:o + 128],
                                               rhs=WA[:, (dc % 4) * 3072 + 2048 + n * 512: (dc % 4) * 3072 + 2048 + (n + 1) * 512],
                                               start=(dc == 0), stop=(dc == 7))
                            return ins
                        P.op('pe', gmm, reads=[("WA", k) for k in range(4)] + [("sc_rep", b, dc) for dc in range(8)],
                             writes=[bankA(b * 2 + n)])

            load_static_weights()
            mod_cols_half(0)
            gate_half(0)
            load_wa(range(4, 8))
            mod_cols_half(1)
            gate_half(1)
            for b in range(2):
                P.op('dve', lambda e, b=b: e.tensor_tensor(
                    out=shf[:, b * 8:(b + 1) * 8], in0=psC[:, b:16:2],
                    in1=bcol[:, 0:8], op=ALU.add),
                    reads=["psC", bcol.name], writes=[("shf", b)])
                P.op('dve', lambda e, b=b: e.scalar_tensor_tensor(
                    out=s1p[:, b * 8:(b + 1) * 8], in0=psC[:, 16 + b:32:2], scalar=1.0, in1=bcol[:, 8:16],
                    op0=ALU.add, op1=ALU.add),
                    reads=["psC", bcol.name], writes=[("s1p", b)])
            for b in range(2):
                for n in range(2):
                    P.op('dve', lambda e, b=b, n=n: e.tensor_tensor(
                        out=gate_bc[:, b * 1024 + n * 512: b * 1024 + (n + 1) * 512],
                        in0=psA[:, (b * 2 + n) * 512:(b * 2 + n + 1) * 512],
                        in1=bgate_bc[:, n * 512:(n + 1) * 512], op=ALU.add),
                        reads=[bankA(b * 2 + n), bgate_bc.name], writes=[("gate_bc", b, n)])
            with nc.Block() as blk1:
                P.emit(blk1)

        WIg = sb("WIg", [128, 8 * 1024], BF16)
        BD = sb("BD", [128, 2 * MD], BF16)
        KT = sb("KT", [128, 8192], BF16)
        VA = sb("VA", [128, 64 * 130], BF16)
        CAR = sb("CAR", [128, 8192], BF16)
        XBF = [sb("xbf%d" % i, [128, 1024], BF16) for i in range(4)]
        uT = sb("uT", [128, 8 * 768], BF16)
        QT = [sb("QT%d" % i, [128, 2 * 512], BF16) for i in range(2)]
        QAT = sb("QAT", [128, 4 * 512], BF16)
        SGb = [sb("SGb%d" % i, [128, 4 * 256], BF16) for i in range(2)]
        SGa = sb("SGa", [128, 4 * 512], BF16)
        KAT = sb("KAT", [128, 768], BF16)
        VAW = sb("VAW", [128, 6 * 2 * 66], BF16)
        OT = sb("OT", [128, 6 * 512], BF16)
        PT = [sb("PT%d" % i, [128, 1024], BF16) for i in range(2)]
        OBf = sb("OBf", [128, 4 * 2 * 128])
        OBn = sb("OBn", [128, 4 * 2 * 128], BF16)
        OAg = sb("OAg", [128, 4 * 512], BF16)
        BX = sb("BX", [128, 1024], BF16)
        BXf = sb("BXf", [128, 1024])
        rec = sb("rec", [128, 16])
        ssq = sb("ssq", [128, 8])
        rstd = sb("rstd", [128, 8])
        sqj = sb("sqj", [128, 128])
        den = sb("den", [128, 16])
        bnst = sb("bnst", [128, 12])
        mv = sb("mv", [128, 8])
        lrs = sb("lrs", [128, 4])

        def load_kv_weights(heads):
            for hl, h in enumerate(heads):
                for dc in range(8):
                    src = dap(win_d, (dc * 128) * 3328 + 1792 + h * 128, [[3328, 128], [512, 2], [1, 128]])
                    def castG(e, fb, dc=dc, hl=hl):
                        return e.tensor_copy(
                            out=dap(WIg, dc * 1024 + 256 + hl * 128, [[8192, 128], [256, 2], [1, 128]]),
                            in_=dap(fb, 0, [[1024, 128], [128, 2], [1, 128]]))
                    stage_cast(src, 256, castG, writes=["WIgkv"], dst_dims=[[1024, 128], [128, 2], [1, 128]])

        def q_weight_thunks(heads):
            th = []
            for hl, h in enumerate(heads):
                for dc in range(8):
                    def t1(dc=dc, hl=hl, h=h):
                        src = dap(win_d, (dc * 128) * 3328 + 1280 + h * 128, [[3328, 128], [1536, 2], [1, 128]])
                        def castG(e, fb):
                            return e.tensor_copy(
                                out=dap(WIg, dc * 1024 + hl * 128, [[8192, 128], [768, 2], [1, 128]]),
                                in_=dap(fb, 0, [[1024, 128], [128, 2], [1, 128]]))
                        stage_cast(src, 256, castG, writes=["WIgq"], dst_dims=[[1024, 128], [128, 2], [1, 128]],
                                   engs=('act', 'dve'))
                    th.append(t1)
                def t2(hl=hl, h=h):
                    def castD1(e, fb):
                        return e.tensor_copy(out=BD[:, hl * MD: hl * MD + 1024], in_=fb[:, 0:1024])
                    stage_cast(bd_d.ap()[:, h * MD: h * MD + 1024], 1024, castD1, writes=[("BD", hl)], engs=('act', 'dve'))
                def t3(hl=hl, h=h):
                    def castD2(e, fb):
                        return e.tensor_copy(out=BD[:, hl * MD + 1024: hl * MD + MD], in_=fb[:, 0:MD - 1024])
                    stage_cast(bd_d.ap()[:, h * MD + 1024: (h + 1) * MD], MD - 1024, castD2, writes=[("BD", hl)],
                               engs=('act', 'dve'))
                th.append(t2)
                th.append(t3)
            return th

        P.op('pool', lambda e: e.memset(VA[:], 1.0), writes=["VA_all"])
        P.op('pool', lambda e: e.memset(VAW[:], 1.0), writes=["VAW_all"])

        xb_state = {'k': 0, 'm': 0}

        def prefetch_x(src_d, row0):
            sl = xb_state['k'] % 4
            xb_state['k'] += 1
            P.op('pool', lambda e, sl=sl: e.dma_start(out=XBF[sl][:], in_=src_d.ap()[row0:row0 + 128, :]),
                 writes=[("xbf", sl)], dma=xbkeys[sl])
            return sl

        def make_uT_from(sl, bslot, b):
            par = xb_state['m'] % 2
            xb_state['m'] += 1
            if par == 0:
                evk, odk, evb, odb, evo, odo = bankA(0), "psC", psA_b, psC_b, 0, 0
            else:
                evk, odk, evb, odb, evo, odo = bankB(0), bankB(1), psB_b, psB_b, 0, 1024

            dve_dcs = [0, 2, 4, 6, 7]
            act_dcs = [1, 3, 5]

            def tsrc(dc):
                if dc in dve_dcs:
                    k = dve_dcs.index(dc)
                    return evb[:, evo + k * 128: evo + (k + 1) * 128]
                k = act_dcs.index(dc)
                return odb[:, odo + k * 128: odo + (k + 1) * 128]

            def tr(e):
                ins = None
                for dc in range(8):
                    ins = e.transpose(out=tsrc(dc), in_=XBF[sl][:, dc * 128:(dc + 1) * 128], identity=identb[:])
                return ins
            P.op('pe', tr, reads=[("xbf", sl), identb.name], writes=[evk, odk])
            for dc in range(8):
                o = dc * 768 + bslot * 128
                if dc in dve_dcs:
                    P.op('dve', lambda e, dc=dc, o=o: e.tensor_scalar(
                        out=uT[:, o:o + 128], in0=tsrc(dc),
                        scalar1=s1p[:, b * 8 + dc:b * 8 + dc + 1], scalar2=shf[:, b * 8 + dc:b * 8 + dc + 1],
                        op0=ALU.mult, op1=ALU.add),
                        reads=[evk], writes=[("uT", bslot, dc)])
                else:
                    P.op('act', lambda e, dc=dc, o=o: e.activation(
                        out=uT[:, o:o + 128], in_=tsrc(dc), func=AF.Identity,
                        bias=shf[:, b * 8 + dc:b * 8 + dc + 1], scale=s1p[:, b * 8 + dc:b * 8 + dc + 1]),
                        reads=[odk], writes=[("uT", bslot, dc)])

        def make_uT_list(src_d, items, b):
            slots = {}
            for i in range(min(NPF, len(items))):
                slots[i] = prefetch_x(src_d, items[i][1] * 128)
            for i, (bs, lb) in enumerate(items):
                if i + NPF < len(items):
                    slots[i + NPF] = prefetch_x(src_d, items[i + NPF][1] * 128)
                make_uT_from(slots[i], bs, b)

        def uT_keys(bslots):
            return [("uT", bs, dc) for bs in bslots for dc in range(8)]

        def proj_fm(W, pitch, col, bslot0, nb, pbank, reads_extra=()):
            def f(e):
                ins = None
                for dc in range(8):
                    ins = e.matmul(out=psA[:, pbank * 512: pbank * 512 + nb * 128],
                                   lhsT=W[:, dc * pitch + col: dc * pitch + col + 128],
                                   rhs=uT[:, dc * 768 + bslot0 * 128: dc * 768 + (bslot0 + nb) * 128],
                                   start=(dc == 0), stop=(dc == 7))
                return ins
            P.op('pe', f, reads=uT_keys(range(bslot0, bslot0 + nb)) + list(reads_extra), writes=[bankA(pbank)])

        def proj_tm(W, pitch, col, ncol, bslot, pbank, reads_extra=()):
            def f(e):
                ins = None
                for dc in range(8):
                    ins = e.matmul(out=psA[:, pbank * 512: pbank * 512 + ncol],
                                   lhsT=uT[:, dc * 768 + bslot * 128: dc * 768 + (bslot + 1) * 128],
                                   rhs=W[:, dc * pitch + col: dc * pitch + col + ncol],
                                   start=(dc == 0), stop=(dc == 7))
                return ins
            P.op('pe', f, reads=uT_keys([bslot]) + list(reads_extra), writes=[bankA(pbank)])

        pb_state = {'i': 0}

        def next_pbank():
            b = 1 + pb_state['i'] % 3
            pb_state['i'] += 1
            return b

        def evac_copy(out_ap, in_ap, reads, writes, scale=None, which=None):
            if which is None:
                which = state['evac'] % 2
                state['evac'] += 1
            if which == 0:
                if scale is None:
                    P.op('dve', lambda e: e.tensor_copy(out=out_ap, in_=in_ap), reads=reads, writes=writes)
                else:
                    P.op('dve', lambda e: e.tensor_scalar(out=out_ap, in0=in_ap, scalar1=scale, scalar2=None, op0=ALU.mult),
                         reads=reads, writes=writes)
            else:
                if scale is None:
                    P.op('act', lambda e: e.copy(out=out_ap, in_=in_ap), reads=reads, writes=writes)
                else:
                    P.op('act', lambda e: e.mul(out=out_ap, in_=in_ap, mul=scale), reads=reads, writes=writes)

        def run_job(x_d, y_d, b, S_kv, n_qt, is_sample, groups, next_first_heads):
            NB = S_kv // 128
            n_groups = len(groups)
            for gi, heads in enumerate(groups):
                G = len(heads)
                final = (gi == n_groups - 1)
                if not state.get('kv_preloaded'):
                    load_kv_weights(heads)
                state['kv_preloaded'] = False
                qw = q_weight_thunks(heads)
                nxt_heads = groups[gi + 1] if gi + 1 < n_groups else next_first_heads

                NT = NB // 2

                kv_slots = {}
                for blk in range(NPF):
                    kv_slots[blk] = prefetch_x(x_d, blk * 128)

                def front(n):
                    base = (n % 3) * 2
                    for k in range(2):
                        blk = 2 * n + k
                        if blk + NPF < NB:
                            kv_slots[blk + NPF] = prefetch_x(x_d, (blk + NPF) * 128)
                        make_uT_from(kv_slots[blk], base + k, b)

                def back(n):
                    base = (n % 3) * 2
                    for hl in range(G):
                        pbk = next_pbank()
                        proj_fm(WIg, 1024, 256 + hl * 128, base, 2, pbk, reads_extra=["WIgkv"])
                        evac_copy(KT[:, hl * S_kv + n * 256: hl * S_kv + (n + 1) * 256],
                                  psA[:, pbk * 512: pbk * 512 + 256],
                                  reads=[bankA(pbk)], writes=[("KT", hl, 2 * n), ("KT", hl, 2 * n + 1)])
                    for k in range(2):
                        tb = 2 * n + k
                        pbk = next_pbank()
                        proj_tm(WIg, 1024, 512, G * 128, base + k, pbk, reads_extra=["WIgkv"])
                        state['evac'] += 1
                        for hl in range(G):
                            o = (tb * G + hl) * 130
                            evac_copy(VA[:, o:o + 128], psA[:, pbk * 512 + hl * 128: pbk * 512 + (hl + 1) * 128],
                                      reads=[bankA(pbk), "VA_all"], writes=[("VA", hl, tb)], which=state['evac'] % 2)

                for n in range(NT + 1):
                    if n < NT:
                        front(n)
                    if kstop == 22:
                        raise _Stop()
                    if n >= 1:
                        back(n - 1)
                    if n >= 1 and qw:
                        qw.pop(0)()
                while qw:
                    qw.pop(0)()
                    if kstop == 23 and n == 1:
                        raise _Stop()
                    if kstop == 2 and n == 2:
                        raise _Stop()
                if kstop == 3:
                    raise _Stop()

                for t in range(n_qt):
                    if t == n_qt - 1 and nxt_heads is not None:
                        load_kv_weights(nxt_heads)
                        state['kv_preloaded'] = True
                    q_phase(x_d, y_d, b, S_kv, NB, t, n_qt, is_sample, heads, final, gi, groups)
                    if kstop == 4:
                        raise _Stop()
                    if kstop == 5 and final:
                        raise _Stop()
                if kstop == 6 and final:
                    raise _Stop()

        def tile_blocks(t, n_qt, NB, is_sample, final):
            blocks = []
            if final:
                if t > 0:
                    blocks.append((0, 4 * t - 1))
                elif is_sample:
                    blocks.append((0, NB - 1))
                for i in range(4):
                    blocks.append((1 + i, 4 * t + i))
                if t < n_qt - 1 or is_sample:
                    blocks.append((5, 4 * t + 4))
            else:
                for i in range(4):
                    blocks.append((1 + i, 4 * t + i))
            return blocks

        def prepA_steps(x_d, b, t, n_qt, NB, is_sample, final, G):
            blocks = tile_blocks(t, n_qt, NB, is_sample, final)
            par = t % 2
            n = len(blocks)
            slots = {}
            steps = []

            def pf(i):
                slots[i] = prefetch_x(x_d, blocks[i][1] * 128)

            def s_pf0():
                for i in range(min(NPF, n)):
                    pf(i)
            steps.append((s_pf0, 0))
            for i, (bs, lb) in enumerate(blocks):
                def s_tr(i=i):
                    if i + NPF < n:
                        pf(i + NPF)
                    sl = slots[i]

                    def tr(e):
                        ins = None
                        for dc in range(8):
                            ins = e.transpose(out=psC_b[:, dc * 128:(dc + 1) * 128], in_=XBF[sl][:, dc * 128:(dc + 1) * 128],
                                              identity=identb[:])
                        return ins
                    P.op('pe', tr, reads=[("xbf", sl), identb.name], writes=["psC"])

                def s_ev(bs=bs):
                    for dc in range(8):
                        o = dc * 768 + bs * 128
                        P.op('dve', lambda e, dc=dc, o=o: e.tensor_scalar(
                            out=uT[:, o:o + 128], in0=psC_b[:, dc * 128:(dc + 1) * 128],
                            scalar1=s1p[:, b * 8 + dc:b * 8 + dc + 1], scalar2=shf[:, b * 8 + dc:b * 8 + dc + 1],
                            op0=ALU.mult, op1=ALU.add),
                            reads=["psC"], writes=[("uT", bs, dc)])
                steps.append((s_tr, 0))
                steps.append((s_ev, 2))

            def fm_part(col, dcs):
                def f(e):
                    ins = None
                    for dc in dcs:
                        ins = e.matmul(out=psC[:, 0:512],
                                       lhsT=WIg[:, dc * 1024 + col: dc * 1024 + col + 128],
                                       rhs=uT[:, dc * 768 + 128: dc * 768 + 640],
                                       start=(dc == 0), stop=(dc == 7))
                    return ins
                P.op('pe', f, reads=uT_keys(range(1, 5)) + ["WIgq"], writes=["psC"])

            for hl in range(G):
                steps.append((lambda hl=hl: fm_part(hl * 128, range(0, 4)), 0))
                steps.append((lambda hl=hl: fm_part(hl * 128, range(4, 8)), 0))

                def s_qe(hl=hl):
                    P.op('dve', lambda e: e.tensor_scalar(out=QT[par][:, hl * 512:(hl + 1) * 512], in0=psC[:, 0:512],
                                                          scalar1=0.125, scalar2=None, op0=ALU.mult),
                         reads=["psC"], writes=[("QT", par, hl)])
                steps.append((s_qe, 2))
            for qb in range(4):
                def s_g(qb=qb):
                    def f(e):
                        ins = None
                        for dc in range(8):
                            ins = e.matmul(out=psC[:, 0:G * 128],
                                           lhsT=uT[:, dc * 768 + (1 + qb) * 128: dc * 768 + (2 + qb) * 128],
                                           rhs=WIg[:, dc * 1024 + 768: dc * 1024 + 768 + G * 128],
                                           start=(dc == 0), stop=(dc == 7))
                        return ins
                    P.op('pe', f, reads=uT_keys([1 + qb]) + ["WIgq"], writes=["psC"])

                def s_ge(qb=qb):
                    P.op('dve', lambda e: e.tensor_copy(out=SGb[par][:, qb * 256: qb * 256 + G * 128], in_=psC[:, 0:G * 128]),
                         reads=["psC"], writes=[("SGb", par, qb)])
                steps.append((s_g, 0))
                steps.append((s_ge, 1))
            return steps

        def bg_drain(bg):
            while bg:
                fn, _ = bg.pop(0)
                fn()

        def q_phase(x_d, y_d, b, S_kv, NB, t, n_qt, is_sample, heads, final, gi, groups):
            G = len(heads)
            par = t % 2
            blocks = tile_blocks(t, n_qt, NB, is_sample, final)
            has_left = any(bs == 0 for bs, _ in blocks)
            has_right = any(bs == 5 for bs, _ in blocks)
            if t == 0:
                make_uT_list(x_d, blocks, b)
                for hl in range(G):
                    pbk = next_pbank()
                    proj_fm(WIg, 1024, hl * 128, 1, 4, pbk, reads_extra=["WIgq"])
                    evac_copy(QT[par][:, hl * 512:(hl + 1) * 512], psA[:, pbk * 512:(pbk + 1) * 512],
                              reads=[bankA(pbk)], writes=[("QT", par, hl)], scale=0.125)
                for qb in range(4):
                    pbk = next_pbank()
                    proj_tm(WIg, 1024, 768, G * 128, 1 + qb, pbk, reads_extra=["WIgq"])
                    evac_copy(SGb[par][:, qb * 256: qb * 256 + G * 128], psA[:, pbk * 512: pbk * 512 + G * 128],
                              reads=[bankA(pbk)], writes=[("SGb", par, qb)])

            if final:
                for c in range(4):
                    pbk = next_pbank()
                    proj_fm(WIs, 1280, c * 128, 1, 4, pbk, reads_extra=["WIs"])
                    evac_copy(QAT[:, c * 512:(c + 1) * 512], psA[:, pbk * 512:(pbk + 1) * 512],
                              reads=[bankA(pbk)], writes=[("QAT", c)], scale=0.125)
                for qb in range(4):
                    pbk = next_pbank()
                    proj_tm(WIs, 1280, 512, 512, 1 + qb, pbk, reads_extra=["WIs"])
                    P.op('act', lambda e, qb=qb, pbk=pbk: e.activation(
                        out=SGa[:, qb * 512:(qb + 1) * 512], in_=psA[:, pbk * 512:(pbk + 1) * 512], func=AF.Silu),
                        reads=[bankA(pbk)], writes=[("SGa", qb)])
                bs0 = blocks[0][0]
                nbk = len(blocks)
                pieces = []
                s0 = bs0
                rem = nbk
                while rem > 0:
                    n = min(4, rem)
                    pieces.append((s0, n))
                    s0 += n
                    rem -= n
                for (s0, n) in pieces:
                    pbk = next_pbank()
                    proj_fm(WIs, 1280, 1024, s0, n, pbk, reads_extra=["WIs"])
                    evac_copy(KAT[:, s0 * 128:(s0 + n) * 128], psA[:, pbk * 512: pbk * 512 + n * 128],
                              reads=[bankA(pbk)], writes=["KAT"])
                for bs, lb in blocks:
                    pbk = next_pbank()
                    proj_tm(WIs, 1280, 1152, 128, bs, pbk, reads_extra=["WIs"])
                    P.op('dve', lambda e, bs=bs, pbk=pbk: e.tensor_copy(
                        out=dap(VAW, bs * 132, [[VAW_pitch, 128], [66, 2], [1, 64]]),
                        in_=dap(psA, pbk * 512, [[2048, 128], [64, 2], [1, 64]])),
                        reads=[bankA(pbk), "VAW_all"], writes=[("VAW", bs)])
                window_attn(t, n_qt, is_sample, blocks, has_left, has_right)

            bg = []
            if state.get('pending_post'):
                bg += state['pending_post']
                state['pending_post'] = None
            if t + 1 < n_qt:
                bg += prepA_steps(x_d, b, t + 1, n_qt, NB, is_sample, final, G)
            for hl, h in enumerate(heads):
                diff_attn(S_kv, NB, t, n_qt, is_sample, hl, h, G, bg)
            bg_drain(bg)
            post = post_diff_steps(t, heads, G, final, gi, groups, n_qt, fg=(final or t == n_qt - 1))
            if final or t == n_qt - 1:
                bg_drain(post)
            else:
                state['pending_post'] = post
            if dbg and final and (not is_sample) and t == 0:
                P.op('sp', lambda e: e.dma_start(out=dbg_ot.ap(), in_=OT[:]),
                     reads=[("OT", fc) for fc in range(6)], writes=["dbg1"], dma='dbg')
                P.op('sp', lambda e: e.dma_start(out=dbg_car.ap()[:, 0:512], in_=CAR[:, 0:512]),
                     reads=[("CAR", 0, 0)], writes=["dbg2"], dma='dbg')
                P.op('sp', lambda e: e.dma_start(out=dbg_car.ap()[:, 512:1024], in_=CAR[:, 4096:4608]),
                     reads=[("CAR", 1, 0)], writes=["dbg3"], dma='dbg')
                P.op('sp', lambda e: e.dma_start(out=dbg_c.ap()[:, 0:16], in_=s1p[:]), writes=["dbg4"], dma='dbg')
                P.op('sp', lambda e: e.dma_start(out=dbg_c.ap()[:, 16:32], in_=shf[:]), writes=["dbg5"], dma='dbg')
                P.op('sp', lambda e: e.dma_start(out=dbg_c.ap()[:, 32:33], in_=neglam[:], allow_slow_non_contiguous=True), writes=["dbg6"], dma='dbg')
                P.op('sp', lambda e: e.dma_start(out=dbg_c.ap()[:, 33:41], in_=expsink[:]), writes=["dbg7"], dma='dbg')
                P.op('sp', lambda e: e.dma_start(out=dbg_c.ap()[:, 41:49], in_=cfar[:]), writes=["dbg8"], dma='dbg')
                P.op('sp', lambda e: e.dma_start(out=dbg_sg.ap()[:, 0:2048], in_=SGa[:]),
                     reads=[("SGa", qb) for qb in range(4)], writes=["dbg9"], dma='dbg')
                P.op('sp', lambda e: e.dma_start(out=dbg_sg.ap()[:, 2048:3072], in_=SGb[0][:]),
                     reads=[("SGb", 0, qb) for qb in range(4)], writes=["dbg10"], dma='dbg')
            if final:
                finalize(x_d, y_d, b, t, n_qt, heads, gi, groups)

        VAW_pitch = 6 * 2 * 66

        def window_attn(t, n_qt, is_sample, blocks, has_left, has_right):
            present = {bs for bs, _ in blocks}
            wrapL = is_sample and t == 0
            wrapR = is_sample and t == n_qt - 1
            def win_chunk(c):
                par = c % 2
                if par == 0:
                    accU = [(psB, 0, bankB(0)), (psB, 512, bankB(1))]
                else:
                    accU = [(psB, 1024, bankB(2)), (psC, 0, "psC")]
                jbs = [jb for jb in range(6) if jb in present]
                first = [True, True]

                def geom(jb):
                    qlo = max(0, jb - 2)
                    qhi = min(3, jb)
                    nq = qhi - qlo + 1
                    return qlo, nq, nq * 128, (qlo - (jb - 2)) * 128

                def qk(jb, slot, c=c):
                    qlo, nq, N, so = geom(jb)
                    use_wrap = (wrapL and jb == 0) or (wrapR and jb == 5)

                    def f(e):
                        ins = None
                        for u in range(2):
                            e.matmul(out=psA[:, (slot * 2 + u) * 512: (slot * 2 + u) * 512 + N],
                                     lhsT=KAT[u * 64:(u + 1) * 64, jb * 128:(jb + 1) * 128],
                                     rhs=QAT[u * 64:(u + 1) * 64, c * 512 + qlo * 128: c * 512 + qlo * 128 + N],
                                     start=True, stop=False)
                        for u in range(2):
                            hp = 2 * c + u
                            if use_wrap:
                                wo = 0 if jb == 0 else 1024
                                rhs = BXW[:, wo + hp * 128: wo + hp * 128 + 128]
                            else:
                                rhs = BW[:, hp * 384 + so: hp * 384 + so + N]
                            ins = e.matmul(out=psA[:, (slot * 2 + u) * 512: (slot * 2 + u) * 512 + N],
                                           lhsT=identb[:], rhs=rhs, start=False, stop=True)
                        return ins
                    rd = ["KAT", ("QAT", c), identb.name]
                    rd += [] if use_wrap else [("BW", 2 * c), ("BW", 2 * c + 1)]
                    P.op('pe', f, reads=rd, writes=[bankA(slot * 2), bankA(slot * 2 + 1)])

                def ex(jb, slot):
                    qlo, nq, N, so = geom(jb)
                    P.op('act', lambda e: e.activation(
                        out=dap(PT[slot], 0, [[1024, 128], [384, 2], [1, N]]),
                        in_=dap(psA, slot * 1024, [[2048, 128], [512, 2], [1, N]]), func=AF.Exp),
                        reads=[bankA(slot * 2), bankA(slot * 2 + 1)], writes=[("PT", slot)])

                def pv(jb, slot):
                    qlo, nq, N, so = geom(jb)
                    sts = []
                    for u in range(2):
                        for qi in range(nq):
                            sts.append(first[u])
                            first[u] = False
                    sts = tuple(sts)

                    def f(e):
                        ins = None
                        k = 0
                        for u in range(2):
                            pt_, base, _ = accU[u]
                            for qi in range(nq):
                                qb = qlo + qi
                                off = base + qb * 66
                                ins = e.matmul(out=pt_[:, off: off + 65],
                                               lhsT=PT[slot][:, u * 384 + qi * 128: u * 384 + (qi + 1) * 128],
                                               rhs=VAW[:, (jb * 2 + u) * 66: (jb * 2 + u) * 66 + 65],
                                               start=sts[k], stop=True, skip_group_check=True)
                                k += 1
                        return ins
                    P.op('pe', f, reads=[("PT", slot), ("VAW", jb)], writes=[accU[0][2], accU[1][2]])

                qk(jbs[0], 0)
                for idx, jb in enumerate(jbs):
                    if idx + 1 < len(jbs):
                        qk(jbs[idx + 1], (idx + 1) % 2)
                    ex(jb, idx % 2)
                    pv(jb, idx % 2)
                for u in range(2):
                    pt_, base, bk = accU[u]
                    pitch = 1536 if pt_ is psB else 512
                    P.op('dve', lambda e, u=u, pt_=pt_, base=base, pitch=pitch, c=c, par=par: e.tensor_scalar(
                        out=den[:, par * 8 + u * 4: par * 8 + u * 4 + 4], in0=dap(pt_, base + 64, [[pitch, 128], [66, 4]]),
                        scalar1=expsink[:, 2 * c + u:2 * c + u + 1], scalar2=None, op0=ALU.add),
                        reads=[bk], writes=[("den", par, u)])
                P.op('dve', lambda e, par=par: e.reciprocal(out=rec[:, par * 8: par * 8 + 8], in_=den[:, par * 8: par * 8 + 8]),
                     reads=[("den", par, 0), ("den", par, 1)], writes=[("recw", par)])
                for u in range(2):
                    pt_, base, bk = accU[u]
                    for qb in range(4):
                        a = u * 4 + qb
                        off = base + qb * 66
                        hp = 2 * c + u
                        P.op('dve', lambda e, a=a, off=off, hp=hp, qb=qb, pt_=pt_, par=par: e.scalar_tensor_tensor(
                            out=OAg[:, qb * 512 + hp * 64: qb * 512 + hp * 64 + 64],
                            in0=pt_[:, off: off + 64], scalar=rec[:, par * 8 + a: par * 8 + a + 1],
                            in1=SGa[:, qb * 512 + hp * 64: qb * 512 + hp * 64 + 64],
                            op0=ALU.mult, op1=ALU.mult),
                            reads=[bk, ("recw", par), ("SGa", qb)], writes=[("OAg", qb, hp)])

            for c in range(4):
                win_chunk(c)
            for fc in range(4):
                def trw(e, fc=fc):
                    ins = None
                    for qb in range(4):
                        ins = e.transpose(out=psC_b[:, qb * 128:(qb + 1) * 128],
                                          in_=OAg[:, qb * 512 + fc * 128: qb * 512 + (fc + 1) * 128],
                                          identity=identb[:])
                    return ins
                P.op('pe', trw, reads=[("OAg", qb, hp) for qb in range(4) for hp in (2 * fc, 2 * fc + 1)] + [identb.name],
                     writes=["psC"])
                P.op('dve', lambda e, fc=fc: e.tensor_copy(out=OT[:, fc * 512:(fc + 1) * 512], in_=psC_b[:, 0:512]),
                     reads=["psC"], writes=[("OT", fc)])

        def diff_attn(S_kv, NB, t, n_qt, is_sample, hl, h, G, bg):
            par = t % 2
            kinds = []
            for i in range(NB):
                own = (not is_sample) or i < 16
                if own:
                    d = 4 * t - i
                    if -4 <= d <= 1:
                        kinds.append(('near', d))
                    elif d >= 2:
                        kinds.append(('far', cfar[:, h:h + 1], cfar.name))
                    else:
                        kinds.append(('far', cfar[:, 4 + h:5 + h], cfar.name))
                else:
                    if i == 16 and t == 3:
                        kinds.append(('wrapR',))
                    elif i == NB - 1 and t == 0:
                        kinds.append(('wrapL',))
                    else:
                        kinds.append(('far', farS[:, h * 64 + i: h * 64 + i + 1], ("farS", h)))
            if is_sample and t == 0:
                P.op('dve', lambda e: e.tensor_scalar(
                    out=BXf[:, 0:512], in0=BD[:, hl * MD + 640: hl * MD + 1152],
                    scalar1=cfar[:, 4 + h:5 + h], scalar2=flags[:, 0:1], op0=ALU.subtract, op1=ALU.mult),
                    reads=[("BD", hl), cfar.name, flags.name], writes=["BXf"])
                P.op('dve', lambda e: e.tensor_scalar(
                    out=BX[:, 0:512], in0=BXf[:, 0:512], scalar1=cfar[:, 4 + h:5 + h], scalar2=None, op0=ALU.add),
                    reads=["BXf"], writes=["BX"])
            if is_sample and t == 3:
                P.op('dve', lambda e: e.tensor_scalar(
                    out=BXf[:, 512:1024], in0=BD[:, hl * MD: hl * MD + 512],
                    scalar1=cfar[:, h:h + 1], scalar2=flags[:, 1:2], op0=ALU.subtract, op1=ALU.mult),
                    reads=[("BD", hl), cfar.name, flags.name], writes=["BXf"])
                P.op('dve', lambda e: e.tensor_scalar(
                    out=BX[:, 512:1024], in0=BXf[:, 512:1024], scalar1=cfar[:, h:h + 1], scalar2=None, op0=ALU.add),
                    reads=["BXf"], writes=["BX"])

            def qk(i):
                kind = kinds[i]
                slot = i % 2

                def f(e):
                    ins = None
                    near = kind[0] != 'far'
                    for m in range(2):
                        ins = e.matmul(out=psA[:, (slot * 2 + m) * 512:(slot * 2 + m + 1) * 512],
                                       lhsT=KT[m * 64:(m + 1) * 64, hl * S_kv + i * 128: hl * S_kv + (i + 1) * 128],
                                       rhs=QT[par][m * 64:(m + 1) * 64, hl * 512:(hl + 1) * 512],
                                       start=True, stop=not near)
                    if near:
                        if kind[0] == 'near':
                            o = hl * MD + 128 * (kind[1] + 4)
                            rhs = BD[:, o:o + 512]
                        elif kind[0] == 'wrapL':
                            rhs = BX[:, 0:512]
                        else:
                            rhs = BX[:, 512:1024]
                        for m in range(2):
                            ins = e.matmul(out=psA[:, (slot * 2 + m) * 512:(slot * 2 + m + 1) * 512],
                                           lhsT=identb[:], rhs=rhs, start=False, stop=True)
                    return ins
                rd = [("KT", hl, i), ("QT", par, hl)]
                if kind[0] == 'near':
                    rd += [("BD", hl), identb.name]
                elif kind[0] != 'far':
                    rd += ["BX", identb.name]
                P.op('pe', f, reads=rd, writes=[bankA(slot * 2), bankA(slot * 2 + 1)])

            def ex(i):
                kind = kinds[i]
                slot = i % 2
                if kind[0] == 'far':
                    bias_ap, bkey = kind[1], kind[2]
                else:
                    bias_ap, bkey = zero_c[:], zero_c.name
                P.op('act', lambda e: e.activation(out=PT[slot][:], in_=psA[:, slot * 1024:(slot + 1) * 1024],
                                                   func=AF.Exp, bias=bias_ap, scale=1.0),
                     reads=[bankA(slot * 2), bankA(slot * 2 + 1), bkey], writes=[("PT", slot)])

            def pv(i):
                slot = i % 2

                def f(e):
                    ins = None
                    for m in range(2):
                        for qb in range(4):
                            a = m * 4 + qb
                            bank = a // 3
                            off = bank * 512 + (a % 3) * 130
                            vo = (i * G + hl) * 130
                            ins = e.matmul(out=psB[:, off:off + 129],
                                           lhsT=PT[slot][:, m * 512 + qb * 128: m * 512 + (qb + 1) * 128],
                                           rhs=VA[:, vo:vo + 129],
                                           start=(i == 0 and a % 3 == 0), stop=(i == NB - 1), skip_group_check=True)
                    return ins
                P.op('pe', f, reads=[("PT", slot), ("VA", hl, i)], writes=[bankB(0), bankB(1), bankB(2)])

            qk(0)
            qk(1)
            cool = 2
            for i in range(NB):
                ex(i)
                if i + 2 < NB:
                    qk(i + 2)
                pv(i)
                if bg:
                    if cool <= 0:
                        fn, cool = bg.pop(0)
                        fn()
                    else:
                        cool -= 1
            for bank in range(3):
                na = 3 if bank < 2 else 2
                P.op('dve', lambda e, bank=bank, na=na: e.reciprocal(
                    out=rec[:, bank * 3: bank * 3 + na], in_=dap(psB, bank * 512 + 128, [[1536, 128], [130, na]])),
                    reads=[bankB(bank)], writes=[("recd", bank)])
            P.op('dve', lambda e: e.tensor_scalar(out=rec[:, 8:12], in0=rec[:, 4:8], scalar1=neglam[:, 0:1], scalar2=None, op0=ALU.mult),
                 reads=[("recd", 1), ("recd", 2), neglam.name], writes=["recn"])
            for qb in range(4):
                a0 = qb
                o0 = (a0 // 3) * 512 + (a0 % 3) * 130
                oo = (qb * 2 + hl) * 128
                P.op('dve', lambda e, a0=a0, o0=o0, oo=oo: e.tensor_scalar(
                    out=OBf[:, oo:oo + 128], in0=psB[:, o0:o0 + 128], scalar1=rec[:, a0:a0 + 1], scalar2=None, op0=ALU.mult),
                    reads=[bankB(a0 // 3), ("recd", a0 // 3)], writes=[("OBf", qb, hl)])
            for qb in range(4):
                a1 = 4 + qb
                o1 = (a1 // 3) * 512 + (a1 % 3) * 130
                P.op('dve', lambda e, qb=qb, o1=o1: e.tensor_scalar(
                    out=BXf[:, qb * 128:(qb + 1) * 128], in0=psB[:, o1:o1 + 128], scalar1=rec[:, 8 + qb:9 + qb], scalar2=None, op0=ALU.mult),
                    reads=[bankB(a1 // 3), "recn"], writes=["BXf"])
            for qb in range(4):
                oo = (qb * 2 + hl) * 128
                P.op('dve', lambda e, qb=qb, oo=oo: e.tensor_tensor(
                    out=OBf[:, oo:oo + 128], in0=OBf[:, oo:oo + 128], in1=BXf[:, qb * 128:(qb + 1) * 128], op=ALU.add),
                    reads=[("OBf", qb, hl), "BXf"], writes=[("OBf", qb, hl)])
                P.op('dve', lambda e, qb=qb, oo=oo: e.tensor_tensor(
                    out=sqj[:], in0=OBf[:, oo:oo + 128], in1=OBf[:, oo:oo + 128], op=ALU.mult),
                    reads=[("OBf", qb, hl)], writes=["sqj"])
                P.op('dve', lambda e, qb=qb: e.reduce_sum(out=ssq[:, qb * 2 + hl: qb * 2 + hl + 1], in_=sqj[:], axis=mybir.AxisListType.X),
                     reads=["sqj"], writes=[("ssq", qb, hl)])

        def post_diff_steps(t, heads, G, final, gi, groups, n_qt, fg=False):
            S_q = n_qt * 512
            par = t % 2
            steps = []
            sgk = [("SGb", par, qb) for qb in range(4)]

            def s1():
                P.op('act', lambda e: e.activation(out=BXf[:], in_=SGb[par][:], func=AF.Exp, scale=-1.0),
                     reads=sgk, writes=["BXf"])
            def s1f():
                P.op('act', lambda e: e.activation(out=SGb[par][:], in_=SGb[par][:], func=AF.Silu), reads=sgk, writes=sgk)
            if fg:
                steps.append((s1f, 0))
            else:
                steps.append((s1, 1))

            def s2():
                P.op('dve', lambda e: e.tensor_scalar(out=BXf[:], in0=BXf[:], scalar1=1.0, scalar2=None, op0=ALU.add),
                     reads=["BXf"], writes=["BXf"])
                P.op('dve', lambda e: e.reciprocal(out=BXf[:], in_=BXf[:]), reads=["BXf"], writes=["BXf"])
                P.op('dve', lambda e: e.tensor_tensor(out=SGb[par][:], in0=SGb[par][:], in1=BXf[:], op=ALU.mult),
                     reads=sgk + ["BXf"], writes=sgk)
            if not fg:
                steps.append((s2, 2))

            def s3():
                P.op('act', lambda e: e.activation(out=rstd[:, 0:8], in_=ssq[:, 0:8], func=AF.Sqrt, bias=eps_sub[:], scale=1.0 / 128.0),
                     reads=[("ssq", qb, hl) for qb in range(4) for hl in range(G)] + [eps_sub.name], writes=["rstd_a"])
                P.op('dve', lambda e: e.reciprocal(out=rstd[:, 0:8], in_=rstd[:, 0:8]), reads=["rstd_a"], writes=["rstd"])
            steps.append((s3, 2))
            for qb in range(4):
                def s4(qb=qb):
                    for hl in range(G):
                        oo = (qb * 2 + hl) * 128
                        P.op('dve', lambda e, hl=hl, oo=oo: e.scalar_tensor_tensor(
                            out=OBf[:, oo:oo + 128], in0=OBf[:, oo:oo + 128], scalar=rstd[:, qb * 2 + hl: qb * 2 + hl + 1],
                            in1=subw_bc[:], op0=ALU.mult, op1=ALU.mult),
                            reads=[("OBf", qb, hl), "rstd", subw_bc.name], writes=[("OBf", qb, hl)])
                        P.op('pool', lambda e, hl=hl, oo=oo: e.tensor_tensor(
                            out=OBn[:, oo:oo + 128], in0=OBf[:, oo:oo + 128],
                            in1=SGb[par][:, qb * 256 + hl * 128: qb * 256 + (hl + 1) * 128], op=ALU.mult),
                            reads=[("OBf", qb, hl), ("SGb", par, qb)], writes=[("OBn", qb, hl)])
                steps.append((s4, 1))
            for hl, h in enumerate(heads):
                def s5(hl=hl):
                    def trd(e):
                        ins = None
                        for qb in range(4):
                            oo = (qb * 2 + hl) * 128
                            ins = e.transpose(out=psC_b[:, qb * 128:(qb + 1) * 128], in_=OBn[:, oo:oo + 128], identity=identb[:])
                        return ins
                    P.op('pe', trd, reads=[("OBn", qb, hl) for qb in range(4)] + [identb.name], writes=["psC"])

                def s6(hl=hl):
                    if final:
                        P.op('dve', lambda e: e.tensor_copy(out=OT[:, (4 + hl) * 512:(5 + hl) * 512], in_=psC_b[:, 0:512]),
                             reads=["psC"], writes=[("OT", 4 + hl)])
                    else:
                        ci = sum(len(g) for g in groups[:gi]) + hl
                        o = ci * S_q + t * 512
                        P.op('dve', lambda e: e.tensor_copy(out=CAR[:, o:o + 512], in_=psC_b[:, 0:512]),
                             reads=["psC"], writes=[("CAR", ci, t)])
                steps.append((s5, 1))
                steps.append((s6, 1))
            return steps

        def finalize(x_d, y_d, b, t, n_qt, heads, gi, groups):
            S_q = n_qt * 512
            n_car = sum(len(g) for g in groups[:gi])
            G = len(heads)
            fslots = [0, 1]

            def xload(qb):
                xs_ = fslots[qb % 2]
                row0 = (4 * t + qb) * 128
                P.op('sp', lambda e: e.dma_start(out=FB[xs_][:], in_=x_d.ap()[row0:row0 + 128, :]),
                     writes=[("FB", xs_)], dma=xkeys[xs_])

            xload(0)
            xload(1)
            for qb in range(4):
                fin_block(x_d, y_d, b, t, qb, S_q, n_car, G, fslots[qb % 2])
                if qb + 2 < 4:
                    xload(qb + 2)

        def fin_block(x_d, y_d, b, t, qb, S_q, n_car, G, xs_):
            row0 = (4 * t + qb) * 128
            ys_ = state['yslot'] % 2
            state['yslot'] += 1
            for n in range(2):
                def om(e, n=n):
                    ins = None
                    for fc in range(8):
                        if fc < 4:
                            lhsT = OT[:, fc * 512 + qb * 128: fc * 512 + (qb + 1) * 128]
                        elif fc - 4 < n_car:
                            o = (fc - 4) * S_q + t * 512 + qb * 128
                            lhsT = CAR[:, o:o + 128]
                        else:
                            o = (4 + fc - 4 - n_car) * 512 + qb * 128
                            lhsT = OT[:, o:o + 128]
                        ins = e.matmul(out=psA[:, (2 + n) * 512:(3 + n) * 512], lhsT=lhsT,
                                       rhs=WO[:, fc * 1024 + n * 512: fc * 1024 + (n + 1) * 512],
                                       start=(fc == 0), stop=(fc == 7))
                    return ins
                rd = [("OT", fc) for fc in range(4 + G)] + [("CAR", ci, t) for ci in range(n_car)] + ["WO"]
                P.op('pe', om, reads=rd, writes=[bankA(2 + n)])
            Z = FB[xs_]
            zk = ("FB", xs_)
            mo = qb * 2
            P.op('dve', lambda e: e.tensor_tensor(out=BXf[:], in0=psA[:, 1024:2048], in1=gate_bc[:, b * 1024:(b + 1) * 1024], op=ALU.mult),
                 reads=[bankA(2), bankA(3), ("gate_bc", b, 0), ("gate_bc", b, 1)], writes=["BXf"])
            P.op('dve', lambda e: e.scalar_tensor_tensor(out=Z[:], in0=Z[:], scalar=ALPHA, in1=BXf[:], op0=ALU.mult, op1=ALU.add),
                 reads=[zk, "BXf"], writes=[zk])
            P.op('dve', lambda e: e.bn_stats(out=bnst[:, 0:6], in_=Z[:, 0:512]), reads=[zk], writes=["bnst0"])
            P.op('dve', lambda e: e.bn_stats(out=bnst[:, 6:12], in_=Z[:, 512:1024]), reads=[zk], writes=["bnst1"])
            P.op('dve', lambda e: e.bn_aggr(out=mv[:, mo:mo + 2], in_=bnst[:]), reads=["bnst0", "bnst1"], writes=[("mv", qb)])
            P.op('act', lambda e: e.activation(out=lrs[:, qb:qb + 1], in_=mv[:, mo + 1:mo + 2], func=AF.Sqrt, bias=eps_ln[:], scale=1.0),
                 reads=[("mv", qb), eps_ln.name], writes=[("lrs_a", qb)])
            P.op('dve', lambda e: e.reciprocal(out=lrs[:, qb:qb + 1], in_=lrs[:, qb:qb + 1]), reads=[("lrs_a", qb)], writes=[("lrs", qb)])
            P.op('dve', lambda e: e.scalar_tensor_tensor(out=Z[:], in0=Z[:], scalar=mv[:, mo:mo + 1], in1=lng_bc[:],
                                                         op0=ALU.subtract, op1=ALU.mult),
                 reads=[zk, ("mv", qb)], writes=[zk])
            P.op('dve', lambda e: e.scalar_tensor_tensor(out=Z[:], in0=Z[:], scalar=lrs[:, qb:qb + 1], in1=lnb_bc[:],
                                                         op0=ALU.mult, op1=ALU.add),
                 reads=[zk, ("lrs", qb)], writes=[zk])
            P.op('sp', lambda e: e.dma_start(out=y_d.ap()[row0:row0 + 128, :], in_=Z[:]),
                 reads=[zk], writes=[("yout", ys_)], dma=['y0', 'y1'][ys_])

        try:
            if kstop == 1:
                raise _Stop()
            run_job(xp_d, yp_d, 0, SP_, 8, False, [[0, 1], [2, 3]], [0])
            run_job(xs_d, ys_d, 1, SS_, 4, True, [[0], [1], [2], [3]], None)
        except _Stop:
            pass
        P.op('sp', lambda e: None, reads=[("yout", 0), ("yout", 1)] + (["dbg%d" % i for i in range(1, 11)] if dbg else []))

        with nc.Block() as blk2:
            P.emit(blk2)
    return nc


_NC_CACHE = {}


def _bucket_table():
    import math
    import jax
    import jax.numpy as jnp
    try:
        dev = jax.devices('cpu')[0]
    except Exception:
        dev = None
    def run():
        rel = jnp.arange(-1400, 1401, dtype=jnp.int32)
        sign = jnp.where(rel > 0, 16, 0)
        n = jnp.abs(rel)
        nf = jnp.maximum(n, 1).astype(jnp.float32)
        large = 8 + (jnp.log(nf / 8) / math.log(128 / 8) * 8).astype(jnp.int32)
        large = jnp.minimum(large, 15)
        return np.asarray((sign + jnp.where(n < 8, n, large)).astype(jnp.int32))
    if dev is not None:
        with jax.default_device(dev):
            return run()
    return run()


def kernel(**inp):
    f = lambda k: np.ascontiguousarray(np.asarray(inp[k], dtype=np.float32))
    x_prompt = f('x_prompt'); x_sample = f('x_sample')
    c_prompt = f('c_prompt'); c_sample = f('c_sample')
    w_in = f('w_in')[0]; w_out = f('w_out')[0]; w_ada = f('w_ada')[0]; b_ada = f('b_ada')[0]
    ln_g = f('ln_g'); ln_b = f('ln_b'); sink = f('attn_sink')[0]
    rel_bias = f('rel_bias')
    subw = f('subln_w')
    lamv = np.concatenate([f('lambda_q1')[0], f('lambda_k1')[0], f('lambda_q2')[0], f('lambda_k2')[0]])[None, :]

    colperm = np.concatenate([np.arange(64) + 64 * h for h in PERM])
    w_in_l = w_in.copy()
    w_in_l[:, 0:512] = w_in[:, 0:512][:, colperm]
    w_in_l[:, 768:1280] = w_in[:, 768:1280][:, colperm]
    w_out_l = w_out.copy()
    w_out_l[0:512, :] = w_out[0:512, :][colperm, :]
    sinkp = sink[PERM][None, :]
    bt = _bucket_table()

    def bk(r):
        return bt[r + 1400]
    kk = np.arange(128)[:, None]
    mm = np.arange(MD)[None, :]
    bd_idx = bk(kk - mm + 512)
    bd_strip = np.stack([rel_bias[bd_idx, 8 + h] for h in range(4)], axis=1).reshape(128, 4 * MD)
    qq = np.arange(384)[None, :]
    relw = kk - qq + 128
    bw_idx = bk(relw)
    bw_strip = np.stack([rel_bias[bw_idx, PERM[hp]] for hp in range(8)], axis=1).reshape(128, 8 * 384)
    maskw = np.where(np.abs(relw) <= 128, 0.0, -BIG).astype(np.float32)
    relfar = np.concatenate([rel_bias[15, 8:12], rel_bias[31, 8:12]])[None, :]
    identf = np.eye(128, dtype=np.float32)

    import os
    dbg = bool(os.environ.get('KDBG'))
    kstop = int(os.environ.get('KSTOP', '0'))
    key = 'nc%d_%d' % (dbg, kstop)
    if key not in _NC_CACHE:
        _NC_CACHE[key] = build_program(dbg, kstop)
    nc = _NC_CACHE[key]

    in_maps = []
    for c in range(NCORES):
        s, j = c // 4, c % 4
        xs = np.ascontiguousarray(np.roll(x_sample[s], -SQS * j, axis=0))
        cc = np.stack([c_prompt[c], c_sample[s]], axis=0)
        cT = np.ascontiguousarray(cc.reshape(2, 8, 128).transpose(2, 1, 0).reshape(128, 16))
        fl = np.zeros((1, 66), np.float32)
        fl[0, 0] = 1.0 if j > 0 else 0.0
        fl[0, 1] = 1.0 if j < 3 else 0.0
        for i in range(64):
            fl[0, 2 + i] = 1.0 if i >= 64 - 16 * j else 0.0
        in_maps.append({
            "xp": x_prompt[c], "xs": xs, "cT": cT,
            "w_in": w_in_l, "w_out": w_out_l, "w_ada": w_ada,
            "bcol": np.ascontiguousarray(b_ada.reshape(24, 128).T),
            "bgate": np.ascontiguousarray(b_ada[None, 2048:3072]),
            "ln_g": ln_g, "ln_b": ln_b, "sinkp": np.ascontiguousarray(sinkp),
            "lamv": np.ascontiguousarray(lamv), "subw": subw,
            "relfar": np.ascontiguousarray(relfar),
            "bd_strip": np.ascontiguousarray(bd_strip.astype(np.float32)),
            "bw_strip": np.ascontiguousarray(bw_strip.astype(np.float32)),
            "maskw": maskw, "flags": fl, "identf": identf,
        })
    res = run_bass_kernel_spmd(nc, in_maps, core_ids=list(range(NCORES)))
    y_prompt = np.empty((8, SP_, D), np.float32)
    y_sample = np.empty((2, SS_, D), np.float32)
    for c in range(NCORES):
        r = res.results[c]
        s, j = c // 4, c % 4
        y_prompt[c] = np.asarray(r["yp"], np.float32)
        y_sample[s, SQS * j: SQS * (j + 1)] = np.asarray(r["ys"], np.float32)
    return (y_prompt, y_sample)
```

```python
import numpy as np
import concourse.bass as bass
import concourse.mybir as mybir
from concourse.bass_utils import run_bass_kernel_spmd

F32 = mybir.dt.float32
BF16 = mybir.dt.bfloat16
AF = mybir.ActivationFunctionType
ALU = mybir.AluOpType

NCORES = 8
D = 1024
DC = 8
SP_ = 4096
SS_ = 8192
SQS = 2048
ALPHA = 2.0 ** 0.25
LAMBDA_INIT = 0.2
BIG = 30000.0
LN_EPS = 1e-5
SUBLN_EPS = 1e-5
PERM = [0, 4, 1, 5, 2, 6, 3, 7]
MD = 1152


class Prog:
    ENG = ('sp', 'act', 'pe', 'dve', 'pool')

    def __init__(self, sems, dsems):
        self.sem = sems
        self.dsem = dsems
        self.dcnt = {k: 0 for k in dsems}
        self.base = {e: 0 for e in self.ENG}
        self.reset_block()

    def reset_block(self):
        self.ops = {e: [] for e in self.ENG}
        self.res = {}

    def op(self, eng, fn, reads=(), writes=(), dma=None):
        ops = self.ops[eng]
        idx = len(ops)
        if dma is not None:
            self.dcnt[dma] += 1
            ev = ('dma', dma, self.dcnt[dma])
        else:
            ev = (eng, idx)
        deps = {}

        def add(d, raw):
            if d[0] == 'dma':
                d = ('dma', d[1], self.dcnt[d[1]])
            if raw or d not in deps:
                deps[d] = raw or deps.get(d, False)

        def is_ps(k):
            return isinstance(k, str) and k.startswith('ps')

        ps_keys = {}
        for k in reads:
            if is_ps(k):
                ps_keys[k] = False
        for k in writes:
            if is_ps(k):
                ps_keys[k] = True
        for k, isw in ps_keys.items():
            r = self.res.get(k)
            if r is not None and r[0] is not None:
                add(r[0], (not isw) and r[2])
        for k in reads:
            if is_ps(k):
                continue
            r = self.res.get(k)
            if r is not None and r[0] is not None:
                add(r[0], True)
        for k in writes:
            if is_ps(k):
                continue
            r = self.res.get(k)
            if r is not None:
                if r[0] is not None:
                    add(r[0], False)
                for rv in r[1].values():
                    add(rv, False)
        for k in reads:
            if is_ps(k):
                continue
            r = self.res.setdefault(k, [None, {}])
            r[1][ev if dma is not None else eng] = ev
        for k in writes:
            if is_ps(k):
                continue
            self.res[k] = [ev, {}]
        for k, isw in ps_keys.items():
            self.res[k] = [ev, {}, isw]
        ops.append((fn, deps, ev, dma))
        return ev

    def emit(self, block):
        need = {e: set() for e in self.ENG}
        for e in self.ENG:
            for (fn, deps, ev, dma) in self.ops[e]:
                for d in deps:
                    if d[0] != 'dma':
                        need[d[0]].add(d[1])
        ordn = {}
        for e in self.ENG:
            s = sorted(need[e])
            ordn[e] = {idx: self.base[e] + i + 1 for i, idx in enumerate(s)}

        def body(en):
            def f(eng):
                seen = {}
                for idx, (fn, deps, ev, dma) in enumerate(self.ops[en]):
                    for d, raw in deps.items():
                        if d[0] == 'dma':
                            if dma is not None and d[1] == dma:
                                continue
                            key = ('dma', d[1])
                            val = 16 * d[2]
                            sem = self.dsem[d[1]]
                        else:
                            if d[0] == en:
                                if (not raw) or en == 'pe' or en == 'sp':
                                    continue
                            key = d[0]
                            val = ordn[d[0]][d[1]]
                            sem = self.sem[d[0]]
                        if seen.get(key, 0) >= val:
                            continue
                        eng.wait_ge(sem, val)
                        seen[key] = val
                    ins = fn(eng)
                    if dma is not None:
                        ins.then_inc(self.dsem[dma], 16)
                    elif idx in ordn[en]:
                        ins.then_inc(self.sem[en], 1)
            return f

        block.sync(body('sp'))
        block.scalar(body('act'))
        block.tensor(body('pe'))
        block.vector(body('dve'))
        block.gpsimd(body('pool'))
        for e in self.ENG:
            self.base[e] += len(ordn[e])
        self.reset_block()


class _CopyShim:
    def __init__(self, e):
        self.e = e

    def tensor_copy(self, out, in_):
        return self.e.copy(out=out, in_=in_)


def dap(t, off, dims):
    return bass.AP(t, off, [list(d) for d in dims])


class _Stop(Exception):
    pass


def build_program(dbg=False, kstop=0):
    nc = bass.Bass("TRN2", target_bir_lowering=False)
    if dbg:
        dbg_ot = nc.dram_tensor("dbg_ot", [128, 3072], BF16, kind="ExternalOutput")
        dbg_car = nc.dram_tensor("dbg_car", [128, 1024], BF16, kind="ExternalOutput")
        dbg_c = nc.dram_tensor("dbg_c", [128, 64], F32, kind="ExternalOutput")
        dbg_sg = nc.dram_tensor("dbg_sg", [128, 2048 + 1024], BF16, kind="ExternalOutput")

    def din(name, shape):
        return nc.dram_tensor(name, list(shape), F32, kind="ExternalInput")

    xp_d = din("xp", [SP_, D])
    xs_d = din("xs", [SS_, D])
    cT_d = din("cT", [128, 16])
    win_d = din("w_in", [D, 3328])
    wout_d = din("w_out", [D, D])
    wada_d = din("w_ada", [D, 3072])
    bcol_d = din("bcol", [128, 24])
    bgate_d = din("bgate", [1, 1024])
    lng_d = din("ln_g", [1, 1024])
    lnb_d = din("ln_b", [1, 1024])
    sink_d = din("sinkp", [1, 8])
    lamv_d = din("lamv", [1, 256])
    subw_d = din("subw", [1, 128])
    relfar_d = din("relfar", [1, 8])
    bd_d = din("bd_strip", [128, 4 * MD])
    bw_d = din("bw_strip", [128, 8 * 384])
    maskw_d = din("maskw", [128, 384])
    flags_d = din("flags", [1, 66])
    ident_d = din("identf", [128, 128])
    yp_d = nc.dram_tensor("yp", [SP_, D], F32, kind="ExternalOutput")
    ys_d = nc.dram_tensor("ys", [SQS, D], F32, kind="ExternalOutput")

    from contextlib import ExitStack
    es = ExitStack()

    def sb(name, shape, dt=F32):
        return es.enter_context(nc.sbuf_tensor("sb_" + name, list(shape), dt))

    def ps(name, shape, dt=F32):
        return es.enter_context(nc.psum_tensor("ps_" + name, list(shape), dt))

    with es:
        sems = {e: es.enter_context(nc.semaphore("s_" + e)) for e in ('act', 'pe', 'dve', 'pool')}
        dkeys = ['c0', 'wa', 'x0', 'x1', 'xb0', 'xb1', 'xb2', 'xb3', 'xb4', 'y0', 'y1', 'dbg']
        dsems = {k: es.enter_context(nc.semaphore("d_" + k)) for k in dkeys}
        P = Prog(sems, dsems)

        identf = sb("identf", [128, 128])
        identb = sb("identb", [128, 128], BF16)
        s1p = sb("s1p", [128, 16])
        shf = sb("shf", [128, 16])
        gate_bc = sb("gate_bc", [128, 2048])
        lng_bc = sb("lng_bc", [128, 1024])
        lnb_bc = sb("lnb_bc", [128, 1024])
        subw_bc = sb("subw_bc", [128, 128])
        neglam = sb("neglam", [128, 1])
        expsink = sb("expsink", [128, 8])
        cfar = sb("cfar", [128, 8])
        cdiff = sb("cdiff", [128, 4])
        farS = sb("farS", [128, 4 * 64])
        flags = sb("flags", [128, 66])
        negLR = sb("negLR", [128, 2])
        omf = sb("omf", [128, 2])
        BW = sb("BW", [128, 8 * 384], BF16)
        BXW = sb("BXW", [128, 2048], BF16)
        zero_c = sb("zero_c", [128, 1])
        eps_ln = sb("eps_ln", [128, 1])
        eps_sub = sb("eps_sub", [128, 1])

        psA = ps("psA", [128, 2048])
        psB = ps("psB", [128, 1536])
        psC = ps("psC", [128, 512])
        psA_b = psA.bitcast(BF16)
        psB_b = psB.bitcast(BF16)
        psC_b = psC.bitcast(BF16)

        def bankA(i):
            return "psA%d" % i

        def bankB(i):
            return "psB%d" % i

        WIs = sb("WIs", [128, 8 * 1280], BF16)
        WO = sb("WO", [128, 8 * 1024], BF16)
        FB = [sb("FB%d" % i, [128, 1024]) for i in range(2)]
        xkeys = ['x0', 'x1']
        xbkeys = ['xb0', 'xb1', 'xb2', 'xb3', 'xb4']
        NPF = 3
        state = {'xslot': 0, 'evac': 0, 'yslot': 0}

        wl_state = {'i': 0}

        def stage_cast(src_ap, ncols, cast_fn, writes, reads_extra=(), dst_dims=None, engs=('pool', 'dve'), q='sp'):
            s_ = state['xslot'] % 2
            state['xslot'] += 1
            P.op(q, lambda e, s_=s_: e.dma_start(
                out=(FB[s_][:, 0:ncols] if dst_dims is None else dap(FB[s_], 0, dst_dims)), in_=src_ap),
                 writes=[("FB", s_)], dma=xkeys[s_])
            eng = engs[wl_state['i'] % 2]
            wl_state['i'] += 1
            P.op(eng, lambda e, s_=s_: cast_fn(_CopyShim(e) if eng == 'act' else e, FB[s_]), reads=[("FB", s_)] + list(reads_extra), writes=writes)

        def load_static_weights():
            for dc in range(8):
                r0 = dc * 128
                def castA(e, fb, dc=dc):
                    e.tensor_copy(out=WIs[:, dc * 1280: dc * 1280 + 512], in_=fb[:, 0:512])
                    return e.tensor_copy(out=WIs[:, dc * 1280 + 1024: dc * 1280 + 1280], in_=fb[:, 512:768])
                stage_cast(win_d.ap()[r0:r0 + 128, 0:768], 768, castA, writes=["WIs"], q='act')
                def castB(e, fb, dc=dc):
                    return e.tensor_copy(out=WIs[:, dc * 1280 + 512: dc * 1280 + 1024], in_=fb[:, 0:512])
                stage_cast(win_d.ap()[r0:r0 + 128, 768:1280], 512, castB, writes=["WIs"], q='act')
                def castO(e, fb, dc=dc):
                    return e.tensor_copy(out=WO[:, dc * 1024:(dc + 1) * 1024], in_=fb[:, 0:1024])
                stage_cast(wout_d.ap()[r0:r0 + 128, :], 1024, castO, writes=["WO"], q='act')

        with ExitStack() as es1:
            def sb1(name, shape, dt=F32):
                return es1.enter_context(nc.sbuf_tensor("sb_" + name, list(shape), dt))

            WA = sb1("WA", [128, 4 * 3072])
            cT = sb1("cT", [128, 16])
            sc = sb1("sc", [128, 16])
            sc_rep = sb1("sc_rep", [128, 2 * 8 * 128])
            bcol = sb1("bcol", [128, 24])
            bgate_bc = sb1("bgate_bc", [128, 1024])
            lamv = sb1("lamv", [128, 256])
            lamp = sb1("lamp", [128, 128])
            lams = sb1("lams", [128, 4])
            sinkb = sb1("sinkb", [128, 8])
            subw_raw = sb1("subw_raw", [128, 128])
            maskw = sb1("maskw", [128, 384])
            BWf = sb1("BWf", [128, 8 * 384])

            def ld(dst_t, src):
                P.op('sp', lambda e: e.dma_start(out=dst_t[:], in_=src), writes=[dst_t.name], dma='c0')

            ld(identf, ident_d.ap())
            ld(cT, cT_d.ap())
            ld(bcol, bcol_d.ap())
            ld(bgate_bc, dap(bgate_d, 0, [[0, 128], [1, 1024]]))
            ld(lng_bc, dap(lng_d, 0, [[0, 128], [1, 1024]]))
            ld(lnb_bc, dap(lnb_d, 0, [[0, 128], [1, 1024]]))
            ld(sinkb, dap(sink_d, 0, [[0, 128], [1, 8]]))
            ld(lamv, dap(lamv_d, 0, [[0, 128], [1, 256]]))
            ld(subw_raw, dap(subw_d, 0, [[0, 128], [1, 128]]))
            ld(cfar, dap(relfar_d, 0, [[0, 128], [1, 8]]))
            ld(flags, dap(flags_d, 0, [[0, 128], [1, 66]]))
            ld(BWf, bw_d.ap())
            ld(maskw, maskw_d.ap())
            def load_wa(dcs):
                for dc in dcs:
                    P.op('sp', lambda e, dc=dc: e.dma_start(
                        out=WA[:, (dc % 4) * 3072:(dc % 4 + 1) * 3072], in_=wada_d.ap()[dc * 128:(dc + 1) * 128, :]),
                        writes=[("WA", dc % 4)], dma='wa')
            load_wa(range(0, 4))

            P.op('pool', lambda e: e.memset(zero_c[:], 0.0), writes=[zero_c.name])
            P.op('pool', lambda e: e.memset(eps_ln[:], LN_EPS), writes=[eps_ln.name])
            P.op('pool', lambda e: e.memset(eps_sub[:], SUBLN_EPS), writes=[eps_sub.name])
            P.op('dve', lambda e: e.tensor_copy(out=identb[:], in_=identf[:]),
                 reads=[identf.name], writes=[identb.name])
            P.op('act', lambda e: e.activation(out=sc[:], in_=cT[:], func=AF.Silu),
                 reads=[cT.name], writes=[sc.name])
            P.op('act', lambda e: e.activation(out=expsink[:], in_=sinkb[:], func=AF.Exp),
                 reads=[sinkb.name], writes=[expsink.name])
            P.op('dve', lambda e: e.tensor_tensor(out=lamp[:, 0:64], in0=lamv[:, 0:64], in1=lamv[:, 64:128], op=ALU.mult),
                 reads=[lamv.name], writes=["lamp0"])
            P.op('dve', lambda e: e.tensor_tensor(out=lamp[:, 64:128], in0=lamv[:, 128:192], in1=lamv[:, 192:256], op=ALU.mult),
                 reads=[lamv.name], writes=["lamp1"])
            P.op('dve', lambda e: e.reduce_sum(out=lams[:, 0:1], in_=lamp[:, 0:64], axis=mybir.AxisListType.X),
                 reads=["lamp0"], writes=["lams0"])
            P.op('dve', lambda e: e.reduce_sum(out=lams[:, 1:2], in_=lamp[:, 64:128], axis=mybir.AxisListType.X),
                 reads=["lamp1"], writes=["lams1"])
            P.op('act', lambda e: e.activation(out=lams[:, 2:4], in_=lams[:, 0:2], func=AF.Exp),
                 reads=["lams0", "lams1"], writes=["lams23"])
            P.op('dve', lambda e: e.tensor_tensor(out=neglam[:], in0=lams[:, 3:4], in1=lams[:, 2:3], op=ALU.subtract),
                 reads=["lams23"], writes=["neglam_a"])
            P.op('dve', lambda e: e.tensor_scalar(out=neglam[:], in0=neglam[:], scalar1=-LAMBDA_INIT, scalar2=None, op0=ALU.add),
                 reads=["neglam_a"], writes=[neglam.name])
            P.op('dve', lambda e: e.tensor_scalar(out=subw_bc[:], in0=subw_raw[:], scalar1=1.0 - LAMBDA_INIT, scalar2=None, op0=ALU.mult),
                 reads=[subw_raw.name], writes=[subw_bc.name])
            P.op('dve', lambda e: e.tensor_tensor(out=cdiff[:], in0=cfar[:, 0:4], in1=cfar[:, 4:8], op=ALU.subtract),
                 reads=[cfar.name], writes=[cdiff.name])
            for h in range(4):
                P.op('dve', lambda e, h=h: e.tensor_scalar(
                    out=farS[:, h * 64:(h + 1) * 64], in0=flags[:, 2:66],
                    scalar1=cdiff[:, h:h + 1], scalar2=cfar[:, 4 + h:5 + h], op0=ALU.mult, op1=ALU.add),
                    reads=[flags.name, cdiff.name, cfar.name], writes=[("farS", h)])
            P.op('dve', lambda e: e.tensor_scalar(out=negLR[:], in0=flags[:, 0:2], scalar1=-1.0, scalar2=BIG, op0=ALU.add, op1=ALU.mult),
                 reads=[flags.name], writes=[negLR.name])
            for h in range(8):
                P.op('dve', lambda e, h=h: e.tensor_tensor(
                    out=BWf[:, h * 384:(h + 1) * 384], in0=BWf[:, h * 384:(h + 1) * 384], in1=maskw[:], op=ALU.add),
                    reads=[BWf.name, maskw.name], writes=[("BWf", h)])
                P.op('pool', lambda e, h=h: e.tensor_copy(out=BW[:, h * 384:(h + 1) * 384], in_=BWf[:, h * 384:(h + 1) * 384]),
                     reads=[("BWf", h)], writes=[("BW", h)])
            P.op('dve', lambda e: e.tensor_scalar(
                out=dap(BXW, 0, [[2048, 128], [128, 8], [1, 128]]),
                in0=dap(BWf, 256, [[8 * 384, 128], [384, 8], [1, 128]]),
                scalar1=negLR[:, 0:1], scalar2=None, op0=ALU.add),
                reads=[("BWf", hh) for hh in range(8)] + [negLR.name], writes=["BXWL"])
            P.op('dve', lambda e: e.tensor_scalar(
                out=dap(BXW, 1024, [[2048, 128], [128, 8], [1, 128]]),
                in0=dap(BWf, 0, [[8 * 384, 128], [384, 8], [1, 128]]),
                scalar1=negLR[:, 1:2], scalar2=None, op0=ALU.add),
                reads=[("BWf", hh) for hh in range(8)] + [negLR.name], writes=["BXWR"])
            for b in range(2):
                for dc in range(8):
                    o = (b * 8 + dc) * 128
                    P.op('pool', lambda e, o=o, b=b, dc=dc: e.tensor_copy(
                        out=sc_rep[:, o:o + 128], in_=sc[:, dc * 2 + b:dc * 2 + b + 1].broadcast_to([128, 128])),
                        reads=[sc.name], writes=[("sc_rep", b, dc)])

            def mod_cols_half(half):
                def f(e):
                    ins = None
                    for dc in range(half * 4, half * 4 + 4):
                        for fc in range(16):
                            ins = e.matmul(out=psC[:, fc * 2:fc * 2 + 2],
                                           lhsT=WA[:, (dc % 4) * 3072 + fc * 128: (dc % 4) * 3072 + fc * 128 + 128],
                                           rhs=sc[:, dc * 2:dc * 2 + 2],
                                           start=(dc == 0 and fc == 0), stop=(dc == 7), skip_group_check=True)
                    return ins
                P.op('pe', f, reads=[("WA", k) for k in range(4)] + [sc.name], writes=["psC"])

            def gate_half(half):
                for b in range(2):
                    for n in range(2):
                        def gmm(e, b=b, n=n):
                            ins = None
                            for dc in range(half * 4, half * 4 + 4):
                                o = (b * 8 + dc) * 128
                                ins = e.matmul(out=psA[:, (b * 2 + n) * 512:(b * 2 + n + 1) * 512],
                                               lhsT=sc_rep[:, o:o + 128],
                                               rhs=WA[:, (dc % 4) * 3072 + 2048 + n * 512: (dc % 4) * 3072 + 2048 + (n + 1) * 512],
                                               start=(dc == 0), stop=(dc == 7))
                            return ins
                        P.op('pe', gmm, reads=[("WA", k) for k in range(4)] + [("sc_rep", b, dc) for dc in range(8)],
                             writes=[bankA(b * 2 + n)])

            load_static_weights()
            mod_cols_half(0)
            gate_half(0)
            load_wa(range(4, 8))
            mod_cols_half(1)
            gate_half(1)
            for b in range(2):
                P.op('dve', lambda e, b=b: e.tensor_tensor(
                    out=shf[:, b * 8:(b + 1) * 8], in0=psC[:, b:16:2],
                    in1=bcol[:, 0:8], op=ALU.add),
                    reads=["psC", bcol.name], writes=[("shf", b)])
                P.op('dve', lambda e, b=b: e.scalar_tensor_tensor(
                    out=s1p[:, b * 8:(b + 1) * 8], in0=psC[:, 16 + b:32:2], scalar=1.0, in1=bcol[:, 8:16],
                    op0=ALU.add, op1=ALU.add),
                    reads=["psC", bcol.name], writes=[("s1p", b)])
            for b in range(2):
                for n in range(2):
                    P.op('dve', lambda e, b=b, n=n: e.tensor_tensor(
                        out=gate_bc[:, b * 1024 + n * 512: b * 1024 + (n + 1) * 512],
                        in0=psA[:, (b * 2 + n) * 512:(b * 2 + n + 1) * 512],
                        in1=bgate_bc[:, n * 512:(n + 1) * 512], op=ALU.add),
                        reads=[bankA(b * 2 + n), bgate_bc.name], writes=[("gate_bc", b, n)])
            with nc.Block() as blk1:
                P.emit(blk1)

        WIg = sb("WIg", [128, 8 * 1024], BF16)
        BD = sb("BD", [128, 2 * MD], BF16)
        KT = sb("KT", [128, 8192], BF16)
        VA = sb("VA", [128, 64 * 130], BF16)
        CAR = sb("CAR", [128, 8192], BF16)
        XBF = [sb("xbf%d" % i, [128, 1024], BF16) for i in range(4)]
        uT = sb("uT", [128, 8 * 768], BF16)
        QT = [sb("QT%d" % i, [128, 2 * 512], BF16) for i in range(2)]
        QAT = sb("QAT", [128, 4 * 512], BF16)
        SGb = [sb("SGb%d" % i, [128, 4 * 256], BF16) for i in range(2)]
        SGa = sb("SGa", [128, 4 * 512], BF16)
        KAT = sb("KAT", [128, 768], BF16)
        VAW = sb("VAW", [128, 6 * 2 * 66], BF16)
        OT = sb("OT", [128, 6 * 512], BF16)
        PT = [sb("PT%d" % i, [128, 1024], BF16) for i in range(2)]
        OBf = sb("OBf", [128, 4 * 2 * 128])
        OBn = sb("OBn", [128, 4 * 2 * 128], BF16)
        OAg = sb("OAg", [128, 4 * 512], BF16)
        BX = sb("BX", [128, 1024], BF16)
        BXf = sb("BXf", [128, 1024])
        rec = sb("rec", [128, 16])
        ssq = sb("ssq", [128, 8])
        rstd = sb("rstd", [128, 8])
        sqj = sb("sqj", [128, 128])
        den = sb("den", [128, 16])
        bnst = sb("bnst", [128, 12])
        mv = sb("mv", [128, 8])
        lrs = sb("lrs", [128, 4])

        def load_kv_weights(heads):
            for hl, h in enumerate(heads):
                for dc in range(8):
                    src = dap(win_d, (dc * 128) * 3328 + 1792 + h * 128, [[3328, 128], [512, 2], [1, 128]])
                    def castG(e, fb, dc=dc, hl=hl):
                        return e.tensor_copy(
                            out=dap(WIg, dc * 1024 + 256 + hl * 128, [[8192, 128], [256, 2], [1, 128]]),
                            in_=dap(fb, 0, [[1024, 128], [128, 2], [1, 128]]))
                    stage_cast(src, 256, castG, writes=["WIgkv"], dst_dims=[[1024, 128], [128, 2], [1, 128]])

        def q_weight_thunks(heads):
            th = []
            for hl, h in enumerate(heads):
                for dc in range(8):
                    def t1(dc=dc, hl=hl, h=h):
                        src = dap(win_d, (dc * 128) * 3328 + 1280 + h * 128, [[3328, 128], [1536, 2], [1, 128]])
                        def castG(e, fb):
                            return e.tensor_copy(
                                out=dap(WIg, dc * 1024 + hl * 128, [[8192, 128], [768, 2], [1, 128]]),
                                in_=dap(fb, 0, [[1024, 128], [128, 2], [1, 128]]))
                        stage_cast(src, 256, castG, writes=["WIgq"], dst_dims=[[1024, 128], [128, 2], [1, 128]],
                                   engs=('act', 'dve'))
                    th.append(t1)
                def t2(hl=hl, h=h):
                    def castD1(e, fb):
                        return e.tensor_copy(out=BD[:, hl * MD: hl * MD + 1024], in_=fb[:, 0:1024])
                    stage_cast(bd_d.ap()[:, h * MD: h * MD + 1024], 1024, castD1, writes=[("BD", hl)], engs=('act', 'dve'))
                def t3(hl=hl, h=h):
                    def castD2(e, fb):
                        return e.tensor_copy(out=BD[:, hl * MD + 1024: hl * MD + MD], in_=fb[:, 0:MD - 1024])
                    stage_cast(bd_d.ap()[:, h * MD + 1024: (h + 1) * MD], MD - 1024, castD2, writes=[("BD", hl)],
                               engs=('act', 'dve'))
                th.append(t2)
                th.append(t3)
            return th

        P.op('pool', lambda e: e.memset(VA[:], 1.0), writes=["VA_all"])
        P.op('pool', lambda e: e.memset(VAW[:], 1.0), writes=["VAW_all"])

        xb_state = {'k': 0, 'm': 0}

        def prefetch_x(src_d, row0):
            sl = xb_state['k'] % 4
            xb_state['k'] += 1
            P.op('pool', lambda e, sl=sl: e.dma_start(out=XBF[sl][:], in_=src_d.ap()[row0:row0 + 128, :]),
                 writes=[("xbf", sl)], dma=xbkeys[sl])
            return sl

        def make_uT_from(sl, bslot, b):
            par = xb_state['m'] % 2
            xb_state['m'] += 1
            if par == 0:
                evk, odk, evb, odb, evo, odo = bankA(0), "psC", psA_b, psC_b, 0, 0
            else:
                evk, odk, evb, odb, evo, odo = bankB(0), bankB(1), psB_b, psB_b, 0, 1024

            dve_dcs = [0, 2, 4, 6, 7]
            act_dcs = [1, 3, 5]

            def tsrc(dc):
                if dc in dve_dcs:
                    k = dve_dcs.index(dc)
                    return evb[:, evo + k * 128: evo + (k + 1) * 128]
                k = act_dcs.index(dc)
                return odb[:, odo + k * 128: odo + (k + 1) * 128]

            def tr(e):
                ins = None
                for dc in range(8):
                    ins = e.transpose(out=tsrc(dc), in_=XBF[sl][:, dc * 128:(dc + 1) * 128], identity=identb[:])
                return ins
            P.op('pe', tr, reads=[("xbf", sl), identb.name], writes=[evk, odk])
            for dc in range(8):
                o = dc * 768 + bslot * 128
                if dc in dve_dcs:
                    P.op('dve', lambda e, dc=dc, o=o: e.tensor_scalar(
                        out=uT[:, o:o + 128], in0=tsrc(dc),
                        scalar1=s1p[:, b * 8 + dc:b * 8 + dc + 1], scalar2=shf[:, b * 8 + dc:b * 8 + dc + 1],
                        op0=ALU.mult, op1=ALU.add),
                        reads=[evk], writes=[("uT", bslot, dc)])
                else:
                    P.op('act', lambda e, dc=dc, o=o: e.activation(
                        out=uT[:, o:o + 128], in_=tsrc(dc), func=AF.Identity,
                        bias=shf[:, b * 8 + dc:b * 8 + dc + 1], scale=s1p[:, b * 8 + dc:b * 8 + dc + 1]),
                        reads=[odk], writes=[("uT", bslot, dc)])

        def make_uT_list(src_d, items, b):
            slots = {}
            for i in range(min(NPF, len(items))):
                slots[i] = prefetch_x(src_d, items[i][1] * 128)
            for i, (bs, lb) in enumerate(items):
                if i + NPF < len(items):
                    slots[i + NPF] = prefetch_x(src_d, items[i + NPF][1] * 128)
                make_uT_from(slots[i], bs, b)

        def uT_keys(bslots):
            return [("uT", bs, dc) for bs in bslots for dc in range(8)]

        def proj_fm(W, pitch, col, bslot0, nb, pbank, reads_extra=()):
            def f(e):
                ins = None
                for dc in range(8):
                    ins = e.matmul(out=psA[:, pbank * 512: pbank * 512 + nb * 128],
                                   lhsT=W[:, dc * pitch + col: dc * pitch + col + 128],
                                   rhs=uT[:, dc * 768 + bslot0 * 128: dc * 768 + (bslot0 + nb) * 128],
                                   start=(dc == 0), stop=(dc == 7))
                return ins
            P.op('pe', f, reads=uT_keys(range(bslot0, bslot0 + nb)) + list(reads_extra), writes=[bankA(pbank)])

        def proj_tm(W, pitch, col, ncol, bslot, pbank, reads_extra=()):
            def f(e):
                ins = None
                for dc in range(8):
                    ins = e.matmul(out=psA[:, pbank * 512: pbank * 512 + ncol],
                                   lhsT=uT[:, dc * 768 + bslot * 128: dc * 768 + (bslot + 1) * 128],
                                   rhs=W[:, dc * pitch + col: dc * pitch + col + ncol],
                                   start=(dc == 0), stop=(dc == 7))
                return ins
            P.op('pe', f, reads=uT_keys([bslot]) + list(reads_extra), writes=[bankA(pbank)])

        pb_state = {'i': 0}

        def next_pbank():
            b = 1 + pb_state['i'] % 3
            pb_state['i'] += 1
            return b

        def evac_copy(out_ap, in_ap, reads, writes, scale=None, which=None):
            if which is None:
                which = state['evac'] % 2
                state['evac'] += 1
            if which == 0:
                if scale is None:
                    P.op('dve', lambda e: e.tensor_copy(out=out_ap, in_=in_ap), reads=reads, writes=writes)
                else:
                    P.op('dve', lambda e: e.tensor_scalar(out=out_ap, in0=in_ap, scalar1=scale, scalar2=None, op0=ALU.mult),
                         reads=reads, writes=writes)
            else:
                if scale is None:
                    P.op('act', lambda e: e.copy(out=out_ap, in_=in_ap), reads=reads, writes=writes)
                else:
                    P.op('act', lambda e: e.mul(out=out_ap, in_=in_ap, mul=scale), reads=reads, writes=writes)

        def run_job(x_d, y_d, b, S_kv, n_qt, is_sample, groups, next_first_heads):
            NB = S_kv // 128
            n_groups = len(groups)
            for gi, heads in enumerate(groups):
                G = len(heads)
                final = (gi == n_groups - 1)
                if not state.get('kv_preloaded'):
                    load_kv_weights(heads)
                state['kv_preloaded'] = False
                qw = q_weight_thunks(heads)
                nxt_heads = groups[gi + 1] if gi + 1 < n_groups else next_first_heads

                NT = NB // 2

                kv_slots = {}
                for blk in range(NPF):
                    kv_slots[blk] = prefetch_x(x_d, blk * 128)

                def front(n):
                    base = (n % 3) * 2
                    for k in range(2):
                        blk = 2 * n + k
                        if blk + NPF < NB:
                            kv_slots[blk + NPF] = prefetch_x(x_d, (blk + NPF) * 128)
                        make_uT_from(kv_slots[blk], base + k, b)

                def back(n):
                    base = (n % 3) * 2
                    for hl in range(G):
                        pbk = next_pbank()
                        proj_fm(WIg, 1024, 256 + hl * 128, base, 2, pbk, reads_extra=["WIgkv"])
                        evac_copy(KT[:, hl * S_kv + n * 256: hl * S_kv + (n + 1) * 256],
                                  psA[:, pbk * 512: pbk * 512 + 256],
                                  reads=[bankA(pbk)], writes=[("KT", hl, 2 * n), ("KT", hl, 2 * n + 1)])
                    for k in range(2):
                        tb = 2 * n + k
                        pbk = next_pbank()
                        proj_tm(WIg, 1024, 512, G * 128, base + k, pbk, reads_extra=["WIgkv"])
                        state['evac'] += 1
                        for hl in range(G):
                            o = (tb * G + hl) * 130
                            evac_copy(VA[:, o:o + 128], psA[:, pbk * 512 + hl * 128: pbk * 512 + (hl + 1) * 128],
                                      reads=[bankA(pbk), "VA_all"], writes=[("VA", hl, tb)], which=state['evac'] % 2)

                for n in range(NT + 1):
                    if n < NT:
                        front(n)
                    if kstop == 22:
                        raise _Stop()
                    if n >= 1:
                        back(n - 1)
                    if n >= 1 and qw:
                        qw.pop(0)()
                while qw:
                    qw.pop(0)()
                    if kstop == 23 and n == 1:
                        raise _Stop()
                    if kstop == 2 and n == 2:
                        raise _Stop()
                if kstop == 3:
                    raise _Stop()

                for t in range(n_qt):
                    if t == n_qt - 1 and nxt_heads is not None:
                        load_kv_weights(nxt_heads)
                        state['kv_preloaded'] = True
                    q_phase(x_d, y_d, b, S_kv, NB, t, n_qt, is_sample, heads, final, gi, groups)
                    if kstop == 4:
                        raise _Stop()
                    if kstop == 5 and final:
                        raise _Stop()
                if kstop == 6 and final:
                    raise _Stop()

        def tile_blocks(t, n_qt, NB, is_sample, final):
            blocks = []
            if final:
                if t > 0:
                    blocks.append((0, 4 * t - 1))
                elif is_sample:
                    blocks.append((0, NB - 1))
                for i in range(4):
                    blocks.append((1 + i, 4 * t + i))
                if t < n_qt - 1 or is_sample:
                    blocks.append((5, 4 * t + 4))
            else:
                for i in range(4):
                    blocks.append((1 + i, 4 * t + i))
            return blocks

        def prepA_steps(x_d, b, t, n_qt, NB, is_sample, final, G):
            blocks = tile_blocks(t, n_qt, NB, is_sample, final)
            par = t % 2
            n = len(blocks)
            slots = {}
            steps = []

            def pf(i):
                slots[i] = prefetch_x(x_d, blocks[i][1] * 128)

            def s_pf0():
                for i in range(min(NPF, n)):
                    pf(i)
            steps.append((s_pf0, 0))
            for i, (bs, lb) in enumerate(blocks):
                def s_tr(i=i):
                    if i + NPF < n:
                        pf(i + NPF)
                    sl = slots[i]

                    def tr(e):
                        ins = None
                        for dc in range(8):
                            ins = e.transpose(out=psC_b[:, dc * 128:(dc + 1) * 128], in_=XBF[sl][:, dc * 128:(dc + 1) * 128],
                                              identity=identb[:])
                        return ins
                    P.op('pe', tr, reads=[("xbf", sl), identb.name], writes=["psC"])

                def s_ev(bs=bs):
                    for dc in range(8):
                        o = dc * 768 + bs * 128
                        P.op('dve', lambda e, dc=dc, o=o: e.tensor_scalar(
                            out=uT[:, o:o + 128], in0=psC_b[:, dc * 128:(dc + 1) * 128],
                            scalar1=s1p[:, b * 8 + dc:b * 8 + dc + 1], scalar2=shf[:, b * 8 + dc:b * 8 + dc + 1],
                            op0=ALU.mult, op1=ALU.add),
                            reads=["psC"], writes=[("uT", bs, dc)])
                steps.append((s_tr, 0))
                steps.append((s_ev, 2))

            def fm_part(col, dcs):
                def f(e):
                    ins = None
                    for dc in dcs:
                        ins = e.matmul(out=psC[:, 0:512],
                                       lhsT=WIg[:, dc * 1024 + col: dc * 1024 + col + 128],
                                       rhs=uT[:, dc * 768 + 128: dc * 768 + 640],
                                       start=(dc == 0), stop=(dc == 7))
                    return ins
                P.op('pe', f, reads=uT_keys(range(1, 5)) + ["WIgq"], writes=["psC"])

            for hl in range(G):
                steps.append((lambda hl=hl: fm_part(hl * 128, range(0, 4)), 0))
                steps.append((lambda hl=hl: fm_part(hl * 128, range(4, 8)), 0))

                def s_qe(hl=hl):
                    P.op('dve', lambda e: e.tensor_scalar(out=QT[par][:, hl * 512:(hl + 1) * 512], in0=psC[:, 0:512],
                                                          scalar1=0.125, scalar2=None, op0=ALU.mult),
                         reads=["psC"], writes=[("QT", par, hl)])
                steps.append((s_qe, 2))
            for qb in range(4):
                def s_g(qb=qb):
                    def f(e):
                        ins = None
                        for dc in range(8):
                            ins = e.matmul(out=psC[:, 0:G * 128],
                                           lhsT=uT[:, dc * 768 + (1 + qb) * 128: dc * 768 + (2 + qb) * 128],
                                           rhs=WIg[:, dc * 1024 + 768: dc * 1024 + 768 + G * 128],
                                           start=(dc == 0), stop=(dc == 7))
                        return ins
                    P.op('pe', f, reads=uT_keys([1 + qb]) + ["WIgq"], writes=["psC"])

                def s_ge(qb=qb):
                    P.op('dve', lambda e: e.tensor_copy(out=SGb[par][:, qb * 256: qb * 256 + G * 128], in_=psC[:, 0:G * 128]),
                         reads=["psC"], writes=[("SGb", par, qb)])
                steps.append((s_g, 0))
                steps.append((s_ge, 1))
            return steps

        def prepB_parts(blocks, acteng):
            wh = 1 if acteng else None

            def p_qa():
                for c in range(4):
                    pbk = next_pbank()
                    proj_fm(WIs, 1280, c * 128, 1, 4, pbk, reads_extra=["WIs"])
                    evac_copy(QAT[:, c * 512:(c + 1) * 512], psA[:, pbk * 512:(pbk + 1) * 512],
                              reads=[bankA(pbk)], writes=[("QAT", c)], scale=0.125, which=wh)

            def p_ga(qbs):
                for qb in qbs:
                    pbk = next_pbank()
                    proj_tm(WIs, 1280, 512, 512, 1 + qb, pbk, reads_extra=["WIs"])
                    P.op('act', lambda e, qb=qb, pbk=pbk: e.activation(
                        out=SGa[:, qb * 512:(qb + 1) * 512], in_=psA[:, pbk * 512:(pbk + 1) * 512], func=AF.Silu),
                        reads=[bankA(pbk)], writes=[("SGa", qb)])

            def p_kv():
                bs0 = blocks[0][0]
                nbk = len(blocks)
                pieces = []
                s0 = bs0
                rem = nbk
                while rem > 0:
                    n = min(4, rem)
                    pieces.append((s0, n))
                    s0 += n
                    rem -= n
                for (s0, n) in pieces:
                    pbk = next_pbank()
                    proj_fm(WIs, 1280, 1024, s0, n, pbk, reads_extra=["WIs"])
                    evac_copy(KAT[:, s0 * 128:(s0 + n) * 128], psA[:, pbk * 512: pbk * 512 + n * 128],
                              reads=[bankA(pbk)], writes=["KAT"], which=wh)
                for bs, lb in blocks:
                    pbk = next_pbank()
                    proj_tm(WIs, 1280, 1152, 128, bs, pbk, reads_extra=["WIs"])
                    eng = 'act' if acteng else 'dve'
                    P.op(eng, lambda e, bs=bs, pbk=pbk: (e.copy if acteng else e.tensor_copy)(
                        out=dap(VAW, bs * 132, [[VAW_pitch, 128], [66, 2], [1, 64]]),
                        in_=dap(psA, pbk * 512, [[2048, 128], [64, 2], [1, 64]])),
                        reads=[bankA(pbk), "VAW_all"], writes=[("VAW", bs)])
            return [p_qa, lambda: p_ga([0, 1]), lambda: p_ga([2, 3]), p_kv]

        def bg_drain(bg):
            while bg:
                fn, _ = bg.pop(0)
                fn()

        def q_phase(x_d, y_d, b, S_kv, NB, t, n_qt, is_sample, heads, final, gi, groups):
            G = len(heads)
            par = t % 2
            blocks = tile_blocks(t, n_qt, NB, is_sample, final)
            has_left = any(bs == 0 for bs, _ in blocks)
            has_right = any(bs == 5 for bs, _ in blocks)
            if t == 0:
                make_uT_list(x_d, blocks, b)
                for hl in range(G):
                    pbk = next_pbank()
                    proj_fm(WIg, 1024, hl * 128, 1, 4, pbk, reads_extra=["WIgq"])
                    evac_copy(QT[par][:, hl * 512:(hl + 1) * 512], psA[:, pbk * 512:(pbk + 1) * 512],
                              reads=[bankA(pbk)], writes=[("QT", par, hl)], scale=0.125)
                for qb in range(4):
                    pbk = next_pbank()
                    proj_tm(WIg, 1024, 768, G * 128, 1 + qb, pbk, reads_extra=["WIgq"])
                    evac_copy(SGb[par][:, qb * 256: qb * 256 + G * 128], psA[:, pbk * 512: pbk * 512 + G * 128],
                              reads=[bankA(pbk)], writes=[("SGb", par, qb)])

            if final:
                if t == 0:
                    for part in prepB_parts(blocks, acteng=False):
                        part()
                window_attn(t, n_qt, is_sample, blocks, has_left, has_right)

            bg = []
            if state.get('pending_post'):
                bg += state['pending_post']
                state['pending_post'] = None
            if t + 1 < n_qt:
                bg += prepA_steps(x_d, b, t + 1, n_qt, NB, is_sample, final, G)
            for hl, h in enumerate(heads):
                diff_attn(S_kv, NB, t, n_qt, is_sample, hl, h, G, bg)
            bg_drain(bg)
            post = post_diff_steps(t, heads, G, final, gi, groups, n_qt, fg=(final or t == n_qt - 1))
            if final or t == n_qt - 1:
                bg_drain(post)
            else:
                state['pending_post'] = post
            if dbg and final and (not is_sample) and t == 0:
                P.op('sp', lambda e: e.dma_start(out=dbg_ot.ap(), in_=OT[:]),
                     reads=[("OT", fc) for fc in range(6)], writes=["dbg1"], dma='dbg')
                P.op('sp', lambda e: e.dma_start(out=dbg_car.ap()[:, 0:512], in_=CAR[:, 0:512]),
                     reads=[("CAR", 0, 0)], writes=["dbg2"], dma='dbg')
                P.op('sp', lambda e: e.dma_start(out=dbg_car.ap()[:, 512:1024], in_=CAR[:, 4096:4608]),
                     reads=[("CAR", 1, 0)], writes=["dbg3"], dma='dbg')
                P.op('sp', lambda e: e.dma_start(out=dbg_c.ap()[:, 0:16], in_=s1p[:]), writes=["dbg4"], dma='dbg')
                P.op('sp', lambda e: e.dma_start(out=dbg_c.ap()[:, 16:32], in_=shf[:]), writes=["dbg5"], dma='dbg')
                P.op('sp', lambda e: e.dma_start(out=dbg_c.ap()[:, 32:33], in_=neglam[:], allow_slow_non_contiguous=True), writes=["dbg6"], dma='dbg')
                P.op('sp', lambda e: e.dma_start(out=dbg_c.ap()[:, 33:41], in_=expsink[:]), writes=["dbg7"], dma='dbg')
                P.op('sp', lambda e: e.dma_start(out=dbg_c.ap()[:, 41:49], in_=cfar[:]), writes=["dbg8"], dma='dbg')
                P.op('sp', lambda e: e.dma_start(out=dbg_sg.ap()[:, 0:2048], in_=SGa[:]),
                     reads=[("SGa", qb) for qb in range(4)], writes=["dbg9"], dma='dbg')
                P.op('sp', lambda e: e.dma_start(out=dbg_sg.ap()[:, 2048:3072], in_=SGb[0][:]),
                     reads=[("SGb", 0, qb) for qb in range(4)], writes=["dbg10"], dma='dbg')
            if final:
                nxt = None
                if t + 1 < n_qt:
                    nxt = prepB_parts(tile_blocks(t + 1, n_qt, NB, is_sample, final), acteng=True)
                finalize(x_d, y_d, b, t, n_qt, heads, gi, groups, nxt)

        VAW_pitch = 6 * 2 * 66

        def window_attn(t, n_qt, is_sample, blocks, has_left, has_right):
            present = {bs for bs, _ in blocks}
            wrapL = is_sample and t == 0
            wrapR = is_sample and t == n_qt - 1
            def win_chunk(c):
                par = c % 2
                if par == 0:
                    accU = [(psB, 0, bankB(0)), (psB, 512, bankB(1))]
                else:
                    accU = [(psB, 1024, bankB(2)), (psC, 0, "psC")]
                jbs = [jb for jb in range(6) if jb in present]
                first = [True, True]

                def geom(jb):
                    qlo = max(0, jb - 2)
                    qhi = min(3, jb)
                    nq = qhi - qlo + 1
                    return qlo, nq, nq * 128, (qlo - (jb - 2)) * 128

                def qk(jb, slot, c=c):
                    qlo, nq, N, so = geom(jb)
                    use_wrap = (wrapL and jb == 0) or (wrapR and jb == 5)

                    def f(e):
                        ins = None
                        for u in range(2):
                            e.matmul(out=psA[:, (slot * 2 + u) * 512: (slot * 2 + u) * 512 + N],
                                     lhsT=KAT[u * 64:(u + 1) * 64, jb * 128:(jb + 1) * 128],
                                     rhs=QAT[u * 64:(u + 1) * 64, c * 512 + qlo * 128: c * 512 + qlo * 128 + N],
                                     start=True, stop=False)
                        for u in range(2):
                            hp = 2 * c + u
                            if use_wrap:
                                wo = 0 if jb == 0 else 1024
                                rhs = BXW[:, wo + hp * 128: wo + hp * 128 + 128]
                            else:
                                rhs = BW[:, hp * 384 + so: hp * 384 + so + N]
                            ins = e.matmul(out=psA[:, (slot * 2 + u) * 512: (slot * 2 + u) * 512 + N],
                                           lhsT=identb[:], rhs=rhs, start=False, stop=True)
                        return ins
                    rd = ["KAT", ("QAT", c), identb.name]
                    rd += [] if use_wrap else [("BW", 2 * c), ("BW", 2 * c + 1)]
                    P.op('pe', f, reads=rd, writes=[bankA(slot * 2), bankA(slot * 2 + 1)])

                def ex(jb, slot):
                    qlo, nq, N, so = geom(jb)
                    P.op('act', lambda e: e.activation(
                        out=dap(PT[slot], 0, [[1024, 128], [384, 2], [1, N]]),
                        in_=dap(psA, slot * 1024, [[2048, 128], [512, 2], [1, N]]), func=AF.Exp),
                        reads=[bankA(slot * 2), bankA(slot * 2 + 1)], writes=[("PT", slot)])

                def pv(jb, slot):
                    qlo, nq, N, so = geom(jb)
                    sts = []
                    for u in range(2):
                        for qi in range(nq):
                            sts.append(first[u])
                            first[u] = False
                    sts = tuple(sts)

                    def f(e):
                        ins = None
                        k = 0
                        for u in range(2):
                            pt_, base, _ = accU[u]
                            for qi in range(nq):
                                qb = qlo + qi
                                off = base + qb * 66
                                ins = e.matmul(out=pt_[:, off: off + 65],
                                               lhsT=PT[slot][:, u * 384 + qi * 128: u * 384 + (qi + 1) * 128],
                                               rhs=VAW[:, (jb * 2 + u) * 66: (jb * 2 + u) * 66 + 65],
                                               start=sts[k], stop=True, skip_group_check=True)
                                k += 1
                        return ins
                    P.op('pe', f, reads=[("PT", slot), ("VAW", jb)], writes=[accU[0][2], accU[1][2]])

                qk(jbs[0], 0)
                for idx, jb in enumerate(jbs):
                    if idx + 1 < len(jbs):
                        qk(jbs[idx + 1], (idx + 1) % 2)
                    ex(jb, idx % 2)
                    pv(jb, idx % 2)
                for u in range(2):
                    pt_, base, bk = accU[u]
                    pitch = 1536 if pt_ is psB else 512
                    P.op('dve', lambda e, u=u, pt_=pt_, base=base, pitch=pitch, c=c, par=par: e.tensor_scalar(
                        out=den[:, par * 8 + u * 4: par * 8 + u * 4 + 4], in0=dap(pt_, base + 64, [[pitch, 128], [66, 4]]),
                        scalar1=expsink[:, 2 * c + u:2 * c + u + 1], scalar2=None, op0=ALU.add),
                        reads=[bk], writes=[("den", par, u)])
                P.op('dve', lambda e, par=par: e.reciprocal(out=rec[:, par * 8: par * 8 + 8], in_=den[:, par * 8: par * 8 + 8]),
                     reads=[("den", par, 0), ("den", par, 1)], writes=[("recw", par)])
                for u in range(2):
                    pt_, base, bk = accU[u]
                    for qb in range(4):
                        a = u * 4 + qb
                        off = base + qb * 66
                        hp = 2 * c + u
                        P.op('dve', lambda e, a=a, off=off, hp=hp, qb=qb, pt_=pt_, par=par: e.scalar_tensor_tensor(
                            out=OAg[:, qb * 512 + hp * 64: qb * 512 + hp * 64 + 64],
                            in0=pt_[:, off: off + 64], scalar=rec[:, par * 8 + a: par * 8 + a + 1],
                            in1=SGa[:, qb * 512 + hp * 64: qb * 512 + hp * 64 + 64],
                            op0=ALU.mult, op1=ALU.mult),
                            reads=[bk, ("recw", par), ("SGa", qb)], writes=[("OAg", qb, hp)])

            for c in range(4):
                win_chunk(c)
            for fc in range(4):
                def trw(e, fc=fc):
                    ins = None
                    for qb in range(4):
                        ins = e.transpose(out=psC_b[:, qb * 128:(qb + 1) * 128],
                                          in_=OAg[:, qb * 512 + fc * 128: qb * 512 + (fc + 1) * 128],
                                          identity=identb[:])
                    return ins
                P.op('pe', trw, reads=[("OAg", qb, hp) for qb in range(4) for hp in (2 * fc, 2 * fc + 1)] + [identb.name],
                     writes=["psC"])
                P.op('dve', lambda e, fc=fc: e.tensor_copy(out=OT[:, fc * 512:(fc + 1) * 512], in_=psC_b[:, 0:512]),
                     reads=["psC"], writes=[("OT", fc)])

        def diff_attn(S_kv, NB, t, n_qt, is_sample, hl, h, G, bg):
            par = t % 2
            kinds = []
            for i in range(NB):
                own = (not is_sample) or i < 16
                if own:
                    d = 4 * t - i
                    if -4 <= d <= 1:
                        kinds.append(('near', d))
                    elif d >= 2:
                        kinds.append(('far', cfar[:, h:h + 1], cfar.name))
                    else:
                        kinds.append(('far', cfar[:, 4 + h:5 + h], cfar.name))
                else:
                    if i == 16 and t == 3:
                        kinds.append(('wrapR',))
                    elif i == NB - 1 and t == 0:
                        kinds.append(('wrapL',))
                    else:
                        kinds.append(('far', farS[:, h * 64 + i: h * 64 + i + 1], ("farS", h)))
            if is_sample and t == 0:
                P.op('dve', lambda e: e.tensor_scalar(
                    out=BXf[:, 0:512], in0=BD[:, hl * MD + 640: hl * MD + 1152],
                    scalar1=cfar[:, 4 + h:5 + h], scalar2=flags[:, 0:1], op0=ALU.subtract, op1=ALU.mult),
                    reads=[("BD", hl), cfar.name, flags.name], writes=["BXf"])
                P.op('dve', lambda e: e.tensor_scalar(
                    out=BX[:, 0:512], in0=BXf[:, 0:512], scalar1=cfar[:, 4 + h:5 + h], scalar2=None, op0=ALU.add),
                    reads=["BXf"], writes=["BX"])
            if is_sample and t == 3:
                P.op('dve', lambda e: e.tensor_scalar(
                    out=BXf[:, 512:1024], in0=BD[:, hl * MD: hl * MD + 512],
                    scalar1=cfar[:, h:h + 1], scalar2=flags[:, 1:2], op0=ALU.subtract, op1=ALU.mult),
                    reads=[("BD", hl), cfar.name, flags.name], writes=["BXf"])
                P.op('dve', lambda e: e.tensor_scalar(
                    out=BX[:, 512:1024], in0=BXf[:, 512:1024], scalar1=cfar[:, h:h + 1], scalar2=None, op0=ALU.add),
                    reads=["BXf"], writes=["BX"])

            def qk(i):
                kind = kinds[i]
                slot = i % 2

                def f(e):
                    ins = None
                    near = kind[0] != 'far'
                    for m in range(2):
                        ins = e.matmul(out=psA[:, (slot * 2 + m) * 512:(slot * 2 + m + 1) * 512],
                                       lhsT=KT[m * 64:(m + 1) * 64, hl * S_kv + i * 128: hl * S_kv + (i + 1) * 128],
                                       rhs=QT[par][m * 64:(m + 1) * 64, hl * 512:(hl + 1) * 512],
                                       start=True, stop=not near)
                    if near:
                        if kind[0] == 'near':
                            o = hl * MD + 128 * (kind[1] + 4)
                            rhs = BD[:, o:o + 512]
                        elif kind[0] == 'wrapL':
                            rhs = BX[:, 0:512]
                        else:
                            rhs = BX[:, 512:1024]
                        for m in range(2):
                            ins = e.matmul(out=psA[:, (slot * 2 + m) * 512:(slot * 2 + m + 1) * 512],
                                           lhsT=identb[:], rhs=rhs, start=False, stop=True)
                    return ins
                rd = [("KT", hl, i), ("QT", par, hl)]
                if kind[0] == 'near':
                    rd += [("BD", hl), identb.name]
                elif kind[0] != 'far':
                    rd += ["BX", identb.name]
                P.op('pe', f, reads=rd, writes=[bankA(slot * 2), bankA(slot * 2 + 1)])

            def ex(i):
                kind = kinds[i]
                slot = i % 2
                if kind[0] == 'far':
                    bias_ap, bkey = kind[1], kind[2]
                else:
                    bias_ap, bkey = zero_c[:], zero_c.name
                P.op('act', lambda e: e.activation(out=PT[slot][:], in_=psA[:, slot * 1024:(slot + 1) * 1024],
                                                   func=AF.Exp, bias=bias_ap, scale=1.0),
                     reads=[bankA(slot * 2), bankA(slot * 2 + 1), bkey], writes=[("PT", slot)])

            def pv(i):
                slot = i % 2

                def f(e):
                    ins = None
                    for m in range(2):
                        for qb in range(4):
                            a = m * 4 + qb
                            bank = a // 3
                            off = bank * 512 + (a % 3) * 130
                            vo = (i * G + hl) * 130
                            ins = e.matmul(out=psB[:, off:off + 129],
                                           lhsT=PT[slot][:, m * 512 + qb * 128: m * 512 + (qb + 1) * 128],
                                           rhs=VA[:, vo:vo + 129],
                                           start=(i == 0 and a % 3 == 0), stop=(i == NB - 1), skip_group_check=True)
                    return ins
                P.op('pe', f, reads=[("PT", slot), ("VA", hl, i)], writes=[bankB(0), bankB(1), bankB(2)])

            qk(0)
            qk(1)
            cool = 2
            for i in range(NB):
                ex(i)
                if i + 2 < NB:
                    qk(i + 2)
                pv(i)
                if bg:
                    if cool <= 0:
                        fn, cool = bg.pop(0)
                        fn()
                    else:
                        cool -= 1
            for bank in range(3):
                na = 3 if bank < 2 else 2
                P.op('dve', lambda e, bank=bank, na=na: e.reciprocal(
                    out=rec[:, bank * 3: bank * 3 + na], in_=dap(psB, bank * 512 + 128, [[1536, 128], [130, na]])),
                    reads=[bankB(bank)], writes=[("recd", bank)])
            P.op('dve', lambda e: e.tensor_scalar(out=rec[:, 8:12], in0=rec[:, 4:8], scalar1=neglam[:, 0:1], scalar2=None, op0=ALU.mult),
                 reads=[("recd", 1), ("recd", 2), neglam.name], writes=["recn"])
            for qb in range(4):
                a0 = qb
                o0 = (a0 // 3) * 512 + (a0 % 3) * 130
                oo = (qb * 2 + hl) * 128
                P.op('dve', lambda e, a0=a0, o0=o0, oo=oo: e.tensor_scalar(
                    out=OBf[:, oo:oo + 128], in0=psB[:, o0:o0 + 128], scalar1=rec[:, a0:a0 + 1], scalar2=None, op0=ALU.mult),
                    reads=[bankB(a0 // 3), ("recd", a0 // 3)], writes=[("OBf", qb, hl)])
            for qb in range(4):
                a1 = 4 + qb
                o1 = (a1 // 3) * 512 + (a1 % 3) * 130
                P.op('dve', lambda e, qb=qb, o1=o1: e.tensor_scalar(
                    out=BXf[:, qb * 128:(qb + 1) * 128], in0=psB[:, o1:o1 + 128], scalar1=rec[:, 8 + qb:9 + qb], scalar2=None, op0=ALU.mult),
                    reads=[bankB(a1 // 3), "recn"], writes=["BXf"])
            for qb in range(4):
                oo = (qb * 2 + hl) * 128
                P.op('dve', lambda e, qb=qb, oo=oo: e.tensor_tensor(
                    out=OBf[:, oo:oo + 128], in0=OBf[:, oo:oo + 128], in1=BXf[:, qb * 128:(qb + 1) * 128], op=ALU.add),
                    reads=[("OBf", qb, hl), "BXf"], writes=[("OBf", qb, hl)])
                P.op('dve', lambda e, qb=qb, oo=oo: e.tensor_tensor(
                    out=sqj[:], in0=OBf[:, oo:oo + 128], in1=OBf[:, oo:oo + 128], op=ALU.mult),
                    reads=[("OBf", qb, hl)], writes=["sqj"])
                P.op('dve', lambda e, qb=qb: e.reduce_sum(out=ssq[:, qb * 2 + hl: qb * 2 + hl + 1], in_=sqj[:], axis=mybir.AxisListType.X),
                     reads=["sqj"], writes=[("ssq", qb, hl)])

        def post_diff_steps(t, heads, G, final, gi, groups, n_qt, fg=False):
            S_q = n_qt * 512
            par = t % 2
            steps = []
            sgk = [("SGb", par, qb) for qb in range(4)]

            def s1():
                P.op('act', lambda e: e.activation(out=BXf[:], in_=SGb[par][:], func=AF.Exp, scale=-1.0),
                     reads=sgk, writes=["BXf"])
            def s1f():
                P.op('act', lambda e: e.activation(out=SGb[par][:], in_=SGb[par][:], func=AF.Silu), reads=sgk, writes=sgk)
            if fg:
                steps.append((s1f, 0))
            else:
                steps.append((s1, 1))

            def s2():
                P.op('dve', lambda e: e.tensor_scalar(out=BXf[:], in0=BXf[:], scalar1=1.0, scalar2=None, op0=ALU.add),
                     reads=["BXf"], writes=["BXf"])
                P.op('dve', lambda e: e.reciprocal(out=BXf[:], in_=BXf[:]), reads=["BXf"], writes=["BXf"])
                P.op('dve', lambda e: e.tensor_tensor(out=SGb[par][:], in0=SGb[par][:], in1=BXf[:], op=ALU.mult),
                     reads=sgk + ["BXf"], writes=sgk)
            if not fg:
                steps.append((s2, 2))

            def s3():
                P.op('act', lambda e: e.activation(out=rstd[:, 0:8], in_=ssq[:, 0:8], func=AF.Sqrt, bias=eps_sub[:], scale=1.0 / 128.0),
                     reads=[("ssq", qb, hl) for qb in range(4) for hl in range(G)] + [eps_sub.name], writes=["rstd_a"])
                P.op('dve', lambda e: e.reciprocal(out=rstd[:, 0:8], in_=rstd[:, 0:8]), reads=["rstd_a"], writes=["rstd"])
            steps.append((s3, 2))
            for qb in range(4):
                def s4(qb=qb):
                    for hl in range(G):
                        oo = (qb * 2 + hl) * 128
                        P.op('dve', lambda e, hl=hl, oo=oo: e.scalar_tensor_tensor(
                            out=OBf[:, oo:oo + 128], in0=OBf[:, oo:oo + 128], scalar=rstd[:, qb * 2 + hl: qb * 2 + hl + 1],
                            in1=subw_bc[:], op0=ALU.mult, op1=ALU.mult),
                            reads=[("OBf", qb, hl), "rstd", subw_bc.name], writes=[("OBf", qb, hl)])
                        P.op('pool', lambda e, hl=hl, oo=oo: e.tensor_tensor(
                            out=OBn[:, oo:oo + 128], in0=OBf[:, oo:oo + 128],
                            in1=SGb[par][:, qb * 256 + hl * 128: qb * 256 + (hl + 1) * 128], op=ALU.mult),
                            reads=[("OBf", qb, hl), ("SGb", par, qb)], writes=[("OBn", qb, hl)])
                steps.append((s4, 1))
            for hl, h in enumerate(heads):
                def s5(hl=hl):
                    def trd(e):
                        ins = None
                        for qb in range(4):
                            oo = (qb * 2 + hl) * 128
                            ins = e.transpose(out=psC_b[:, qb * 128:(qb + 1) * 128], in_=OBn[:, oo:oo + 128], identity=identb[:])
                        return ins
                    P.op('pe', trd, reads=[("OBn", qb, hl) for qb in range(4)] + [identb.name], writes=["psC"])

                def s6(hl=hl):
                    if final:
                        P.op('dve', lambda e: e.tensor_copy(out=OT[:, (4 + hl) * 512:(5 + hl) * 512], in_=psC_b[:, 0:512]),
                             reads=["psC"], writes=[("OT", 4 + hl)])
                    else:
                        ci = sum(len(g) for g in groups[:gi]) + hl
                        o = ci * S_q + t * 512
                        P.op('dve', lambda e: e.tensor_copy(out=CAR[:, o:o + 512], in_=psC_b[:, 0:512]),
                             reads=["psC"], writes=[("CAR", ci, t)])
                steps.append((s5, 1))
                steps.append((s6, 1))
            return steps

        def finalize(x_d, y_d, b, t, n_qt, heads, gi, groups, nxt_parts=None):
            S_q = n_qt * 512
            n_car = sum(len(g) for g in groups[:gi])
            G = len(heads)
            fslots = [0, 1]

            def xload(qb):
                xs_ = fslots[qb % 2]
                row0 = (4 * t + qb) * 128
                P.op('sp', lambda e: e.dma_start(out=FB[xs_][:], in_=x_d.ap()[row0:row0 + 128, :]),
                     writes=[("FB", xs_)], dma=xkeys[xs_])

            xload(0)
            xload(1)
            for qb in range(4):
                fin_block(x_d, y_d, b, t, qb, S_q, n_car, G, fslots[qb % 2])
                if qb + 2 < 4:
                    xload(qb + 2)
                if nxt_parts:
                    nxt_parts.pop(0)()

        def fin_block(x_d, y_d, b, t, qb, S_q, n_car, G, xs_):
            row0 = (4 * t + qb) * 128
            ys_ = state['yslot'] % 2
            state['yslot'] += 1
            for n in range(2):
                def om(e, n=n):
                    ins = None
                    for fc in range(8):
                        if fc < 4:
                            lhsT = OT[:, fc * 512 + qb * 128: fc * 512 + (qb + 1) * 128]
                        elif fc - 4 < n_car:
                            o = (fc - 4) * S_q + t * 512 + qb * 128
                            lhsT = CAR[:, o:o + 128]
                        else:
                            o = (4 + fc - 4 - n_car) * 512 + qb * 128
                            lhsT = OT[:, o:o + 128]
                        ins = e.matmul(out=psB[:, n * 512:(n + 1) * 512], lhsT=lhsT,
                                       rhs=WO[:, fc * 1024 + n * 512: fc * 1024 + (n + 1) * 512],
                                       start=(fc == 0), stop=(fc == 7))
                    return ins
                rd = [("OT", fc) for fc in range(4 + G)] + [("CAR", ci, t) for ci in range(n_car)] + ["WO"]
                P.op('pe', om, reads=rd, writes=[bankB(n)])
            Z = FB[xs_]
            zk = ("FB", xs_)
            mo = qb * 2
            P.op('dve', lambda e: e.tensor_tensor(out=BXf[:], in0=psB[:, 0:1024], in1=gate_bc[:, b * 1024:(b + 1) * 1024], op=ALU.mult),
                 reads=[bankB(0), bankB(1), ("gate_bc", b, 0), ("gate_bc", b, 1)], writes=["BXf"])
            P.op('dve', lambda e: e.scalar_tensor_tensor(out=Z[:], in0=Z[:], scalar=ALPHA, in1=BXf[:], op0=ALU.mult, op1=ALU.add),
                 reads=[zk, "BXf"], writes=[zk])
            P.op('dve', lambda e: e.bn_stats(out=bnst[:, 0:6], in_=Z[:, 0:512]), reads=[zk], writes=["bnst0"])
            P.op('dve', lambda e: e.bn_stats(out=bnst[:, 6:12], in_=Z[:, 512:1024]), reads=[zk], writes=["bnst1"])
            P.op('dve', lambda e: e.bn_aggr(out=mv[:, mo:mo + 2], in_=bnst[:]), reads=["bnst0", "bnst1"], writes=[("mv", qb)])
            P.op('act', lambda e: e.activation(out=lrs[:, qb:qb + 1], in_=mv[:, mo + 1:mo + 2], func=AF.Sqrt, bias=eps_ln[:], scale=1.0),
                 reads=[("mv", qb), eps_ln.name], writes=[("lrs_a", qb)])
            P.op('dve', lambda e: e.reciprocal(out=lrs[:, qb:qb + 1], in_=lrs[:, qb:qb + 1]), reads=[("lrs_a", qb)], writes=[("lrs", qb)])
            P.op('dve', lambda e: e.scalar_tensor_tensor(out=Z[:], in0=Z[:], scalar=mv[:, mo:mo + 1], in1=lng_bc[:],
                                                         op0=ALU.subtract, op1=ALU.mult),
                 reads=[zk, ("mv", qb)], writes=[zk])
            P.op('dve', lambda e: e.scalar_tensor_tensor(out=Z[:], in0=Z[:], scalar=lrs[:, qb:qb + 1], in1=lnb_bc[:],
                                                         op0=ALU.mult, op1=ALU.add),
                 reads=[zk, ("lrs", qb)], writes=[zk])
            P.op('sp', lambda e: e.dma_start(out=y_d.ap()[row0:row0 + 128, :], in_=Z[:]),
                 reads=[zk], writes=[("yout", ys_)], dma=['y0', 'y1'][ys_])

        try:
            if kstop == 1:
                raise _Stop()
            run_job(xp_d, yp_d, 0, SP_, 8, False, [[0, 1], [2, 3]], [0])
            run_job(xs_d, ys_d, 1, SS_, 4, True, [[0], [1], [2], [3]], None)
        except _Stop:
            pass
        P.op('sp', lambda e: None, reads=[("yout", 0), ("yout", 1)] + (["dbg%d" % i for i in range(1, 11)] if dbg else []))

        with nc.Block() as blk2:
            P.emit(blk2)
    return nc


_NC_CACHE = {}


def _bucket_table():
    import math
    import jax
    import jax.numpy as jnp
    try:
        dev = jax.devices('cpu')[0]
    except Exception:
        dev = None
    def run():
        rel = jnp.arange(-1400, 1401, dtype=jnp.int32)
        sign = jnp.where(rel > 0, 16, 0)
        n = jnp.abs(rel)
        nf = jnp.maximum(n, 1).astype(jnp.float32)
        large = 8 + (jnp.log(nf / 8) / math.log(128 / 8) * 8).astype(jnp.int32)
        large = jnp.minimum(large, 15)
        return np.asarray((sign + jnp.where(n < 8, n, large)).astype(jnp.int32))
    if dev is not None:
        with jax.default_device(dev):
            return run()
    return run()


def kernel(**inp):
    f = lambda k: np.ascontiguousarray(np.asarray(inp[k], dtype=np.float32))
    x_prompt = f('x_prompt'); x_sample = f('x_sample')
    c_prompt = f('c_prompt'); c_sample = f('c_sample')
    w_in = f('w_in')[0]; w_out = f('w_out')[0]; w_ada = f('w_ada')[0]; b_ada = f('b_ada')[0]
    ln_g = f('ln_g'); ln_b = f('ln_b'); sink = f('attn_sink')[0]
    rel_bias = f('rel_bias')
    subw = f('subln_w')
    lamv = np.concatenate([f('lambda_q1')[0], f('lambda_k1')[0], f('lambda_q2')[0], f('lambda_k2')[0]])[None, :]

    colperm = np.concatenate([np.arange(64) + 64 * h for h in PERM])
    w_in_l = w_in.copy()
    w_in_l[:, 0:512] = w_in[:, 0:512][:, colperm]
    w_in_l[:, 768:1280] = w_in[:, 768:1280][:, colperm]
    w_out_l = w_out.copy()
    w_out_l[0:512, :] = w_out[0:512, :][colperm, :]
    sinkp = sink[PERM][None, :]
    bt = _bucket_table()

    def bk(r):
        return bt[r + 1400]
    kk = np.arange(128)[:, None]
    mm = np.arange(MD)[None, :]
    bd_idx = bk(kk - mm + 512)
    bd_strip = np.stack([rel_bias[bd_idx, 8 + h] for h in range(4)], axis=1).reshape(128, 4 * MD)
    qq = np.arange(384)[None, :]
    relw = kk - qq + 128
    bw_idx = bk(relw)
    bw_strip = np.stack([rel_bias[bw_idx, PERM[hp]] for hp in range(8)], axis=1).reshape(128, 8 * 384)
    maskw = np.where(np.abs(relw) <= 128, 0.0, -BIG).astype(np.float32)
    relfar = np.concatenate([rel_bias[15, 8:12], rel_bias[31, 8:12]])[None, :]
    identf = np.eye(128, dtype=np.float32)

    import os
    dbg = bool(os.environ.get('KDBG'))
    kstop = int(os.environ.get('KSTOP', '0'))
    key = 'nc%d_%d' % (dbg, kstop)
    if key not in _NC_CACHE:
        _NC_CACHE[key] = build_program(dbg, kstop)
    nc = _NC_CACHE[key]

    in_maps = []
    for c in range(NCORES):
        s, j = c // 4, c % 4
        xs = np.ascontiguousarray(np.roll(x_sample[s], -SQS * j, axis=0))
        cc = np.stack([c_prompt[c], c_sample[s]], axis=0)
        cT = np.ascontiguousarray(cc.reshape(2, 8, 128).transpose(2, 1, 0).reshape(128, 16))
        fl = np.zeros((1, 66), np.float32)
        fl[0, 0] = 1.0 if j > 0 else 0.0
        fl[0, 1] = 1.0 if j < 3 else 0.0
        for i in range(64):
            fl[0, 2 + i] = 1.0 if i >= 64 - 16 * j else 0.0
        in_maps.append({
            "xp": x_prompt[c], "xs": xs, "cT": cT,
            "w_in": w_in_l, "w_out": w_out_l, "w_ada": w_ada,
            "bcol": np.ascontiguousarray(b_ada.reshape(24, 128).T),
            "bgate": np.ascontiguousarray(b_ada[None, 2048:3072]),
            "ln_g": ln_g, "ln_b": ln_b, "sinkp": np.ascontiguousarray(sinkp),
            "lamv": np.ascontiguousarray(lamv), "subw": subw,
            "relfar": np.ascontiguousarray(relfar),
            "bd_strip": np.ascontiguousarray(bd_strip.astype(np.float32)),
            "bw_strip": np.ascontiguousarray(bw_strip.astype(np.float32)),
            "maskw": maskw, "flags": fl, "identf": identf,
        })
    res = run_bass_kernel_spmd(nc, in_maps, core_ids=list(range(NCORES)))
    y_prompt = np.empty((8, SP_, D), np.float32)
    y_sample = np.empty((2, SS_, D), np.float32)
    for c in range(NCORES):
        r = res.results[c]
        s, j = c // 4, c % 4
        y_prompt[c] = np.asarray(r["yp"], np.float32)
        y_sample[s, SQS * j: SQS * (j + 1)] = np.asarray(r["ys"], np.float32)
    return (y_prompt, y_sample)
```

```python
import numpy as np
import concourse.bass as bass
import concourse.mybir as mybir
from concourse.bass_utils import run_bass_kernel_spmd

F32 = mybir.dt.float32
BF16 = mybir.dt.bfloat16
AF = mybir.ActivationFunctionType
ALU = mybir.AluOpType

NCORES = 8
D = 1024
DC = 8
SP_ = 4096
SS_ = 8192
SQS = 2048
ALPHA = 2.0 ** 0.25
LAMBDA_INIT = 0.2
BIG = 30000.0
LN_EPS = 1e-5
SUBLN_EPS = 1e-5
PERM = [0, 4, 1, 5, 2, 6, 3, 7]
MD = 1152


class Prog:
    ENG = ('sp', 'act', 'pe', 'dve', 'pool')

    def __init__(self, sems, dsems):
        self.sem = sems
        self.dsem = dsems
        self.dcnt = {k: 0 for k in dsems}
        self.base = {e: 0 for e in self.ENG}
        self.reset_block()

    def reset_block(self):
        self.ops = {e: [] for e in self.ENG}
        self.res = {}

    def op(self, eng, fn, reads=(), writes=(), dma=None):
        ops = self.ops[eng]
        idx = len(ops)
        if dma is not None:
            self.dcnt[dma] += 1
            ev = ('dma', dma, self.dcnt[dma])
        else:
            ev = (eng, idx)
        deps = {}

        def add(d, raw):
            if d[0] == 'dma':
                d = ('dma', d[1], self.dcnt[d[1]])
            if raw or d not in deps:
                deps[d] = raw or deps.get(d, False)

        def is_ps(k):
            return isinstance(k, str) and k.startswith('ps')

        ps_keys = {}
        for k in reads:
            if is_ps(k):
                ps_keys[k] = False
        for k in writes:
            if is_ps(k):
                ps_keys[k] = True
        for k, isw in ps_keys.items():
            r = self.res.get(k)
            if r is not None and r[0] is not None:
                add(r[0], (not isw) and r[2])
        for k in reads:
            if is_ps(k):
                continue
            r = self.res.get(k)
            if r is not None and r[0] is not None:
                add(r[0], True)
        for k in writes:
            if is_ps(k):
                continue
            r = self.res.get(k)
            if r is not None:
                if r[0] is not None:
                    add(r[0], False)
                for rv in r[1].values():
                    add(rv, False)
        for k in reads:
            if is_ps(k):
                continue
            r = self.res.setdefault(k, [None, {}])
            r[1][ev if dma is not None else eng] = ev
        for k in writes:
            if is_ps(k):
                continue
            self.res[k] = [ev, {}]
        for k, isw in ps_keys.items():
            self.res[k] = [ev, {}, isw]
        ops.append((fn, deps, ev, dma))
        return ev

    def emit(self, block):
        need = {e: set() for e in self.ENG}
        for e in self.ENG:
            for (fn, deps, ev, dma) in self.ops[e]:
                for d in deps:
                    if d[0] != 'dma':
                        need[d[0]].add(d[1])
        ordn = {}
        for e in self.ENG:
            s = sorted(need[e])
            ordn[e] = {idx: self.base[e] + i + 1 for i, idx in enumerate(s)}

        def body(en):
            def f(eng):
                seen = {}
                for idx, (fn, deps, ev, dma) in enumerate(self.ops[en]):
                    for d, raw in deps.items():
                        if d[0] == 'dma':
                            if dma is not None and d[1] == dma:
                                continue
                            key = ('dma', d[1])
                            val = 16 * d[2]
                            sem = self.dsem[d[1]]
                        else:
                            if d[0] == en:
                                if (not raw) or en == 'pe' or en == 'sp':
                                    continue
                            key = d[0]
                            val = ordn[d[0]][d[1]]
                            sem = self.sem[d[0]]
                        if seen.get(key, 0) >= val:
                            continue
                        eng.wait_ge(sem, val)
                        seen[key] = val
                    ins = fn(eng)
                    if dma is not None:
                        ins.then_inc(self.dsem[dma], 16)
                    elif idx in ordn[en]:
                        ins.then_inc(self.sem[en], 1)
            return f

        block.sync(body('sp'))
        block.scalar(body('act'))
        block.tensor(body('pe'))
        block.vector(body('dve'))
        block.gpsimd(body('pool'))
        for e in self.ENG:
            self.base[e] += len(ordn[e])
        self.reset_block()


class _CopyShim:
    def __init__(self, e):
        self.e = e

    def tensor_copy(self, out, in_):
        return self.e.copy(out=out, in_=in_)


def dap(t, off, dims):
    return bass.AP(t, off, [list(d) for d in dims])


class _Stop(Exception):
    pass


def build_program(dbg=False, kstop=0):
    nc = bass.Bass("TRN2", target_bir_lowering=False)
    if dbg:
        dbg_ot = nc.dram_tensor("dbg_ot", [128, 3072], BF16, kind="ExternalOutput")
        dbg_car = nc.dram_tensor("dbg_car", [128, 1024], BF16, kind="ExternalOutput")
        dbg_c = nc.dram_tensor("dbg_c", [128, 64], F32, kind="ExternalOutput")
        dbg_sg = nc.dram_tensor("dbg_sg", [128, 2048 + 1024], BF16, kind="ExternalOutput")

    def din(name, shape):
        return nc.dram_tensor(name, list(shape), F32, kind="ExternalInput")

    xp_d = din("xp", [SP_, D])
    xs_d = din("xs", [SS_, D])
    cT_d = din("cT", [128, 16])
    win_d = din("w_in", [D, 3328])
    wout_d = din("w_out", [D, D])
    wada_d = din("w_ada", [D, 3072])
    bcol_d = din("bcol", [128, 24])
    bgate_d = din("bgate", [1, 1024])
    lng_d = din("ln_g", [1, 1024])
    lnb_d = din("ln_b", [1, 1024])
    sink_d = din("sinkp", [1, 8])
    lamv_d = din("lamv", [1, 256])
    subw_d = din("subw", [1, 128])
    relfar_d = din("relfar", [1, 8])
    bd_d = din("bd_strip", [128, 4 * MD])
    bw_d = din("bw_strip", [128, 8 * 384])
    maskw_d = din("maskw", [128, 384])
    flags_d = din("flags", [1, 66])
    ident_d = din("identf", [128, 128])
    yp_d = nc.dram_tensor("yp", [SP_, D], F32, kind="ExternalOutput")
    ys_d = nc.dram_tensor("ys", [SQS, D], F32, kind="ExternalOutput")

    from contextlib import ExitStack
    es = ExitStack()

    def sb(name, shape, dt=F32):
        return es.enter_context(nc.sbuf_tensor("sb_" + name, list(shape), dt))

    def ps(name, shape, dt=F32):
        return es.enter_context(nc.psum_tensor("ps_" + name, list(shape), dt))

    with es:
        sems = {e: es.enter_context(nc.semaphore("s_" + e)) for e in ('act', 'pe', 'dve', 'pool')}
        dkeys = ['c0', 'wa', 'x0', 'x1', 'xb0', 'xb1', 'xb2', 'xb3', 'xb4', 'y0', 'y1', 'dbg']
        dsems = {k: es.enter_context(nc.semaphore("d_" + k)) for k in dkeys}
        P = Prog(sems, dsems)

        identf = sb("identf", [128, 128])
        identb = sb("identb", [128, 128], BF16)
        s1p = sb("s1p", [128, 16])
        shf = sb("shf", [128, 16])
        gate_bc = sb("gate_bc", [128, 2048])
        lng_bc = sb("lng_bc", [128, 1024])
        lnb_bc = sb("lnb_bc", [128, 1024])
        subw_bc = sb("subw_bc", [128, 128])
        neglam = sb("neglam", [128, 1])
        expsink = sb("expsink", [128, 8])
        cfar = sb("cfar", [128, 8])
        cdiff = sb("cdiff", [128, 4])
        farS = sb("farS", [128, 4 * 64])
        flags = sb("flags", [128, 66])
        negLR = sb("negLR", [128, 2])
        omf = sb("omf", [128, 2])
        BW = sb("BW", [128, 8 * 384], BF16)
        BXW = sb("BXW", [128, 2048], BF16)
        zero_c = sb("zero_c", [128, 1])
        eps_ln = sb("eps_ln", [128, 1])
        eps_sub = sb("eps_sub", [128, 1])

        psA = ps("psA", [128, 2048])
        psB = ps("psB", [128, 1536])
        psC = ps("psC", [128, 512])
        psA_b = psA.bitcast(BF16)
        psB_b = psB.bitcast(BF16)
        psC_b = psC.bitcast(BF16)

        def bankA(i):
            return "psA%d" % i

        def bankB(i):
            return "psB%d" % i

        WIs = sb("WIs", [128, 8 * 1280], BF16)
        WO = sb("WO", [128, 8 * 1024], BF16)
        FB = [sb("FB%d" % i, [128, 1024]) for i in range(2)]
        xkeys = ['x0', 'x1']
        xbkeys = ['xb0', 'xb1', 'xb2', 'xb3', 'xb4']
        NPF = 3
        state = {'xslot': 0, 'evac': 0, 'yslot': 0}

        wl_state = {'i': 0}

        def stage_cast(src_ap, ncols, cast_fn, writes, reads_extra=(), dst_dims=None, engs=('pool', 'dve'), q='sp'):
            s_ = state['xslot'] % 2
            state['xslot'] += 1
            P.op(q, lambda e, s_=s_: e.dma_start(
                out=(FB[s_][:, 0:ncols] if dst_dims is None else dap(FB[s_], 0, dst_dims)), in_=src_ap),
                 writes=[("FB", s_)], dma=xkeys[s_])
            eng = engs[wl_state['i'] % 2]
            wl_state['i'] += 1
            P.op(eng, lambda e, s_=s_: cast_fn(_CopyShim(e) if eng == 'act' else e, FB[s_]), reads=[("FB", s_)] + list(reads_extra), writes=writes)

        def load_static_weights():
            for dc in range(8):
                r0 = dc * 128
                def castA(e, fb, dc=dc):
                    e.tensor_copy(out=WIs[:, dc * 1280: dc * 1280 + 512], in_=fb[:, 0:512])
                    return e.tensor_copy(out=WIs[:, dc * 1280 + 1024: dc * 1280 + 1280], in_=fb[:, 512:768])
                stage_cast(win_d.ap()[r0:r0 + 128, 0:768], 768, castA, writes=["WIs"], q='act')
                def castB(e, fb, dc=dc):
                    return e.tensor_copy(out=WIs[:, dc * 1280 + 512: dc * 1280 + 1024], in_=fb[:, 0:512])
                stage_cast(win_d.ap()[r0:r0 + 128, 768:1280], 512, castB, writes=["WIs"], q='act')
                def castO(e, fb, dc=dc):
                    return e.tensor_copy(out=WO[:, dc * 1024:(dc + 1) * 1024], in_=fb[:, 0:1024])
                stage_cast(wout_d.ap()[r0:r0 + 128, :], 1024, castO, writes=["WO"], q='act')

        with ExitStack() as es1:
            def sb1(name, shape, dt=F32):
                return es1.enter_context(nc.sbuf_tensor("sb_" + name, list(shape), dt))

            WA = sb1("WA", [128, 4 * 3072])
            cT = sb1("cT", [128, 16])
            sc = sb1("sc", [128, 16])
            sc_rep = sb1("sc_rep", [128, 2 * 8 * 128])
            bcol = sb1("bcol", [128, 24])
            bgate_bc = sb1("bgate_bc", [128, 1024])
            lamv = sb1("lamv", [128, 256])
            lamp = sb1("lamp", [128, 128])
            lams = sb1("lams", [128, 4])
            sinkb = sb1("sinkb", [128, 8])
            subw_raw = sb1("subw_raw", [128, 128])
            maskw = sb1("maskw", [128, 384])
            BWf = sb1("BWf", [128, 8 * 384])

            def ld(dst_t, src):
                P.op('sp', lambda e: e.dma_start(out=dst_t[:], in_=src), writes=[dst_t.name], dma='c0')

            ld(identf, ident_d.ap())
            ld(cT, cT_d.ap())
            ld(bcol, bcol_d.ap())
            ld(bgate_bc, dap(bgate_d, 0, [[0, 128], [1, 1024]]))
            ld(lng_bc, dap(lng_d, 0, [[0, 128], [1, 1024]]))
            ld(lnb_bc, dap(lnb_d, 0, [[0, 128], [1, 1024]]))
            ld(sinkb, dap(sink_d, 0, [[0, 128], [1, 8]]))
            ld(lamv, dap(lamv_d, 0, [[0, 128], [1, 256]]))
            ld(subw_raw, dap(subw_d, 0, [[0, 128], [1, 128]]))
            ld(cfar, dap(relfar_d, 0, [[0, 128], [1, 8]]))
            ld(flags, dap(flags_d, 0, [[0, 128], [1, 66]]))
            ld(BWf, bw_d.ap())
            ld(maskw, maskw_d.ap())
            def load_wa(dcs):
                for dc in dcs:
                    P.op('sp', lambda e, dc=dc: e.dma_start(
                        out=WA[:, (dc % 4) * 3072:(dc % 4 + 1) * 3072], in_=wada_d.ap()[dc * 128:(dc + 1) * 128, :]),
                        writes=[("WA", dc % 4)], dma='wa')
            load_wa(range(0, 4))

            P.op('pool', lambda e: e.memset(zero_c[:], 0.0), writes=[zero_c.name])
            P.op('pool', lambda e: e.memset(eps_ln[:], LN_EPS), writes=[eps_ln.name])
            P.op('pool', lambda e: e.memset(eps_sub[:], SUBLN_EPS), writes=[eps_sub.name])
            P.op('dve', lambda e: e.tensor_copy(out=identb[:], in_=identf[:]),
                 reads=[identf.name], writes=[identb.name])
            P.op('act', lambda e: e.activation(out=sc[:], in_=cT[:], func=AF.Silu),
                 reads=[cT.name], writes=[sc.name])
            P.op('act', lambda e: e.activation(out=expsink[:], in_=sinkb[:], func=AF.Exp),
                 reads=[sinkb.name], writes=[expsink.name])
            P.op('dve', lambda e: e.tensor_tensor(out=lamp[:, 0:64], in0=lamv[:, 0:64], in1=lamv[:, 64:128], op=ALU.mult),
                 reads=[lamv.name], writes=["lamp0"])
            P.op('dve', lambda e: e.tensor_tensor(out=lamp[:, 64:128], in0=lamv[:, 128:192], in1=lamv[:, 192:256], op=ALU.mult),
                 reads=[lamv.name], writes=["lamp1"])
            P.op('dve', lambda e: e.reduce_sum(out=lams[:, 0:1], in_=lamp[:, 0:64], axis=mybir.AxisListType.X),
                 reads=["lamp0"], writes=["lams0"])
            P.op('dve', lambda e: e.reduce_sum(out=lams[:, 1:2], in_=lamp[:, 64:128], axis=mybir.AxisListType.X),
                 reads=["lamp1"], writes=["lams1"])
            P.op('act', lambda e: e.activation(out=lams[:, 2:4], in_=lams[:, 0:2], func=AF.Exp),
                 reads=["lams0", "lams1"], writes=["lams23"])
            P.op('dve', lambda e: e.tensor_tensor(out=neglam[:], in0=lams[:, 3:4], in1=lams[:, 2:3], op=ALU.subtract),
                 reads=["lams23"], writes=["neglam_a"])
            P.op('dve', lambda e: e.tensor_scalar(out=neglam[:], in0=neglam[:], scalar1=-LAMBDA_INIT, scalar2=None, op0=ALU.add),
                 reads=["neglam_a"], writes=[neglam.name])
            P.op('dve', lambda e: e.tensor_scalar(out=subw_bc[:], in0=subw_raw[:], scalar1=1.0 - LAMBDA_INIT, scalar2=None, op0=ALU.mult),
                 reads=[subw_raw.name], writes=[subw_bc.name])
            P.op('dve', lambda e: e.tensor_tensor(out=cdiff[:], in0=cfar[:, 0:4], in1=cfar[:, 4:8], op=ALU.subtract),
                 reads=[cfar.name], writes=[cdiff.name])
            for h in range(4):
                P.op('dve', lambda e, h=h: e.tensor_scalar(
                    out=farS[:, h * 64:(h + 1) * 64], in0=flags[:, 2:66],
                    scalar1=cdiff[:, h:h + 1], scalar2=cfar[:, 4 + h:5 + h], op0=ALU.mult, op1=ALU.add),
                    reads=[flags.name, cdiff.name, cfar.name], writes=[("farS", h)])
            P.op('dve', lambda e: e.tensor_scalar(out=negLR[:], in0=flags[:, 0:2], scalar1=-1.0, scalar2=BIG, op0=ALU.add, op1=ALU.mult),
                 reads=[flags.name], writes=[negLR.name])
            for h in range(8):
                P.op('dve', lambda e, h=h: e.tensor_tensor(
                    out=BWf[:, h * 384:(h + 1) * 384], in0=BWf[:, h * 384:(h + 1) * 384], in1=maskw[:], op=ALU.add),
                    reads=[BWf.name, maskw.name], writes=[("BWf", h)])
                P.op('pool', lambda e, h=h: e.tensor_copy(out=BW[:, h * 384:(h + 1) * 384], in_=BWf[:, h * 384:(h + 1) * 384]),
                     reads=[("BWf", h)], writes=[("BW", h)])
            P.op('dve', lambda e: e.tensor_scalar(
                out=dap(BXW, 0, [[2048, 128], [128, 8], [1, 128]]),
                in0=dap(BWf, 256, [[8 * 384, 128], [384, 8], [1, 128]]),
                scalar1=negLR[:, 0:1], scalar2=None, op0=ALU.add),
                reads=[("BWf", hh) for hh in range(8)] + [negLR.name], writes=["BXWL"])
            P.op('dve', lambda e: e.tensor_scalar(
                out=dap(BXW, 1024, [[2048, 128], [128, 8], [1, 128]]),
                in0=dap(BWf, 0, [[8 * 384, 128], [384, 8], [1, 128]]),
                scalar1=negLR[:, 1:2], scalar2=None, op0=ALU.add),
                reads=[("BWf", hh) for hh in range(8)] + [negLR.name], writes=["BXWR"])
            for b in range(2):
                for dc in range(8):
                    o = (b * 8 + dc) * 128
                    P.op('pool', lambda e, o=o, b=b, dc=dc: e.tensor_copy(
                        out=sc_rep[:, o:o + 128], in_=sc[:, dc * 2 + b:dc * 2 + b + 1].broadcast_to([128, 128])),
                        reads=[sc.name], writes=[("sc_rep", b, dc)])

            def mod_cols_half(half):
                def f(e):
                    ins = None
                    for dc in range(half * 4, half * 4 + 4):
                        for fc in range(16):
                            ins = e.matmul(out=psC[:, fc * 2:fc * 2 + 2],
                                           lhsT=WA[:, (dc % 4) * 3072 + fc * 128: (dc % 4) * 3072 + fc * 128 + 128],
                                           rhs=sc[:, dc * 2:dc * 2 + 2],
                                           start=(dc == 0 and fc == 0), stop=(dc == 7), skip_group_check=True)
                    return ins
                P.op('pe', f, reads=[("WA", k) for k in range(4)] + [sc.name], writes=["psC"])

            def gate_half(half):
                for b in range(2):
                    for n in range(2):
                        def gmm(e, b=b, n=n):
                            ins = None
                            for dc in range(half * 4, half * 4 + 4):
                                o = (b * 8 + dc) * 128
                                ins = e.matmul(out=psA[:, (b * 2 + n) * 512:(b * 2 + n + 1) * 512],
                                               lhsT=sc_rep[:, o:o + 128],
                                               rhs=WA[:, (dc % 4) * 3072 + 2048 + n * 512: (dc % 4) * 3072 + 2048 + (n + 1) * 512],
                                               start=(dc == 0), stop=(dc == 7))
                            return ins
                        P.op('pe', gmm, reads=[("WA", k) for k in range(4)] + [("sc_rep", b, dc) for dc in range(8)],
                             writes=[bankA(b * 2 + n)])

            load_static_weights()
            mod_cols_half(0)
            gate_half(0)
            load_wa(range(4, 8))
            mod_cols_half(1)
            gate_half(1)
            for b in range(2):
                P.op('dve', lambda e, b=b: e.tensor_tensor(
                    out=shf[:, b * 8:(b + 1) * 8], in0=psC[:, b:16:2],
                    in1=bcol[:, 0:8], op=ALU.add),
                    reads=["psC", bcol.name], writes=[("shf", b)])
                P.op('dve', lambda e, b=b: e.scalar_tensor_tensor(
                    out=s1p[:, b * 8:(b + 1) * 8], in0=psC[:, 16 + b:32:2], scalar=1.0, in1=bcol[:, 8:16],
                    op0=ALU.add, op1=ALU.add),
                    reads=["psC", bcol.name], writes=[("s1p", b)])
            for b in range(2):
                for n in range(2):
                    P.op('dve', lambda e, b=b, n=n: e.tensor_tensor(
                        out=gate_bc[:, b * 1024 + n * 512: b * 1024 + (n + 1) * 512],
                        in0=psA[:, (b * 2 + n) * 512:(b * 2 + n + 1) * 512],
                        in1=bgate_bc[:, n * 512:(n + 1) * 512], op=ALU.add),
                        reads=[bankA(b * 2 + n), bgate_bc.name], writes=[("gate_bc", b, n)])
            with nc.Block() as blk1:
                P.emit(blk1)

        WIg = sb("WIg", [128, 8 * 1024], BF16)
        BD = sb("BD", [128, 2 * MD], BF16)
        KT = sb("KT", [128, 8192], BF16)
        VA = sb("VA", [128, 64 * 130], BF16)
        CAR = sb("CAR", [128, 8192], BF16)
        XBF = [sb("xbf%d" % i, [128, 1024], BF16) for i in range(4)]
        uT = sb("uT", [128, 8 * 768], BF16)
        QT = [sb("QT%d" % i, [128, 2 * 512], BF16) for i in range(2)]
        QAT = sb("QAT", [128, 4 * 512], BF16)
        SGb = [sb("SGb%d" % i, [128, 4 * 256], BF16) for i in range(2)]
        SGa = sb("SGa", [128, 4 * 512], BF16)
        KAT = sb("KAT", [128, 768], BF16)
        VAW = sb("VAW", [128, 6 * 2 * 66], BF16)
        OT = sb("OT", [128, 6 * 512], BF16)
        PT = [sb("PT%d" % i, [128, 1024], BF16) for i in range(2)]
        OBf = sb("OBf", [128, 4 * 2 * 128])
        OBn = sb("OBn", [128, 4 * 2 * 128], BF16)
        OAg = sb("OAg", [128, 4 * 512], BF16)
        BX = sb("BX", [128, 1024], BF16)
        BXf = sb("BXf", [128, 1024])
        rec = sb("rec", [128, 16])
        ssq = sb("ssq", [128, 8])
        rstd = sb("rstd", [128, 8])
        sqj = sb("sqj", [128, 128])
        den = sb("den", [128, 16])
        bnst = sb("bnst", [128, 12])
        mv = sb("mv", [128, 8])
        lrs = sb("lrs", [128, 4])

        def load_kv_weights(heads):
            for hl, h in enumerate(heads):
                for dc in range(8):
                    src = dap(win_d, (dc * 128) * 3328 + 1792 + h * 128, [[3328, 128], [512, 2], [1, 128]])
                    def castG(e, fb, dc=dc, hl=hl):
                        return e.tensor_copy(
                            out=dap(WIg, dc * 1024 + 256 + hl * 128, [[8192, 128], [256, 2], [1, 128]]),
                            in_=dap(fb, 0, [[1024, 128], [128, 2], [1, 128]]))
                    stage_cast(src, 256, castG, writes=["WIgkv"], dst_dims=[[1024, 128], [128, 2], [1, 128]])

        def q_weight_thunks(heads):
            th = []
            for hl, h in enumerate(heads):
                for dc in range(8):
                    def t1(dc=dc, hl=hl, h=h):
                        src = dap(win_d, (dc * 128) * 3328 + 1280 + h * 128, [[3328, 128], [1536, 2], [1, 128]])
                        def castG(e, fb):
                            return e.tensor_copy(
                                out=dap(WIg, dc * 1024 + hl * 128, [[8192, 128], [768, 2], [1, 128]]),
                                in_=dap(fb, 0, [[1024, 128], [128, 2], [1, 128]]))
                        stage_cast(src, 256, castG, writes=["WIgq"], dst_dims=[[1024, 128], [128, 2], [1, 128]],
                                   engs=('act', 'dve'))
                    th.append(t1)
                def t2(hl=hl, h=h):
                    def castD1(e, fb):
                        return e.tensor_copy(out=BD[:, hl * MD: hl * MD + 1024], in_=fb[:, 0:1024])
                    stage_cast(bd_d.ap()[:, h * MD: h * MD + 1024], 1024, castD1, writes=[("BD", hl)], engs=('act', 'dve'))
                def t3(hl=hl, h=h):
                    def castD2(e, fb):
                        return e.tensor_copy(out=BD[:, hl * MD + 1024: hl * MD + MD], in_=fb[:, 0:MD - 1024])
                    stage_cast(bd_d.ap()[:, h * MD + 1024: (h + 1) * MD], MD - 1024, castD2, writes=[("BD", hl)],
                               engs=('act', 'dve'))
                th.append(t2)
                th.append(t3)
            return th

        P.op('pool', lambda e: e.memset(VA[:], 1.0), writes=["VA_all"])
        P.op('pool', lambda e: e.memset(VAW[:], 1.0), writes=["VAW_all"])

        xb_state = {'k': 0, 'm': 0}

        def prefetch_x(src_d, row0):
            sl = xb_state['k'] % 4
            xb_state['k'] += 1
            P.op('pool', lambda e, sl=sl: e.dma_start(out=XBF[sl][:], in_=src_d.ap()[row0:row0 + 128, :]),
                 writes=[("xbf", sl)], dma=xbkeys[sl])
            return sl

        def make_uT_from(sl, bslot, b):
            par = xb_state['m'] % 2
            xb_state['m'] += 1
            if par == 0:
                evk, odk, evb, odb, evo, odo = bankA(0), "psC", psA_b, psC_b, 0, 0
            else:
                evk, odk, evb, odb, evo, odo = bankB(0), bankB(1), psB_b, psB_b, 0, 1024

            dve_dcs = [0, 2, 4, 6, 7]
            act_dcs = [1, 3, 5]

            def tsrc(dc):
                if dc in dve_dcs:
                    k = dve_dcs.index(dc)
                    return evb[:, evo + k * 128: evo + (k + 1) * 128]
                k = act_dcs.index(dc)
                return odb[:, odo + k * 128: odo + (k + 1) * 128]

            def tr(e):
                ins = None
                for dc in range(8):
                    ins = e.transpose(out=tsrc(dc), in_=XBF[sl][:, dc * 128:(dc + 1) * 128], identity=identb[:])
                return ins
            P.op('pe', tr, reads=[("xbf", sl), identb.name], writes=[evk, odk])
            for dc in range(8):
                o = dc * 768 + bslot * 128
                if dc in dve_dcs:
                    P.op('dve', lambda e, dc=dc, o=o: e.tensor_scalar(
                        out=uT[:, o:o + 128], in0=tsrc(dc),
                        scalar1=s1p[:, b * 8 + dc:b * 8 + dc + 1], scalar2=shf[:, b * 8 + dc:b * 8 + dc + 1],
                        op0=ALU.mult, op1=ALU.add),
                        reads=[evk], writes=[("uT", bslot, dc)])
                else:
                    P.op('act', lambda e, dc=dc, o=o: e.activation(
                        out=uT[:, o:o + 128], in_=tsrc(dc), func=AF.Identity,
                        bias=shf[:, b * 8 + dc:b * 8 + dc + 1], scale=s1p[:, b * 8 + dc:b * 8 + dc + 1]),
                        reads=[odk], writes=[("uT", bslot, dc)])

        def make_uT_list(src_d, items, b):
            slots = {}
            for i in range(min(NPF, len(items))):
                slots[i] = prefetch_x(src_d, items[i][1] * 128)
            for i, (bs, lb) in enumerate(items):
                if i + NPF < len(items):
                    slots[i + NPF] = prefetch_x(src_d, items[i + NPF][1] * 128)
                make_uT_from(slots[i], bs, b)

        def uT_keys(bslots):
            return [("uT", bs, dc) for bs in bslots for dc in range(8)]

        def proj_fm(W, pitch, col, bslot0, nb, pbank, reads_extra=()):
            def f(e):
                ins = None
                for dc in range(8):
                    ins = e.matmul(out=psA[:, pbank * 512: pbank * 512 + nb * 128],
                                   lhsT=W[:, dc * pitch + col: dc * pitch + col + 128],
                                   rhs=uT[:, dc * 768 + bslot0 * 128: dc * 768 + (bslot0 + nb) * 128],
                                   start=(dc == 0), stop=(dc == 7))
                return ins
            P.op('pe', f, reads=uT_keys(range(bslot0, bslot0 + nb)) + list(reads_extra), writes=[bankA(pbank)])

        def proj_tm(W, pitch, col, ncol, bslot, pbank, reads_extra=()):
            def f(e):
                ins = None
                for dc in range(8):
                    ins = e.matmul(out=psA[:, pbank * 512: pbank * 512 + ncol],
                                   lhsT=uT[:, dc * 768 + bslot * 128: dc * 768 + (bslot + 1) * 128],
                                   rhs=W[:, dc * pitch + col: dc * pitch + col + ncol],
                                   start=(dc == 0), stop=(dc == 7))
                return ins
            P.op('pe', f, reads=uT_keys([bslot]) + list(reads_extra), writes=[bankA(pbank)])

        pb_state = {'i': 0}

        def next_pbank():
            b = 1 + pb_state['i'] % 3
            pb_state['i'] += 1
            return b

        def evac_copy(out_ap, in_ap, reads, writes, scale=None, which=None):
            if which is None:
                which = state['evac'] % 2
                state['evac'] += 1
            if which == 0:
                if scale is None:
                    P.op('dve', lambda e: e.tensor_copy(out=out_ap, in_=in_ap), reads=reads, writes=writes)
                else:
                    P.op('dve', lambda e: e.tensor_scalar(out=out_ap, in0=in_ap, scalar1=scale, scalar2=None, op0=ALU.mult),
                         reads=reads, writes=writes)
            else:
                if scale is None:
                    P.op('act', lambda e: e.copy(out=out_ap, in_=in_ap), reads=reads, writes=writes)
                else:
                    P.op('act', lambda e: e.mul(out=out_ap, in_=in_ap, mul=scale), reads=reads, writes=writes)

        def run_job(x_d, y_d, b, S_kv, n_qt, is_sample, groups, next_first_heads):
            NB = S_kv // 128
            n_groups = len(groups)
            for gi, heads in enumerate(groups):
                G = len(heads)
                final = (gi == n_groups - 1)
                if not state.get('kv_preloaded'):
                    load_kv_weights(heads)
                state['kv_preloaded'] = False
                qw = q_weight_thunks(heads)
                nxt_heads = groups[gi + 1] if gi + 1 < n_groups else next_first_heads

                NT = NB // 2

                kv_slots = {}
                for blk in range(NPF):
                    kv_slots[blk] = prefetch_x(x_d, blk * 128)

                def front(n):
                    base = (n % 3) * 2
                    for k in range(2):
                        blk = 2 * n + k
                        if blk + NPF < NB:
                            kv_slots[blk + NPF] = prefetch_x(x_d, (blk + NPF) * 128)
                        make_uT_from(kv_slots[blk], base + k, b)

                def back(n):
                    base = (n % 3) * 2
                    for hl in range(G):
                        pbk = next_pbank()
                        proj_fm(WIg, 1024, 256 + hl * 128, base, 2, pbk, reads_extra=["WIgkv"])
                        evac_copy(KT[:, hl * S_kv + n * 256: hl * S_kv + (n + 1) * 256],
                                  psA[:, pbk * 512: pbk * 512 + 256],
                                  reads=[bankA(pbk)], writes=[("KT", hl, 2 * n), ("KT", hl, 2 * n + 1)])
                    for k in range(2):
                        tb = 2 * n + k
                        pbk = next_pbank()
                        proj_tm(WIg, 1024, 512, G * 128, base + k, pbk, reads_extra=["WIgkv"])
                        state['evac'] += 1
                        for hl in range(G):
                            o = (tb * G + hl) * 130
                            evac_copy(VA[:, o:o + 128], psA[:, pbk * 512 + hl * 128: pbk * 512 + (hl + 1) * 128],
                                      reads=[bankA(pbk), "VA_all"], writes=[("VA", hl, tb)], which=state['evac'] % 2)

                for n in range(NT + 1):
                    if n < NT:
                        front(n)
                    if kstop == 22:
                        raise _Stop()
                    if n >= 1:
                        back(n - 1)
                    if n >= 1 and qw:
                        qw.pop(0)()
                while qw:
                    qw.pop(0)()
                    if kstop == 23 and n == 1:
                        raise _Stop()
                    if kstop == 2 and n == 2:
                        raise _Stop()
                if kstop == 3:
                    raise _Stop()

                for t in range(n_qt):
                    if t == n_qt - 1 and nxt_heads is not None:
                        load_kv_weights(nxt_heads)
                        state['kv_preloaded'] = True
                    q_phase(x_d, y_d, b, S_kv, NB, t, n_qt, is_sample, heads, final, gi, groups)
                    if kstop == 4:
                        raise _Stop()
                    if kstop == 5 and final:
                        raise _Stop()
                if kstop == 6 and final:
                    raise _Stop()

        def tile_blocks(t, n_qt, NB, is_sample, final):
            blocks = []
            if final:
                if t > 0:
                    blocks.append((0, 4 * t - 1))
                elif is_sample:
                    blocks.append((0, NB - 1))
                for i in range(4):
                    blocks.append((1 + i, 4 * t + i))
                if t < n_qt - 1 or is_sample:
                    blocks.append((5, 4 * t + 4))
            else:
                for i in range(4):
                    blocks.append((1 + i, 4 * t + i))
            return blocks

        def prepA_steps(x_d, b, t, n_qt, NB, is_sample, final, G):
            blocks = tile_blocks(t, n_qt, NB, is_sample, final)
            par = t % 2
            n = len(blocks)
            slots = {}
            steps = []

            def pf(i):
                slots[i] = prefetch_x(x_d, blocks[i][1] * 128)

            def s_pf0():
                for i in range(min(NPF, n)):
                    pf(i)
            steps.append((s_pf0, 4))
            for i, (bs, lb) in enumerate(blocks):
                def s_tr(i=i):
                    if i + NPF < n:
                        pf(i + NPF)
                    sl = slots[i]

                    def tr(e):
                        ins = None
                        for dc in range(8):
                            ins = e.transpose(out=psC_b[:, dc * 128:(dc + 1) * 128], in_=XBF[sl][:, dc * 128:(dc + 1) * 128],
                                              identity=identb[:])
                        return ins
                    P.op('pe', tr, reads=[("xbf", sl), identb.name], writes=["psC"])

                def s_ev(bs=bs):
                    for dc in range(8):
                        o = dc * 768 + bs * 128
                        P.op('dve', lambda e, dc=dc, o=o: e.tensor_scalar(
                            out=uT[:, o:o + 128], in0=psC_b[:, dc * 128:(dc + 1) * 128],
                            scalar1=s1p[:, b * 8 + dc:b * 8 + dc + 1], scalar2=shf[:, b * 8 + dc:b * 8 + dc + 1],
                            op0=ALU.mult, op1=ALU.add),
                            reads=["psC"], writes=[("uT", bs, dc)])
                steps.append((s_tr, 0))
                steps.append((s_ev, 2))

            def fm_part(col, dcs):
                def f(e):
                    ins = None
                    for dc in dcs:
                        ins = e.matmul(out=psC[:, 0:512],
                                       lhsT=WIg[:, dc * 1024 + col: dc * 1024 + col + 128],
                                       rhs=uT[:, dc * 768 + 128: dc * 768 + 640],
                                       start=(dc == 0), stop=(dc == 7))
                    return ins
                P.op('pe', f, reads=uT_keys(range(1, 5)) + ["WIgq"], writes=["psC"])

            for hl in range(G):
                steps.append((lambda hl=hl: fm_part(hl * 128, range(0, 4)), 0))
                steps.append((lambda hl=hl: fm_part(hl * 128, range(4, 8)), 0))

                def s_qe(hl=hl):
                    P.op('dve', lambda e: e.tensor_scalar(out=QT[par][:, hl * 512:(hl + 1) * 512], in0=psC[:, 0:512],
                                                          scalar1=0.125, scalar2=None, op0=ALU.mult),
                         reads=["psC"], writes=[("QT", par, hl)])
                steps.append((s_qe, 2))
            for qb in range(4):
                def s_g(qb=qb):
                    def f(e):
                        ins = None
                        for dc in range(8):
                            ins = e.matmul(out=psC[:, 0:G * 128],
                                           lhsT=uT[:, dc * 768 + (1 + qb) * 128: dc * 768 + (2 + qb) * 128],
                                           rhs=WIg[:, dc * 1024 + 768: dc * 1024 + 768 + G * 128],
                                           start=(dc == 0), stop=(dc == 7))
                        return ins
                    P.op('pe', f, reads=uT_keys([1 + qb]) + ["WIgq"], writes=["psC"])

                def s_ge(qb=qb):
                    P.op('dve', lambda e: e.tensor_copy(out=SGb[par][:, qb * 256: qb * 256 + G * 128], in_=psC[:, 0:G * 128]),
                         reads=["psC"], writes=[("SGb", par, qb)])
                steps.append((s_g, 0))
                steps.append((s_ge, 1))
            return steps

        def prepB_parts(blocks, acteng):
            wh = 1 if acteng else None

            def p_qa():
                for c in range(4):
                    pbk = next_pbank()
                    proj_fm(WIs, 1280, c * 128, 1, 4, pbk, reads_extra=["WIs"])
                    evac_copy(QAT[:, c * 512:(c + 1) * 512], psA[:, pbk * 512:(pbk + 1) * 512],
                              reads=[bankA(pbk)], writes=[("QAT", c)], scale=0.125, which=wh)

            def p_ga(qbs):
                for qb in qbs:
                    pbk = next_pbank()
                    proj_tm(WIs, 1280, 512, 512, 1 + qb, pbk, reads_extra=["WIs"])
                    P.op('act', lambda e, qb=qb, pbk=pbk: e.activation(
                        out=SGa[:, qb * 512:(qb + 1) * 512], in_=psA[:, pbk * 512:(pbk + 1) * 512], func=AF.Silu),
                        reads=[bankA(pbk)], writes=[("SGa", qb)])

            def p_kv():
                bs0 = blocks[0][0]
                nbk = len(blocks)
                pieces = []
                s0 = bs0
                rem = nbk
                while rem > 0:
                    n = min(4, rem)
                    pieces.append((s0, n))
                    s0 += n
                    rem -= n
                for (s0, n) in pieces:
                    pbk = next_pbank()
                    proj_fm(WIs, 1280, 1024, s0, n, pbk, reads_extra=["WIs"])
                    evac_copy(KAT[:, s0 * 128:(s0 + n) * 128], psA[:, pbk * 512: pbk * 512 + n * 128],
                              reads=[bankA(pbk)], writes=["KAT"], which=wh)
                for bs, lb in blocks:
                    pbk = next_pbank()
                    proj_tm(WIs, 1280, 1152, 128, bs, pbk, reads_extra=["WIs"])
                    eng = 'act' if acteng else 'dve'
                    P.op(eng, lambda e, bs=bs, pbk=pbk: (e.copy if acteng else e.tensor_copy)(
                        out=dap(VAW, bs * 132, [[VAW_pitch, 128], [66, 2], [1, 64]]),
                        in_=dap(psA, pbk * 512, [[2048, 128], [64, 2], [1, 64]])),
                        reads=[bankA(pbk), "VAW_all"], writes=[("VAW", bs)])
            return [p_qa, lambda: p_ga([0, 1]), lambda: p_ga([2, 3]), p_kv]

        def bg_drain(bg):
            while bg:
                fn, _ = bg.pop(0)
                fn()

        def q_phase(x_d, y_d, b, S_kv, NB, t, n_qt, is_sample, heads, final, gi, groups):
            G = len(heads)
            par = t % 2
            blocks = tile_blocks(t, n_qt, NB, is_sample, final)
            has_left = any(bs == 0 for bs, _ in blocks)
            has_right = any(bs == 5 for bs, _ in blocks)
            if t == 0:
                make_uT_list(x_d, blocks, b)
                for hl in range(G):
                    pbk = next_pbank()
                    proj_fm(WIg, 1024, hl * 128, 1, 4, pbk, reads_extra=["WIgq"])
                    evac_copy(QT[par][:, hl * 512:(hl + 1) * 512], psA[:, pbk * 512:(pbk + 1) * 512],
                              reads=[bankA(pbk)], writes=[("QT", par, hl)], scale=0.125)
                for qb in range(4):
                    pbk = next_pbank()
                    proj_tm(WIg, 1024, 768, G * 128, 1 + qb, pbk, reads_extra=["WIgq"])
                    evac_copy(SGb[par][:, qb * 256: qb * 256 + G * 128], psA[:, pbk * 512: pbk * 512 + G * 128],
                              reads=[bankA(pbk)], writes=[("SGb", par, qb)])

            if final:
                if t == 0:
                    for part in prepB_parts(blocks, acteng=False):
                        part()
                window_attn(t, n_qt, is_sample, blocks, has_left, has_right)

            bg = []
            if state.get('pending_post'):
                bg += state['pending_post']
                state['pending_post'] = None
            if t + 1 < n_qt:
                bg += prepA_steps(x_d, b, t + 1, n_qt, NB, is_sample, final, G)
            for hl, h in enumerate(heads):
                diff_attn(S_kv, NB, t, n_qt, is_sample, hl, h, G, bg)
            bg_drain(bg)
            post = post_diff_steps(t, heads, G, final, gi, groups, n_qt, fg=(final or t == n_qt - 1))
            if final or t == n_qt - 1:
                bg_drain(post)
            else:
                state['pending_post'] = post
            if dbg and final and (not is_sample) and t == 0:
                P.op('sp', lambda e: e.dma_start(out=dbg_ot.ap(), in_=OT[:]),
                     reads=[("OT", fc) for fc in range(6)], writes=["dbg1"], dma='dbg')
                P.op('sp', lambda e: e.dma_start(out=dbg_car.ap()[:, 0:512], in_=CAR[:, 0:512]),
                     reads=[("CAR", 0, 0)], writes=["dbg2"], dma='dbg')
                P.op('sp', lambda e: e.dma_start(out=dbg_car.ap()[:, 512:1024], in_=CAR[:, 4096:4608]),
                     reads=[("CAR", 1, 0)], writes=["dbg3"], dma='dbg')
                P.op('sp', lambda e: e.dma_start(out=dbg_c.ap()[:, 0:16], in_=s1p[:]), writes=["dbg4"], dma='dbg')
                P.op('sp', lambda e: e.dma_start(out=dbg_c.ap()[:, 16:32], in_=shf[:]), writes=["dbg5"], dma='dbg')
                P.op('sp', lambda e: e.dma_start(out=dbg_c.ap()[:, 32:33], in_=neglam[:], allow_slow_non_contiguous=True), writes=["dbg6"], dma='dbg')
                P.op('sp', lambda e: e.dma_start(out=dbg_c.ap()[:, 33:41], in_=expsink[:]), writes=["dbg7"], dma='dbg')
                P.op('sp', lambda e: e.dma_start(out=dbg_c.ap()[:, 41:49], in_=cfar[:]), writes=["dbg8"], dma='dbg')
                P.op('sp', lambda e: e.dma_start(out=dbg_sg.ap()[:, 0:2048], in_=SGa[:]),
                     reads=[("SGa", qb) for qb in range(4)], writes=["dbg9"], dma='dbg')
                P.op('sp', lambda e: e.dma_start(out=dbg_sg.ap()[:, 2048:3072], in_=SGb[0][:]),
                     reads=[("SGb", 0, qb) for qb in range(4)], writes=["dbg10"], dma='dbg')
            if final:
                nxt = None
                if t + 1 < n_qt:
                    nxt = prepB_parts(tile_blocks(t + 1, n_qt, NB, is_sample, final), acteng=True)
                finalize(x_d, y_d, b, t, n_qt, heads, gi, groups, nxt)

        VAW_pitch = 6 * 2 * 66

        def window_attn(t, n_qt, is_sample, blocks, has_left, has_right):
            present = {bs for bs, _ in blocks}
            wrapL = is_sample and t == 0
            wrapR = is_sample and t == n_qt - 1
            def win_chunk(c):
                par = c % 2
                if par == 0:
                    accU = [(psB, 0, bankB(0)), (psB, 512, bankB(1))]
                else:
                    accU = [(psB, 1024, bankB(2)), (psC, 0, "psC")]
                jbs = [jb for jb in range(6) if jb in present]
                first = [True, True]

                def geom(jb):
                    qlo = max(0, jb - 2)
                    qhi = min(3, jb)
                    nq = qhi - qlo + 1
                    return qlo, nq, nq * 128, (qlo - (jb - 2)) * 128

                def qk(jb, slot, c=c):
                    qlo, nq, N, so = geom(jb)
                    use_wrap = (wrapL and jb == 0) or (wrapR and jb == 5)

                    def f(e):
                        ins = None
                        for u in range(2):
                            e.matmul(out=psA[:, (slot * 2 + u) * 512: (slot * 2 + u) * 512 + N],
                                     lhsT=KAT[u * 64:(u + 1) * 64, jb * 128:(jb + 1) * 128],
                                     rhs=QAT[u * 64:(u + 1) * 64, c * 512 + qlo * 128: c * 512 + qlo * 128 + N],
                                     start=True, stop=False)
                        for u in range(2):
                            hp = 2 * c + u
                            if use_wrap:
                                wo = 0 if jb == 0 else 1024
                                rhs = BXW[:, wo + hp * 128: wo + hp * 128 + 128]
                            else:
                                rhs = BW[:, hp * 384 + so: hp * 384 + so + N]
                            ins = e.matmul(out=psA[:, (slot * 2 + u) * 512: (slot * 2 + u) * 512 + N],
                                           lhsT=identb[:], rhs=rhs, start=False, stop=True)
                        return ins
                    rd = ["KAT", ("QAT", c), identb.name]
                    rd += [] if use_wrap else [("BW", 2 * c), ("BW", 2 * c + 1)]
                    P.op('pe', f, reads=rd, writes=[bankA(slot * 2), bankA(slot * 2 + 1)])

                def ex(jb, slot):
                    qlo, nq, N, so = geom(jb)
                    P.op('act', lambda e: e.activation(
                        out=dap(PT[slot], 0, [[1024, 128], [384, 2], [1, N]]),
                        in_=dap(psA, slot * 1024, [[2048, 128], [512, 2], [1, N]]), func=AF.Exp),
                        reads=[bankA(slot * 2), bankA(slot * 2 + 1)], writes=[("PT", slot)])

                def pv(jb, slot):
                    qlo, nq, N, so = geom(jb)
                    sts = []
                    for u in range(2):
                        for qi in range(nq):
                            sts.append(first[u])
                            first[u] = False
                    sts = tuple(sts)

                    def f(e):
                        ins = None
                        k = 0
                        for u in range(2):
                            pt_, base, _ = accU[u]
                            for qi in range(nq):
                                qb = qlo + qi
                                off = base + qb * 66
                                ins = e.matmul(out=pt_[:, off: off + 65],
                                               lhsT=PT[slot][:, u * 384 + qi * 128: u * 384 + (qi + 1) * 128],
                                               rhs=VAW[:, (jb * 2 + u) * 66: (jb * 2 + u) * 66 + 65],
                                               start=sts[k], stop=True, skip_group_check=True)
                                k += 1
                        return ins
                    P.op('pe', f, reads=[("PT", slot), ("VAW", jb)], writes=[accU[0][2], accU[1][2]])

                qk(jbs[0], 0)
                for idx, jb in enumerate(jbs):
                    if idx + 1 < len(jbs):
                        qk(jbs[idx + 1], (idx + 1) % 2)
                    ex(jb, idx % 2)
                    pv(jb, idx % 2)
                for u in range(2):
                    pt_, base, bk = accU[u]
                    pitch = 1536 if pt_ is psB else 512
                    P.op('dve', lambda e, u=u, pt_=pt_, base=base, pitch=pitch, c=c, par=par: e.tensor_scalar(
                        out=den[:, par * 8 + u * 4: par * 8 + u * 4 + 4], in0=dap(pt_, base + 64, [[pitch, 128], [66, 4]]),
                        scalar1=expsink[:, 2 * c + u:2 * c + u + 1], scalar2=None, op0=ALU.add),
                        reads=[bk], writes=[("den", par, u)])
                P.op('dve', lambda e, par=par: e.reciprocal(out=rec[:, par * 8: par * 8 + 8], in_=den[:, par * 8: par * 8 + 8]),
                     reads=[("den", par, 0), ("den", par, 1)], writes=[("recw", par)])
                for u in range(2):
                    pt_, base, bk = accU[u]
                    for qb in range(4):
                        a = u * 4 + qb
                        off = base + qb * 66
                        hp = 2 * c + u
                        P.op('dve', lambda e, a=a, off=off, hp=hp, qb=qb, pt_=pt_, par=par: e.scalar_tensor_tensor(
                            out=OAg[:, qb * 512 + hp * 64: qb * 512 + hp * 64 + 64],
                            in0=pt_[:, off: off + 64], scalar=rec[:, par * 8 + a: par * 8 + a + 1],
                            in1=SGa[:, qb * 512 + hp * 64: qb * 512 + hp * 64 + 64],
                            op0=ALU.mult, op1=ALU.mult),
                            reads=[bk, ("recw", par), ("SGa", qb)], writes=[("OAg", qb, hp)])

            for c in range(4):
                win_chunk(c)
            for fc in range(4):
                def trw(e, fc=fc):
                    ins = None
                    for qb in range(4):
                        ins = e.transpose(out=psC_b[:, qb * 128:(qb + 1) * 128],
                                          in_=OAg[:, qb * 512 + fc * 128: qb * 512 + (fc + 1) * 128],
                                          identity=identb[:])
                    return ins
                P.op('pe', trw, reads=[("OAg", qb, hp) for qb in range(4) for hp in (2 * fc, 2 * fc + 1)] + [identb.name],
                     writes=["psC"])
                P.op('dve', lambda e, fc=fc: e.tensor_copy(out=OT[:, fc * 512:(fc + 1) * 512], in_=psC_b[:, 0:512]),
                     reads=["psC"], writes=[("OT", fc)])

        def diff_attn(S_kv, NB, t, n_qt, is_sample, hl, h, G, bg):
            par = t % 2
            kinds = []
            for i in range(NB):
                own = (not is_sample) or i < 16
                if own:
                    d = 4 * t - i
                    if -4 <= d <= 1:
                        kinds.append(('near', d))
                    elif d >= 2:
                        kinds.append(('far', cfar[:, h:h + 1], cfar.name))
                    else:
                        kinds.append(('far', cfar[:, 4 + h:5 + h], cfar.name))
                else:
                    if i == 16 and t == 3:
                        kinds.append(('wrapR',))
                    elif i == NB - 1 and t == 0:
                        kinds.append(('wrapL',))
                    else:
                        kinds.append(('far', farS[:, h * 64 + i: h * 64 + i + 1], ("farS", h)))
            if is_sample and t == 0:
                P.op('dve', lambda e: e.tensor_scalar(
                    out=BXf[:, 0:512], in0=BD[:, hl * MD + 640: hl * MD + 1152],
                    scalar1=cfar[:, 4 + h:5 + h], scalar2=flags[:, 0:1], op0=ALU.subtract, op1=ALU.mult),
                    reads=[("BD", hl), cfar.name, flags.name], writes=["BXf"])
                P.op('dve', lambda e: e.tensor_scalar(
                    out=BX[:, 0:512], in0=BXf[:, 0:512], scalar1=cfar[:, 4 + h:5 + h], scalar2=None, op0=ALU.add),
                    reads=["BXf"], writes=["BX"])
            if is_sample and t == 3:
                P.op('dve', lambda e: e.tensor_scalar(
                    out=BXf[:, 512:1024], in0=BD[:, hl * MD: hl * MD + 512],
                    scalar1=cfar[:, h:h + 1], scalar2=flags[:, 1:2], op0=ALU.subtract, op1=ALU.mult),
                    reads=[("BD", hl), cfar.name, flags.name], writes=["BXf"])
                P.op('dve', lambda e: e.tensor_scalar(
                    out=BX[:, 512:1024], in0=BXf[:, 512:1024], scalar1=cfar[:, h:h + 1], scalar2=None, op0=ALU.add),
                    reads=["BXf"], writes=["BX"])

            def qk(i):
                kind = kinds[i]
                slot = i % 2

                def f(e):
                    ins = None
                    near = kind[0] != 'far'
                    for m in range(2):
                        ins = e.matmul(out=psA[:, (slot * 2 + m) * 512:(slot * 2 + m + 1) * 512],
                                       lhsT=KT[m * 64:(m + 1) * 64, hl * S_kv + i * 128: hl * S_kv + (i + 1) * 128],
                                       rhs=QT[par][m * 64:(m + 1) * 64, hl * 512:(hl + 1) * 512],
                                       start=True, stop=not near)
                    if near:
                        if kind[0] == 'near':
                            o = hl * MD + 128 * (kind[1] + 4)
                            rhs = BD[:, o:o + 512]
                        elif kind[0] == 'wrapL':
                            rhs = BX[:, 0:512]
                        else:
                            rhs = BX[:, 512:1024]
                        for m in range(2):
                            ins = e.matmul(out=psA[:, (slot * 2 + m) * 512:(slot * 2 + m + 1) * 512],
                                           lhsT=identb[:], rhs=rhs, start=False, stop=True)
                    return ins
                rd = [("KT", hl, i), ("QT", par, hl)]
                if kind[0] == 'near':
                    rd += [("BD", hl), identb.name]
                elif kind[0] != 'far':
                    rd += ["BX", identb.name]
                P.op('pe', f, reads=rd, writes=[bankA(slot * 2), bankA(slot * 2 + 1)])

            def ex(i):
                kind = kinds[i]
                slot = i % 2
                if kind[0] == 'far':
                    bias_ap, bkey = kind[1], kind[2]
                else:
                    bias_ap, bkey = zero_c[:], zero_c.name
                P.op('act', lambda e: e.activation(out=PT[slot][:], in_=psA[:, slot * 1024:(slot + 1) * 1024],
                                                   func=AF.Exp, bias=bias_ap, scale=1.0),
                     reads=[bankA(slot * 2), bankA(slot * 2 + 1), bkey], writes=[("PT", slot)])

            def pv(i):
                slot = i % 2

                def f(e):
                    ins = None
                    for m in range(2):
                        for qb in range(4):
                            a = m * 4 + qb
                            bank = a // 3
                            off = bank * 512 + (a % 3) * 130
                            vo = (i * G + hl) * 130
                            ins = e.matmul(out=psB[:, off:off + 129],
                                           lhsT=PT[slot][:, m * 512 + qb * 128: m * 512 + (qb + 1) * 128],
                                           rhs=VA[:, vo:vo + 129],
                                           start=(i == 0 and a % 3 == 0), stop=(i == NB - 1), skip_group_check=True)
                    return ins
                P.op('pe', f, reads=[("PT", slot), ("VA", hl, i)], writes=[bankB(0), bankB(1), bankB(2)])

            qk(0)
            qk(1)
            cool = 2
            for i in range(NB):
                ex(i)
                if i + 2 < NB:
                    qk(i + 2)
                pv(i)
                if bg:
                    if cool <= 0:
                        fn, cool = bg.pop(0)
                        fn()
                    else:
                        cool -= 1
            for bank in range(3):
                na = 3 if bank < 2 else 2
                P.op('dve', lambda e, bank=bank, na=na: e.reciprocal(
                    out=rec[:, bank * 3: bank * 3 + na], in_=dap(psB, bank * 512 + 128, [[1536, 128], [130, na]])),
                    reads=[bankB(bank)], writes=[("recd", bank)])
            P.op('dve', lambda e: e.tensor_scalar(out=rec[:, 8:12], in0=rec[:, 4:8], scalar1=neglam[:, 0:1], scalar2=None, op0=ALU.mult),
                 reads=[("recd", 1), ("recd", 2), neglam.name], writes=["recn"])
            for qb in range(4):
                a0 = qb
                o0 = (a0 // 3) * 512 + (a0 % 3) * 130
                oo = (qb * 2 + hl) * 128
                P.op('dve', lambda e, a0=a0, o0=o0, oo=oo: e.tensor_scalar(
                    out=OBf[:, oo:oo + 128], in0=psB[:, o0:o0 + 128], scalar1=rec[:, a0:a0 + 1], scalar2=None, op0=ALU.mult),
                    reads=[bankB(a0 // 3), ("recd", a0 // 3)], writes=[("OBf", qb, hl)])
            for qb in range(4):
                a1 = 4 + qb
                o1 = (a1 // 3) * 512 + (a1 % 3) * 130
                P.op('dve', lambda e, qb=qb, o1=o1: e.tensor_scalar(
                    out=BXf[:, qb * 128:(qb + 1) * 128], in0=psB[:, o1:o1 + 128], scalar1=rec[:, 8 + qb:9 + qb], scalar2=None, op0=ALU.mult),
                    reads=[bankB(a1 // 3), "recn"], writes=["BXf"])
            for qb in range(4):
                oo = (qb * 2 + hl) * 128
                P.op('dve', lambda e, qb=qb, oo=oo: e.tensor_tensor(
                    out=OBf[:, oo:oo + 128], in0=OBf[:, oo:oo + 128], in1=BXf[:, qb * 128:(qb + 1) * 128], op=ALU.add),
                    reads=[("OBf", qb, hl), "BXf"], writes=[("OBf", qb, hl)])
                P.op('dve', lambda e, qb=qb, oo=oo: e.tensor_tensor(
                    out=sqj[:], in0=OBf[:, oo:oo + 128], in1=OBf[:, oo:oo + 128], op=ALU.mult),
                    reads=[("OBf", qb, hl)], writes=["sqj"])
                P.op('dve', lambda e, qb=qb: e.reduce_sum(out=ssq[:, qb * 2 + hl: qb * 2 + hl + 1], in_=sqj[:], axis=mybir.AxisListType.X),
                     reads=["sqj"], writes=[("ssq", qb, hl)])

        def post_diff_steps(t, heads, G, final, gi, groups, n_qt, fg=False):
            S_q = n_qt * 512
            par = t % 2
            steps = []
            sgk = [("SGb", par, qb) for qb in range(4)]

            def s1():
                P.op('act', lambda e: e.activation(out=BXf[:], in_=SGb[par][:], func=AF.Exp, scale=-1.0),
                     reads=sgk, writes=["BXf"])
            def s1f():
                P.op('act', lambda e: e.activation(out=SGb[par][:], in_=SGb[par][:], func=AF.Silu), reads=sgk, writes=sgk)
            if fg:
                steps.append((s1f, 0))
            else:
                steps.append((s1, 1))

            def s2():
                P.op('dve', lambda e: e.tensor_scalar(out=BXf[:], in0=BXf[:], scalar1=1.0, scalar2=None, op0=ALU.add),
                     reads=["BXf"], writes=["BXf"])
                P.op('dve', lambda e: e.reciprocal(out=BXf[:], in_=BXf[:]), reads=["BXf"], writes=["BXf"])
                P.op('dve', lambda e: e.tensor_tensor(out=SGb[par][:], in0=SGb[par][:], in1=BXf[:], op=ALU.mult),
                     reads=sgk + ["BXf"], writes=sgk)
            if not fg:
                steps.append((s2, 2))

            def s3():
                P.op('act', lambda e: e.activation(out=rstd[:, 0:8], in_=ssq[:, 0:8], func=AF.Sqrt, bias=eps_sub[:], scale=1.0 / 128.0),
                     reads=[("ssq", qb, hl) for qb in range(4) for hl in range(G)] + [eps_sub.name], writes=["rstd_a"])
                P.op('dve', lambda e: e.reciprocal(out=rstd[:, 0:8], in_=rstd[:, 0:8]), reads=["rstd_a"], writes=["rstd"])
            steps.append((s3, 2))
            for qb in range(4):
                def s4(qb=qb):
                    for hl in range(G):
                        oo = (qb * 2 + hl) * 128
                        P.op('dve', lambda e, hl=hl, oo=oo: e.scalar_tensor_tensor(
                            out=OBf[:, oo:oo + 128], in0=OBf[:, oo:oo + 128], scalar=rstd[:, qb * 2 + hl: qb * 2 + hl + 1],
                            in1=subw_bc[:], op0=ALU.mult, op1=ALU.mult),
                            reads=[("OBf", qb, hl), "rstd", subw_bc.name], writes=[("OBf", qb, hl)])
                        P.op('pool', lambda e, hl=hl, oo=oo: e.tensor_tensor(
                            out=OBn[:, oo:oo + 128], in0=OBf[:, oo:oo + 128],
                            in1=SGb[par][:, qb * 256 + hl * 128: qb * 256 + (hl + 1) * 128], op=ALU.mult),
                            reads=[("OBf", qb, hl), ("SGb", par, qb)], writes=[("OBn", qb, hl)])
                steps.append((s4, 2 if qb == 3 else 1))
            for hl, h in enumerate(heads):
                def s5(hl=hl):
                    def trd(e):
                        ins = None
                        for qb in range(4):
                            oo = (qb * 2 + hl) * 128
                            ins = e.transpose(out=psC_b[:, qb * 128:(qb + 1) * 128], in_=OBn[:, oo:oo + 128], identity=identb[:])
                        return ins
                    P.op('pe', trd, reads=[("OBn", qb, hl) for qb in range(4)] + [identb.name], writes=["psC"])

                def s6(hl=hl):
                    if final:
                        P.op('dve', lambda e: e.tensor_copy(out=OT[:, (4 + hl) * 512:(5 + hl) * 512], in_=psC_b[:, 0:512]),
                             reads=["psC"], writes=[("OT", 4 + hl)])
                    else:
                        ci = sum(len(g) for g in groups[:gi]) + hl
                        o = ci * S_q + t * 512
                        P.op('dve', lambda e: e.tensor_copy(out=CAR[:, o:o + 512], in_=psC_b[:, 0:512]),
                             reads=["psC"], writes=[("CAR", ci, t)])
                steps.append((s5, 1))
                steps.append((s6, 1))
            return steps

        def finalize(x_d, y_d, b, t, n_qt, heads, gi, groups, nxt_parts=None):
            S_q = n_qt * 512
            n_car = sum(len(g) for g in groups[:gi])
            G = len(heads)
            fslots = [0, 1]

            def xload(qb):
                xs_ = fslots[qb % 2]
                row0 = (4 * t + qb) * 128
                P.op('sp', lambda e: e.dma_start(out=FB[xs_][:], in_=x_d.ap()[row0:row0 + 128, :]),
                     writes=[("FB", xs_)], dma=xkeys[xs_])

            xload(0)
            xload(1)
            for qb in range(4):
                fin_block(x_d, y_d, b, t, qb, S_q, n_car, G, fslots[qb % 2])
                if qb + 2 < 4:
                    xload(qb + 2)
                if nxt_parts:
                    nxt_parts.pop(0)()

        def fin_block(x_d, y_d, b, t, qb, S_q, n_car, G, xs_):
            row0 = (4 * t + qb) * 128
            ys_ = state['yslot'] % 2
            state['yslot'] += 1
            for n in range(2):
                def om(e, n=n):
                    ins = None
                    for fc in range(8):
                        if fc < 4:
                            lhsT = OT[:, fc * 512 + qb * 128: fc * 512 + (qb + 1) * 128]
                        elif fc - 4 < n_car:
                            o = (fc - 4) * S_q + t * 512 + qb * 128
                            lhsT = CAR[:, o:o + 128]
                        else:
                            o = (4 + fc - 4 - n_car) * 512 + qb * 128
                            lhsT = OT[:, o:o + 128]
                        ins = e.matmul(out=psB[:, n * 512:(n + 1) * 512], lhsT=lhsT,
                                       rhs=WO[:, fc * 1024 + n * 512: fc * 1024 + (n + 1) * 512],
                                       start=(fc == 0), stop=(fc == 7))
                    return ins
                rd = [("OT", fc) for fc in range(4 + G)] + [("CAR", ci, t) for ci in range(n_car)] + ["WO"]
                P.op('pe', om, reads=rd, writes=[bankB(n)])
            Z = FB[xs_]
            zk = ("FB", xs_)
            mo = qb * 2
            P.op('dve', lambda e: e.tensor_tensor(out=BXf[:], in0=psB[:, 0:1024], in1=gate_bc[:, b * 1024:(b + 1) * 1024], op=ALU.mult),
                 reads=[bankB(0), bankB(1), ("gate_bc", b, 0), ("gate_bc", b, 1)], writes=["BXf"])
            P.op('dve', lambda e: e.scalar_tensor_tensor(out=Z[:], in0=Z[:], scalar=ALPHA, in1=BXf[:], op0=ALU.mult, op1=ALU.add),
                 reads=[zk, "BXf"], writes=[zk])
            P.op('dve', lambda e: e.bn_stats(out=bnst[:, 0:6], in_=Z[:, 0:512]), reads=[zk], writes=["bnst0"])
            P.op('dve', lambda e: e.bn_stats(out=bnst[:, 6:12], in_=Z[:, 512:1024]), reads=[zk], writes=["bnst1"])
            P.op('dve', lambda e: e.bn_aggr(out=mv[:, mo:mo + 2], in_=bnst[:]), reads=["bnst0", "bnst1"], writes=[("mv", qb)])
            P.op('act', lambda e: e.activation(out=lrs[:, qb:qb + 1], in_=mv[:, mo + 1:mo + 2], func=AF.Sqrt, bias=eps_ln[:], scale=1.0),
                 reads=[("mv", qb), eps_ln.name], writes=[("lrs_a", qb)])
            P.op('dve', lambda e: e.reciprocal(out=lrs[:, qb:qb + 1], in_=lrs[:, qb:qb + 1]), reads=[("lrs_a", qb)], writes=[("lrs", qb)])
            P.op('dve', lambda e: e.scalar_tensor_tensor(out=Z[:], in0=Z[:], scalar=mv[:, mo:mo + 1], in1=lng_bc[:],
                                                         op0=ALU.subtract, op1=ALU.mult),
                 reads=[zk, ("mv", qb)], writes=[zk])
            P.op('dve', lambda e: e.scalar_tensor_tensor(out=Z[:], in0=Z[:], scalar=lrs[:, qb:qb + 1], in1=lnb_bc[:],
                                                         op0=ALU.mult, op1=ALU.add),
                 reads=[zk, ("lrs", qb)], writes=[zk])
            P.op('sp', lambda e: e.dma_start(out=y_d.ap()[row0:row0 + 128, :], in_=Z[:]),
                 reads=[zk], writes=[("yout", ys_)], dma=['y0', 'y1'][ys_])

        try:
            if kstop == 1:
                raise _Stop()
            run_job(xp_d, yp_d, 0, SP_, 8, False, [[0, 1], [2, 3]], [0])
            run_job(xs_d, ys_d, 1, SS_, 4, True, [[0], [1], [2], [3]], None)
        except _Stop:
            pass
        P.op('sp', lambda e: None, reads=[("yout", 0), ("yout", 1)] + (["dbg%d" % i for i in range(1, 11)] if dbg else []))

        with nc.Block() as blk2:
            P.emit(blk2)
    return nc


_NC_CACHE = {}


def _bucket_table():
    import math
    import jax
    import jax.numpy as jnp
    try:
        dev = jax.devices('cpu')[0]
    except Exception:
        dev = None
    def run():
        rel = jnp.arange(-1400, 1401, dtype=jnp.int32)
        sign = jnp.where(rel > 0, 16, 0)
        n = jnp.abs(rel)
        nf = jnp.maximum(n, 1).astype(jnp.float32)
        large = 8 + (jnp.log(nf / 8) / math.log(128 / 8) * 8).astype(jnp.int32)
        large = jnp.minimum(large, 15)
        return np.asarray((sign + jnp.where(n < 8, n, large)).astype(jnp.int32))
    if dev is not None:
        with jax.default_device(dev):
            return run()
    return run()


def kernel(**inp):
    f = lambda k: np.ascontiguousarray(np.asarray(inp[k], dtype=np.float32))
    x_prompt = f('x_prompt'); x_sample = f('x_sample')
    c_prompt = f('c_prompt'); c_sample = f('c_sample')
    w_in = f('w_in')[0]; w_out = f('w_out')[0]; w_ada = f('w_ada')[0]; b_ada = f('b_ada')[0]
    ln_g = f('ln_g'); ln_b = f('ln_b'); sink = f('attn_sink')[0]
    rel_bias = f('rel_bias')
    subw = f('subln_w')
    lamv = np.concatenate([f('lambda_q1')[0], f('lambda_k1')[0], f('lambda_q2')[0], f('lambda_k2')[0]])[None, :]

    colperm = np.concatenate([np.arange(64) + 64 * h for h in PERM])
    w_in_l = w_in.copy()
    w_in_l[:, 0:512] = w_in[:, 0:512][:, colperm]
    w_in_l[:, 768:1280] = w_in[:, 768:1280][:, colperm]
    w_out_l = w_out.copy()
    w_out_l[0:512, :] = w_out[0:512, :][colperm, :]
    sinkp = sink[PERM][None, :]
    bt = _bucket_table()

    def bk(r):
        return bt[r + 1400]
    kk = np.arange(128)[:, None]
    mm = np.arange(MD)[None, :]
    bd_idx = bk(kk - mm + 512)
    bd_strip = np.stack([rel_bias[bd_idx, 8 + h] for h in range(4)], axis=1).reshape(128, 4 * MD)
    qq = np.arange(384)[None, :]
    relw = kk - qq + 128
    bw_idx = bk(relw)
    bw_strip = np.stack([rel_bias[bw_idx, PERM[hp]] for hp in range(8)], axis=1).reshape(128, 8 * 384)
    maskw = np.where(np.abs(relw) <= 128, 0.0, -BIG).astype(np.float32)
    relfar = np.concatenate([rel_bias[15, 8:12], rel_bias[31, 8:12]])[None, :]
    identf = np.eye(128, dtype=np.float32)

    import os
    dbg = bool(os.environ.get('KDBG'))
    kstop = int(os.environ.get('KSTOP', '0'))
    key = 'nc%d_%d' % (dbg, kstop)
    if key not in _NC_CACHE:
        _NC_CACHE[key] = build_program(dbg, kstop)
    nc = _NC_CACHE[key]

    in_maps = []
    for c in range(NCORES):
        s, j = c // 4, c % 4
        xs = np.ascontiguousarray(np.roll(x_sample[s], -SQS * j, axis=0))
        cc = np.stack([c_prompt[c], c_sample[s]], axis=0)
        cT = np.ascontiguousarray(cc.reshape(2, 8, 128).transpose(2, 1, 0).reshape(128, 16))
        fl = np.zeros((1, 66), np.float32)
        fl[0, 0] = 1.0 if j > 0 else 0.0
        fl[0, 1] = 1.0 if j < 3 else 0.0
        for i in range(64):
            fl[0, 2 + i] = 1.0 if i >= 64 - 16 * j else 0.0
        in_maps.append({
            "xp": x_prompt[c], "xs": xs, "cT": cT,
            "w_in": w_in_l, "w_out": w_out_l, "w_ada": w_ada,
            "bcol": np.ascontiguousarray(b_ada.reshape(24, 128).T),
            "bgate": np.ascontiguousarray(b_ada[None, 2048:3072]),
            "ln_g": ln_g, "ln_b": ln_b, "sinkp": np.ascontiguousarray(sinkp),
            "lamv": np.ascontiguousarray(lamv), "subw": subw,
            "relfar": np.ascontiguousarray(relfar),
            "bd_strip": np.ascontiguousarray(bd_strip.astype(np.float32)),
            "bw_strip": np.ascontiguousarray(bw_strip.astype(np.float32)),
            "maskw": maskw, "flags": fl, "identf": identf,
        })
    res = run_bass_kernel_spmd(nc, in_maps, core_ids=list(range(NCORES)))
    y_prompt = np.empty((8, SP_, D), np.float32)
    y_sample = np.empty((2, SS_, D), np.float32)
    for c in range(NCORES):
        r = res.results[c]
        s, j = c // 4, c % 4
        y_prompt[c] = np.asarray(r["yp"], np.float32)
        y_sample[s, SQS * j: SQS * (j + 1)] = np.asarray(r["ys"], np.float32)
    return (y_prompt, y_sample)
```

```python
import numpy as np
import concourse.bass as bass
import concourse.mybir as mybir
from concourse.bass_utils import run_bass_kernel_spmd

F32 = mybir.dt.float32
BF16 = mybir.dt.bfloat16
AF = mybir.ActivationFunctionType
ALU = mybir.AluOpType

NCORES = 8
D = 1024
DC = 8
SP_ = 4096
SS_ = 8192
SQS = 2048
ALPHA = 2.0 ** 0.25
LAMBDA_INIT = 0.2
BIG = 30000.0
LN_EPS = 1e-5
SUBLN_EPS = 1e-5
PERM = [0, 4, 1, 5, 2, 6, 3, 7]
MD = 1152


class Prog:
    ENG = ('sp', 'act', 'pe', 'dve', 'pool')

    def __init__(self, sems, dsems):
        self.sem = sems
        self.dsem = dsems
        self.dcnt = {k: 0 for k in dsems}
        self.base = {e: 0 for e in self.ENG}
        self.reset_block()

    def reset_block(self):
        self.ops = {e: [] for e in self.ENG}
        self.res = {}

    def op(self, eng, fn, reads=(), writes=(), dma=None):
        ops = self.ops[eng]
        idx = len(ops)
        if dma is not None:
            self.dcnt[dma] += 1
            ev = ('dma', dma, self.dcnt[dma])
        else:
            ev = (eng, idx)
        deps = {}

        def add(d, raw):
            if d[0] == 'dma':
                d = ('dma', d[1], self.dcnt[d[1]])
            if raw or d not in deps:
                deps[d] = raw or deps.get(d, False)

        def is_ps(k):
            return isinstance(k, str) and k.startswith('ps')

        ps_keys = {}
        for k in reads:
            if is_ps(k):
                ps_keys[k] = False
        for k in writes:
            if is_ps(k):
                ps_keys[k] = True
        for k, isw in ps_keys.items():
            r = self.res.get(k)
            if r is not None and r[0] is not None:
                add(r[0], (not isw) and r[2])
        for k in reads:
            if is_ps(k):
                continue
            r = self.res.get(k)
            if r is not None and r[0] is not None:
                add(r[0], True)
        for k in writes:
            if is_ps(k):
                continue
            r = self.res.get(k)
            if r is not None:
                if r[0] is not None:
                    add(r[0], False)
                for rv in r[1].values():
                    add(rv, False)
        for k in reads:
            if is_ps(k):
                continue
            r = self.res.setdefault(k, [None, {}])
            r[1][ev if dma is not None else eng] = ev
        for k in writes:
            if is_ps(k):
                continue
            self.res[k] = [ev, {}]
        for k, isw in ps_keys.items():
            self.res[k] = [ev, {}, isw]
        ops.append((fn, deps, ev, dma))
        return ev

    def emit(self, block):
        need = {e: set() for e in self.ENG}
        for e in self.ENG:
            for (fn, deps, ev, dma) in self.ops[e]:
                for d in deps:
                    if d[0] != 'dma':
                        need[d[0]].add(d[1])
        ordn = {}
        for e in self.ENG:
            s = sorted(need[e])
            ordn[e] = {idx: self.base[e] + i + 1 for i, idx in enumerate(s)}

        def body(en):
            def f(eng):
                seen = {}
                for idx, (fn, deps, ev, dma) in enumerate(self.ops[en]):
                    for d, raw in deps.items():
                        if d[0] == 'dma':
                            if dma is not None and d[1] == dma:
                                continue
                            key = ('dma', d[1])
                            val = 16 * d[2]
                            sem = self.dsem[d[1]]
                        else:
                            if d[0] == en:
                                if (not raw) or en == 'pe' or en == 'sp':
                                    continue
                            key = d[0]
                            val = ordn[d[0]][d[1]]
                            sem = self.sem[d[0]]
                        if seen.get(key, 0) >= val:
                            continue
                        eng.wait_ge(sem, val)
                        seen[key] = val
                    ins = fn(eng)
                    if dma is not None:
                        ins.then_inc(self.dsem[dma], 16)
                    elif idx in ordn[en]:
                        ins.then_inc(self.sem[en], 1)
            return f

        block.sync(body('sp'))
        block.scalar(body('act'))
        block.tensor(body('pe'))
        block.vector(body('dve'))
        block.gpsimd(body('pool'))
        for e in self.ENG:
            self.base[e] += len(ordn[e])
        self.reset_block()


class _CopyShim:
    def __init__(self, e):
        self.e = e

    def tensor_copy(self, out, in_):
        return self.e.copy(out=out, in_=in_)


def dap(t, off, dims):
    return bass.AP(t, off, [list(d) for d in dims])


class _Stop(Exception):
    pass


def build_program(dbg=False, kstop=0):
    nc = bass.Bass("TRN2", target_bir_lowering=False)
    if dbg:
        dbg_ot = nc.dram_tensor("dbg_ot", [128, 3072], BF16, kind="ExternalOutput")
        dbg_car = nc.dram_tensor("dbg_car", [128, 1024], BF16, kind="ExternalOutput")
        dbg_c = nc.dram_tensor("dbg_c", [128, 64], F32, kind="ExternalOutput")
        dbg_sg = nc.dram_tensor("dbg_sg", [128, 2048 + 1024], BF16, kind="ExternalOutput")

    def din(name, shape):
        return nc.dram_tensor(name, list(shape), F32, kind="ExternalInput")

    xp_d = din("xp", [SP_, D])
    xs_d = din("xs", [SS_, D])
    cT_d = din("cT", [128, 16])
    win_d = din("w_in", [D, 3328])
    wout_d = din("w_out", [D, D])
    wada_d = din("w_ada", [D, 3072])
    bcol_d = din("bcol", [128, 24])
    bgate_d = din("bgate", [1, 1024])
    lng_d = din("ln_g", [1, 1024])
    lnb_d = din("ln_b", [1, 1024])
    sink_d = din("sinkp", [1, 8])
    lamv_d = din("lamv", [1, 256])
    subw_d = din("subw", [1, 128])
    relfar_d = din("relfar", [1, 8])
    bd_d = din("bd_strip", [128, 4 * MD])
    bw_d = din("bw_strip", [128, 8 * 384])
    maskw_d = din("maskw", [128, 384])
    flags_d = din("flags", [1, 66])
    ident_d = din("identf", [128, 128])
    yp_d = nc.dram_tensor("yp", [SP_, D], F32, kind="ExternalOutput")
    ys_d = nc.dram_tensor("ys", [SQS, D], F32, kind="ExternalOutput")

    from contextlib import ExitStack
    es = ExitStack()

    def sb(name, shape, dt=F32):
        return es.enter_context(nc.sbuf_tensor("sb_" + name, list(shape), dt))

    def ps(name, shape, dt=F32):
        return es.enter_context(nc.psum_tensor("ps_" + name, list(shape), dt))

    with es:
        sems = {e: es.enter_context(nc.semaphore("s_" + e)) for e in ('act', 'pe', 'dve', 'pool')}
        dkeys = ['c0', 'wa', 'x0', 'x1', 'xb0', 'xb1', 'xb2', 'xb3', 'xb4', 'y0', 'y1', 'dbg']
        dsems = {k: es.enter_context(nc.semaphore("d_" + k)) for k in dkeys}
        P = Prog(sems, dsems)

        identf = sb("identf", [128, 128])
        identb = sb("identb", [128, 128], BF16)
        s1p = sb("s1p", [128, 16])
        shf = sb("shf", [128, 16])
        gate_bc = sb("gate_bc", [128, 2048])
        lng_bc = sb("lng_bc", [128, 1024])
        lnb_bc = sb("lnb_bc", [128, 1024])
        subw_bc = sb("subw_bc", [128, 128])
        neglam = sb("neglam", [128, 1])
        expsink = sb("expsink", [128, 8])
        cfar = sb("cfar", [128, 8])
        cdiff = sb("cdiff", [128, 4])
        farS = sb("farS", [128, 4 * 64])
        flags = sb("flags", [128, 66])
        negLR = sb("negLR", [128, 2])
        omf = sb("omf", [128, 2])
        BW = sb("BW", [128, 8 * 384], BF16)
        BXW = sb("BXW", [128, 2048], BF16)
        zero_c = sb("zero_c", [128, 1])
        eps_ln = sb("eps_ln", [128, 1])
        eps_sub = sb("eps_sub", [128, 1])

        psA = ps("psA", [128, 2048])
        psB = ps("psB", [128, 1536])
        psC = ps("psC", [128, 512])
        psA_b = psA.bitcast(BF16)
        psB_b = psB.bitcast(BF16)
        psC_b = psC.bitcast(BF16)

        def bankA(i):
            return "psA%d" % i

        def bankB(i):
            return "psB%d" % i

        WIs = sb("WIs", [128, 8 * 1280], BF16)
        WO = sb("WO", [128, 8 * 1024], BF16)
        FB = [sb("FB%d" % i, [128, 1024]) for i in range(2)]
        xkeys = ['x0', 'x1']
        xbkeys = ['xb0', 'xb1', 'xb2', 'xb3', 'xb4']
        NPF = 3
        state = {'xslot': 0, 'evac': 0, 'yslot': 0}

        wl_state = {'i': 0}

        def stage_cast(src_ap, ncols, cast_fn, writes, reads_extra=(), dst_dims=None, engs=('pool', 'dve'), q='sp'):
            s_ = state['xslot'] % 2
            state['xslot'] += 1
            P.op(q, lambda e, s_=s_: e.dma_start(
                out=(FB[s_][:, 0:ncols] if dst_dims is None else dap(FB[s_], 0, dst_dims)), in_=src_ap),
                 writes=[("FB", s_)], dma=xkeys[s_])
            eng = engs[wl_state['i'] % 2]
            wl_state['i'] += 1
            P.op(eng, lambda e, s_=s_: cast_fn(_CopyShim(e) if eng == 'act' else e, FB[s_]), reads=[("FB", s_)] + list(reads_extra), writes=writes)

        def load_static_weights():
            for dc in range(8):
                r0 = dc * 128
                def castA(e, fb, dc=dc):
                    e.tensor_copy(out=WIs[:, dc * 1280: dc * 1280 + 512], in_=fb[:, 0:512])
                    return e.tensor_copy(out=WIs[:, dc * 1280 + 1024: dc * 1280 + 1280], in_=fb[:, 512:768])
                stage_cast(win_d.ap()[r0:r0 + 128, 0:768], 768, castA, writes=["WIs"], q='act')
                def castB(e, fb, dc=dc):
                    return e.tensor_copy(out=WIs[:, dc * 1280 + 512: dc * 1280 + 1024], in_=fb[:, 0:512])
                stage_cast(win_d.ap()[r0:r0 + 128, 768:1280], 512, castB, writes=["WIs"], q='act')
                def castO(e, fb, dc=dc):
                    return e.tensor_copy(out=WO[:, dc * 1024:(dc + 1) * 1024], in_=fb[:, 0:1024])
                stage_cast(wout_d.ap()[r0:r0 + 128, :], 1024, castO, writes=["WO"], q='act')

        with ExitStack() as es1:
            def sb1(name, shape, dt=F32):
                return es1.enter_context(nc.sbuf_tensor("sb_" + name, list(shape), dt))

            WA = sb1("WA", [128, 4 * 3072])
            cT = sb1("cT", [128, 16])
            sc = sb1("sc", [128, 16])
            sc_rep = sb1("sc_rep", [128, 2 * 8 * 128])
            bcol = sb1("bcol", [128, 24])
            bgate_bc = sb1("bgate_bc", [128, 1024])
            lamv = sb1("lamv", [128, 256])
            lamp = sb1("lamp", [128, 128])
            lams = sb1("lams", [128, 4])
            sinkb = sb1("sinkb", [128, 8])
            subw_raw = sb1("subw_raw", [128, 128])
            maskw = sb1("maskw", [128, 384])
            BWf = sb1("BWf", [128, 8 * 384])

            def ld(dst_t, src):
                P.op('sp', lambda e: e.dma_start(out=dst_t[:], in_=src), writes=[dst_t.name], dma='c0')

            ld(identf, ident_d.ap())
            ld(cT, cT_d.ap())
            ld(bcol, bcol_d.ap())
            ld(bgate_bc, dap(bgate_d, 0, [[0, 128], [1, 1024]]))
            ld(lng_bc, dap(lng_d, 0, [[0, 128], [1, 1024]]))
            ld(lnb_bc, dap(lnb_d, 0, [[0, 128], [1, 1024]]))
            ld(sinkb, dap(sink_d, 0, [[0, 128], [1, 8]]))
            ld(lamv, dap(lamv_d, 0, [[0, 128], [1, 256]]))
            ld(subw_raw, dap(subw_d, 0, [[0, 128], [1, 128]]))
            ld(cfar, dap(relfar_d, 0, [[0, 128], [1, 8]]))
            ld(flags, dap(flags_d, 0, [[0, 128], [1, 66]]))
            ld(BWf, bw_d.ap())
            ld(maskw, maskw_d.ap())
            def load_wa(dcs):
                for dc in dcs:
                    P.op('sp', lambda e, dc=dc: e.dma_start(
                        out=WA[:, (dc % 4) * 3072:(dc % 4 + 1) * 3072], in_=wada_d.ap()[dc * 128:(dc + 1) * 128, :]),
                        writes=[("WA", dc % 4)], dma='wa')
            load_wa(range(0, 4))

            P.op('pool', lambda e: e.memset(zero_c[:], 0.0), writes=[zero_c.name])
            P.op('pool', lambda e: e.memset(eps_ln[:], LN_EPS), writes=[eps_ln.name])
            P.op('pool', lambda e: e.memset(eps_sub[:], SUBLN_EPS), writes=[eps_sub.name])
            P.op('dve', lambda e: e.tensor_copy(out=identb[:], in_=identf[:]),
                 reads=[identf.name], writes=[identb.name])
            P.op('act', lambda e: e.activation(out=sc[:], in_=cT[:], func=AF.Silu),
                 reads=[cT.name], writes=[sc.name])
            P.op('act', lambda e: e.activation(out=expsink[:], in_=sinkb[:], func=AF.Exp),
                 reads=[sinkb.name], writes=[expsink.name])
            P.op('dve', lambda e: e.tensor_tensor(out=lamp[:, 0:64], in0=lamv[:, 0:64], in1=lamv[:, 64:128], op=ALU.mult),
                 reads=[lamv.name], writes=["lamp0"])
            P.op('dve', lambda e: e.tensor_tensor(out=lamp[:, 64:128], in0=lamv[:, 128:192], in1=lamv[:, 192:256], op=ALU.mult),
                 reads=[lamv.name], writes=["lamp1"])
            P.op('dve', lambda e: e.reduce_sum(out=lams[:, 0:1], in_=lamp[:, 0:64], axis=mybir.AxisListType.X),
                 reads=["lamp0"], writes=["lams0"])
            P.op('dve', lambda e: e.reduce_sum(out=lams[:, 1:2], in_=lamp[:, 64:128], axis=mybir.AxisListType.X),
                 reads=["lamp1"], writes=["lams1"])
            P.op('act', lambda e: e.activation(out=lams[:, 2:4], in_=lams[:, 0:2], func=AF.Exp),
                 reads=["lams0", "lams1"], writes=["lams23"])
            P.op('dve', lambda e: e.tensor_tensor(out=neglam[:], in0=lams[:, 3:4], in1=lams[:, 2:3], op=ALU.subtract),
                 reads=["lams23"], writes=["neglam_a"])
            P.op('dve', lambda e: e.tensor_scalar(out=neglam[:], in0=neglam[:], scalar1=-LAMBDA_INIT, scalar2=None, op0=ALU.add),
                 reads=["neglam_a"], writes=[neglam.name])
            P.op('dve', lambda e: e.tensor_scalar(out=subw_bc[:], in0=subw_raw[:], scalar1=1.0 - LAMBDA_INIT, scalar2=None, op0=ALU.mult),
                 reads=[subw_raw.name], writes=[subw_bc.name])
            P.op('dve', lambda e: e.tensor_tensor(out=cdiff[:], in0=cfar[:, 0:4], in1=cfar[:, 4:8], op=ALU.subtract),
                 reads=[cfar.name], writes=[cdiff.name])
            for h in range(4):
                P.op('dve', lambda e, h=h: e.tensor_scalar(
                    out=farS[:, h * 64:(h + 1) * 64], in0=flags[:, 2:66],
                    scalar1=cdiff[:, h:h + 1], scalar2=cfar[:, 4 + h:5 + h], op0=ALU.mult, op1=ALU.add),
                    reads=[flags.name, cdiff.name, cfar.name], writes=[("farS", h)])
            P.op('dve', lambda e: e.tensor_scalar(out=negLR[:], in0=flags[:, 0:2], scalar1=-1.0, scalar2=BIG, op0=ALU.add, op1=ALU.mult),
                 reads=[flags.name], writes=[negLR.name])
            for h in range(8):
                P.op('dve', lambda e, h=h: e.tensor_tensor(
                    out=BWf[:, h * 384:(h + 1) * 384], in0=BWf[:, h * 384:(h + 1) * 384], in1=maskw[:], op=ALU.add),
                    reads=[BWf.name, maskw.name], writes=[("BWf", h)])
                P.op('pool', lambda e, h=h: e.tensor_copy(out=BW[:, h * 384:(h + 1) * 384], in_=BWf[:, h * 384:(h + 1) * 384]),
                     reads=[("BWf", h)], writes=[("BW", h)])
            P.op('dve', lambda e: e.tensor_scalar(
                out=dap(BXW, 0, [[2048, 128], [128, 8], [1, 128]]),
                in0=dap(BWf, 256, [[8 * 384, 128], [384, 8], [1, 128]]),
                scalar1=negLR[:, 0:1], scalar2=None, op0=ALU.add),
                reads=[("BWf", hh) for hh in range(8)] + [negLR.name], writes=["BXWL"])
            P.op('dve', lambda e: e.tensor_scalar(
                out=dap(BXW, 1024, [[2048, 128], [128, 8], [1, 128]]),
                in0=dap(BWf, 0, [[8 * 384, 128], [384, 8], [1, 128]]),
                scalar1=negLR[:, 1:2], scalar2=None, op0=ALU.add),
                reads=[("BWf", hh) for hh in range(8)] + [negLR.name], writes=["BXWR"])
            for b in range(2):
                for dc in range(8):
                    o = (b * 8 + dc) * 128
                    P.op('pool', lambda e, o=o, b=b, dc=dc: e.tensor_copy(
                        out=sc_rep[:, o:o + 128], in_=sc[:, dc * 2 + b:dc * 2 + b + 1].broadcast_to([128, 128])),
                        reads=[sc.name], writes=[("sc_rep", b, dc)])

            def mod_cols_half(half):
                def f(e):
                    ins = None
                    for dc in range(half * 4, half * 4 + 4):
                        for fc in range(16):
                            ins = e.matmul(out=psC[:, fc * 2:fc * 2 + 2],
                                           lhsT=WA[:, (dc % 4) * 3072 + fc * 128: (dc % 4) * 3072 + fc * 128 + 128],
                                           rhs=sc[:, dc * 2:dc * 2 + 2],
                                           start=(dc == 0 and fc == 0), stop=(dc == 7), skip_group_check=True)
                    return ins
                P.op('pe', f, reads=[("WA", k) for k in range(4)] + [sc.name], writes=["psC"])

            def gate_half(half):
                for b in range(2):
                    for n in range(2):
                        def gmm(e, b=b, n=n):
                            ins = None
                            for dc in range(half * 4, half * 4 + 4):
                                o = (b * 8 + dc) * 128
                                ins = e.matmul(out=psA[:, (b * 2 + n) * 512:(b * 2 + n + 1) * 512],
                                               lhsT=sc_rep[:, o:o + 128],
                                               rhs=WA[:, (dc % 4) * 3072 + 2048 + n * 512: (dc % 4) * 3072 + 2048 + (n + 1) * 512],
                                               start=(dc == 0), stop=(dc == 7))
                            return ins
                        P.op('pe', gmm, reads=[("WA", k) for k in range(4)] + [("sc_rep", b, dc) for dc in range(8)],
                             writes=[bankA(b * 2 + n)])

            load_static_weights()
            mod_cols_half(0)
            gate_half(0)
            load_wa(range(4, 8))
            mod_cols_half(1)
            gate_half(1)
            for b in range(2):
                P.op('dve', lambda e, b=b: e.tensor_tensor(
                    out=shf[:, b * 8:(b + 1) * 8], in0=psC[:, b:16:2],
                    in1=bcol[:, 0:8], op=ALU.add),
                    reads=["psC", bcol.name], writes=[("shf", b)])
                P.op('dve', lambda e, b=b: e.scalar_tensor_tensor(
                    out=s1p[:, b * 8:(b + 1) * 8], in0=psC[:, 16 + b:32:2], scalar=1.0, in1=bcol[:, 8:16],
                    op0=ALU.add, op1=ALU.add),
                    reads=["psC", bcol.name], writes=[("s1p", b)])
            for b in range(2):
                for n in range(2):
                    P.op('dve', lambda e, b=b, n=n: e.tensor_tensor(
                        out=gate_bc[:, b * 1024 + n * 512: b * 1024 + (n + 1) * 512],
                        in0=psA[:, (b * 2 + n) * 512:(b * 2 + n + 1) * 512],
                        in1=bgate_bc[:, n * 512:(n + 1) * 512], op=ALU.add),
                        reads=[bankA(b * 2 + n), bgate_bc.name], writes=[("gate_bc", b, n)])
            with nc.Block() as blk1:
                P.emit(blk1)

        WIg = sb("WIg", [128, 8 * 1024], BF16)
        BD = sb("BD", [128, 2 * MD], BF16)
        KT = sb("KT", [128, 8192], BF16)
        VA = sb("VA", [128, 64 * 130], BF16)
        CAR = sb("CAR", [128, 8192], BF16)
        XBF = [sb("xbf%d" % i, [128, 1024], BF16) for i in range(4)]
        uT = sb("uT", [128, 8 * 768], BF16)
        QT = [sb("QT%d" % i, [128, 2 * 512], BF16) for i in range(2)]
        QAT = sb("QAT", [128, 4 * 512], BF16)
        SGb = [sb("SGb%d" % i, [128, 4 * 256], BF16) for i in range(2)]
        SGa = sb("SGa", [128, 4 * 512], BF16)
        KAT = sb("KAT", [128, 768], BF16)
        VAW = sb("VAW", [128, 6 * 2 * 66], BF16)
        OT = sb("OT", [128, 6 * 512], BF16)
        PT = [sb("PT%d" % i, [128, 1024], BF16) for i in range(2)]
        OBf = sb("OBf", [128, 4 * 2 * 128])
        OBn = sb("OBn", [128, 4 * 2 * 128], BF16)
        OAg = sb("OAg", [128, 4 * 512], BF16)
        BX = sb("BX", [128, 1024], BF16)
        BXf = sb("BXf", [128, 1024])
        rec = sb("rec", [128, 16])
        ssq = sb("ssq", [128, 8])
        rstd = sb("rstd", [128, 8])
        sqj = sb("sqj", [128, 128])
        den = sb("den", [128, 16])
        bnst = sb("bnst", [128, 12])
        mv = sb("mv", [128, 8])
        lrs = sb("lrs", [128, 4])

        def load_kv_weights(heads):
            for hl, h in enumerate(heads):
                for dc in range(8):
                    src = dap(win_d, (dc * 128) * 3328 + 1792 + h * 128, [[3328, 128], [512, 2], [1, 128]])
                    def castG(e, fb, dc=dc, hl=hl):
                        return e.tensor_copy(
                            out=dap(WIg, dc * 1024 + 256 + hl * 128, [[8192, 128], [256, 2], [1, 128]]),
                            in_=dap(fb, 0, [[1024, 128], [128, 2], [1, 128]]))
                    stage_cast(src, 256, castG, writes=["WIgkv"], dst_dims=[[1024, 128], [128, 2], [1, 128]])

        def q_weight_thunks(heads):
            th = []
            for hl, h in enumerate(heads):
                for dc in range(8):
                    def t1(dc=dc, hl=hl, h=h):
                        src = dap(win_d, (dc * 128) * 3328 + 1280 + h * 128, [[3328, 128], [1536, 2], [1, 128]])
                        def castG(e, fb):
                            return e.tensor_copy(
                                out=dap(WIg, dc * 1024 + hl * 128, [[8192, 128], [768, 2], [1, 128]]),
                                in_=dap(fb, 0, [[1024, 128], [128, 2], [1, 128]]))
                        stage_cast(src, 256, castG, writes=["WIgq"], dst_dims=[[1024, 128], [128, 2], [1, 128]],
                                   engs=('act', 'dve'))
                    th.append(t1)
                def t2(hl=hl, h=h):
                    def castD1(e, fb):
                        return e.tensor_copy(out=BD[:, hl * MD: hl * MD + 1024], in_=fb[:, 0:1024])
                    stage_cast(bd_d.ap()[:, h * MD: h * MD + 1024], 1024, castD1, writes=[("BD", hl)], engs=('act', 'dve'))
                def t3(hl=hl, h=h):
                    def castD2(e, fb):
                        return e.tensor_copy(out=BD[:, hl * MD + 1024: hl * MD + MD], in_=fb[:, 0:MD - 1024])
                    stage_cast(bd_d.ap()[:, h * MD + 1024: (h + 1) * MD], MD - 1024, castD2, writes=[("BD", hl)],
                               engs=('act', 'dve'))
                th.append(t2)
                th.append(t3)
            return th

        P.op('pool', lambda e: e.memset(VA[:], 1.0), writes=["VA_all"])
        P.op('pool', lambda e: e.memset(VAW[:], 1.0), writes=["VAW_all"])

        xb_state = {'k': 0, 'm': 0}

        def prefetch_x(src_d, row0):
            sl = xb_state['k'] % 4
            xb_state['k'] += 1
            P.op('pool', lambda e, sl=sl: e.dma_start(out=XBF[sl][:], in_=src_d.ap()[row0:row0 + 128, :]),
                 writes=[("xbf", sl)], dma=xbkeys[sl])
            return sl

        def make_uT_from(sl, bslot, b):
            par = xb_state['m'] % 2
            xb_state['m'] += 1
            if par == 0:
                evk, odk, evb, odb, evo, odo = bankA(0), "psC", psA_b, psC_b, 0, 0
            else:
                evk, odk, evb, odb, evo, odo = bankB(0), bankB(1), psB_b, psB_b, 0, 1024

            dve_dcs = [0, 2, 4, 6, 7]
            act_dcs = [1, 3, 5]

            def tsrc(dc):
                if dc in dve_dcs:
                    k = dve_dcs.index(dc)
                    return evb[:, evo + k * 128: evo + (k + 1) * 128]
                k = act_dcs.index(dc)
                return odb[:, odo + k * 128: odo + (k + 1) * 128]

            def tr(e):
                ins = None
                for dc in range(8):
                    ins = e.transpose(out=tsrc(dc), in_=XBF[sl][:, dc * 128:(dc + 1) * 128], identity=identb[:])
                return ins
            P.op('pe', tr, reads=[("xbf", sl), identb.name], writes=[evk, odk])
            for dc in range(8):
                o = dc * 768 + bslot * 128
                if dc in dve_dcs:
                    P.op('dve', lambda e, dc=dc, o=o: e.tensor_scalar(
                        out=uT[:, o:o + 128], in0=tsrc(dc),
                        scalar1=s1p[:, b * 8 + dc:b * 8 + dc + 1], scalar2=shf[:, b * 8 + dc:b * 8 + dc + 1],
                        op0=ALU.mult, op1=ALU.add),
                        reads=[evk], writes=[("uT", bslot, dc)])
                else:
                    P.op('act', lambda e, dc=dc, o=o: e.activation(
                        out=uT[:, o:o + 128], in_=tsrc(dc), func=AF.Identity,
                        bias=shf[:, b * 8 + dc:b * 8 + dc + 1], scale=s1p[:, b * 8 + dc:b * 8 + dc + 1]),
                        reads=[odk], writes=[("uT", bslot, dc)])

        def make_uT_list(src_d, items, b):
            slots = {}
            for i in range(min(NPF, len(items))):
                slots[i] = prefetch_x(src_d, items[i][1] * 128)
            for i, (bs, lb) in enumerate(items):
                if i + NPF < len(items):
                    slots[i + NPF] = prefetch_x(src_d, items[i + NPF][1] * 128)
                make_uT_from(slots[i], bs, b)

        def uT_keys(bslots):
            return [("uT", bs, dc) for bs in bslots for dc in range(8)]

        def proj_fm(W, pitch, col, bslot0, nb, pbank, reads_extra=()):
            def f(e):
                ins = None
                for dc in range(8):
                    ins = e.matmul(out=psA[:, pbank * 512: pbank * 512 + nb * 128],
                                   lhsT=W[:, dc * pitch + col: dc * pitch + col + 128],
                                   rhs=uT[:, dc * 768 + bslot0 * 128: dc * 768 + (bslot0 + nb) * 128],
                                   start=(dc == 0), stop=(dc == 7))
                return ins
            P.op('pe', f, reads=uT_keys(range(bslot0, bslot0 + nb)) + list(reads_extra), writes=[bankA(pbank)])

        def proj_tm(W, pitch, col, ncol, bslot, pbank, reads_extra=()):
            def f(e):
                ins = None
                for dc in range(8):
                    ins = e.matmul(out=psA[:, pbank * 512: pbank * 512 + ncol],
                                   lhsT=uT[:, dc * 768 + bslot * 128: dc * 768 + (bslot + 1) * 128],
                                   rhs=W[:, dc * pitch + col: dc * pitch + col + ncol],
                                   start=(dc == 0), stop=(dc == 7))
                return ins
            P.op('pe', f, reads=uT_keys([bslot]) + list(reads_extra), writes=[bankA(pbank)])

        pb_state = {'i': 0}

        def next_pbank():
            b = 1 + pb_state['i'] % 3
            pb_state['i'] += 1
            return b

        def evac_copy(out_ap, in_ap, reads, writes, scale=None, which=None):
            if which is None:
                which = state['evac'] % 2
                state['evac'] += 1
            if which == 0:
                if scale is None:
                    P.op('dve', lambda e: e.tensor_copy(out=out_ap, in_=in_ap), reads=reads, writes=writes)
                else:
                    P.op('dve', lambda e: e.tensor_scalar(out=out_ap, in0=in_ap, scalar1=scale, scalar2=None, op0=ALU.mult),
                         reads=reads, writes=writes)
            else:
                if scale is None:
                    P.op('act', lambda e: e.copy(out=out_ap, in_=in_ap), reads=reads, writes=writes)
                else:
                    P.op('act', lambda e: e.mul(out=out_ap, in_=in_ap, mul=scale), reads=reads, writes=writes)

        def run_job(x_d, y_d, b, S_kv, n_qt, is_sample, groups, next_first_heads):
            NB = S_kv // 128
            n_groups = len(groups)
            for gi, heads in enumerate(groups):
                G = len(heads)
                final = (gi == n_groups - 1)
                if not state.get('kv_preloaded'):
                    load_kv_weights(heads)
                state['kv_preloaded'] = False
                qw = q_weight_thunks(heads)
                nxt_heads = groups[gi + 1] if gi + 1 < n_groups else next_first_heads

                NT = NB // 2

                kv_slots = {}
                for blk in range(NPF):
                    kv_slots[blk] = prefetch_x(x_d, blk * 128)

                def front(n):
                    base = (n % 3) * 2
                    for k in range(2):
                        blk = 2 * n + k
                        if blk + NPF < NB:
                            kv_slots[blk + NPF] = prefetch_x(x_d, (blk + NPF) * 128)
                        make_uT_from(kv_slots[blk], base + k, b)

                def back(n):
                    base = (n % 3) * 2
                    for hl in range(G):
                        pbk = next_pbank()
                        proj_fm(WIg, 1024, 256 + hl * 128, base, 2, pbk, reads_extra=["WIgkv"])
                        evac_copy(KT[:, hl * S_kv + n * 256: hl * S_kv + (n + 1) * 256],
                                  psA[:, pbk * 512: pbk * 512 + 256],
                                  reads=[bankA(pbk)], writes=[("KT", hl, 2 * n), ("KT", hl, 2 * n + 1)])
                    for k in range(2):
                        tb = 2 * n + k
                        pbk = next_pbank()
                        proj_tm(WIg, 1024, 512, G * 128, base + k, pbk, reads_extra=["WIgkv"])
                        state['evac'] += 1
                        for hl in range(G):
                            o = (tb * G + hl) * 130
                            evac_copy(VA[:, o:o + 128], psA[:, pbk * 512 + hl * 128: pbk * 512 + (hl + 1) * 128],
                                      reads=[bankA(pbk), "VA_all"], writes=[("VA", hl, tb)], which=state['evac'] % 2)

                for n in range(NT + 1):
                    if n < NT:
                        front(n)
                    if kstop == 22:
                        raise _Stop()
                    if n >= 1:
                        back(n - 1)
                    if n >= 1 and qw:
                        qw.pop(0)()
                while qw:
                    qw.pop(0)()
                    if kstop == 23 and n == 1:
                        raise _Stop()
                    if kstop == 2 and n == 2:
                        raise _Stop()
                if kstop == 3:
                    raise _Stop()

                for t in range(n_qt):
                    if t == n_qt - 1 and nxt_heads is not None:
                        load_kv_weights(nxt_heads)
                        state['kv_preloaded'] = True
                    q_phase(x_d, y_d, b, S_kv, NB, t, n_qt, is_sample, heads, final, gi, groups)
                    if kstop == 4:
                        raise _Stop()
                    if kstop == 5 and final:
                        raise _Stop()
                if kstop == 6 and final:
                    raise _Stop()

        def tile_blocks(t, n_qt, NB, is_sample, final):
            blocks = []
            if final:
                if t > 0:
                    blocks.append((0, 4 * t - 1))
                elif is_sample:
                    blocks.append((0, NB - 1))
                for i in range(4):
                    blocks.append((1 + i, 4 * t + i))
                if t < n_qt - 1 or is_sample:
                    blocks.append((5, 4 * t + 4))
            else:
                for i in range(4):
                    blocks.append((1 + i, 4 * t + i))
            return blocks

        def prepA_steps(x_d, b, t, n_qt, NB, is_sample, final, G):
            blocks = tile_blocks(t, n_qt, NB, is_sample, final)
            par = t % 2
            n = len(blocks)
            slots = {}
            steps = []

            def pf(i):
                slots[i] = prefetch_x(x_d, blocks[i][1] * 128)

            def s_pf0():
                for i in range(min(NPF, n)):
                    pf(i)
            steps.append((s_pf0, 4))
            for i, (bs, lb) in enumerate(blocks):
                def s_tr(i=i):
                    if i + NPF < n:
                        pf(i + NPF)
                    sl = slots[i]

                    def tr(e):
                        ins = None
                        for dc in range(8):
                            ins = e.transpose(out=psC_b[:, dc * 128:(dc + 1) * 128], in_=XBF[sl][:, dc * 128:(dc + 1) * 128],
                                              identity=identb[:])
                        return ins
                    P.op('pe', tr, reads=[("xbf", sl), identb.name], writes=["psC"])

                def s_ev(bs=bs):
                    for dc in range(8):
                        o = dc * 768 + bs * 128
                        P.op('dve', lambda e, dc=dc, o=o: e.tensor_scalar(
                            out=uT[:, o:o + 128], in0=psC_b[:, dc * 128:(dc + 1) * 128],
                            scalar1=s1p[:, b * 8 + dc:b * 8 + dc + 1], scalar2=shf[:, b * 8 + dc:b * 8 + dc + 1],
                            op0=ALU.mult, op1=ALU.add),
                            reads=["psC"], writes=[("uT", bs, dc)])
                steps.append((s_tr, 0))
                steps.append((s_ev, 2))

            def fm_part(col, dcs):
                def f(e):
                    ins = None
                    for dc in dcs:
                        ins = e.matmul(out=psC[:, 0:512],
                                       lhsT=WIg[:, dc * 1024 + col: dc * 1024 + col + 128],
                                       rhs=uT[:, dc * 768 + 128: dc * 768 + 640],
                                       start=(dc == 0), stop=(dc == 7))
                    return ins
                P.op('pe', f, reads=uT_keys(range(1, 5)) + ["WIgq"], writes=["psC"])

            for hl in range(G):
                steps.append((lambda hl=hl: fm_part(hl * 128, range(0, 4)), 0))
                steps.append((lambda hl=hl: fm_part(hl * 128, range(4, 8)), 0))

                def s_qe(hl=hl):
                    P.op('dve', lambda e: e.tensor_scalar(out=QT[par][:, hl * 512:(hl + 1) * 512], in0=psC[:, 0:512],
                                                          scalar1=0.125, scalar2=None, op0=ALU.mult),
                         reads=["psC"], writes=[("QT", par, hl)])
                steps.append((s_qe, 2))
            for qb in range(4):
                def s_g(qb=qb):
                    def f(e):
                        ins = None
                        for dc in range(8):
                            ins = e.matmul(out=psC[:, 0:G * 128],
                                           lhsT=uT[:, dc * 768 + (1 + qb) * 128: dc * 768 + (2 + qb) * 128],
                                           rhs=WIg[:, dc * 1024 + 768: dc * 1024 + 768 + G * 128],
                                           start=(dc == 0), stop=(dc == 7))
                        return ins
                    P.op('pe', f, reads=uT_keys([1 + qb]) + ["WIgq"], writes=["psC"])

                def s_ge(qb=qb):
                    P.op('dve', lambda e: e.tensor_copy(out=SGb[par][:, qb * 256: qb * 256 + G * 128], in_=psC[:, 0:G * 128]),
                         reads=["psC"], writes=[("SGb", par, qb)])
                steps.append((s_g, 0))
                steps.append((s_ge, 1))
            return steps

        def prepB_parts(blocks, acteng):
            wh = 1 if acteng else None

            def p_qa():
                for c in range(4):
                    pbk = next_pbank()
                    proj_fm(WIs, 1280, c * 128, 1, 4, pbk, reads_extra=["WIs"])
                    evac_copy(QAT[:, c * 512:(c + 1) * 512], psA[:, pbk * 512:(pbk + 1) * 512],
                              reads=[bankA(pbk)], writes=[("QAT", c)], scale=0.125, which=wh)

            def p_ga(qbs):
                for qb in qbs:
                    pbk = next_pbank()
                    proj_tm(WIs, 1280, 512, 512, 1 + qb, pbk, reads_extra=["WIs"])
                    P.op('act', lambda e, qb=qb, pbk=pbk: e.activation(
                        out=SGa[:, qb * 512:(qb + 1) * 512], in_=psA[:, pbk * 512:(pbk + 1) * 512], func=AF.Silu),
                        reads=[bankA(pbk)], writes=[("SGa", qb)])

            def p_kv():
                bs0 = blocks[0][0]
                nbk = len(blocks)
                pieces = []
                s0 = bs0
                rem = nbk
                while rem > 0:
                    n = min(4, rem)
                    pieces.append((s0, n))
                    s0 += n
                    rem -= n
                for (s0, n) in pieces:
                    pbk = next_pbank()
                    proj_fm(WIs, 1280, 1024, s0, n, pbk, reads_extra=["WIs"])
                    evac_copy(KAT[:, s0 * 128:(s0 + n) * 128], psA[:, pbk * 512: pbk * 512 + n * 128],
                              reads=[bankA(pbk)], writes=["KAT"], which=wh)
                for bs, lb in blocks:
                    pbk = next_pbank()
                    proj_tm(WIs, 1280, 1152, 128, bs, pbk, reads_extra=["WIs"])
                    eng = 'act' if acteng else 'dve'
                    P.op(eng, lambda e, bs=bs, pbk=pbk: (e.copy if acteng else e.tensor_copy)(
                        out=dap(VAW, bs * 132, [[VAW_pitch, 128], [66, 2], [1, 64]]),
                        in_=dap(psA, pbk * 512, [[2048, 128], [64, 2], [1, 64]])),
                        reads=[bankA(pbk), "VAW_all"], writes=[("VAW", bs)])
            return [p_qa, lambda: p_ga([0, 1]), lambda: p_ga([2, 3]), p_kv]

        def bg_drain(bg):
            while bg:
                fn, _ = bg.pop(0)
                fn()

        def q_phase(x_d, y_d, b, S_kv, NB, t, n_qt, is_sample, heads, final, gi, groups):
            G = len(heads)
            par = t % 2
            blocks = tile_blocks(t, n_qt, NB, is_sample, final)
            has_left = any(bs == 0 for bs, _ in blocks)
            has_right = any(bs == 5 for bs, _ in blocks)
            if t == 0:
                make_uT_list(x_d, blocks, b)
                for hl in range(G):
                    pbk = next_pbank()
                    proj_fm(WIg, 1024, hl * 128, 1, 4, pbk, reads_extra=["WIgq"])
                    evac_copy(QT[par][:, hl * 512:(hl + 1) * 512], psA[:, pbk * 512:(pbk + 1) * 512],
                              reads=[bankA(pbk)], writes=[("QT", par, hl)], scale=0.125)
                for qb in range(4):
                    pbk = next_pbank()
                    proj_tm(WIg, 1024, 768, G * 128, 1 + qb, pbk, reads_extra=["WIgq"])
                    evac_copy(SGb[par][:, qb * 256: qb * 256 + G * 128], psA[:, pbk * 512: pbk * 512 + G * 128],
                              reads=[bankA(pbk)], writes=[("SGb", par, qb)])

            if final:
                if t == 0:
                    for part in prepB_parts(blocks, acteng=False):
                        part()
                window_attn(t, n_qt, is_sample, blocks, has_left, has_right)

            bg = []
            prep = prepA_steps(x_d, b, t + 1, n_qt, NB, is_sample, final, G) if t + 1 < n_qt else []
            pend = state.get('pending_post') or []
            state['pending_post'] = None
            if prep and pend:
                bg = [(prep[0][0], 0)] + pend + prep[1:]
            else:
                bg = pend + prep
            for hl, h in enumerate(heads):
                diff_attn(S_kv, NB, t, n_qt, is_sample, hl, h, G, bg)
            bg_drain(bg)
            post = post_diff_steps(t, heads, G, final, gi, groups, n_qt, fg=(final or t == n_qt - 1))
            if final or t == n_qt - 1:
                bg_drain(post)
            else:
                state['pending_post'] = post
            if dbg and final and (not is_sample) and t == 0:
                P.op('sp', lambda e: e.dma_start(out=dbg_ot.ap(), in_=OT[:]),
                     reads=[("OT", fc) for fc in range(6)], writes=["dbg1"], dma='dbg')
                P.op('sp', lambda e: e.dma_start(out=dbg_car.ap()[:, 0:512], in_=CAR[:, 0:512]),
                     reads=[("CAR", 0, 0)], writes=["dbg2"], dma='dbg')
                P.op('sp', lambda e: e.dma_start(out=dbg_car.ap()[:, 512:1024], in_=CAR[:, 4096:4608]),
                     reads=[("CAR", 1, 0)], writes=["dbg3"], dma='dbg')
                P.op('sp', lambda e: e.dma_start(out=dbg_c.ap()[:, 0:16], in_=s1p[:]), writes=["dbg4"], dma='dbg')
                P.op('sp', lambda e: e.dma_start(out=dbg_c.ap()[:, 16:32], in_=shf[:]), writes=["dbg5"], dma='dbg')
                P.op('sp', lambda e: e.dma_start(out=dbg_c.ap()[:, 32:33], in_=neglam[:], allow_slow_non_contiguous=True), writes=["dbg6"], dma='dbg')
                P.op('sp', lambda e: e.dma_start(out=dbg_c.ap()[:, 33:41], in_=expsink[:]), writes=["dbg7"], dma='dbg')
                P.op('sp', lambda e: e.dma_start(out=dbg_c.ap()[:, 41:49], in_=cfar[:]), writes=["dbg8"], dma='dbg')
                P.op('sp', lambda e: e.dma_start(out=dbg_sg.ap()[:, 0:2048], in_=SGa[:]),
                     reads=[("SGa", qb) for qb in range(4)], writes=["dbg9"], dma='dbg')
                P.op('sp', lambda e: e.dma_start(out=dbg_sg.ap()[:, 2048:3072], in_=SGb[0][:]),
                     reads=[("SGb", 0, qb) for qb in range(4)], writes=["dbg10"], dma='dbg')
            if final:
                nxt = None
                if t + 1 < n_qt:
                    nxt = prepB_parts(tile_blocks(t + 1, n_qt, NB, is_sample, final), acteng=True)
                finalize(x_d, y_d, b, t, n_qt, heads, gi, groups, nxt)

        VAW_pitch = 6 * 2 * 66

        def window_attn(t, n_qt, is_sample, blocks, has_left, has_right):
            present = {bs for bs, _ in blocks}
            wrapL = is_sample and t == 0
            wrapR = is_sample and t == n_qt - 1
            def win_chunk(c):
                par = c % 2
                if par == 0:
                    accU = [(psB, 0, bankB(0)), (psB, 512, bankB(1))]
                else:
                    accU = [(psB, 1024, bankB(2)), (psC, 0, "psC")]
                jbs = [jb for jb in range(6) if jb in present]
                first = [True, True]

                def geom(jb):
                    qlo = max(0, jb - 2)
                    qhi = min(3, jb)
                    nq = qhi - qlo + 1
                    return qlo, nq, nq * 128, (qlo - (jb - 2)) * 128

                def qk(jb, slot, c=c):
                    qlo, nq, N, so = geom(jb)
                    use_wrap = (wrapL and jb == 0) or (wrapR and jb == 5)

                    def f(e):
                        ins = None
                        for u in range(2):
                            e.matmul(out=psA[:, (slot * 2 + u) * 512: (slot * 2 + u) * 512 + N],
                                     lhsT=KAT[u * 64:(u + 1) * 64, jb * 128:(jb + 1) * 128],
                                     rhs=QAT[u * 64:(u + 1) * 64, c * 512 + qlo * 128: c * 512 + qlo * 128 + N],
                                     start=True, stop=False)
                        for u in range(2):
                            hp = 2 * c + u
                            if use_wrap:
                                wo = 0 if jb == 0 else 1024
                                rhs = BXW[:, wo + hp * 128: wo + hp * 128 + 128]
                            else:
                                rhs = BW[:, hp * 384 + so: hp * 384 + so + N]
                            ins = e.matmul(out=psA[:, (slot * 2 + u) * 512: (slot * 2 + u) * 512 + N],
                                           lhsT=identb[:], rhs=rhs, start=False, stop=True)
                        return ins
                    rd = ["KAT", ("QAT", c), identb.name]
                    rd += [] if use_wrap else [("BW", 2 * c), ("BW", 2 * c + 1)]
                    P.op('pe', f, reads=rd, writes=[bankA(slot * 2), bankA(slot * 2 + 1)])

                def ex(jb, slot):
                    qlo, nq, N, so = geom(jb)
                    P.op('act', lambda e: e.activation(
                        out=dap(PT[slot], 0, [[1024, 128], [384, 2], [1, N]]),
                        in_=dap(psA, slot * 1024, [[2048, 128], [512, 2], [1, N]]), func=AF.Exp),
                        reads=[bankA(slot * 2), bankA(slot * 2 + 1)], writes=[("PT", slot)])

                def pv(jb, slot):
                    qlo, nq, N, so = geom(jb)
                    sts = []
                    for u in range(2):
                        for qi in range(nq):
                            sts.append(first[u])
                            first[u] = False
                    sts = tuple(sts)

                    def f(e):
                        ins = None
                        k = 0
                        for u in range(2):
                            pt_, base, _ = accU[u]
                            for qi in range(nq):
                                qb = qlo + qi
                                off = base + qb * 66
                                ins = e.matmul(out=pt_[:, off: off + 65],
                                               lhsT=PT[slot][:, u * 384 + qi * 128: u * 384 + (qi + 1) * 128],
                                               rhs=VAW[:, (jb * 2 + u) * 66: (jb * 2 + u) * 66 + 65],
                                               start=sts[k], stop=True, skip_group_check=True)
                                k += 1
                        return ins
                    P.op('pe', f, reads=[("PT", slot), ("VAW", jb)], writes=[accU[0][2], accU[1][2]])

                qk(jbs[0], 0)
                for idx, jb in enumerate(jbs):
                    if idx + 1 < len(jbs):
                        qk(jbs[idx + 1], (idx + 1) % 2)
                    ex(jb, idx % 2)
                    pv(jb, idx % 2)
                for u in range(2):
                    pt_, base, bk = accU[u]
                    pitch = 1536 if pt_ is psB else 512
                    P.op('dve', lambda e, u=u, pt_=pt_, base=base, pitch=pitch, c=c, par=par: e.tensor_scalar(
                        out=den[:, par * 8 + u * 4: par * 8 + u * 4 + 4], in0=dap(pt_, base + 64, [[pitch, 128], [66, 4]]),
                        scalar1=expsink[:, 2 * c + u:2 * c + u + 1], scalar2=None, op0=ALU.add),
                        reads=[bk], writes=[("den", par, u)])
                P.op('dve', lambda e, par=par: e.reciprocal(out=rec[:, par * 8: par * 8 + 8], in_=den[:, par * 8: par * 8 + 8]),
                     reads=[("den", par, 0), ("den", par, 1)], writes=[("recw", par)])
                for u in range(2):
                    pt_, base, bk = accU[u]
                    for qb in range(4):
                        a = u * 4 + qb
                        off = base + qb * 66
                        hp = 2 * c + u
                        P.op('dve', lambda e, a=a, off=off, hp=hp, qb=qb, pt_=pt_, par=par: e.scalar_tensor_tensor(
                            out=OAg[:, qb * 512 + hp * 64: qb * 512 + hp * 64 + 64],
                            in0=pt_[:, off: off + 64], scalar=rec[:, par * 8 + a: par * 8 + a + 1],
                            in1=SGa[:, qb * 512 + hp * 64: qb * 512 + hp * 64 + 64],
                            op0=ALU.mult, op1=ALU.mult),
                            reads=[bk, ("recw", par), ("SGa", qb)], writes=[("OAg", qb, hp)])

            for c in range(4):
                win_chunk(c)
            for fc in range(4):
                def trw(e, fc=fc):
                    ins = None
                    for qb in range(4):
                        ins = e.transpose(out=psC_b[:, qb * 128:(qb + 1) * 128],
                                          in_=OAg[:, qb * 512 + fc * 128: qb * 512 + (fc + 1) * 128],
                                          identity=identb[:])
                    return ins
                P.op('pe', trw, reads=[("OAg", qb, hp) for qb in range(4) for hp in (2 * fc, 2 * fc + 1)] + [identb.name],
                     writes=["psC"])
                P.op('dve', lambda e, fc=fc: e.tensor_copy(out=OT[:, fc * 512:(fc + 1) * 512], in_=psC_b[:, 0:512]),
                     reads=["psC"], writes=[("OT", fc)])

        def diff_attn(S_kv, NB, t, n_qt, is_sample, hl, h, G, bg):
            par = t % 2
            kinds = []
            for i in range(NB):
                own = (not is_sample) or i < 16
                if own:
                    d = 4 * t - i
                    if -4 <= d <= 1:
                        kinds.append(('near', d))
                    elif d >= 2:
                        kinds.append(('far', cfar[:, h:h + 1], cfar.name))
                    else:
                        kinds.append(('far', cfar[:, 4 + h:5 + h], cfar.name))
                else:
                    if i == 16 and t == 3:
                        kinds.append(('wrapR',))
                    elif i == NB - 1 and t == 0:
                        kinds.append(('wrapL',))
                    else:
                        kinds.append(('far', farS[:, h * 64 + i: h * 64 + i + 1], ("farS", h)))
            if is_sample and t == 0:
                P.op('dve', lambda e: e.tensor_scalar(
                    out=BXf[:, 0:512], in0=BD[:, hl * MD + 640: hl * MD + 1152],
                    scalar1=cfar[:, 4 + h:5 + h], scalar2=flags[:, 0:1], op0=ALU.subtract, op1=ALU.mult),
                    reads=[("BD", hl), cfar.name, flags.name], writes=["BXf"])
                P.op('dve', lambda e: e.tensor_scalar(
                    out=BX[:, 0:512], in0=BXf[:, 0:512], scalar1=cfar[:, 4 + h:5 + h], scalar2=None, op0=ALU.add),
                    reads=["BXf"], writes=["BX"])
            if is_sample and t == 3:
                P.op('dve', lambda e: e.tensor_scalar(
                    out=BXf[:, 512:1024], in0=BD[:, hl * MD: hl * MD + 512],
                    scalar1=cfar[:, h:h + 1], scalar2=flags[:, 1:2], op0=ALU.subtract, op1=ALU.mult),
                    reads=[("BD", hl), cfar.name, flags.name], writes=["BXf"])
                P.op('dve', lambda e: e.tensor_scalar(
                    out=BX[:, 512:1024], in0=BXf[:, 512:1024], scalar1=cfar[:, h:h + 1], scalar2=None, op0=ALU.add),
                    reads=["BXf"], writes=["BX"])

            def qk(i):
                kind = kinds[i]
                slot = i % 2

                def f(e):
                    ins = None
                    near = kind[0] != 'far'
                    for m in range(2):
                        ins = e.matmul(out=psA[:, (slot * 2 + m) * 512:(slot * 2 + m + 1) * 512],
                                       lhsT=KT[m * 64:(m + 1) * 64, hl * S_kv + i * 128: hl * S_kv + (i + 1) * 128],
                                       rhs=QT[par][m * 64:(m + 1) * 64, hl * 512:(hl + 1) * 512],
                                       start=True, stop=not near)
                    if near:
                        if kind[0] == 'near':
                            o = hl * MD + 128 * (kind[1] + 4)
                            rhs = BD[:, o:o + 512]
                        elif kind[0] == 'wrapL':
                            rhs = BX[:, 0:512]
                        else:
                            rhs = BX[:, 512:1024]
                        for m in range(2):
                            ins = e.matmul(out=psA[:, (slot * 2 + m) * 512:(slot * 2 + m + 1) * 512],
                                           lhsT=identb[:], rhs=rhs, start=False, stop=True)
                    return ins
                rd = [("KT", hl, i), ("QT", par, hl)]
                if kind[0] == 'near':
                    rd += [("BD", hl), identb.name]
                elif kind[0] != 'far':
                    rd += ["BX", identb.name]
                P.op('pe', f, reads=rd, writes=[bankA(slot * 2), bankA(slot * 2 + 1)])

            def ex(i):
                kind = kinds[i]
                slot = i % 2
                if kind[0] == 'far':
                    bias_ap, bkey = kind[1], kind[2]
                else:
                    bias_ap, bkey = zero_c[:], zero_c.name
                P.op('act', lambda e: e.activation(out=PT[slot][:], in_=psA[:, slot * 1024:(slot + 1) * 1024],
                                                   func=AF.Exp, bias=bias_ap, scale=1.0),
                     reads=[bankA(slot * 2), bankA(slot * 2 + 1), bkey], writes=[("PT", slot)])

            def pv(i):
                slot = i % 2

                def f(e):
                    ins = None
                    for m in range(2):
                        for qb in range(4):
                            a = m * 4 + qb
                            bank = a // 3
                            off = bank * 512 + (a % 3) * 130
                            vo = (i * G + hl) * 130
                            ins = e.matmul(out=psB[:, off:off + 129],
                                           lhsT=PT[slot][:, m * 512 + qb * 128: m * 512 + (qb + 1) * 128],
                                           rhs=VA[:, vo:vo + 129],
                                           start=(i == 0 and a % 3 == 0), stop=(i == NB - 1), skip_group_check=True)
                    return ins
                P.op('pe', f, reads=[("PT", slot), ("VA", hl, i)], writes=[bankB(0), bankB(1), bankB(2)])

            qk(0)
            qk(1)
            cool = 2
            for i in range(NB):
                ex(i)
                if i + 2 < NB:
                    qk(i + 2)
                pv(i)
                if bg:
                    if cool <= 0:
                        fn, cool = bg.pop(0)
                        fn()
                    else:
                        cool -= 1
            for bank in range(3):
                na = 3 if bank < 2 else 2
                P.op('dve', lambda e, bank=bank, na=na: e.reciprocal(
                    out=rec[:, bank * 3: bank * 3 + na], in_=dap(psB, bank * 512 + 128, [[1536, 128], [130, na]])),
                    reads=[bankB(bank)], writes=[("recd", bank)])
            P.op('dve', lambda e: e.tensor_scalar(out=rec[:, 8:12], in0=rec[:, 4:8], scalar1=neglam[:, 0:1], scalar2=None, op0=ALU.mult),
                 reads=[("recd", 1), ("recd", 2), neglam.name], writes=["recn"])
            for qb in range(4):
                a0 = qb
                o0 = (a0 // 3) * 512 + (a0 % 3) * 130
                oo = (qb * 2 + hl) * 128
                P.op('dve', lambda e, a0=a0, o0=o0, oo=oo: e.tensor_scalar(
                    out=OBf[:, oo:oo + 128], in0=psB[:, o0:o0 + 128], scalar1=rec[:, a0:a0 + 1], scalar2=None, op0=ALU.mult),
                    reads=[bankB(a0 // 3), ("recd", a0 // 3)], writes=[("OBf", qb, hl)])
            for qb in range(4):
                a1 = 4 + qb
                o1 = (a1 // 3) * 512 + (a1 % 3) * 130
                P.op('dve', lambda e, qb=qb, o1=o1: e.tensor_scalar(
                    out=BXf[:, qb * 128:(qb + 1) * 128], in0=psB[:, o1:o1 + 128], scalar1=rec[:, 8 + qb:9 + qb], scalar2=None, op0=ALU.mult),
                    reads=[bankB(a1 // 3), "recn"], writes=["BXf"])
            for qb in range(4):
                oo = (qb * 2 + hl) * 128
                P.op('dve', lambda e, qb=qb, oo=oo: e.tensor_tensor(
                    out=OBf[:, oo:oo + 128], in0=OBf[:, oo:oo + 128], in1=BXf[:, qb * 128:(qb + 1) * 128], op=ALU.add),
                    reads=[("OBf", qb, hl), "BXf"], writes=[("OBf", qb, hl)])
                P.op('dve', lambda e, qb=qb, oo=oo: e.tensor_tensor(
                    out=sqj[:], in0=OBf[:, oo:oo + 128], in1=OBf[:, oo:oo + 128], op=ALU.mult),
                    reads=[("OBf", qb, hl)], writes=["sqj"])
                P.op('dve', lambda e, qb=qb: e.reduce_sum(out=ssq[:, qb * 2 + hl: qb * 2 + hl + 1], in_=sqj[:], axis=mybir.AxisListType.X),
                     reads=["sqj"], writes=[("ssq", qb, hl)])

        def post_diff_steps(t, heads, G, final, gi, groups, n_qt, fg=False):
            S_q = n_qt * 512
            par = t % 2
            steps = []
            sgk = [("SGb", par, qb) for qb in range(4)]

            def s1():
                P.op('act', lambda e: e.activation(out=BXf[:], in_=SGb[par][:], func=AF.Exp, scale=-1.0),
                     reads=sgk, writes=["BXf"])
            def s1f():
                P.op('act', lambda e: e.activation(out=SGb[par][:], in_=SGb[par][:], func=AF.Silu), reads=sgk, writes=sgk)
            if fg:
                steps.append((s1f, 0))
            else:
                steps.append((s1, 1))

            def s2():
                P.op('dve', lambda e: e.tensor_scalar(out=BXf[:], in0=BXf[:], scalar1=1.0, scalar2=None, op0=ALU.add),
                     reads=["BXf"], writes=["BXf"])
                P.op('dve', lambda e: e.reciprocal(out=BXf[:], in_=BXf[:]), reads=["BXf"], writes=["BXf"])
                P.op('dve', lambda e: e.tensor_tensor(out=SGb[par][:], in0=SGb[par][:], in1=BXf[:], op=ALU.mult),
                     reads=sgk + ["BXf"], writes=sgk)
            if not fg:
                steps.append((s2, 2))

            def s3():
                P.op('act', lambda e: e.activation(out=rstd[:, 0:8], in_=ssq[:, 0:8], func=AF.Sqrt, bias=eps_sub[:], scale=1.0 / 128.0),
                     reads=[("ssq", qb, hl) for qb in range(4) for hl in range(G)] + [eps_sub.name], writes=["rstd_a"])
                P.op('dve', lambda e: e.reciprocal(out=rstd[:, 0:8], in_=rstd[:, 0:8]), reads=["rstd_a"], writes=["rstd"])
            steps.append((s3, 2))
            for qb in range(4):
                def s4(qb=qb):
                    for hl in range(G):
                        oo = (qb * 2 + hl) * 128
                        P.op('dve', lambda e, hl=hl, oo=oo: e.scalar_tensor_tensor(
                            out=OBf[:, oo:oo + 128], in0=OBf[:, oo:oo + 128], scalar=rstd[:, qb * 2 + hl: qb * 2 + hl + 1],
                            in1=subw_bc[:], op0=ALU.mult, op1=ALU.mult),
                            reads=[("OBf", qb, hl), "rstd", subw_bc.name], writes=[("OBf", qb, hl)])
                        P.op('pool', lambda e, hl=hl, oo=oo: e.tensor_tensor(
                            out=OBn[:, oo:oo + 128], in0=OBf[:, oo:oo + 128],
                            in1=SGb[par][:, qb * 256 + hl * 128: qb * 256 + (hl + 1) * 128], op=ALU.mult),
                            reads=[("OBf", qb, hl), ("SGb", par, qb)], writes=[("OBn", qb, hl)])
                steps.append((s4, 2 if qb == 3 else 1))
            for hl, h in enumerate(heads):
                def s5(hl=hl):
                    def trd(e):
                        ins = None
                        for qb in range(4):
                            oo = (qb * 2 + hl) * 128
                            ins = e.transpose(out=psC_b[:, qb * 128:(qb + 1) * 128], in_=OBn[:, oo:oo + 128], identity=identb[:])
                        return ins
                    P.op('pe', trd, reads=[("OBn", qb, hl) for qb in range(4)] + [identb.name], writes=["psC"])

                def s6(hl=hl):
                    if final:
                        P.op('dve', lambda e: e.tensor_copy(out=OT[:, (4 + hl) * 512:(5 + hl) * 512], in_=psC_b[:, 0:512]),
                             reads=["psC"], writes=[("OT", 4 + hl)])
                    else:
                        ci = sum(len(g) for g in groups[:gi]) + hl
                        o = ci * S_q + t * 512
                        P.op('dve', lambda e: e.tensor_copy(out=CAR[:, o:o + 512], in_=psC_b[:, 0:512]),
                             reads=["psC"], writes=[("CAR", ci, t)])
                steps.append((s5, 1))
                steps.append((s6, 1))
            return steps

        def finalize(x_d, y_d, b, t, n_qt, heads, gi, groups, nxt_parts=None):
            S_q = n_qt * 512
            n_car = sum(len(g) for g in groups[:gi])
            G = len(heads)
            fslots = [0, 1]

            def xload(qb):
                xs_ = fslots[qb % 2]
                row0 = (4 * t + qb) * 128
                P.op('sp', lambda e: e.dma_start(out=FB[xs_][:], in_=x_d.ap()[row0:row0 + 128, :]),
                     writes=[("FB", xs_)], dma=xkeys[xs_])

            xload(0)
            xload(1)
            for qb in range(4):
                fin_block(x_d, y_d, b, t, qb, S_q, n_car, G, fslots[qb % 2])
                if qb + 2 < 4:
                    xload(qb + 2)
                if nxt_parts:
                    nxt_parts.pop(0)()

        def fin_block(x_d, y_d, b, t, qb, S_q, n_car, G, xs_):
            row0 = (4 * t + qb) * 128
            ys_ = state['yslot'] % 2
            state['yslot'] += 1
            for n in range(2):
                def om(e, n=n):
                    ins = None
                    for fc in range(8):
                        if fc < 4:
                            lhsT = OT[:, fc * 512 + qb * 128: fc * 512 + (qb + 1) * 128]
                        elif fc - 4 < n_car:
                            o = (fc - 4) * S_q + t * 512 + qb * 128
                            lhsT = CAR[:, o:o + 128]
                        else:
                            o = (4 + fc - 4 - n_car) * 512 + qb * 128
                            lhsT = OT[:, o:o + 128]
                        ins = e.matmul(out=psB[:, n * 512:(n + 1) * 512], lhsT=lhsT,
                                       rhs=WO[:, fc * 1024 + n * 512: fc * 1024 + (n + 1) * 512],
                                       start=(fc == 0), stop=(fc == 7))
                    return ins
                rd = [("OT", fc) for fc in range(4 + G)] + [("CAR", ci, t) for ci in range(n_car)] + ["WO"]
                P.op('pe', om, reads=rd, writes=[bankB(n)])
            Z = FB[xs_]
            zk = ("FB", xs_)
            mo = qb * 2
            P.op('dve', lambda e: e.tensor_tensor(out=BXf[:], in0=psB[:, 0:1024], in1=gate_bc[:, b * 1024:(b + 1) * 1024], op=ALU.mult),
                 reads=[bankB(0), bankB(1), ("gate_bc", b, 0), ("gate_bc", b, 1)], writes=["BXf"])
            P.op('dve', lambda e: e.scalar_tensor_tensor(out=Z[:], in0=Z[:], scalar=ALPHA, in1=BXf[:], op0=ALU.mult, op1=ALU.add),
                 reads=[zk, "BXf"], writes=[zk])
            P.op('dve', lambda e: e.bn_stats(out=bnst[:, 0:6], in_=Z[:, 0:512]), reads=[zk], writes=["bnst0"])
            P.op('dve', lambda e: e.bn_stats(out=bnst[:, 6:12], in_=Z[:, 512:1024]), reads=[zk], writes=["bnst1"])
            P.op('dve', lambda e: e.bn_aggr(out=mv[:, mo:mo + 2], in_=bnst[:]), reads=["bnst0", "bnst1"], writes=[("mv", qb)])
            P.op('act', lambda e: e.activation(out=lrs[:, qb:qb + 1], in_=mv[:, mo + 1:mo + 2], func=AF.Sqrt, bias=eps_ln[:], scale=1.0),
                 reads=[("mv", qb), eps_ln.name], writes=[("lrs_a", qb)])
            P.op('dve', lambda e: e.reciprocal(out=lrs[:, qb:qb + 1], in_=lrs[:, qb:qb + 1]), reads=[("lrs_a", qb)], writes=[("lrs", qb)])
            P.op('dve', lambda e: e.scalar_tensor_tensor(out=Z[:], in0=Z[:], scalar=mv[:, mo:mo + 1], in1=lng_bc[:],
                                                         op0=ALU.subtract, op1=ALU.mult),
                 reads=[zk, ("mv", qb)], writes=[zk])
            P.op('dve', lambda e: e.scalar_tensor_tensor(out=Z[:], in0=Z[:], scalar=lrs[:, qb:qb + 1], in1=lnb_bc[:],
                                                         op0=ALU.mult, op1=ALU.add),
                 reads=[zk, ("lrs", qb)], writes=[zk])
            P.op('sp', lambda e: e.dma_start(out=y_d.ap()[row0:row0 + 128, :], in_=Z[:]),
                 reads=[zk], writes=[("yout", ys_)], dma=['y0', 'y1'][ys_])

        try:
            if kstop == 1:
                raise _Stop()
            run_job(xp_d, yp_d, 0, SP_, 8, False, [[0, 1], [2, 3]], [0])
            run_job(xs_d, ys_d, 1, SS_, 4, True, [[0], [1], [2], [3]], None)
        except _Stop:
            pass
        P.op('sp', lambda e: None, reads=[("yout", 0), ("yout", 1)] + (["dbg%d" % i for i in range(1, 11)] if dbg else []))

        with nc.Block() as blk2:
            P.emit(blk2)
    return nc


_NC_CACHE = {}


def _bucket_table():
    import math
    import jax
    import jax.numpy as jnp
    try:
        dev = jax.devices('cpu')[0]
    except Exception:
        dev = None
    def run():
        rel = jnp.arange(-1400, 1401, dtype=jnp.int32)
        sign = jnp.where(rel > 0, 16, 0)
        n = jnp.abs(rel)
        nf = jnp.maximum(n, 1).astype(jnp.float32)
        large = 8 + (jnp.log(nf / 8) / math.log(128 / 8) * 8).astype(jnp.int32)
        large = jnp.minimum(large, 15)
        return np.asarray((sign + jnp.where(n < 8, n, large)).astype(jnp.int32))
    if dev is not None:
        with jax.default_device(dev):
            return run()
    return run()


def kernel(**inp):
    f = lambda k: np.ascontiguousarray(np.asarray(inp[k], dtype=np.float32))
    x_prompt = f('x_prompt'); x_sample = f('x_sample')
    c_prompt = f('c_prompt'); c_sample = f('c_sample')
    w_in = f('w_in')[0]; w_out = f('w_out')[0]; w_ada = f('w_ada')[0]; b_ada = f('b_ada')[0]
    ln_g = f('ln_g'); ln_b = f('ln_b'); sink = f('attn_sink')[0]
    rel_bias = f('rel_bias')
    subw = f('subln_w')
    lamv = np.concatenate([f('lambda_q1')[0], f('lambda_k1')[0], f('lambda_q2')[0], f('lambda_k2')[0]])[None, :]

    colperm = np.concatenate([np.arange(64) + 64 * h for h in PERM])
    w_in_l = w_in.copy()
    w_in_l[:, 0:512] = w_in[:, 0:512][:, colperm]
    w_in_l[:, 768:1280] = w_in[:, 768:1280][:, colperm]
    w_out_l = w_out.copy()
    w_out_l[0:512, :] = w_out[0:512, :][colperm, :]
    sinkp = sink[PERM][None, :]
    bt = _bucket_table()

    def bk(r):
        return bt[r + 1400]
    kk = np.arange(128)[:, None]
    mm = np.arange(MD)[None, :]
    bd_idx = bk(kk - mm + 512)
    bd_strip = np.stack([rel_bias[bd_idx, 8 + h] for h in range(4)], axis=1).reshape(128, 4 * MD)
    qq = np.arange(384)[None, :]
    relw = kk - qq + 128
    bw_idx = bk(relw)
    bw_strip = np.stack([rel_bias[bw_idx, PERM[hp]] for hp in range(8)], axis=1).reshape(128, 8 * 384)
    maskw = np.where(np.abs(relw) <= 128, 0.0, -BIG).astype(np.float32)
    relfar = np.concatenate([rel_bias[15, 8:12], rel_bias[31, 8:12]])[None, :]
    identf = np.eye(128, dtype=np.float32)

    import os
    dbg = bool(os.environ.get('KDBG'))
    kstop = int(os.environ.get('KSTOP', '0'))
    key = 'nc%d_%d' % (dbg, kstop)
    if key not in _NC_CACHE:
        _NC_CACHE[key] = build_program(dbg, kstop)
    nc = _NC_CACHE[key]

    in_maps = []
    for c in range(NCORES):
        s, j = c // 4, c % 4
        xs = np.ascontiguousarray(np.roll(x_sample[s], -SQS * j, axis=0))
        cc = np.stack([c_prompt[c], c_sample[s]], axis=0)
        cT = np.ascontiguousarray(cc.reshape(2, 8, 128).transpose(2, 1, 0).reshape(128, 16))
        fl = np.zeros((1, 66), np.float32)
        fl[0, 0] = 1.0 if j > 0 else 0.0
        fl[0, 1] = 1.0 if j < 3 else 0.0
        for i in range(64):
            fl[0, 2 + i] = 1.0 if i >= 64 - 16 * j else 0.0
        in_maps.append({
            "xp": x_prompt[c], "xs": xs, "cT": cT,
            "w_in": w_in_l, "w_out": w_out_l, "w_ada": w_ada,
            "bcol": np.ascontiguousarray(b_ada.reshape(24, 128).T),
            "bgate": np.ascontiguousarray(b_ada[None, 2048:3072]),
            "ln_g": ln_g, "ln_b": ln_b, "sinkp": np.ascontiguousarray(sinkp),
            "lamv": np.ascontiguousarray(lamv), "subw": subw,
            "relfar": np.ascontiguousarray(relfar),
            "bd_strip": np.ascontiguousarray(bd_strip.astype(np.float32)),
            "bw_strip": np.ascontiguousarray(bw_strip.astype(np.float32)),
            "maskw": maskw, "flags": fl, "identf": identf,
        })
    res = run_bass_kernel_spmd(nc, in_maps, core_ids=list(range(NCORES)))
    y_prompt = np.empty((8, SP_, D), np.float32)
    y_sample = np.empty((2, SS_, D), np.float32)
    for c in range(NCORES):
        r = res.results[c]
        s, j = c // 4, c % 4
        y_prompt[c] = np.asarray(r["yp"], np.float32)
        y_sample[s, SQS * j: SQS * (j + 1)] = np.asarray(r["ys"], np.float32)
    return (y_prompt, y_sample)
```

```python
import numpy as np
import concourse.bass as bass
import concourse.mybir as mybir
from concourse.bass_utils import run_bass_kernel_spmd

F32 = mybir.dt.float32
BF16 = mybir.dt.bfloat16
AF = mybir.ActivationFunctionType
ALU = mybir.AluOpType

NCORES = 8
D = 1024
DC = 8
SP_ = 4096
SS_ = 8192
SQS = 2048
ALPHA = 2.0 ** 0.25
LAMBDA_INIT = 0.2
BIG = 30000.0
LN_EPS = 1e-5
SUBLN_EPS = 1e-5
PERM = [0, 4, 1, 5, 2, 6, 3, 7]
MD = 1152


class Prog:
    ENG = ('sp', 'act', 'pe', 'dve', 'pool')

    def __init__(self, sems, dsems):
        self.sem = sems
        self.dsem = dsems
        self.dcnt = {k: 0 for k in dsems}
        self.base = {e: 0 for e in self.ENG}
        self.reset_block()

    def reset_block(self):
        self.ops = {e: [] for e in self.ENG}
        self.res = {}

    def op(self, eng, fn, reads=(), writes=(), dma=None):
        ops = self.ops[eng]
        idx = len(ops)
        if dma is not None:
            self.dcnt[dma] += 1
            ev = ('dma', dma, self.dcnt[dma])
        else:
            ev = (eng, idx)
        deps = {}

        def add(d, raw):
            if d[0] == 'dma':
                d = ('dma', d[1], self.dcnt[d[1]])
            if raw or d not in deps:
                deps[d] = raw or deps.get(d, False)

        def is_ps(k):
            return isinstance(k, str) and k.startswith('ps')

        ps_keys = {}
        for k in reads:
            if is_ps(k):
                ps_keys[k] = False
        for k in writes:
            if is_ps(k):
                ps_keys[k] = True
        for k, isw in ps_keys.items():
            r = self.res.get(k)
            if r is not None and r[0] is not None:
                add(r[0], (not isw) and r[2])
        for k in reads:
            if is_ps(k):
                continue
            r = self.res.get(k)
            if r is not None and r[0] is not None:
                add(r[0], True)
        for k in writes:
            if is_ps(k):
                continue
            r = self.res.get(k)
            if r is not None:
                if r[0] is not None:
                    add(r[0], False)
                for rv in r[1].values():
                    add(rv, False)
        for k in reads:
            if is_ps(k):
                continue
            r = self.res.setdefault(k, [None, {}])
            r[1][ev if dma is not None else eng] = ev
        for k in writes:
            if is_ps(k):
                continue
            self.res[k] = [ev, {}]
        for k, isw in ps_keys.items():
            self.res[k] = [ev, {}, isw]
        ops.append((fn, deps, ev, dma))
        return ev

    def emit(self, block):
        need = {e: set() for e in self.ENG}
        for e in self.ENG:
            for (fn, deps, ev, dma) in self.ops[e]:
                for d in deps:
                    if d[0] != 'dma':
                        need[d[0]].add(d[1])
        ordn = {}
        for e in self.ENG:
            s = sorted(need[e])
            ordn[e] = {idx: self.base[e] + i + 1 for i, idx in enumerate(s)}

        def body(en):
            def f(eng):
                seen = {}
                for idx, (fn, deps, ev, dma) in enumerate(self.ops[en]):
                    for d, raw in deps.items():
                        if d[0] == 'dma':
                            if dma is not None and d[1] == dma:
                                continue
                            key = ('dma', d[1])
                            val = 16 * d[2]
                            sem = self.dsem[d[1]]
                        else:
                            if d[0] == en:
                                if (not raw) or en == 'pe' or en == 'sp':
                                    continue
                            key = d[0]
                            val = ordn[d[0]][d[1]]
                            sem = self.sem[d[0]]
                        if seen.get(key, 0) >= val:
                            continue
                        eng.wait_ge(sem, val)
                        seen[key] = val
                    ins = fn(eng)
                    if dma is not None:
                        ins.then_inc(self.dsem[dma], 16)
                    elif idx in ordn[en]:
                        ins.then_inc(self.sem[en], 1)
            return f

        block.sync(body('sp'))
        block.scalar(body('act'))
        block.tensor(body('pe'))
        block.vector(body('dve'))
        block.gpsimd(body('pool'))
        for e in self.ENG:
            self.base[e] += len(ordn[e])
        self.reset_block()


class _CopyShim:
    def __init__(self, e):
        self.e = e

    def tensor_copy(self, out, in_):
        return self.e.copy(out=out, in_=in_)


def dap(t, off, dims):
    return bass.AP(t, off, [list(d) for d in dims])


class _Stop(Exception):
    pass


def build_program(dbg=False, kstop=0):
    nc = bass.Bass("TRN2", target_bir_lowering=False)
    if dbg:
        dbg_ot = nc.dram_tensor("dbg_ot", [128, 3072], BF16, kind="ExternalOutput")
        dbg_car = nc.dram_tensor("dbg_car", [128, 1024], BF16, kind="ExternalOutput")
        dbg_c = nc.dram_tensor("dbg_c", [128, 64], F32, kind="ExternalOutput")
        dbg_sg = nc.dram_tensor("dbg_sg", [128, 2048 + 1024], BF16, kind="ExternalOutput")

    def din(name, shape):
        return nc.dram_tensor(name, list(shape), F32, kind="ExternalInput")

    xp_d = din("xp", [SP_, D])
    xs_d = din("xs", [SS_, D])
    cT_d = din("cT", [128, 16])
    win_d = din("w_in", [D, 3328])
    wout_d = din("w_out", [D, D])
    wada_d = din("w_ada", [D, 3072])
    bcol_d = din("bcol", [128, 24])
    bgate_d = din("bgate", [1, 1024])
    lng_d = din("ln_g", [1, 1024])
    lnb_d = din("ln_b", [1, 1024])
    sink_d = din("sinkp", [1, 8])
    lamv_d = din("lamv", [1, 256])
    subw_d = din("subw", [1, 128])
    relfar_d = din("relfar", [1, 8])
    bd_d = din("bd_strip", [128, 4 * MD])
    bw_d = din("bw_strip", [128, 8 * 384])
    maskw_d = din("maskw", [128, 384])
    flags_d = din("flags", [1, 66])
    ident_d = din("identf", [128, 128])
    yp_d = nc.dram_tensor("yp", [SP_, D], F32, kind="ExternalOutput")
    ys_d = nc.dram_tensor("ys", [SQS, D], F32, kind="ExternalOutput")

    from contextlib import ExitStack
    es = ExitStack()

    def sb(name, shape, dt=F32):
        return es.enter_context(nc.sbuf_tensor("sb_" + name, list(shape), dt))

    def ps(name, shape, dt=F32):
        return es.enter_context(nc.psum_tensor("ps_" + name, list(shape), dt))

    with es:
        sems = {e: es.enter_context(nc.semaphore("s_" + e)) for e in ('act', 'pe', 'dve', 'pool')}
        dkeys = ['c0', 'wa', 'x0', 'x1', 'xb0', 'xb1', 'xb2', 'xb3', 'xb4', 'y0', 'y1', 'dbg']
        dsems = {k: es.enter_context(nc.semaphore("d_" + k)) for k in dkeys}
        P = Prog(sems, dsems)

        identf = sb("identf", [128, 128])
        identb = sb("identb", [128, 128], BF16)
        s1p = sb("s1p", [128, 16])
        shf = sb("shf", [128, 16])
        gate_bc = sb("gate_bc", [128, 2048])
        lng_bc = sb("lng_bc", [128, 1024])
        lnb_bc = sb("lnb_bc", [128, 1024])
        subw_bc = sb("subw_bc", [128, 128])
        neglam = sb("neglam", [128, 1])
        expsink = sb("expsink", [128, 8])
        cfar = sb("cfar", [128, 8])
        cdiff = sb("cdiff", [128, 4])
        farS = sb("farS", [128, 4 * 64])
        flags = sb("flags", [128, 66])
        negLR = sb("negLR", [128, 2])
        omf = sb("omf", [128, 2])
        BW = sb("BW", [128, 8 * 384], BF16)
        BXW = sb("BXW", [128, 2048], BF16)
        zero_c = sb("zero_c", [128, 1])
        eps_ln = sb("eps_ln", [128, 1])
        eps_sub = sb("eps_sub", [128, 1])

        psA = ps("psA", [128, 2048])
        psB = ps("psB", [128, 1536])
        psC = ps("psC", [128, 512])
        psA_b = psA.bitcast(BF16)
        psB_b = psB.bitcast(BF16)
        psC_b = psC.bitcast(BF16)

        def bankA(i):
            return "psA%d" % i

        def bankB(i):
            return "psB%d" % i

        WIs = sb("WIs", [128, 8 * 1280], BF16)
        WO = sb("WO", [128, 8 * 1024], BF16)
        FB = [sb("FB%d" % i, [128, 1024]) for i in range(2)]
        xkeys = ['x0', 'x1']
        xbkeys = ['xb0', 'xb1', 'xb2', 'xb3', 'xb4']
        NPF = 3
        state = {'xslot': 0, 'evac': 0, 'yslot': 0}

        wl_state = {'i': 0}

        def stage_cast(src_ap, ncols, cast_fn, writes, reads_extra=(), dst_dims=None, engs=('pool', 'dve'), q='sp'):
            s_ = state['xslot'] % 2
            state['xslot'] += 1
            P.op(q, lambda e, s_=s_: e.dma_start(
                out=(FB[s_][:, 0:ncols] if dst_dims is None else dap(FB[s_], 0, dst_dims)), in_=src_ap),
                 writes=[("FB", s_)], dma=xkeys[s_])
            eng = engs[wl_state['i'] % 2]
            wl_state['i'] += 1
            P.op(eng, lambda e, s_=s_: cast_fn(_CopyShim(e) if eng == 'act' else e, FB[s_]), reads=[("FB", s_)] + list(reads_extra), writes=writes)

        def load_static_weights():
            for dc in range(8):
                r0 = dc * 128
                def castA(e, fb, dc=dc):
                    e.tensor_copy(out=WIs[:, dc * 1280: dc * 1280 + 512], in_=fb[:, 0:512])
                    return e.tensor_copy(out=WIs[:, dc * 1280 + 1024: dc * 1280 + 1280], in_=fb[:, 512:768])
                stage_cast(win_d.ap()[r0:r0 + 128, 0:768], 768, castA, writes=["WIs"], q='act')
                def castB(e, fb, dc=dc):
                    return e.tensor_copy(out=WIs[:, dc * 1280 + 512: dc * 1280 + 1024], in_=fb[:, 0:512])
                stage_cast(win_d.ap()[r0:r0 + 128, 768:1280], 512, castB, writes=["WIs"], q='act')
                def castO(e, fb, dc=dc):
                    return e.tensor_copy(out=WO[:, dc * 1024:(dc + 1) * 1024], in_=fb[:, 0:1024])
                stage_cast(wout_d.ap()[r0:r0 + 128, :], 1024, castO, writes=["WO"], q='act')

        with ExitStack() as es1:
            def sb1(name, shape, dt=F32):
                return es1.enter_context(nc.sbuf_tensor("sb_" + name, list(shape), dt))

            WA = sb1("WA", [128, 4 * 3072])
            cT = sb1("cT", [128, 16])
            sc = sb1("sc", [128, 16])
            sc_rep = sb1("sc_rep", [128, 2 * 8 * 128])
            bcol = sb1("bcol", [128, 24])
            bgate_bc = sb1("bgate_bc", [128, 1024])
            lamv = sb1("lamv", [128, 256])
            lamp = sb1("lamp", [128, 128])
            lams = sb1("lams", [128, 4])
            sinkb = sb1("sinkb", [128, 8])
            subw_raw = sb1("subw_raw", [128, 128])
            maskw = sb1("maskw", [128, 384])
            BWf = sb1("BWf", [128, 8 * 384])

            def ld(dst_t, src):
                P.op('sp', lambda e: e.dma_start(out=dst_t[:], in_=src), writes=[dst_t.name], dma='c0')

            ld(identf, ident_d.ap())
            ld(cT, cT_d.ap())
            ld(bcol, bcol_d.ap())
            ld(bgate_bc, dap(bgate_d, 0, [[0, 128], [1, 1024]]))
            ld(lng_bc, dap(lng_d, 0, [[0, 128], [1, 1024]]))
            ld(lnb_bc, dap(lnb_d, 0, [[0, 128], [1, 1024]]))
            ld(sinkb, dap(sink_d, 0, [[0, 128], [1, 8]]))
            ld(lamv, dap(lamv_d, 0, [[0, 128], [1, 256]]))
            ld(subw_raw, dap(subw_d, 0, [[0, 128], [1, 128]]))
            ld(cfar, dap(relfar_d, 0, [[0, 128], [1, 8]]))
            ld(flags, dap(flags_d, 0, [[0, 128], [1, 66]]))
            ld(BWf, bw_d.ap())
            ld(maskw, maskw_d.ap())
            def load_wa(dcs):
                for dc in dcs:
                    P.op('sp', lambda e, dc=dc: e.dma_start(
                        out=WA[:, (dc % 4) * 3072:(dc % 4 + 1) * 3072], in_=wada_d.ap()[dc * 128:(dc + 1) * 128, :]),
                        writes=[("WA", dc % 4)], dma='wa')
            load_wa(range(0, 4))

            P.op('pool', lambda e: e.memset(zero_c[:], 0.0), writes=[zero_c.name])
            P.op('pool', lambda e: e.memset(eps_ln[:], LN_EPS), writes=[eps_ln.name])
            P.op('pool', lambda e: e.memset(eps_sub[:], SUBLN_EPS), writes=[eps_sub.name])
            P.op('dve', lambda e: e.tensor_copy(out=identb[:], in_=identf[:]),
                 reads=[identf.name], writes=[identb.name])
            P.op('act', lambda e: e.activation(out=sc[:], in_=cT[:], func=AF.Silu),
                 reads=[cT.name], writes=[sc.name])
            P.op('act', lambda e: e.activation(out=expsink[:], in_=sinkb[:], func=AF.Exp),
                 reads=[sinkb.name], writes=[expsink.name])
            P.op('dve', lambda e: e.tensor_tensor(out=lamp[:, 0:64], in0=lamv[:, 0:64], in1=lamv[:, 64:128], op=ALU.mult),
                 reads=[lamv.name], writes=["lamp0"])
            P.op('dve', lambda e: e.tensor_tensor(out=lamp[:, 64:128], in0=lamv[:, 128:192], in1=lamv[:, 192:256], op=ALU.mult),
                 reads=[lamv.name], writes=["lamp1"])
            P.op('dve', lambda e: e.reduce_sum(out=lams[:, 0:1], in_=lamp[:, 0:64], axis=mybir.AxisListType.X),
                 reads=["lamp0"], writes=["lams0"])
            P.op('dve', lambda e: e.reduce_sum(out=lams[:, 1:2], in_=lamp[:, 64:128], axis=mybir.AxisListType.X),
                 reads=["lamp1"], writes=["lams1"])
            P.op('act', lambda e: e.activation(out=lams[:, 2:4], in_=lams[:, 0:2], func=AF.Exp),
                 reads=["lams0", "lams1"], writes=["lams23"])
            P.op('dve', lambda e: e.tensor_tensor(out=neglam[:], in0=lams[:, 3:4], in1=lams[:, 2:3], op=ALU.subtract),
                 reads=["lams23"], writes=["neglam_a"])
            P.op('dve', lambda e: e.tensor_scalar(out=neglam[:], in0=neglam[:], scalar1=-LAMBDA_INIT, scalar2=None, op0=ALU.add),
                 reads=["neglam_a"], writes=[neglam.name])
            P.op('dve', lambda e: e.tensor_scalar(out=subw_bc[:], in0=subw_raw[:], scalar1=1.0 - LAMBDA_INIT, scalar2=None, op0=ALU.mult),
                 reads=[subw_raw.name], writes=[subw_bc.name])
            P.op('dve', lambda e: e.tensor_tensor(out=cdiff[:], in0=cfar[:, 0:4], in1=cfar[:, 4:8], op=ALU.subtract),
                 reads=[cfar.name], writes=[cdiff.name])
            for h in range(4):
                P.op('dve', lambda e, h=h: e.tensor_scalar(
                    out=farS[:, h * 64:(h + 1) * 64], in0=flags[:, 2:66],
                    scalar1=cdiff[:, h:h + 1], scalar2=cfar[:, 4 + h:5 + h], op0=ALU.mult, op1=ALU.add),
                    reads=[flags.name, cdiff.name, cfar.name], writes=[("farS", h)])
            P.op('dve', lambda e: e.tensor_scalar(out=negLR[:], in0=flags[:, 0:2], scalar1=-1.0, scalar2=BIG, op0=ALU.add, op1=ALU.mult),
                 reads=[flags.name], writes=[negLR.name])
            for h in range(8):
                P.op('dve', lambda e, h=h: e.tensor_tensor(
                    out=BWf[:, h * 384:(h + 1) * 384], in0=BWf[:, h * 384:(h + 1) * 384], in1=maskw[:], op=ALU.add),
                    reads=[BWf.name, maskw.name], writes=[("BWf", h)])
                P.op('pool', lambda e, h=h: e.tensor_copy(out=BW[:, h * 384:(h + 1) * 384], in_=BWf[:, h * 384:(h + 1) * 384]),
                     reads=[("BWf", h)], writes=[("BW", h)])
            P.op('dve', lambda e: e.tensor_scalar(
                out=dap(BXW, 0, [[2048, 128], [128, 8], [1, 128]]),
                in0=dap(BWf, 256, [[8 * 384, 128], [384, 8], [1, 128]]),
                scalar1=negLR[:, 0:1], scalar2=None, op0=ALU.add),
                reads=[("BWf", hh) for hh in range(8)] + [negLR.name], writes=["BXWL"])
            P.op('dve', lambda e: e.tensor_scalar(
                out=dap(BXW, 1024, [[2048, 128], [128, 8], [1, 128]]),
                in0=dap(BWf, 0, [[8 * 384, 128], [384, 8], [1, 128]]),
                scalar1=negLR[:, 1:2], scalar2=None, op0=ALU.add),
                reads=[("BWf", hh) for hh in range(8)] + [negLR.name], writes=["BXWR"])
            for b in range(2):
                for dc in range(8):
                    o = (b * 8 + dc) * 128
                    P.op('pool', lambda e, o=o, b=b, dc=dc: e.tensor_copy(
                        out=sc_rep[:, o:o + 128], in_=sc[:, dc * 2 + b:dc * 2 + b + 1].broadcast_to([128, 128])),
                        reads=[sc.name], writes=[("sc_rep", b, dc)])

            def mod_cols_half(half):
                def f(e):
                    ins = None
                    for dc in range(half * 4, half * 4 + 4):
                        for fc in range(16):
                            ins = e.matmul(out=psC[:, fc * 2:fc * 2 + 2],
                                           lhsT=WA[:, (dc % 4) * 3072 + fc * 128: (dc % 4) * 3072 + fc * 128 + 128],
                                           rhs=sc[:, dc * 2:dc * 2 + 2],
                                           start=(dc == 0 and fc == 0), stop=(dc == 7), skip_group_check=True)
                    return ins
                P.op('pe', f, reads=[("WA", k) for k in range(4)] + [sc.name], writes=["psC"])

            def gate_half(half):
                for b in range(2):
                    for n in range(2):
                        def gmm(e, b=b, n=n):
                            ins = None
                            for dc in range(half * 4, half * 4 + 4):
                                o = (b * 8 + dc) * 128
                                ins = e.matmul(out=psA[:, (b * 2 + n) * 512:(b * 2 + n + 1) * 512],
                                               lhsT=sc_rep[:, o:o + 128],
                                               rhs=WA[:, (dc % 4) * 3072 + 2048 + n * 512: (dc % 4) * 3072 + 2048 + (n + 1) * 512],
                                               start=(dc == 0), stop=(dc == 7))
                            return ins
                        P.op('pe', gmm, reads=[("WA", k) for k in range(4)] + [("sc_rep", b, dc) for dc in range(8)],
                             writes=[bankA(b * 2 + n)])

            load_static_weights()
            mod_cols_half(0)
            gate_half(0)
            load_wa(range(4, 8))
            mod_cols_half(1)
            gate_half(1)
            for b in range(2):
                P.op('dve', lambda e, b=b: e.tensor_tensor(
                    out=shf[:, b * 8:(b + 1) * 8], in0=psC[:, b:16:2],
                    in1=bcol[:, 0:8], op=ALU.add),
                    reads=["psC", bcol.name], writes=[("shf", b)])
                P.op('dve', lambda e, b=b: e.scalar_tensor_tensor(
                    out=s1p[:, b * 8:(b + 1) * 8], in0=psC[:, 16 + b:32:2], scalar=1.0, in1=bcol[:, 8:16],
                    op0=ALU.add, op1=ALU.add),
                    reads=["psC", bcol.name], writes=[("s1p", b)])
            for b in range(2):
                for n in range(2):
                    P.op('dve', lambda e, b=b, n=n: e.tensor_tensor(
                        out=gate_bc[:, b * 1024 + n * 512: b * 1024 + (n + 1) * 512],
                        in0=psA[:, (b * 2 + n) * 512:(b * 2 + n + 1) * 512],
                        in1=bgate_bc[:, n * 512:(n + 1) * 512], op=ALU.add),
                        reads=[bankA(b * 2 + n), bgate_bc.name], writes=[("gate_bc", b, n)])
            with nc.Block() as blk1:
                P.emit(blk1)

        WIg = sb("WIg", [128, 8 * 1024], BF16)
        BD = sb("BD", [128, 2 * MD], BF16)
        KT = sb("KT", [128, 8192], BF16)
        VA = sb("VA", [128, 64 * 130], BF16)
        CAR = sb("CAR", [128, 8192], BF16)
        XBF = [sb("xbf%d" % i, [128, 1024], BF16) for i in range(4)]
        uT = sb("uT", [128, 8 * 768], BF16)
        QT = [sb("QT%d" % i, [128, 2 * 512], BF16) for i in range(2)]
        QAT = sb("QAT", [128, 4 * 512], BF16)
        SGb = [sb("SGb%d" % i, [128, 4 * 256], BF16) for i in range(2)]
        SGa = sb("SGa", [128, 4 * 512], BF16)
        KAT = sb("KAT", [128, 768], BF16)
        VAW = sb("VAW", [128, 6 * 2 * 66], BF16)
        OT = sb("OT", [128, 6 * 512], BF16)
        PT = [sb("PT%d" % i, [128, 1024], BF16) for i in range(2)]
        OBf = sb("OBf", [128, 4 * 2 * 128])
        OBn = sb("OBn", [128, 4 * 2 * 128], BF16)
        OAg = sb("OAg", [128, 4 * 512], BF16)
        BX = sb("BX", [128, 1024], BF16)
        BXf = sb("BXf", [128, 1024])
        rec = sb("rec", [128, 16])
        ssq = sb("ssq", [128, 8])
        rstd = sb("rstd", [128, 8])
        sqj = sb("sqj", [128, 128])
        den = sb("den", [128, 16])
        bnst = sb("bnst", [128, 12])
        mv = sb("mv", [128, 8])
        lrs = sb("lrs", [128, 4])

        def load_kv_weights(heads):
            for hl, h in enumerate(heads):
                for dc in range(8):
                    src = dap(win_d, (dc * 128) * 3328 + 1792 + h * 128, [[3328, 128], [512, 2], [1, 128]])
                    def castG(e, fb, dc=dc, hl=hl):
                        return e.tensor_copy(
                            out=dap(WIg, dc * 1024 + 256 + hl * 128, [[8192, 128], [256, 2], [1, 128]]),
                            in_=dap(fb, 0, [[1024, 128], [128, 2], [1, 128]]))
                    stage_cast(src, 256, castG, writes=["WIgkv"], dst_dims=[[1024, 128], [128, 2], [1, 128]])

        def q_weight_thunks(heads):
            th = []
            for hl, h in enumerate(heads):
                for dc in range(8):
                    def t1(dc=dc, hl=hl, h=h):
                        src = dap(win_d, (dc * 128) * 3328 + 1280 + h * 128, [[3328, 128], [1536, 2], [1, 128]])
                        def castG(e, fb):
                            return e.tensor_copy(
                                out=dap(WIg, dc * 1024 + hl * 128, [[8192, 128], [768, 2], [1, 128]]),
                                in_=dap(fb, 0, [[1024, 128], [128, 2], [1, 128]]))
                        stage_cast(src, 256, castG, writes=["WIgq"], dst_dims=[[1024, 128], [128, 2], [1, 128]],
                                   engs=('act', 'dve'))
                    th.append(t1)
                def t2(hl=hl, h=h):
                    def castD1(e, fb):
                        return e.tensor_copy(out=BD[:, hl * MD: hl * MD + 1024], in_=fb[:, 0:1024])
                    stage_cast(bd_d.ap()[:, h * MD: h * MD + 1024], 1024, castD1, writes=[("BD", hl)], engs=('act', 'dve'))
                def t3(hl=hl, h=h):
                    def castD2(e, fb):
                        return e.tensor_copy(out=BD[:, hl * MD + 1024: hl * MD + MD], in_=fb[:, 0:MD - 1024])
                    stage_cast(bd_d.ap()[:, h * MD + 1024: (h + 1) * MD], MD - 1024, castD2, writes=[("BD", hl)],
                               engs=('act', 'dve'))
                th.append(t2)
                th.append(t3)
            return th

        P.op('pool', lambda e: e.memset(VA[:], 1.0), writes=["VA_all"])
        P.op('pool', lambda e: e.memset(VAW[:], 1.0), writes=["VAW_all"])

        xb_state = {'k': 0, 'm': 0}

        def prefetch_x(src_d, row0):
            sl = xb_state['k'] % 4
            xb_state['k'] += 1
            P.op('pool', lambda e, sl=sl: e.dma_start(out=XBF[sl][:], in_=src_d.ap()[row0:row0 + 128, :]),
                 writes=[("xbf", sl)], dma=xbkeys[sl])
            return sl

        def make_uT_from(sl, bslot, b):
            par = xb_state['m'] % 2
            xb_state['m'] += 1
            if par == 0:
                evk, odk, evb, odb, evo, odo = bankA(0), "psC", psA_b, psC_b, 0, 0
            else:
                evk, odk, evb, odb, evo, odo = bankB(0), bankB(1), psB_b, psB_b, 0, 1024

            dve_dcs = [0, 2, 4, 6, 7]
            act_dcs = [1, 3, 5]

            def tsrc(dc):
                if dc in dve_dcs:
                    k = dve_dcs.index(dc)
                    return evb[:, evo + k * 128: evo + (k + 1) * 128]
                k = act_dcs.index(dc)
                return odb[:, odo + k * 128: odo + (k + 1) * 128]

            def tr(e):
                ins = None
                for dc in range(8):
                    ins = e.transpose(out=tsrc(dc), in_=XBF[sl][:, dc * 128:(dc + 1) * 128], identity=identb[:])
                return ins
            P.op('pe', tr, reads=[("xbf", sl), identb.name], writes=[evk, odk])
            for dc in range(8):
                o = dc * 768 + bslot * 128
                if dc in dve_dcs:
                    P.op('dve', lambda e, dc=dc, o=o: e.tensor_scalar(
                        out=uT[:, o:o + 128], in0=tsrc(dc),
                        scalar1=s1p[:, b * 8 + dc:b * 8 + dc + 1], scalar2=shf[:, b * 8 + dc:b * 8 + dc + 1],
                        op0=ALU.mult, op1=ALU.add),
                        reads=[evk], writes=[("uT", bslot, dc)])
                else:
                    P.op('act', lambda e, dc=dc, o=o: e.activation(
                        out=uT[:, o:o + 128], in_=tsrc(dc), func=AF.Identity,
                        bias=shf[:, b * 8 + dc:b * 8 + dc + 1], scale=s1p[:, b * 8 + dc:b * 8 + dc + 1]),
                        reads=[odk], writes=[("uT", bslot, dc)])

        def make_uT_list(src_d, items, b):
            slots = {}
            for i in range(min(NPF, len(items))):
                slots[i] = prefetch_x(src_d, items[i][1] * 128)
            for i, (bs, lb) in enumerate(items):
                if i + NPF < len(items):
                    slots[i + NPF] = prefetch_x(src_d, items[i + NPF][1] * 128)
                make_uT_from(slots[i], bs, b)

        def uT_keys(bslots):
            return [("uT", bs, dc) for bs in bslots for dc in range(8)]

        def proj_fm(W, pitch, col, bslot0, nb, pbank, reads_extra=()):
            def f(e):
                ins = None
                for dc in range(8):
                    ins = e.matmul(out=psA[:, pbank * 512: pbank * 512 + nb * 128],
                                   lhsT=W[:, dc * pitch + col: dc * pitch + col + 128],
                                   rhs=uT[:, dc * 768 + bslot0 * 128: dc * 768 + (bslot0 + nb) * 128],
                                   start=(dc == 0), stop=(dc == 7))
                return ins
            P.op('pe', f, reads=uT_keys(range(bslot0, bslot0 + nb)) + list(reads_extra), writes=[bankA(pbank)])

        def proj_tm(W, pitch, col, ncol, bslot, pbank, reads_extra=()):
            def f(e):
                ins = None
                for dc in range(8):
                    ins = e.matmul(out=psA[:, pbank * 512: pbank * 512 + ncol],
                                   lhsT=uT[:, dc * 768 + bslot * 128: dc * 768 + (bslot + 1) * 128],
                                   rhs=W[:, dc * pitch + col: dc * pitch + col + ncol],
                                   start=(dc == 0), stop=(dc == 7))
                return ins
            P.op('pe', f, reads=uT_keys([bslot]) + list(reads_extra), writes=[bankA(pbank)])

        pb_state = {'i': 0}

        def next_pbank():
            b = 1 + pb_state['i'] % 3
            pb_state['i'] += 1
            return b

        def evac_copy(out_ap, in_ap, reads, writes, scale=None, which=None):
            if which is None:
                which = state['evac'] % 2
                state['evac'] += 1
            if which == 0:
                if scale is None:
                    P.op('dve', lambda e: e.tensor_copy(out=out_ap, in_=in_ap), reads=reads, writes=writes)
                else:
                    P.op('dve', lambda e: e.tensor_scalar(out=out_ap, in0=in_ap, scalar1=scale, scalar2=None, op0=ALU.mult),
                         reads=reads, writes=writes)
            else:
                if scale is None:
                    P.op('act', lambda e: e.copy(out=out_ap, in_=in_ap), reads=reads, writes=writes)
                else:
                    P.op('act', lambda e: e.mul(out=out_ap, in_=in_ap, mul=scale), reads=reads, writes=writes)

        def run_job(x_d, y_d, b, S_kv, n_qt, is_sample, groups, next_first_heads):
            NB = S_kv // 128
            n_groups = len(groups)
            for gi, heads in enumerate(groups):
                G = len(heads)
                final = (gi == n_groups - 1)
                if not state.get('kv_preloaded'):
                    load_kv_weights(heads)
                state['kv_preloaded'] = False
                qw = q_weight_thunks(heads)
                nxt_heads = groups[gi + 1] if gi + 1 < n_groups else next_first_heads

                NT = NB // 2

                kv_slots = {}
                for blk in range(NPF):
                    kv_slots[blk] = prefetch_x(x_d, blk * 128)

                def front(n):
                    base = (n % 3) * 2
                    for k in range(2):
                        blk = 2 * n + k
                        if blk + NPF < NB:
                            kv_slots[blk + NPF] = prefetch_x(x_d, (blk + NPF) * 128)
                        make_uT_from(kv_slots[blk], base + k, b)

                def back(n):
                    base = (n % 3) * 2
                    for hl in range(G):
                        pbk = next_pbank()
                        proj_fm(WIg, 1024, 256 + hl * 128, base, 2, pbk, reads_extra=["WIgkv"])
                        evac_copy(KT[:, hl * S_kv + n * 256: hl * S_kv + (n + 1) * 256],
                                  psA[:, pbk * 512: pbk * 512 + 256],
                                  reads=[bankA(pbk)], writes=[("KT", hl, 2 * n), ("KT", hl, 2 * n + 1)])
                    for k in range(2):
                        tb = 2 * n + k
                        pbk = next_pbank()
                        proj_tm(WIg, 1024, 512, G * 128, base + k, pbk, reads_extra=["WIgkv"])
                        state['evac'] += 1
                        for hl in range(G):
                            o = (tb * G + hl) * 130
                            evac_copy(VA[:, o:o + 128], psA[:, pbk * 512 + hl * 128: pbk * 512 + (hl + 1) * 128],
                                      reads=[bankA(pbk), "VA_all"], writes=[("VA", hl, tb)], which=state['evac'] % 2)

                for n in range(NT + 1):
                    if n < NT:
                        front(n)
                    if kstop == 22:
                        raise _Stop()
                    if n >= 1:
                        back(n - 1)
                    if n >= 1 and qw:
                        qw.pop(0)()
                while qw:
                    qw.pop(0)()
                    if kstop == 23 and n == 1:
                        raise _Stop()
                    if kstop == 2 and n == 2:
                        raise _Stop()
                if kstop == 3:
                    raise _Stop()

                for t in range(n_qt):
                    if t == n_qt - 1 and nxt_heads is not None:
                        load_kv_weights(nxt_heads)
                        state['kv_preloaded'] = True
                    q_phase(x_d, y_d, b, S_kv, NB, t, n_qt, is_sample, heads, final, gi, groups)
                    if kstop == 4:
                        raise _Stop()
                    if kstop == 5 and final:
                        raise _Stop()
                if kstop == 6 and final:
                    raise _Stop()

        def tile_blocks(t, n_qt, NB, is_sample, final):
            blocks = []
            if final:
                if t > 0:
                    blocks.append((0, 4 * t - 1))
                elif is_sample:
                    blocks.append((0, NB - 1))
                for i in range(4):
                    blocks.append((1 + i, 4 * t + i))
                if t < n_qt - 1 or is_sample:
                    blocks.append((5, 4 * t + 4))
            else:
                for i in range(4):
                    blocks.append((1 + i, 4 * t + i))
            return blocks

        def prepA_steps(x_d, b, t, n_qt, NB, is_sample, final, G):
            blocks = tile_blocks(t, n_qt, NB, is_sample, final)
            par = t % 2
            n = len(blocks)
            slots = {}
            steps = []

            def pf(i):
                slots[i] = prefetch_x(x_d, blocks[i][1] * 128)

            def s_pf0():
                for i in range(min(NPF, n)):
                    pf(i)
            steps.append((s_pf0, 4))
            for i, (bs, lb) in enumerate(blocks):
                def s_tr(i=i):
                    if i + NPF < n:
                        pf(i + NPF)
                    sl = slots[i]

                    def tr(e):
                        ins = None
                        for dc in range(8):
                            ins = e.transpose(out=psC_b[:, dc * 128:(dc + 1) * 128], in_=XBF[sl][:, dc * 128:(dc + 1) * 128],
                                              identity=identb[:])
                        return ins
                    P.op('pe', tr, reads=[("xbf", sl), identb.name], writes=["psC"])

                def s_ev(bs=bs):
                    for dc in range(8):
                        o = dc * 768 + bs * 128
                        P.op('dve', lambda e, dc=dc, o=o: e.tensor_scalar(
                            out=uT[:, o:o + 128], in0=psC_b[:, dc * 128:(dc + 1) * 128],
                            scalar1=s1p[:, b * 8 + dc:b * 8 + dc + 1], scalar2=shf[:, b * 8 + dc:b * 8 + dc + 1],
                            op0=ALU.mult, op1=ALU.add),
                            reads=["psC"], writes=[("uT", bs, dc)])
                steps.append((s_tr, 0))
                steps.append((s_ev, 2))

            def fm_part(col, dcs):
                def f(e):
                    ins = None
                    for dc in dcs:
                        ins = e.matmul(out=psC[:, 0:512],
                                       lhsT=WIg[:, dc * 1024 + col: dc * 1024 + col + 128],
                                       rhs=uT[:, dc * 768 + 128: dc * 768 + 640],
                                       start=(dc == 0), stop=(dc == 7))
                    return ins
                P.op('pe', f, reads=uT_keys(range(1, 5)) + ["WIgq"], writes=["psC"])

            for hl in range(G):
                steps.append((lambda hl=hl: fm_part(hl * 128, range(0, 4)), 0))
                steps.append((lambda hl=hl: fm_part(hl * 128, range(4, 8)), 0))

                def s_qe(hl=hl):
                    P.op('dve', lambda e: e.tensor_scalar(out=QT[par][:, hl * 512:(hl + 1) * 512], in0=psC[:, 0:512],
                                                          scalar1=0.125, scalar2=None, op0=ALU.mult),
                         reads=["psC"], writes=[("QT", par, hl)])
                steps.append((s_qe, 2))
            for qb in range(4):
                def s_g(qb=qb):
                    def f(e):
                        ins = None
                        for dc in range(8):
                            ins = e.matmul(out=psC[:, 0:G * 128],
                                           lhsT=uT[:, dc * 768 + (1 + qb) * 128: dc * 768 + (2 + qb) * 128],
                                           rhs=WIg[:, dc * 1024 + 768: dc * 1024 + 768 + G * 128],
                                           start=(dc == 0), stop=(dc == 7))
                        return ins
                    P.op('pe', f, reads=uT_keys([1 + qb]) + ["WIgq"], writes=["psC"])

                def s_ge(qb=qb):
                    P.op('dve', lambda e: e.tensor_copy(out=SGb[par][:, qb * 256: qb * 256 + G * 128], in_=psC[:, 0:G * 128]),
                         reads=["psC"], writes=[("SGb", par, qb)])
                steps.append((s_g, 0))
                steps.append((s_ge, 1))
            return steps

        def prepB_parts(blocks, acteng):
            wh = 1 if acteng else None

            def p_qa():
                for c in range(4):
                    pbk = next_pbank()
                    proj_fm(WIs, 1280, c * 128, 1, 4, pbk, reads_extra=["WIs"])
                    evac_copy(QAT[:, c * 512:(c + 1) * 512], psA[:, pbk * 512:(pbk + 1) * 512],
                              reads=[bankA(pbk)], writes=[("QAT", c)], scale=0.125, which=wh)

            def p_ga(qbs):
                for qb in qbs:
                    pbk = next_pbank()
                    proj_tm(WIs, 1280, 512, 512, 1 + qb, pbk, reads_extra=["WIs"])
                    P.op('act', lambda e, qb=qb, pbk=pbk: e.activation(
                        out=SGa[:, qb * 512:(qb + 1) * 512], in_=psA[:, pbk * 512:(pbk + 1) * 512], func=AF.Silu),
                        reads=[bankA(pbk)], writes=[("SGa", qb)])

            def p_kv():
                bs0 = blocks[0][0]
                nbk = len(blocks)
                pieces = []
                s0 = bs0
                rem = nbk
                while rem > 0:
                    n = min(4, rem)
                    pieces.append((s0, n))
                    s0 += n
                    rem -= n
                for (s0, n) in pieces:
                    pbk = next_pbank()
                    proj_fm(WIs, 1280, 1024, s0, n, pbk, reads_extra=["WIs"])
                    evac_copy(KAT[:, s0 * 128:(s0 + n) * 128], psA[:, pbk * 512: pbk * 512 + n * 128],
                              reads=[bankA(pbk)], writes=["KAT"], which=wh)
                for bs, lb in blocks:
                    pbk = next_pbank()
                    proj_tm(WIs, 1280, 1152, 128, bs, pbk, reads_extra=["WIs"])
                    eng = 'act' if acteng else 'dve'
                    P.op(eng, lambda e, bs=bs, pbk=pbk: (e.copy if acteng else e.tensor_copy)(
                        out=dap(VAW, bs * 132, [[VAW_pitch, 128], [66, 2], [1, 64]]),
                        in_=dap(psA, pbk * 512, [[2048, 128], [64, 2], [1, 64]])),
                        reads=[bankA(pbk), "VAW_all"], writes=[("VAW", bs)])
            return [p_qa, lambda: p_ga([0, 1]), lambda: p_ga([2, 3]), p_kv]

        def bg_drain(bg):
            while bg:
                fn, _ = bg.pop(0)
                fn()

        def q_phase(x_d, y_d, b, S_kv, NB, t, n_qt, is_sample, heads, final, gi, groups):
            G = len(heads)
            par = t % 2
            blocks = tile_blocks(t, n_qt, NB, is_sample, final)
            has_left = any(bs == 0 for bs, _ in blocks)
            has_right = any(bs == 5 for bs, _ in blocks)
            if t == 0:
                make_uT_list(x_d, blocks, b)
                for hl in range(G):
                    pbk = next_pbank()
                    proj_fm(WIg, 1024, hl * 128, 1, 4, pbk, reads_extra=["WIgq"])
                    evac_copy(QT[par][:, hl * 512:(hl + 1) * 512], psA[:, pbk * 512:(pbk + 1) * 512],
                              reads=[bankA(pbk)], writes=[("QT", par, hl)], scale=0.125)
                for qb in range(4):
                    pbk = next_pbank()
                    proj_tm(WIg, 1024, 768, G * 128, 1 + qb, pbk, reads_extra=["WIgq"])
                    evac_copy(SGb[par][:, qb * 256: qb * 256 + G * 128], psA[:, pbk * 512: pbk * 512 + G * 128],
                              reads=[bankA(pbk)], writes=[("SGb", par, qb)])

            if final:
                if t == 0:
                    for part in prepB_parts(blocks, acteng=False):
                        part()
                window_attn(t, n_qt, is_sample, blocks, has_left, has_right)

            bg = []
            prep = prepA_steps(x_d, b, t + 1, n_qt, NB, is_sample, final, G) if t + 1 < n_qt else []
            pend = state.get('pending_post') or []
            state['pending_post'] = None
            if prep and pend:
                bg = [(prep[0][0], 0)] + pend + prep[1:]
            else:
                bg = pend + prep
            for hl, h in enumerate(heads):
                diff_attn(S_kv, NB, t, n_qt, is_sample, hl, h, G, bg)
            bg_drain(bg)
            if final:
                state['win_tail']()
            post = post_diff_steps(t, heads, G, final, gi, groups, n_qt, fg=(final or t == n_qt - 1))
            if final or t == n_qt - 1:
                bg_drain(post)
            else:
                state['pending_post'] = post
            if dbg and final and (not is_sample) and t == 0:
                P.op('sp', lambda e: e.dma_start(out=dbg_ot.ap(), in_=OT[:]),
                     reads=[("OT", fc) for fc in range(6)], writes=["dbg1"], dma='dbg')
                P.op('sp', lambda e: e.dma_start(out=dbg_car.ap()[:, 0:512], in_=CAR[:, 0:512]),
                     reads=[("CAR", 0, 0)], writes=["dbg2"], dma='dbg')
                P.op('sp', lambda e: e.dma_start(out=dbg_car.ap()[:, 512:1024], in_=CAR[:, 4096:4608]),
                     reads=[("CAR", 1, 0)], writes=["dbg3"], dma='dbg')
                P.op('sp', lambda e: e.dma_start(out=dbg_c.ap()[:, 0:16], in_=s1p[:]), writes=["dbg4"], dma='dbg')
                P.op('sp', lambda e: e.dma_start(out=dbg_c.ap()[:, 16:32], in_=shf[:]), writes=["dbg5"], dma='dbg')
                P.op('sp', lambda e: e.dma_start(out=dbg_c.ap()[:, 32:33], in_=neglam[:], allow_slow_non_contiguous=True), writes=["dbg6"], dma='dbg')
                P.op('sp', lambda e: e.dma_start(out=dbg_c.ap()[:, 33:41], in_=expsink[:]), writes=["dbg7"], dma='dbg')
                P.op('sp', lambda e: e.dma_start(out=dbg_c.ap()[:, 41:49], in_=cfar[:]), writes=["dbg8"], dma='dbg')
                P.op('sp', lambda e: e.dma_start(out=dbg_sg.ap()[:, 0:2048], in_=SGa[:]),
                     reads=[("SGa", qb) for qb in range(4)], writes=["dbg9"], dma='dbg')
                P.op('sp', lambda e: e.dma_start(out=dbg_sg.ap()[:, 2048:3072], in_=SGb[0][:]),
                     reads=[("SGb", 0, qb) for qb in range(4)], writes=["dbg10"], dma='dbg')
            if final:
                nxt = None
                if t + 1 < n_qt:
                    nxt = prepB_parts(tile_blocks(t + 1, n_qt, NB, is_sample, final), acteng=True)
                finalize(x_d, y_d, b, t, n_qt, heads, gi, groups, nxt)

        VAW_pitch = 6 * 2 * 66

        def window_attn(t, n_qt, is_sample, blocks, has_left, has_right):
            present = {bs for bs, _ in blocks}
            wrapL = is_sample and t == 0
            wrapR = is_sample and t == n_qt - 1
            def win_chunk(c):
                par = c % 2
                if par == 0:
                    accU = [(psB, 0, bankB(0)), (psB, 512, bankB(1))]
                else:
                    accU = [(psB, 1024, bankB(2)), (psC, 0, "psC")]
                jbs = [jb for jb in range(6) if jb in present]
                first = [True, True]

                def geom(jb):
                    qlo = max(0, jb - 2)
                    qhi = min(3, jb)
                    nq = qhi - qlo + 1
                    return qlo, nq, nq * 128, (qlo - (jb - 2)) * 128

                def qk(jb, slot, c=c):
                    qlo, nq, N, so = geom(jb)
                    use_wrap = (wrapL and jb == 0) or (wrapR and jb == 5)

                    def f(e):
                        ins = None
                        for u in range(2):
                            e.matmul(out=psA[:, (slot * 2 + u) * 512: (slot * 2 + u) * 512 + N],
                                     lhsT=KAT[u * 64:(u + 1) * 64, jb * 128:(jb + 1) * 128],
                                     rhs=QAT[u * 64:(u + 1) * 64, c * 512 + qlo * 128: c * 512 + qlo * 128 + N],
                                     start=True, stop=False)
                        for u in range(2):
                            hp = 2 * c + u
                            if use_wrap:
                                wo = 0 if jb == 0 else 1024
                                rhs = BXW[:, wo + hp * 128: wo + hp * 128 + 128]
                            else:
                                rhs = BW[:, hp * 384 + so: hp * 384 + so + N]
                            ins = e.matmul(out=psA[:, (slot * 2 + u) * 512: (slot * 2 + u) * 512 + N],
                                           lhsT=identb[:], rhs=rhs, start=False, stop=True)
                        return ins
                    rd = ["KAT", ("QAT", c), identb.name]
                    rd += [] if use_wrap else [("BW", 2 * c), ("BW", 2 * c + 1)]
                    P.op('pe', f, reads=rd, writes=[bankA(slot * 2), bankA(slot * 2 + 1)])

                def ex(jb, slot):
                    qlo, nq, N, so = geom(jb)
                    P.op('act', lambda e: e.activation(
                        out=dap(PT[slot], 0, [[1024, 128], [384, 2], [1, N]]),
                        in_=dap(psA, slot * 1024, [[2048, 128], [512, 2], [1, N]]), func=AF.Exp),
                        reads=[bankA(slot * 2), bankA(slot * 2 + 1)], writes=[("PT", slot)])

                def pv(jb, slot):
                    qlo, nq, N, so = geom(jb)
                    sts = []
                    for u in range(2):
                        for qi in range(nq):
                            sts.append(first[u])
                            first[u] = False
                    sts = tuple(sts)

                    def f(e):
                        ins = None
                        k = 0
                        for u in range(2):
                            pt_, base, _ = accU[u]
                            for qi in range(nq):
                                qb = qlo + qi
                                off = base + qb * 66
                                ins = e.matmul(out=pt_[:, off: off + 65],
                                               lhsT=PT[slot][:, u * 384 + qi * 128: u * 384 + (qi + 1) * 128],
                                               rhs=VAW[:, (jb * 2 + u) * 66: (jb * 2 + u) * 66 + 65],
                                               start=sts[k], stop=True, skip_group_check=True)
                                k += 1
                        return ins
                    P.op('pe', f, reads=[("PT", slot), ("VAW", jb)], writes=[accU[0][2], accU[1][2]])

                qk(jbs[0], 0)
                for idx, jb in enumerate(jbs):
                    if idx + 1 < len(jbs):
                        qk(jbs[idx + 1], (idx + 1) % 2)
                    ex(jb, idx % 2)
                    pv(jb, idx % 2)
                for u in range(2):
                    pt_, base, bk = accU[u]
                    pitch = 1536 if pt_ is psB else 512
                    P.op('dve', lambda e, u=u, pt_=pt_, base=base, pitch=pitch, c=c, par=par: e.tensor_scalar(
                        out=den[:, par * 8 + u * 4: par * 8 + u * 4 + 4], in0=dap(pt_, base + 64, [[pitch, 128], [66, 4]]),
                        scalar1=expsink[:, 2 * c + u:2 * c + u + 1], scalar2=None, op0=ALU.add),
                        reads=[bk], writes=[("den", par, u)])
                P.op('dve', lambda e, par=par: e.reciprocal(out=rec[:, par * 8: par * 8 + 8], in_=den[:, par * 8: par * 8 + 8]),
                     reads=[("den", par, 0), ("den", par, 1)], writes=[("recw", par)])
                for u in range(2):
                    pt_, base, bk = accU[u]
                    for qb in range(4):
                        a = u * 4 + qb
                        off = base + qb * 66
                        hp = 2 * c + u
                        P.op('dve', lambda e, a=a, off=off, hp=hp, qb=qb, pt_=pt_, par=par: e.scalar_tensor_tensor(
                            out=OAg[:, qb * 512 + hp * 64: qb * 512 + hp * 64 + 64],
                            in0=pt_[:, off: off + 64], scalar=rec[:, par * 8 + a: par * 8 + a + 1],
                            in1=SGa[:, qb * 512 + hp * 64: qb * 512 + hp * 64 + 64],
                            op0=ALU.mult, op1=ALU.mult),
                            reads=[bk, ("recw", par), ("SGa", qb)], writes=[("OAg", qb, hp)])

            for c in range(4):
                win_chunk(c)
            def win_tail():
                for fc in range(4):
                    def trw(e, fc=fc):
                        ins = None
                        for qb in range(4):
                            ins = e.transpose(out=psC_b[:, qb * 128:(qb + 1) * 128],
                                              in_=OAg[:, qb * 512 + fc * 128: qb * 512 + (fc + 1) * 128],
                                              identity=identb[:])
                        return ins
                    P.op('pe', trw, reads=[("OAg", qb, hp) for qb in range(4) for hp in (2 * fc, 2 * fc + 1)] + [identb.name],
                         writes=["psC"])
                    P.op('dve', lambda e, fc=fc: e.tensor_copy(out=OT[:, fc * 512:(fc + 1) * 512], in_=psC_b[:, 0:512]),
                         reads=["psC"], writes=[("OT", fc)])
            state['win_tail'] = win_tail

        def diff_attn(S_kv, NB, t, n_qt, is_sample, hl, h, G, bg):
            par = t % 2
            kinds = []
            for i in range(NB):
                own = (not is_sample) or i < 16
                if own:
                    d = 4 * t - i
                    if -4 <= d <= 1:
                        kinds.append(('near', d))
                    elif d >= 2:
                        kinds.append(('far', cfar[:, h:h + 1], cfar.name))
                    else:
                        kinds.append(('far', cfar[:, 4 + h:5 + h], cfar.name))
                else:
                    if i == 16 and t == 3:
                        kinds.append(('wrapR',))
                    elif i == NB - 1 and t == 0:
                        kinds.append(('wrapL',))
                    else:
                        kinds.append(('far', farS[:, h * 64 + i: h * 64 + i + 1], ("farS", h)))
            if is_sample and t == 0:
                P.op('dve', lambda e: e.tensor_scalar(
                    out=BXf[:, 0:512], in0=BD[:, hl * MD + 640: hl * MD + 1152],
                    scalar1=cfar[:, 4 + h:5 + h], scalar2=flags[:, 0:1], op0=ALU.subtract, op1=ALU.mult),
                    reads=[("BD", hl), cfar.name, flags.name], writes=["BXf"])
                P.op('dve', lambda e: e.tensor_scalar(
                    out=BX[:, 0:512], in0=BXf[:, 0:512], scalar1=cfar[:, 4 + h:5 + h], scalar2=None, op0=ALU.add),
                    reads=["BXf"], writes=["BX"])
            if is_sample and t == 3:
                P.op('dve', lambda e: e.tensor_scalar(
                    out=BXf[:, 512:1024], in0=BD[:, hl * MD: hl * MD + 512],
                    scalar1=cfar[:, h:h + 1], scalar2=flags[:, 1:2], op0=ALU.subtract, op1=ALU.mult),
                    reads=[("BD", hl), cfar.name, flags.name], writes=["BXf"])
                P.op('dve', lambda e: e.tensor_scalar(
                    out=BX[:, 512:1024], in0=BXf[:, 512:1024], scalar1=cfar[:, h:h + 1], scalar2=None, op0=ALU.add),
                    reads=["BXf"], writes=["BX"])

            def qk(i):
                kind = kinds[i]
                slot = i % 2

                def f(e):
                    ins = None
                    near = kind[0] != 'far'
                    for m in range(2):
                        ins = e.matmul(out=psA[:, (slot * 2 + m) * 512:(slot * 2 + m + 1) * 512],
                                       lhsT=KT[m * 64:(m + 1) * 64, hl * S_kv + i * 128: hl * S_kv + (i + 1) * 128],
                                       rhs=QT[par][m * 64:(m + 1) * 64, hl * 512:(hl + 1) * 512],
                                       start=True, stop=not near)
                    if near:
                        if kind[0] == 'near':
                            o = hl * MD + 128 * (kind[1] + 4)
                            rhs = BD[:, o:o + 512]
                        elif kind[0] == 'wrapL':
                            rhs = BX[:, 0:512]
                        else:
                            rhs = BX[:, 512:1024]
                        for m in range(2):
                            ins = e.matmul(out=psA[:, (slot * 2 + m) * 512:(slot * 2 + m + 1) * 512],
                                           lhsT=identb[:], rhs=rhs, start=False, stop=True)
                    return ins
                rd = [("KT", hl, i), ("QT", par, hl)]
                if kind[0] == 'near':
                    rd += [("BD", hl), identb.name]
                elif kind[0] != 'far':
                    rd += ["BX", identb.name]
                P.op('pe', f, reads=rd, writes=[bankA(slot * 2), bankA(slot * 2 + 1)])

            def ex(i):
                kind = kinds[i]
                slot = i % 2
                if kind[0] == 'far':
                    bias_ap, bkey = kind[1], kind[2]
                else:
                    bias_ap, bkey = zero_c[:], zero_c.name
                P.op('act', lambda e: e.activation(out=PT[slot][:], in_=psA[:, slot * 1024:(slot + 1) * 1024],
                                                   func=AF.Exp, bias=bias_ap, scale=1.0),
                     reads=[bankA(slot * 2), bankA(slot * 2 + 1), bkey], writes=[("PT", slot)])

            def pv(i):
                slot = i % 2

                def f(e):
                    ins = None
                    for m in range(2):
                        for qb in range(4):
                            a = m * 4 + qb
                            bank = a // 3
                            off = bank * 512 + (a % 3) * 130
                            vo = (i * G + hl) * 130
                            ins = e.matmul(out=psB[:, off:off + 129],
                                           lhsT=PT[slot][:, m * 512 + qb * 128: m * 512 + (qb + 1) * 128],
                                           rhs=VA[:, vo:vo + 129],
                                           start=(i == 0 and a % 3 == 0), stop=(i == NB - 1), skip_group_check=True)
                    return ins
                P.op('pe', f, reads=[("PT", slot), ("VA", hl, i)], writes=[bankB(0), bankB(1), bankB(2)])

            qk(0)
            qk(1)
            cool = 2
            for i in range(NB):
                ex(i)
                if i + 2 < NB:
                    qk(i + 2)
                pv(i)
                if bg:
                    if cool <= 0:
                        fn, cool = bg.pop(0)
                        fn()
                    else:
                        cool -= 1
            for bank in range(3):
                na = 3 if bank < 2 else 2
                P.op('dve', lambda e, bank=bank, na=na: e.reciprocal(
                    out=rec[:, bank * 3: bank * 3 + na], in_=dap(psB, bank * 512 + 128, [[1536, 128], [130, na]])),
                    reads=[bankB(bank)], writes=[("recd", bank)])
            P.op('dve', lambda e: e.tensor_scalar(out=rec[:, 8:12], in0=rec[:, 4:8], scalar1=neglam[:, 0:1], scalar2=None, op0=ALU.mult),
                 reads=[("recd", 1), ("recd", 2), neglam.name], writes=["recn"])
            for qb in range(4):
                a0 = qb
                o0 = (a0 // 3) * 512 + (a0 % 3) * 130
                oo = (qb * 2 + hl) * 128
                P.op('dve', lambda e, a0=a0, o0=o0, oo=oo: e.tensor_scalar(
                    out=OBf[:, oo:oo + 128], in0=psB[:, o0:o0 + 128], scalar1=rec[:, a0:a0 + 1], scalar2=None, op0=ALU.mult),
                    reads=[bankB(a0 // 3), ("recd", a0 // 3)], writes=[("OBf", qb, hl)])
            for qb in range(4):
                a1 = 4 + qb
                o1 = (a1 // 3) * 512 + (a1 % 3) * 130
                P.op('dve', lambda e, qb=qb, o1=o1: e.tensor_scalar(
                    out=BXf[:, qb * 128:(qb + 1) * 128], in0=psB[:, o1:o1 + 128], scalar1=rec[:, 8 + qb:9 + qb], scalar2=None, op0=ALU.mult),
                    reads=[bankB(a1 // 3), "recn"], writes=["BXf"])
            for qb in range(4):
                oo = (qb * 2 + hl) * 128
                P.op('dve', lambda e, qb=qb, oo=oo: e.tensor_tensor(
                    out=OBf[:, oo:oo + 128], in0=OBf[:, oo:oo + 128], in1=BXf[:, qb * 128:(qb + 1) * 128], op=ALU.add),
                    reads=[("OBf", qb, hl), "BXf"], writes=[("OBf", qb, hl)])
                P.op('dve', lambda e, qb=qb, oo=oo: e.tensor_tensor(
                    out=sqj[:], in0=OBf[:, oo:oo + 128], in1=OBf[:, oo:oo + 128], op=ALU.mult),
                    reads=[("OBf", qb, hl)], writes=["sqj"])
                P.op('dve', lambda e, qb=qb: e.reduce_sum(out=ssq[:, qb * 2 + hl: qb * 2 + hl + 1], in_=sqj[:], axis=mybir.AxisListType.X),
                     reads=["sqj"], writes=[("ssq", qb, hl)])

        def post_diff_steps(t, heads, G, final, gi, groups, n_qt, fg=False):
            S_q = n_qt * 512
            par = t % 2
            steps = []
            sgk = [("SGb", par, qb) for qb in range(4)]

            def s1():
                P.op('act', lambda e: e.activation(out=BXf[:], in_=SGb[par][:], func=AF.Exp, scale=-1.0),
                     reads=sgk, writes=["BXf"])
            def s1f():
                P.op('act', lambda e: e.activation(out=SGb[par][:], in_=SGb[par][:], func=AF.Silu), reads=sgk, writes=sgk)
            if fg:
                steps.append((s1f, 0))
            else:
                steps.append((s1, 1))

            def s2():
                P.op('dve', lambda e: e.tensor_scalar(out=BXf[:], in0=BXf[:], scalar1=1.0, scalar2=None, op0=ALU.add),
                     reads=["BXf"], writes=["BXf"])
                P.op('dve', lambda e: e.reciprocal(out=BXf[:], in_=BXf[:]), reads=["BXf"], writes=["BXf"])
                P.op('dve', lambda e: e.tensor_tensor(out=SGb[par][:], in0=SGb[par][:], in1=BXf[:], op=ALU.mult),
                     reads=sgk + ["BXf"], writes=sgk)
            if not fg:
                steps.append((s2, 2))

            def s3():
                P.op('act', lambda e: e.activation(out=rstd[:, 0:8], in_=ssq[:, 0:8], func=AF.Sqrt, bias=eps_sub[:], scale=1.0 / 128.0),
                     reads=[("ssq", qb, hl) for qb in range(4) for hl in range(G)] + [eps_sub.name], writes=["rstd_a"])
                P.op('dve', lambda e: e.reciprocal(out=rstd[:, 0:8], in_=rstd[:, 0:8]), reads=["rstd_a"], writes=["rstd"])
            steps.append((s3, 2))
            for qb in range(4):
                def s4(qb=qb):
                    for hl in range(G):
                        oo = (qb * 2 + hl) * 128
                        P.op('dve', lambda e, hl=hl, oo=oo: e.scalar_tensor_tensor(
                            out=OBf[:, oo:oo + 128], in0=OBf[:, oo:oo + 128], scalar=rstd[:, qb * 2 + hl: qb * 2 + hl + 1],
                            in1=subw_bc[:], op0=ALU.mult, op1=ALU.mult),
                            reads=[("OBf", qb, hl), "rstd", subw_bc.name], writes=[("OBf", qb, hl)])
                        P.op('pool', lambda e, hl=hl, oo=oo: e.tensor_tensor(
                            out=OBn[:, oo:oo + 128], in0=OBf[:, oo:oo + 128],
                            in1=SGb[par][:, qb * 256 + hl * 128: qb * 256 + (hl + 1) * 128], op=ALU.mult),
                            reads=[("OBf", qb, hl), ("SGb", par, qb)], writes=[("OBn", qb, hl)])
                steps.append((s4, 2 if qb == 3 else 1))
            for hl, h in enumerate(heads):
                def s5(hl=hl):
                    def trd(e):
                        ins = None
                        for qb in range(4):
                            oo = (qb * 2 + hl) * 128
                            ins = e.transpose(out=psC_b[:, qb * 128:(qb + 1) * 128], in_=OBn[:, oo:oo + 128], identity=identb[:])
                        return ins
                    P.op('pe', trd, reads=[("OBn", qb, hl) for qb in range(4)] + [identb.name], writes=["psC"])

                def s6(hl=hl):
                    if final:
                        P.op('dve', lambda e: e.tensor_copy(out=OT[:, (4 + hl) * 512:(5 + hl) * 512], in_=psC_b[:, 0:512]),
                             reads=["psC"], writes=[("OT", 4 + hl)])
                    else:
                        ci = sum(len(g) for g in groups[:gi]) + hl
                        o = ci * S_q + t * 512
                        P.op('dve', lambda e: e.tensor_copy(out=CAR[:, o:o + 512], in_=psC_b[:, 0:512]),
                             reads=["psC"], writes=[("CAR", ci, t)])
                steps.append((s5, 1))
                steps.append((s6, 1))
            return steps

        def finalize(x_d, y_d, b, t, n_qt, heads, gi, groups, nxt_parts=None):
            S_q = n_qt * 512
            n_car = sum(len(g) for g in groups[:gi])
            G = len(heads)
            fslots = [0, 1]

            def xload(qb):
                xs_ = fslots[qb % 2]
                row0 = (4 * t + qb) * 128
                P.op('sp', lambda e: e.dma_start(out=FB[xs_][:], in_=x_d.ap()[row0:row0 + 128, :]),
                     writes=[("FB", xs_)], dma=xkeys[xs_])

            xload(0)
            xload(1)
            for qb in range(4):
                fin_block(x_d, y_d, b, t, qb, S_q, n_car, G, fslots[qb % 2])
                if qb + 2 < 4:
                    xload(qb + 2)
                if nxt_parts:
                    nxt_parts.pop(0)()

        def fin_block(x_d, y_d, b, t, qb, S_q, n_car, G, xs_):
            row0 = (4 * t + qb) * 128
            ys_ = state['yslot'] % 2
            state['yslot'] += 1
            for n in range(2):
                def om(e, n=n):
                    ins = None
                    for fc in range(8):
                        if fc < 4:
                            lhsT = OT[:, fc * 512 + qb * 128: fc * 512 + (qb + 1) * 128]
                        elif fc - 4 < n_car:
                            o = (fc - 4) * S_q + t * 512 + qb * 128
                            lhsT = CAR[:, o:o + 128]
                        else:
                            o = (4 + fc - 4 - n_car) * 512 + qb * 128
                            lhsT = OT[:, o:o + 128]
                        ins = e.matmul(out=psB[:, n * 512:(n + 1) * 512], lhsT=lhsT,
                                       rhs=WO[:, fc * 1024 + n * 512: fc * 1024 + (n + 1) * 512],
                                       start=(fc == 0), stop=(fc == 7))
                    return ins
                rd = [("OT", fc) for fc in range(4 + G)] + [("CAR", ci, t) for ci in range(n_car)] + ["WO"]
                P.op('pe', om, reads=rd, writes=[bankB(n)])
            Z = FB[xs_]
            zk = ("FB", xs_)
            mo = qb * 2
            P.op('dve', lambda e: e.tensor_tensor(out=BXf[:], in0=psB[:, 0:1024], in1=gate_bc[:, b * 1024:(b + 1) * 1024], op=ALU.mult),
                 reads=[bankB(0), bankB(1), ("gate_bc", b, 0), ("gate_bc", b, 1)], writes=["BXf"])
            P.op('dve', lambda e: e.scalar_tensor_tensor(out=Z[:], in0=Z[:], scalar=ALPHA, in1=BXf[:], op0=ALU.mult, op1=ALU.add),
                 reads=[zk, "BXf"], writes=[zk])
            P.op('dve', lambda e: e.bn_stats(out=bnst[:, 0:6], in_=Z[:, 0:512]), reads=[zk], writes=["bnst0"])
            P.op('dve', lambda e: e.bn_stats(out=bnst[:, 6:12], in_=Z[:, 512:1024]), reads=[zk], writes=["bnst1"])
            P.op('dve', lambda e: e.bn_aggr(out=mv[:, mo:mo + 2], in_=bnst[:]), reads=["bnst0", "bnst1"], writes=[("mv", qb)])
            P.op('act', lambda e: e.activation(out=lrs[:, qb:qb + 1], in_=mv[:, mo + 1:mo + 2], func=AF.Sqrt, bias=eps_ln[:], scale=1.0),
                 reads=[("mv", qb), eps_ln.name], writes=[("lrs_a", qb)])
            P.op('dve', lambda e: e.reciprocal(out=lrs[:, qb:qb + 1], in_=lrs[:, qb:qb + 1]), reads=[("lrs_a", qb)], writes=[("lrs", qb)])
            P.op('dve', lambda e: e.scalar_tensor_tensor(out=Z[:], in0=Z[:], scalar=mv[:, mo:mo + 1], in1=lng_bc[:],
                                                         op0=ALU.subtract, op1=ALU.mult),
                 reads=[zk, ("mv", qb)], writes=[zk])
            P.op('dve', lambda e: e.scalar_tensor_tensor(out=Z[:], in0=Z[:], scalar=lrs[:, qb:qb + 1], in1=lnb_bc[:],
                                                         op0=ALU.mult, op1=ALU.add),
                 reads=[zk, ("lrs", qb)], writes=[zk])
            P.op('sp', lambda e: e.dma_start(out=y_d.ap()[row0:row0 + 128, :], in_=Z[:]),
                 reads=[zk], writes=[("yout", ys_)], dma=['y0', 'y1'][ys_])

        try:
            if kstop == 1:
                raise _Stop()
            run_job(xp_d, yp_d, 0, SP_, 8, False, [[0, 1], [2, 3]], [0])
            run_job(xs_d, ys_d, 1, SS_, 4, True, [[0], [1], [2], [3]], None)
        except _Stop:
            pass
        P.op('sp', lambda e: None, reads=[("yout", 0), ("yout", 1)] + (["dbg%d" % i for i in range(1, 11)] if dbg else []))

        with nc.Block() as blk2:
            P.emit(blk2)
    return nc


_NC_CACHE = {}


def _bucket_table():
    import math
    import jax
    import jax.numpy as jnp
    try:
        dev = jax.devices('cpu')[0]
    except Exception:
        dev = None
    def run():
        rel = jnp.arange(-1400, 1401, dtype=jnp.int32)
        sign = jnp.where(rel > 0, 16, 0)
        n = jnp.abs(rel)
        nf = jnp.maximum(n, 1).astype(jnp.float32)
        large = 8 + (jnp.log(nf / 8) / math.log(128 / 8) * 8).astype(jnp.int32)
        large = jnp.minimum(large, 15)
        return np.asarray((sign + jnp.where(n < 8, n, large)).astype(jnp.int32))
    if dev is not None:
        with jax.default_device(dev):
            return run()
    return run()


def kernel(**inp):
    f = lambda k: np.ascontiguousarray(np.asarray(inp[k], dtype=np.float32))
    x_prompt = f('x_prompt'); x_sample = f('x_sample')
    c_prompt = f('c_prompt'); c_sample = f('c_sample')
    w_in = f('w_in')[0]; w_out = f('w_out')[0]; w_ada = f('w_ada')[0]; b_ada = f('b_ada')[0]
    ln_g = f('ln_g'); ln_b = f('ln_b'); sink = f('attn_sink')[0]
    rel_bias = f('rel_bias')
    subw = f('subln_w')
    lamv = np.concatenate([f('lambda_q1')[0], f('lambda_k1')[0], f('lambda_q2')[0], f('lambda_k2')[0]])[None, :]

    colperm = np.concatenate([np.arange(64) + 64 * h for h in PERM])
    w_in_l = w_in.copy()
    w_in_l[:, 0:512] = w_in[:, 0:512][:, colperm]
    w_in_l[:, 768:1280] = w_in[:, 768:1280][:, colperm]
    w_out_l = w_out.copy()
    w_out_l[0:512, :] = w_out[0:512, :][colperm, :]
    sinkp = sink[PERM][None, :]
    bt = _bucket_table()

    def bk(r):
        return bt[r + 1400]
    kk = np.arange(128)[:, None]
    mm = np.arange(MD)[None, :]
    bd_idx = bk(kk - mm + 512)
    bd_strip = np.stack([rel_bias[bd_idx, 8 + h] for h in range(4)], axis=1).reshape(128, 4 * MD)
    qq = np.arange(384)[None, :]
    relw = kk - qq + 128
    bw_idx = bk(relw)
    bw_strip = np.stack([rel_bias[bw_idx, PERM[hp]] for hp in range(8)], axis=1).reshape(128, 8 * 384)
    maskw = np.where(np.abs(relw) <= 128, 0.0, -BIG).astype(np.float32)
    relfar = np.concatenate([rel_bias[15, 8:12], rel_bias[31, 8:12]])[None, :]
    identf = np.eye(128, dtype=np.float32)

    import os
    dbg = bool(os.environ.get('KDBG'))
    kstop = int(os.environ.get('KSTOP', '0'))
    key = 'nc%d_%d' % (dbg, kstop)
    if key not in _NC_CACHE:
        _NC_CACHE[key] = build_program(dbg, kstop)
    nc = _NC_CACHE[key]

    in_maps = []
    for c in range(NCORES):
        s, j = c // 4, c % 4
        xs = np.ascontiguousarray(np.roll(x_sample[s], -SQS * j, axis=0))
        cc = np.stack([c_prompt[c], c_sample[s]], axis=0)
        cT = np.ascontiguousarray(cc.reshape(2, 8, 128).transpose(2, 1, 0).reshape(128, 16))
        fl = np.zeros((1, 66), np.float32)
        fl[0, 0] = 1.0 if j > 0 else 0.0
        fl[0, 1] = 1.0 if j < 3 else 0.0
        for i in range(64):
            fl[0, 2 + i] = 1.0 if i >= 64 - 16 * j else 0.0
        in_maps.append({
            "xp": x_prompt[c], "xs": xs, "cT": cT,
            "w_in": w_in_l, "w_out": w_out_l, "w_ada": w_ada,
            "bcol": np.ascontiguousarray(b_ada.reshape(24, 128).T),
            "bgate": np.ascontiguousarray(b_ada[None, 2048:3072]),
            "ln_g": ln_g, "ln_b": ln_b, "sinkp": np.ascontiguousarray(sinkp),
            "lamv": np.ascontiguousarray(lamv), "subw": subw,
            "relfar": np.ascontiguousarray(relfar),
            "bd_strip": np.ascontiguousarray(bd_strip.astype(np.float32)),
            "bw_strip": np.ascontiguousarray(bw_strip.astype(np.float32)),
            "maskw": maskw, "flags": fl, "identf": identf,
        })
    res = run_bass_kernel_spmd(nc, in_maps, core_ids=list(range(NCORES)))
    y_prompt = np.empty((8, SP_, D), np.float32)
    y_sample = np.empty((2, SS_, D), np.float32)
    for c in range(NCORES):
        r = res.results[c]
        s, j = c // 4, c % 4
        y_prompt[c] = np.asarray(r["yp"], np.float32)
        y_sample[s, SQS * j: SQS * (j + 1)] = np.asarray(r["ys"], np.float32)
    return (y_prompt, y_sample)
```
